# Optimizing a Trainium2 kernel written in Bass

```python
import jax, jax.numpy as jnp
from jax import lax
import numpy as np

D_MODEL = 1024
BATCH = 2
SEQ = 8192
DEPTH = 2

N_MEM = 256
POOL_WINDOWS = (2, 4, 8, 16)
POOL_GROUPS = 4
POOL_W = D_MODEL // 2
POOL_GW = POOL_W // POOL_GROUPS
LRU_W = D_MODEL
LRU_HEADS = 8
LRU_HD = LRU_W // LRU_HEADS
CONV_W = 4
LRU_C = 8.0
FOX_HEADS = 8
FOX_HD = 64
FOX_W = FOX_HEADS * FOX_HD
Q_BLOCK = 128
X_HEADS = 4
X_HD = D_MODEL // X_HEADS
D_FF = ((8 * D_MODEL // 3 + 127) // 128) * 128
N_BRANCH = 3
EPS = 1e-6
IN_SIZES = (POOL_W, LRU_W, LRU_W, FOX_W, FOX_W, FOX_W, FOX_HEADS, N_BRANCH * D_MODEL)
IN_W = sum(IN_SIZES)

kernel_name = "hybrid_pool_rglru_fox_macaron_block"


def rmsnorm(x, g):
    xf = x.astype(jnp.float32)
    y = xf * lax.rsqrt(jnp.mean(xf * xf, axis=-1, keepdims=True) + EPS)
    return (y * g.astype(jnp.float32)).astype(x.dtype)


def swiglu(h, w_in, w_out):
    a, b = jnp.split(h @ w_in, 2, axis=-1)
    return (jax.nn.silu(a) * b) @ w_out


def pool_mixer(xa, w_grp, scale):
    B, S, _ = xa.shape
    xf = xa.astype(jnp.float32)
    cs = jnp.pad(jnp.cumsum(xf, axis=1), ((0, 0), (1, 0), (0, 0)))
    pos = jnp.arange(1, S + 1, dtype=jnp.float32)
    outs = []
    for g, w in enumerate(POOL_WINDOWS):
        c = cs[:, :, g * POOL_GW:(g + 1) * POOL_GW]
        lo = jnp.pad(c[:, :S + 1 - w], ((0, 0), (w - 1, 0), (0, 0)))
        cnt = jnp.minimum(pos, float(w))[None, :, None]
        outs.append((c[:, 1:] - lo) / cnt)
    mean = jnp.concatenate(outs, axis=-1)
    d = (mean - xf).astype(xa.dtype).reshape(B, S, POOL_GROUPS, POOL_GW)
    y = jnp.einsum('bsgc,gcd->bsgd', d, w_grp).reshape(B, S, POOL_W)
    return y * scale


def causal_depthwise_conv(x, w, b):
    S = x.shape[1]
    xp = jnp.pad(x, ((0, 0), (CONV_W - 1, 0), (0, 0)))
    y = b
    for k in range(CONV_W):
        y = y + xp[:, k:k + S] * w[k]
    return y


def rglru(xb, w_a, b_a, w_x, b_x, lam):
    B, S, _ = xb.shape
    xh = xb.reshape(B, S, LRU_HEADS, LRU_HD)
    r = jax.nn.sigmoid(jnp.einsum('bshc,hcd->bshd', xh, w_a).reshape(B, S, LRU_W) + b_a)
    i = jax.nn.sigmoid(jnp.einsum('bshc,hcd->bshd', xh, w_x).reshape(B, S, LRU_W) + b_x)
    log_a = -LRU_C * r.astype(jnp.float32) * jax.nn.softplus(-lam.astype(jnp.float32))
    a = jnp.exp(log_a)
    mult = jnp.sqrt(-jnp.expm1(2.0 * log_a))
    u = mult * (i * xb).astype(jnp.float32)

    def combine(l, rr):
        a1, b1 = l
        a2, b2 = rr
        return a1 * a2, a2 * b1 + b2

    _, h = lax.associative_scan(combine, (a, u), axis=1)
    return h.astype(xb.dtype)


def forgetting_attention(q, k, v, logf):
    B, S, _ = q.shape
    nb = S // Q_BLOCK
    q = q.reshape(B, S, FOX_HEADS, FOX_HD).transpose(0, 2, 1, 3)
    k = k.reshape(B, S, FOX_HEADS, FOX_HD).transpose(0, 2, 1, 3)
    v = v.reshape(B, S, FOX_HEADS, FOX_HD).transpose(0, 2, 1, 3)
    c = jnp.cumsum(logf.astype(jnp.float32), axis=1).transpose(0, 2, 1)
    qb = q.reshape(B, FOX_HEADS, nb, Q_BLOCK, FOX_HD).transpose(2, 0, 1, 3, 4)
    cb = c.reshape(B, FOX_HEADS, nb, Q_BLOCK).transpose(2, 0, 1, 3)
    starts = jnp.arange(nb, dtype=jnp.int32) * Q_BLOCK
    kpos = jnp.arange(S, dtype=jnp.int32)
    scale = FOX_HD ** -0.5

    def block(args):
        qi, ci, start = args
        s = jnp.einsum('bhqd,bhkd->bhqk', qi, k).astype(jnp.float32) * scale
        s = s + ci[..., None] - c[:, :, None, :]
        qpos = start + jnp.arange(Q_BLOCK, dtype=jnp.int32)
        s = jnp.where(kpos[None, :] <= qpos[:, None], s, -jnp.inf)
        p = jax.nn.softmax(s, axis=-1).astype(v.dtype)
        return jnp.einsum('bhqk,bhkd->bhqd', p, v)

    o = lax.map(block, (qb, cb, starts))
    return o.transpose(1, 0, 3, 2, 4).reshape(B, S, FOX_W)


def memory_cross_attention(h, m, w_q, w_kv, w_o):
    B, S, _ = h.shape
    q = (h @ w_q).reshape(B, S, X_HEADS, X_HD)
    k, v = jnp.split(m @ w_kv, 2, axis=-1)
    k = k.reshape(B, -1, X_HEADS, X_HD)
    v = v.reshape(B, -1, X_HEADS, X_HD)
    s = jnp.einsum('bshd,bmhd->bhsm', q, k).astype(jnp.float32) * (X_HD ** -0.5)
    p = jax.nn.softmax(s, axis=-1).astype(v.dtype)
    o = jnp.einsum('bhsm,bmhd->bshd', p, v).reshape(B, S, D_MODEL)
    return o @ w_o


def setup_inputs(seed: int = 0) -> dict:
    key = jax.random.key(seed)
    ks = iter(jax.random.split(key, 64))
    f32 = jnp.float32

    def dense(shape, fan_in):
        return jax.random.normal(next(ks), shape, f32) * (fan_in ** -0.5)

    def gain(shape):
        return 1.0 + 0.02 * jax.random.normal(next(ks), shape, f32)

    def bias(shape, s=0.01):
        return s * jax.random.normal(next(ks), shape, f32)

    L, D = DEPTH, D_MODEL
    a0 = jax.random.uniform(next(ks), (L, LRU_W), f32, 0.9, 0.999)
    s0 = a0 ** (1.0 / LRU_C)
    lru_lambda = jnp.log(s0) - jnp.log1p(-s0)
    return {
        "x": jax.random.normal(next(ks), (BATCH, SEQ, D), f32),
        "mem": jax.random.normal(next(ks), (BATCH, N_MEM, D), f32),
        "g_ffn1": gain((L, D)),
        "w_ffn1_in": dense((L, D, 2 * D_FF), D),
        "w_ffn1_out": dense((L, D_FF, D), D_FF),
        "g_mix": gain((L, D)),
        "w_in": dense((L, D, IN_W), D),
        "b_f": 2.0 + 0.5 * jax.random.normal(next(ks), (L, FOX_HEADS), f32),
        "b_gate": bias((L, N_BRANCH * D)),
        "w_pool": dense((L, POOL_GROUPS, POOL_GW, POOL_GW), POOL_GW),
        "pool_scale": gain((L, POOL_W)),
        "w_up_a": dense((L, POOL_W, D), POOL_W),
        "conv_w": dense((L, CONV_W, LRU_W), CONV_W),
        "conv_b": bias((L, LRU_W)),
        "w_rg_a": dense((L, LRU_HEADS, LRU_HD, LRU_HD), LRU_HD),
        "b_rg_a": bias((L, LRU_W)),
        "w_rg_x": dense((L, LRU_HEADS, LRU_HD, LRU_HD), LRU_HD),
        "b_rg_x": bias((L, LRU_W)),
        "lru_lambda": lru_lambda,
        "w_up_b": dense((L, LRU_W, D), LRU_W),
        "w_up_c": dense((L, FOX_W, D), FOX_W),
        "w_o": dense((L, D, D), D),
        "g_cross": gain((L, D)),
        "g_mem": gain((L, D)),
        "w_xq": dense((L, D, D), D),
        "w_xkv": dense((L, D, 2 * D), D),
        "w_xo": dense((L, D, D), D),
        "g_ffn2": gain((L, D)),
        "w_ffn2_in": dense((L, D, 2 * D_FF), D),
        "w_ffn2_out": dense((L, D_FF, D), D_FF),
        "g_final": gain((D,)),
    }


def reference(x, mem, g_ffn1, w_ffn1_in, w_ffn1_out, g_mix, w_in, b_f, b_gate, w_pool, pool_scale,
              w_up_a, conv_w, conv_b, w_rg_a, b_rg_a, w_rg_x, b_rg_x, lru_lambda, w_up_b, w_up_c, w_o,
              g_cross, g_mem, w_xq, w_xkv, w_xo, g_ffn2, w_ffn2_in, w_ffn2_out, g_final):
    B, S, D = x.shape
    offs = []
    acc = 0
    for n in IN_SIZES[:-1]:
        acc += n
        offs.append(acc)
    for l in range(DEPTH):
        x = x + 0.5 * swiglu(rmsnorm(x, g_ffn1[l]), w_ffn1_in[l], w_ffn1_out[l])
        u = rmsnorm(x, g_mix[l])
        xa, xb, gb, q, k, v, fl, gl = jnp.split(u @ w_in[l], offs, axis=-1)
        y_a = pool_mixer(xa, w_pool[l], pool_scale[l]) @ w_up_a[l]
        xb = causal_depthwise_conv(xb, conv_w[l], conv_b[l])
        h_b = rglru(xb, w_rg_a[l], b_rg_a[l], w_rg_x[l], b_rg_x[l], lru_lambda[l])
        y_b = (h_b * jax.nn.gelu(gb)) @ w_up_b[l]
        logf = jax.nn.log_sigmoid((fl + b_f[l]).astype(jnp.float32))
        y_c = forgetting_attention(q, k, v, logf) @ w_up_c[l]
        g = jax.nn.sigmoid(gl + b_gate[l]).reshape(B, S, N_BRANCH, D)
        merged = g[:, :, 0] * y_a + g[:, :, 1] * y_b + g[:, :, 2] * y_c
        x = x + merged @ w_o[l]
        x = x + memory_cross_attention(rmsnorm(x, g_cross[l]), rmsnorm(mem, g_mem[l]),
                                       w_xq[l], w_xkv[l], w_xo[l])
        x = x + 0.5 * swiglu(rmsnorm(x, g_ffn2[l]), w_ffn2_in[l], w_ffn2_out[l])
    return rmsnorm(x, g_final)
```

```python
import numpy as np
from contextlib import ExitStack
import concourse.bass as bass
import concourse.mybir as mybir
from concourse.bass_utils import run_bass_kernel_spmd

F32 = mybir.dt.float32
BF16 = mybir.dt.bfloat16
AF = mybir.ActivationFunctionType
ALU = mybir.AluOpType
AX = mybir.AxisListType

D = 1024
KC = 8
T = 2048
TT = 512
NT = T // TT
DFF = 2816
NF = DFF // 128
EPS = 1e-6
L = 2
NCORES = 8


class Res:
    __slots__ = ("name", "w", "r")

    def __init__(self, name=""):
        self.name = name
        self.w = None
        self.r = {}


class DSem:
    __slots__ = ("key", "sem", "count")

    def __init__(self, key, sem):
        self.key = key
        self.sem = sem
        self.count = 0


class Eng:
    def __init__(self, name, sem):
        self.name = name
        self.sem = sem
        self.count = 0
        self.seen = {}
        self.q = []


class K:
    def __init__(self, nc, stack):
        self.nc = nc
        self.stack = stack
        self.engs = {}
        for n in ("pe", "act", "dve", "pool", "sp"):
            self.engs[n] = Eng(n, stack.enter_context(nc.semaphore("sem_" + n)))
        self.dsems = {}
        self.ninst = 0

    NPOOL = 64

    def dsem(self, key):
        return None

    def _pool_sem(self):
        if not hasattr(self, "dpool"):
            self.dpool = []
            self.dnext = 0
        if len(self.dpool) < self.NPOOL:
            i = len(self.dpool)
            d = DSem("dp%d" % i, self.stack.enter_context(self.nc.semaphore("dsem_p%d" % i)))
            self.dpool.append(d)
            self.dsems[d.key] = d
            return d
        d = self.dpool[self.dnext % self.NPOOL]
        self.dnext += 1
        return d

    def _waits(self, eng, reads, writes):
        deps = {}

        def add(tok):
            if tok is None:
                return
            key, sem, val = tok
            if key not in deps or deps[key][1] < val:
                deps[key] = (sem, val)

        for r in reads:
            add(r.w)
        for w in writes:
            add(w.w)
            for tok in w.r.values():
                add(tok)
        out = []
        for key, (sem, val) in deps.items():
            if eng.seen.get(key, 0) >= val:
                continue
            if key == "pe" and eng.name == "pe":
                continue
            eng.seen[key] = val
            out.append((sem, val))
        return out

    def op(self, ename, fn, reads=(), writes=()):
        eng = self.engs[ename]
        waits = self._waits(eng, reads, writes)
        sem = eng.sem
        eng.count += 1
        tok = (ename, sem, eng.count)

        def run(e, waits=waits, fn=fn, sem=sem):
            for s, v in waits:
                e.wait_ge(s, v)
            fn(e).then_inc(sem, 1)

        eng.q.append(run)
        for r in reads:
            r.r[ename] = tok
        for w in writes:
            w.w = tok
            w.r = {}
        self.ninst += 1

    def dma(self, qname, ds, out, in_, reads=(), writes=(), **kw):
        eng = self.engs[qname]
        ds = self._pool_sem()
        waits = self._waits(eng, reads, writes)
        if ds.count > 0 and eng.seen.get(ds.key, 0) < ds.count:
            eng.seen[ds.key] = ds.count
            waits.append((ds.sem, ds.count))
        ds.count += 16
        tok = (ds.key, ds.sem, ds.count)

        def run(e, waits=waits, out=out, in_=in_, kw=kw, sem=ds.sem, cnt=ds.count):
            for s, v in waits:
                e.wait_ge(s, v)
            e.dma_start(out=out, in_=in_(e) if callable(in_) else in_, **kw).then_inc(sem, 16)

        eng.q.append(run)
        for r in reads:
            r.r[ds.key] = tok
        for w in writes:
            w.w = tok
            w.r = {}
        self.ninst += 1

    def coll(self, kind, groups, ins, outs, reads=(), writes=()):
        eng = self.engs["pool"]
        i = len([k_ for k_ in self.dsems if k_.startswith("cc")])
        ds = DSem("cc%d" % i, self.stack.enter_context(self.nc.semaphore("ccsem_%d" % i)))
        self.dsems[ds.key] = ds
        waits = self._waits(eng, reads, writes)
        ds.count += 1
        tok = (ds.key, ds.sem, ds.count)

        def run(e, waits=waits, sem=ds.sem):
            for s, v in waits:
                e.wait_ge(s, v)
            e.collective_compute(kind, ALU.bypass, replica_groups=groups, ins=[a.opt() for a in ins],
                                 outs=[a.opt() for a in outs]).then_inc(sem, 1)

        eng.q.append(run)
        for r in reads:
            r.r[ds.key] = tok
        for w in writes:
            w.w = tok
            w.r = {}

    def barrier(self):
        toks = [(e.name, e.sem, e.count) for e in self.engs.values() if e.count > 0]
        toks += [(d.key, d.sem, d.count) for d in self.dsems.values() if d.count > 0]
        for eng in self.engs.values():
            waits = []
            for key, sem, val in toks:
                if eng.seen.get(key, 0) >= val:
                    continue
                eng.seen[key] = val
                waits.append((sem, val))

            def run(e, waits=waits):
                for s, v in waits:
                    e.wait_ge(s, v)

            eng.q.append(run)

    def final_wait(self, qname="sp"):
        eng = self.engs[qname]
        toks = [(d.key, d.sem, d.count) for d in self.dsems.values() if d.count > 0]
        toks += [(e.name, e.sem, e.count) for e in self.engs.values() if e.count > 0]

        def run(e, toks=toks):
            for key, s, v in toks:
                e.wait_ge(s, v)

        eng.q.append(run)

    def emit(self):
        nc = self.nc
        with nc.Block() as block:
            @block.tensor
            def _(e):
                for f in self.engs["pe"].q:
                    f(e)

            @block.scalar
            def _(e):
                for f in self.engs["act"].q:
                    f(e)

            @block.vector
            def _(e):
                for f in self.engs["dve"].q:
                    f(e)

            @block.gpsimd
            def _(e):
                for f in self.engs["pool"].q:
                    f(e)

            @block.sync
            def _(e):
                for f in self.engs["sp"].q:
                    f(e)


class Arena:
    def __init__(self, ap_f32, nbytes):
        self.ap = ap_f32
        self.nbytes = nbytes
        self.off = 0
        self.marks = []

    def alloc(self, shape_free, dtype, parts=128):
        n = int(np.prod(shape_free))
        esz = 2 if dtype == BF16 else 4
        nb = (n * esz + 31) // 32 * 32
        assert self.off + nb <= self.nbytes, ("arena overflow", self.off, nb, self.nbytes)
        a = self.ap[:, self.off // 4:(self.off + nb) // 4]
        self.off += nb
        if dtype == BF16:
            a = a.bitcast(BF16)
        a = a[:, 0:n]
        if len(shape_free) == 2:
            a = a.rearrange("p (a b) -> p a b", a=shape_free[0])
        elif len(shape_free) == 3:
            a = a.rearrange("p (a b c) -> p a b c", a=shape_free[0], b=shape_free[1])
        return a

    def mark(self):
        self.marks.append(self.off)

    def release(self):
        self.off = self.marks.pop()


class Prog:
    def __init__(self, nc, k, arena, psum):
        self.nc = nc
        self.k = k
        self.ar = arena
        self.psum = psum
        self.pres = [Res("ps%d" % i) for i in range(8)]
        self.pidx = 0

    def bank(self):
        i = self.pidx
        self.pidx = (self.pidx + 1) % 8
        return self.psum[:, i, :], self.pres[i]


def alloc_normtmp(P, tw=TT):
    ar = P.ar
    return dict(sq=ar.alloc([KC, tw], F32), sq_r=Res(), ssum=ar.alloc([tw], F32), ssum_r=Res(),
                rstd=ar.alloc([tw], F32), rstd_r=Res())


def emit_rmsnorm(P, cst, nt, x, xres, gvec, gres, dst, ntiles=NT, tw=TT, after=None):
    k = P.k
    ones, ones_res = cst["ones_f"], cst["ones_f_r"]
    sq, sq_res, ssum, ssum_res, rstd, rstd_res = nt["sq"], nt["sq_r"], nt["ssum"], nt["ssum_r"], nt["rstd"], nt["rstd_r"]
    for tt in range(ntiles):
        sl = slice(tt * tw, (tt + 1) * tw)
        o, ores = dst(tt)
        k.op("act", lambda e, sl=sl: e.activation(out=sq[:, :, 0:tw], in_=x[:, :, sl], func=AF.Square),
             reads=[xres[tt]], writes=[sq_res])
        k.op("dve", lambda e: e.tensor_reduce(out=ssum[:, 0:tw], in_=sq[:, :, 0:tw].rearrange("p k t -> p t k"),
                                              axis=AX.X, op=ALU.add),
             reads=[sq_res], writes=[ssum_res])
        ps, pr = P.bank()
        k.op("pe", lambda e, ps=ps: e.matmul(ps[:, 0:tw], lhsT=ones, rhs=ssum[:, 0:tw], start=True, stop=True),
             reads=[ssum_res, ones_res], writes=[pr])
        k.op("act", lambda e, ps=ps: e.activation(out=rstd[:, 0:tw], in_=ps[:, 0:tw], func=AF.Sqrt,
                                                  bias=cst["eps"], scale=1.0 / D),
             reads=[pr, cst["eps_r"]], writes=[rstd_res])
        k.op("dve", lambda e: e.reciprocal(out=rstd[:, 0:tw], in_=rstd[:, 0:tw]),
             reads=[rstd_res], writes=[rstd_res])
        for kc in range(KC):
            k.op("dve", lambda e, kc=kc, sl=sl, o=o: e.scalar_tensor_tensor(
                out=o[:, kc, :], in0=x[:, kc, sl], scalar=gvec[:, kc:kc + 1], in1=rstd[:, 0:tw],
                op0=ALU.mult, op1=ALU.mult),
                reads=[xres[tt], gres, rstd_res], writes=[ores])
        if after is not None:
            after(tt, o, ores)


def consts_eps(P):
    return P.eps_ap


def emit_ffn(P, x, xres, xn, xnres, w1d, w2d, wb, tag):
    k = P.k
    (w1buf, w1res, w2buf, w2res, g, gres, stmp, stres) = wb
    ds1 = [k.dsem("w1_0"), k.dsem("w1_1")]
    ds2 = [k.dsem("w2_0"), k.dsem("w2_1")]
    cnt1 = 0
    cnt2 = 0
    for st in range(2):
        for f in range(NF):
            s = cnt1 % 2
            cnt1 += 1
            k.dma("pool", ds1[s], out=w1buf[s], in_=w1d[f], reads=[], writes=[w1res[s]])
            for t2 in range(2):
                tt = st * 2 + t2
                sl = slice(tt * TT, (tt + 1) * TT)
                pa, par = P.bank()
                pb, pbr = P.bank()
                for kc in range(KC):
                    k.op("pe", lambda e, pa=pa, s=s, kc=kc, sl=sl: e.matmul(
                        pa, lhsT=w1buf[s][:, kc * 256:kc * 256 + 128], rhs=xn[:, kc, sl],
                        start=(kc == 0), stop=(kc == KC - 1)),
                        reads=[w1res[s], xnres[tt]], writes=[par])
                for kc in range(KC):
                    k.op("pe", lambda e, pb=pb, s=s, kc=kc, sl=sl: e.matmul(
                        pb, lhsT=w1buf[s][:, kc * 256 + 128:kc * 256 + 256], rhs=xn[:, kc, sl],
                        start=(kc == 0), stop=(kc == KC - 1)),
                        reads=[w1res[s], xnres[tt]], writes=[pbr])
                ts_ = (f * 2 + t2) % 2
                k.op("act", lambda e, pa=pa, ts_=ts_: e.activation(out=stmp[ts_], in_=pa, func=AF.Silu),
                     reads=[par], writes=[stres[ts_]])
                k.op("dve", lambda e, pb=pb, ts_=ts_, f=f, t2=t2: e.tensor_tensor(
                    out=g[:, f, t2 * TT:(t2 + 1) * TT], in0=stmp[ts_], in1=pb, op=ALU.mult),
                    reads=[stres[ts_], pbr], writes=[gres[f][t2]])
        for dc in range(KC):
            s = cnt2 % 2
            cnt2 += 1
            k.dma("pool", ds2[s], out=w2buf[s], in_=w2d[dc], reads=[], writes=[w2res[s]],
                  max_dma_last_dim=1408 * 4)
            for t2 in range(2):
                tt = st * 2 + t2
                sl = slice(tt * TT, (tt + 1) * TT)
                po, por = P.bank()
                for f in range(NF):
                    k.op("pe", lambda e, po=po, s=s, f=f, t2=t2: e.matmul(
                        po, lhsT=w2buf[s][:, f * 128:(f + 1) * 128], rhs=g[:, f, t2 * TT:(t2 + 1) * TT],
                        start=(f == 0), stop=(f == NF - 1)),
                        reads=[w2res[s], gres[f][t2]], writes=[por])
                k.op("dve", lambda e, po=po, dc=dc, sl=sl: e.scalar_tensor_tensor(
                    out=x[:, dc, sl], in0=po, scalar=0.5, in1=x[:, dc, sl], op0=ALU.mult, op1=ALU.add),
                    reads=[por, xres[tt]], writes=[xres[tt]])


def pack_w1(w):
    a = w[:, :DFF].reshape(KC, 128, NF, 128)
    b = w[:, DFF:].reshape(KC, 128, NF, 128)
    ab = np.stack([a, b], axis=3)
    return np.ascontiguousarray(ab.transpose(2, 1, 0, 3, 4)).reshape(NF, 128, KC * 256)


def pack_w2(w):
    a = w.reshape(NF, 128, KC, 128)
    return np.ascontiguousarray(a.transpose(2, 1, 0, 3)).reshape(KC, 128, NF * 128)


def pack_vec(v):
    return np.ascontiguousarray(v.reshape(KC, 128).T)


def pack_xT(xc):
    return np.ascontiguousarray(xc.reshape(T, KC, 128).transpose(2, 1, 0))


def unpack_xT(a):
    return np.ascontiguousarray(a.transpose(2, 1, 0)).reshape(T, D)


def make_consts(P, mask_d):
    k, ar = P.k, P.ar
    c = {}
    c["ones_f"] = ar.alloc([128], F32); c["ones_f_r"] = Res()
    c["ones_bf"] = ar.alloc([128], BF16); c["ones_bf_r"] = Res()
    c["mask"] = ar.alloc([128], BF16); c["mask_r"] = Res()
    c["eps"] = ar.alloc([1], F32); c["eps_r"] = Res()
    P.eps_ap = c["eps"]; P.eps_res = c["eps_r"]
    k.op("pool", lambda e: e.memset(c["ones_f"], 1.0), writes=[c["ones_f_r"]])
    k.op("pool", lambda e: e.memset(c["ones_bf"], 1.0), writes=[c["ones_bf_r"]])
    k.op("pool", lambda e: e.memset(c["eps"], EPS), writes=[c["eps_r"]])
    k.dma("pool", k.dsem("cmask"), out=c["mask"], in_=mask_d, writes=[c["mask_r"]])
    return c


NGT = 16
NCB = 1026
VB_PSCALE = 0
VB_CONVW = 1
VB_CONVB = 9
VB_BA = 11
VB_BX = 13
VB_LAM = 15
VB_BF = 17
VB_SEL = 19
VB_CORR = 23
NVB = 39


def emit_phaseB(P, cst, usrc, wB_d, wsB_d, vecB_d, mdst):
    k = P.k
    ar = P.ar
    ar.mark()
    ones_f, ones_f_r, mask, mask_r = cst["ones_f"], cst["ones_f_r"], cst["mask"], cst["mask_r"]
    GB = [5, 6, 7]
    SB = [2, 3, 4]
    OB = [0, 1]
    gcnt = [0]
    scnt = [0]

    def gbank():
        i = GB[gcnt[0] % 3]
        gcnt[0] += 1
        return P.psum[:, i, :], P.pres[i]

    def sbank():
        i = SB[scnt[0] % 3]
        scnt[0] += 1
        return P.psum[:, i, :], P.pres[i]

    utile = ar.alloc([KC, TT], BF16); utile_r = Res()
    wB = ar.alloc([KC, NCB], BF16); wB_r = Res()
    wsB = ar.alloc([640], BF16); wsB_r = Res()
    vec = ar.alloc([NVB], F32); vec_r = Res()
    nbf = ar.alloc([2], F32); nbf_r = Res()
    nsp8 = ar.alloc([2], F32); nsp8_r = Res()
    Kaug = [ar.alloc([NGT * TT], BF16) for _ in range(2)]
    K_r = [[Res() for _ in range(NGT)] for _ in range(2)]
    Vaug = ar.alloc([NGT * 4, 2, 65], BF16)
    V_r = [Res() for _ in range(NGT)]
    negc = ar.alloc([NGT * 4, 2], F32)
    negc_r = [[Res() for _ in range(NGT)] for _ in range(2)]
    xa_buf = ar.alloc([528], F32); xa_r = Res()
    sA = ar.alloc([528], F32); sA_r = Res()
    sBf = ar.alloc([528], F32); sB_r = Res()
    acc = ar.alloc([TT], F32); acc_r = Res()
    d_bf = [ar.alloc([TT], BF16) for _ in range(2)]; d_r = [Res(), Res()]
    xb_buf = [ar.alloc([515], F32) for _ in range(2)]; xb_r = [Res(), Res()]
    xc = [[ar.alloc([TT], F32) for _ in range(2)] for _ in range(2)]
    xc_r = [[Res(), Res()] for _ in range(2)]
    xc_bf = [[ar.alloc([TT], BF16) for _ in range(2)] for _ in range(2)]
    xcb_r = [[Res(), Res()] for _ in range(2)]
    gg = [[ar.alloc([TT], BF16) for _ in range(2)] for _ in range(2)]
    gg_r = [[Res(), Res()] for _ in range(2)]
    t1 = ar.alloc([TT], F32); t1_r = Res()
    Qaug = [[ar.alloc([TT], BF16) for _ in range(2)] for _ in range(2)]
    Q_r = [[Res(), Res()] for _ in range(2)]
    rowe = ar.alloc([TT], F32); rowe_r = Res()
    crow = [ar.alloc([TT], F32) for _ in range(2)]; crow_r = [Res(), Res()]
    clast = ar.alloc([2], F32); clast_r = [Res(), Res()]
    ra = ar.alloc([TT], F32); ra_r = Res()
    tmp = ar.alloc([TT], F32); tmp_r = Res()
    ig = ar.alloc([TT], F32); ig_r = Res()
    hbuf = t1; hbuf_r = t1_r
    hlast = ar.alloc([2], F32); hlast_r = [Res(), Res()]
    Pt = [ar.alloc([TT], BF16) for _ in range(3)]; Pt_r = [Res(), Res(), Res()]
    rec = rowe; rec_r = rowe_r
    osb = ar.alloc([TT], F32); osb_r = Res()
    outA = [ar.alloc([TT], BF16)] * 2; outA_r = [Res()] * 2
    outB = [ar.alloc([2, TT], BF16) for _ in range(2)]; outB_r = [Res(), Res()]
    outC = [[ar.alloc([TT], BF16)] * 2 for _ in range(2)]
    outC_r = [[Res()] * 2 for _ in range(2)]
    ds_u = k.dsem("bu")
    ds_o = k.dsem("bo")

    k.dma("pool", k.dsem("bw"), out=wB, in_=wB_d, writes=[wB_r])
    k.dma("pool", k.dsem("bws"), out=wsB, in_=wsB_d, writes=[wsB_r])
    k.dma("sp", k.dsem("bv"), out=vec, in_=vecB_d, writes=[vec_r])
    for h in range(2):
        k.op("pool", lambda e, h=h: e.memset(Kaug[h][64:65, :], 1.0), writes=K_r[h])
    k.op("pool", lambda e: e.memset(Vaug[:, :, :, 64:65], 1.0), writes=V_r)
    k.op("dve", lambda e: e.tensor_scalar(out=nbf, in0=vec[:, VB_BF:VB_BF + 2], scalar1=-1.0, scalar2=None,
                                          op0=ALU.mult), reads=[vec_r], writes=[nbf_r])
    k.op("act", lambda e: e.activation(out=nsp8, in_=vec[:, VB_LAM:VB_LAM + 2], func=AF.Exp, scale=-1.0),
         reads=[vec_r], writes=[nsp8_r])
    k.op("act", lambda e: e.activation(out=nsp8, in_=nsp8, func=AF.Ln, bias=ones_f[:, 0:1], scale=1.0),
         reads=[nsp8_r, ones_f_r], writes=[nsp8_r])
    k.op("dve", lambda e: e.tensor_scalar(out=nsp8, in0=nsp8, scalar1=-8.0, scalar2=None, op0=ALU.mult),
         reads=[nsp8_r], writes=[nsp8_r])

    def S1(gt):
        r, tt = gt // 4, gt % 4
        par = gt % 2

        def proj(c0, ncol, nparts):
            ps, pr = gbank()
            for kc in range(KC):
                k.op("pe", lambda e, ps=ps, kc=kc: e.matmul(ps[0:nparts, :], lhsT=wB[:, kc, c0:c0 + ncol],
                                                          rhs=utile[:, kc, :], start=(kc == 0), stop=(kc == KC - 1)),
                     reads=[wB_r, utile_r], writes=[pr])
            return ps, pr

        def st_load():
            k.dma("sp", ds_u, out=utile, in_=usrc(r, tt), writes=[utile_r])

        def st_xa():
            if gt == 0:
                k.op("dve", lambda e: e.memset(xa_buf[:, 0:16], 0.0), writes=[xa_r])
            else:
                k.op("dve", lambda e: e.tensor_copy(out=xa_buf[:, 0:16], in_=xa_buf[:, 512:528]),
                     reads=[xa_r], writes=[xa_r])
            ps, pr = proj(0, 128, 128)
            k.op("dve", lambda e, ps=ps: e.tensor_copy(out=xa_buf[:, 16:528], in_=ps), reads=[pr], writes=[xa_r])

            def padd(dst, dr, srcb, sr, lo, sh):
                k.op("dve", lambda e: e.tensor_tensor(out=dst[:, lo:528], in0=srcb[:, lo:528],
                                                      in1=srcb[:, lo - sh:528 - sh], op=ALU.add),
                     reads=[sr], writes=[dr])

            def pacc(srcb, sr, i):
                if i == 0:
                    k.op("dve", lambda e: e.tensor_scalar(out=acc, in0=srcb[:, 16:528],
                                                          scalar1=vec[:, VB_SEL:VB_SEL + 1], scalar2=None, op0=ALU.mult),
                         reads=[sr, vec_r], writes=[acc_r])
                else:
                    k.op("dve", lambda e: e.scalar_tensor_tensor(out=acc, in0=srcb[:, 16:528],
                                                                 scalar=vec[:, VB_SEL + i:VB_SEL + i + 1], in1=acc,
                                                                 op0=ALU.mult, op1=ALU.add),
                         reads=[sr, vec_r, acc_r], writes=[acc_r])

            padd(sA, sA_r, xa_buf, xa_r, 1, 1)
            pacc(sA, sA_r, 0)
            padd(sBf, sB_r, sA, sA_r, 3, 2)
            pacc(sBf, sB_r, 1)
            padd(sA, sA_r, sBf, sB_r, 7, 4)
            pacc(sA, sA_r, 2)
            padd(sBf, sB_r, sA, sA_r, 15, 8)
            pacc(sBf, sB_r, 3)
            if gt == 0:
                k.op("dve", lambda e: e.tensor_tensor(out=acc[:, 0:16], in0=acc[:, 0:16],
                                                      in1=vec[:, VB_CORR:VB_CORR + 16], op=ALU.mult),
                     reads=[acc_r, vec_r], writes=[acc_r])
            k.op("dve", lambda e: e.tensor_tensor(out=d_bf[par], in0=acc, in1=xa_buf[:, 16:528], op=ALU.subtract),
                 reads=[xa_r, acc_r], writes=[d_r[par]])

        def st_xb(hh):
            if gt == 0:
                k.op("dve", lambda e: e.memset(xb_buf[hh][:, 0:3], 0.0), writes=[xb_r[hh]])
            else:
                k.op("dve", lambda e: e.tensor_copy(out=xb_buf[hh][:, 0:3], in_=xb_buf[hh][:, 512:515]),
                     reads=[xb_r[hh]], writes=[xb_r[hh]])
            ps, pr = proj(128 + 128 * hh, 128, 128)
            k.op("dve", lambda e, ps=ps: e.tensor_copy(out=xb_buf[hh][:, 3:515], in_=ps), reads=[pr], writes=[xb_r[hh]])
            cw = VB_CONVW + 4 * hh
            k.op("dve", lambda e: e.tensor_scalar(
                out=xc[hh][par], in0=xb_buf[hh][:, 0:512], scalar1=vec[:, cw:cw + 1],
                scalar2=vec[:, VB_CONVB + hh:VB_CONVB + hh + 1], op0=ALU.mult, op1=ALU.add),
                reads=[xb_r[hh], vec_r], writes=[xc_r[hh][par]])
            for kk in range(1, 4):
                k.op("dve", lambda e, kk=kk: e.scalar_tensor_tensor(
                    out=xc[hh][par], in0=xb_buf[hh][:, kk:kk + 512], scalar=vec[:, cw + kk:cw + kk + 1],
                    in1=xc[hh][par], op0=ALU.mult, op1=ALU.add),
                    reads=[xb_r[hh], vec_r, xc_r[hh][par]], writes=[xc_r[hh][par]])
            k.op("dve", lambda e: e.tensor_copy(out=xc_bf[hh][par], in_=xc[hh][par]),
                 reads=[xc_r[hh][par]], writes=[xcb_r[hh][par]])

        def st_gb(hh):
            ps, pr = proj(384 + 128 * hh, 128, 128)
            k.op("act", lambda e, ps=ps: e.activation(out=t1, in_=ps, func=AF.Square), reads=[pr], writes=[t1_r])
            k.op("dve", lambda e: e.tensor_scalar(out=t1, in0=t1, scalar1=0.044715, scalar2=1.0, op0=ALU.mult,
                                                  op1=ALU.add), reads=[t1_r], writes=[t1_r])
            k.op("dve", lambda e, ps=ps: e.tensor_tensor(out=t1, in0=t1, in1=ps, op=ALU.mult),
                 reads=[t1_r, pr], writes=[t1_r])
            k.op("act", lambda e: e.activation(out=t1, in_=t1, func=AF.Sigmoid, scale=1.5957691216057308),
                 reads=[t1_r], writes=[t1_r])
            k.op("dve", lambda e, ps=ps: e.tensor_tensor(out=gg[hh][par], in0=t1, in1=ps, op=ALU.mult),
                 reads=[t1_r, pr], writes=[gg_r[hh][par]])

        def st_q(h):
            ps, pr = proj(640 + 65 * h, 65, 65)
            k.op("dve", lambda e, ps=ps: e.tensor_copy(out=Qaug[h][par][0:64, :], in_=ps[0:64, :]),
                 reads=[pr], writes=[Q_r[h][par]])
            k.op("act", lambda e, ps=ps: e.activation(out=rowe[64:65, :], in_=ps[64:65, :], func=AF.Exp,
                                                      bias=nbf[64:65, h:h + 1], scale=-1.0),
                 reads=[pr, nbf_r], writes=[rowe_r])
            k.op("act", lambda e: e.activation(out=rowe[64:65, :], in_=rowe[64:65, :], func=AF.Ln,
                                               bias=ones_f[64:65, 0:1], scale=1.0),
                 reads=[rowe_r, ones_f_r], writes=[rowe_r])
            init = 0.0 if gt == 0 else clast[64:65, h:h + 1]
            k.op("dve", lambda e: e.tensor_tensor_scan(
                out=crow[h][64:65, :], data0=ones_f[64:65, 0:1].to_broadcast([1, TT]),
                data1=rowe[64:65, :], initial=init, op0=ALU.mult, op1=ALU.subtract),
                reads=[rowe_r, clast_r[h], ones_f_r], writes=[crow_r[h]])
            k.op("dve", lambda e: e.tensor_copy(out=clast[64:65, h:h + 1], in_=crow[h][64:65, TT - 1:TT]),
                 reads=[crow_r[h]], writes=[clast_r[h]])
            k.op("act", lambda e: e.activation(out=Qaug[h][par][64:65, :], in_=crow[h][64:65, :],
                                               func=AF.Copy, scale=8.0),
                 reads=[crow_r[h]], writes=[Q_r[h][par]])

        def st_negc(h):
            pt_, ptr_ = gbank()
            for blk in range(4):
                k.op("pe", lambda e, blk=blk: e.matmul(
                    pt_[:, blk:blk + 1], lhsT=crow[h][64:65, blk * 128:(blk + 1) * 128], rhs=ones_f[64:65, 0:1],
                    start=True, stop=True),
                    reads=[crow_r[h], ones_f_r], writes=[ptr_])
            k.op("dve", lambda e: e.tensor_scalar(
                out=negc[:, 4 * gt:4 * gt + 4, h], in0=pt_[:, 0:4], scalar1=-1.0, scalar2=None, op0=ALU.mult),
                reads=[ptr_], writes=[negc_r[h][gt]])

        def st_k(h):
            ps, pr = proj(770 + 64 * h, 64, 64)
            k.op("act", lambda e, ps=ps: e.activation(out=Kaug[h][0:64, gt * TT:(gt + 1) * TT],
                                                      in_=ps[0:64, :], func=AF.Copy),
                 reads=[pr], writes=[K_r[h][gt]])

        def st_v():
            ps, pr = gbank()
            for blk in range(4):
                for kc in range(KC):
                    k.op("pe", lambda e, blk=blk, kc=kc: e.matmul(
                        ps[:, blk * 128:(blk + 1) * 128], lhsT=utile[:, kc, blk * 128:(blk + 1) * 128],
                        rhs=wB[:, kc, 898:1026], start=(kc == 0), stop=(kc == KC - 1)),
                        reads=[wB_r, utile_r], writes=[pr])
            k.op("dve", lambda e: e.tensor_copy(
                out=Vaug[:, 4 * gt:4 * gt + 4, :, 0:64], in_=ps.rearrange("p (b h d) -> p b h d", b=4, h=2)),
                reads=[pr], writes=[V_r[gt]])

        return [st_load, lambda: st_q(0), lambda: st_q(1), st_xa, lambda: st_xb(0), lambda: st_xb(1),
                lambda: st_gb(0), lambda: st_gb(1), lambda: st_k(0), lambda: st_k(1), st_v,
                lambda: st_negc(0), lambda: st_negc(1)]

    def S2(gt, extra):
        r, tt = gt // 4, gt % 4
        par = gt % 2
        tsl = slice(tt * TT, (tt + 1) * TT)
        steps = []

        def st_pool():
            ps, pr = gbank()
            k.op("pe", lambda e, ps=ps: e.matmul(ps, lhsT=wsB[:, 0:128], rhs=d_bf[par], start=True, stop=True),
                 reads=[wsB_r, d_r[par]], writes=[pr])
            k.op("act", lambda e, ps=ps: e.activation(out=outA[par], in_=ps, func=AF.Identity,
                                                      scale=vec[:, VB_PSCALE:VB_PSCALE + 1]),
                 reads=[pr, vec_r], writes=[outA_r[par]])
            for dap, ps_ in mdst("A", r, tsl):
                k.dma("pool", ds_o, out=dap, in_=outA[par][ps_, :], reads=[outA_r[par]])
        steps.append(st_pool)

        for hh in range(2):
            def st1(hh=hh):
                ps, pr = gbank()
                k.op("pe", lambda e, ps=ps: e.matmul(ps, lhsT=wsB[:, 128 + 128 * hh:256 + 128 * hh],
                                                     rhs=xc_bf[hh][par], start=True, stop=True),
                     reads=[wsB_r, xcb_r[hh][par]], writes=[pr])
                k.op("act", lambda e, ps=ps: e.activation(out=ra, in_=ps, func=AF.Sigmoid,
                                                          bias=vec[:, VB_BA + hh:VB_BA + hh + 1], scale=1.0),
                     reads=[pr, vec_r], writes=[ra_r])
                ps2, pr2 = gbank()
                k.op("pe", lambda e, ps2=ps2: e.matmul(ps2, lhsT=wsB[:, 384 + 128 * hh:512 + 128 * hh],
                                                       rhs=xc_bf[hh][par], start=True, stop=True),
                     reads=[wsB_r, xcb_r[hh][par]], writes=[pr2])
                k.op("act", lambda e, ps2=ps2: e.activation(out=ig, in_=ps2, func=AF.Sigmoid,
                                                            bias=vec[:, VB_BX + hh:VB_BX + hh + 1], scale=1.0),
                     reads=[pr2, vec_r], writes=[ig_r])
                k.op("act", lambda e: e.activation(out=ra, in_=ra, func=AF.Exp, scale=nsp8[:, hh:hh + 1]),
                     reads=[ra_r, nsp8_r], writes=[ra_r])

            def st2(hh=hh):
                k.op("dve", lambda e: e.tensor_tensor(out=tmp, in0=ra, in1=ra, op=ALU.mult),
                     reads=[ra_r], writes=[tmp_r])
                k.op("dve", lambda e: e.tensor_tensor(out=ig, in0=ig, in1=xc[hh][par], op=ALU.mult),
                     reads=[ig_r, xc_r[hh][par]], writes=[ig_r])

            def st3(hh=hh):
                k.op("act", lambda e: e.activation(out=tmp, in_=tmp, func=AF.Sqrt, bias=ones_f[:, 0:1], scale=-1.0),
                     reads=[tmp_r, ones_f_r], writes=[tmp_r])

            def st4(hh=hh):
                k.op("dve", lambda e: e.tensor_tensor(out=ig, in0=ig, in1=tmp, op=ALU.mult),
                     reads=[ig_r, tmp_r], writes=[ig_r])
                init = 0.0 if gt == 0 else hlast[:, hh:hh + 1]
                k.op("dve", lambda e: e.tensor_tensor_scan(out=hbuf, data0=ra, data1=ig, initial=init,
                                                           op0=ALU.mult, op1=ALU.add),
                     reads=[ra_r, ig_r, hlast_r[hh]], writes=[hbuf_r])
                k.op("dve", lambda e: e.tensor_copy(out=hlast[:, hh:hh + 1], in_=hbuf[:, TT - 1:TT]),
                     reads=[hbuf_r], writes=[hlast_r[hh]])
                k.op("dve", lambda e: e.tensor_tensor(out=outB[par][:, hh, :], in0=hbuf, in1=gg[hh][par], op=ALU.mult),
                     reads=[hbuf_r, gg_r[hh][par]], writes=[outB_r[par]])
                if hh == 1:
                    for dap, ps_, hx in mdst("B", r, tsl):
                        k.dma("pool", ds_o, out=dap, in_=outB[par][ps_, hx, :], reads=[outB_r[par]])
            steps += [st1, st2, st3, st4]

        mixed = []
        while steps or extra:
            if steps:
                mixed.append(steps.pop(0))
            if extra:
                mixed.append(extra.pop(0))
        steps = mixed

        nkb = 4 * gt + 4

        def attn(h):
            oi = OB[(2 * gt + h) % 2]
            ob, obr = P.psum[:, oi, :], P.pres[oi]
            sbs = {}

            def emit_S(kb):
                dd = kb - 4 * gt
                q0 = 128 * dd if dd > 0 else 0
                sb, sbr = sbank()
                sbs[kb] = (sb, sbr, q0, dd)
                k.op("pe", lambda e, sb=sb, q0=q0, kb=kb: e.matmul(
                    sb[:, q0:TT], lhsT=Kaug[h][0:65, kb * 128:(kb + 1) * 128], rhs=Qaug[h][par][0:65, q0:TT],
                    start=True, stop=True),
                    reads=[K_r[h][kb // 4], Q_r[h][par]], writes=[sbr])

            emit_S(0)
            if nkb > 1:
                emit_S(1)
            for kb in range(nkb):
                if kb + 2 < nkb:
                    emit_S(kb + 2)
                sb, sbr, q0, dd = sbs.pop(kb)
                pi = kb % 3
                k.op("act", lambda e, sb=sb, q0=q0, kb=kb, pi=pi: e.activation(
                    out=Pt[pi][:, q0:TT], in_=sb[:, q0:TT], func=AF.Exp, bias=negc[:, kb, h:h + 1], scale=0.125),
                    reads=[sbr, negc_r[h][kb // 4]], writes=[Pt_r[pi]])
                if dd >= 0:
                    k.op("dve", lambda e, q0=q0, pi=pi: e.tensor_tensor(
                        out=Pt[pi][:, q0:q0 + 128], in0=Pt[pi][:, q0:q0 + 128], in1=mask, op=ALU.min),
                        reads=[Pt_r[pi], mask_r], writes=[Pt_r[pi]])
                k.op("pe", lambda e, q0=q0, kb=kb, pi=pi: e.matmul(
                    ob[0:65, q0:TT], lhsT=Vaug[:, kb, h, :], rhs=Pt[pi][:, q0:TT],
                    start=(kb == 0), stop=(kb == nkb - 1)),
                    reads=[V_r[kb // 4], Pt_r[pi]], writes=[obr])
                if steps:
                    steps.pop(0)()
            k.op("dve", lambda e: e.reciprocal(out=rec[64:65, :], in_=ob[64:65, :]), reads=[obr], writes=[rec_r])
            bc, bcr = gbank()
            k.op("pe", lambda e, bc=bc: e.matmul(bc[0:64, :], lhsT=ones_f[64:65, 0:64], rhs=rec[64:65, :],
                                                 start=True, stop=True),
                 reads=[rec_r, ones_f_r], writes=[bcr])
            k.op("act", lambda e: e.activation(out=osb[0:64, :], in_=ob[0:64, :], func=AF.Copy),
                 reads=[obr], writes=[osb_r])
            k.op("dve", lambda e, bc=bc: e.tensor_tensor(out=outC[h][par][0:64, :], in0=osb[0:64, :], in1=bc[0:64, :],
                                                         op=ALU.mult),
                 reads=[osb_r, bcr], writes=[outC_r[h][par]])
            for dap, ps_ in mdst("C", r, tsl, h):
                k.dma("pool", ds_o, out=dap, in_=outC[h][par][ps_, :], reads=[outC_r[h][par]])
        for h in range(2):
            attn(h)
        while steps:
            steps.pop(0)()

    for s_ in S1(0):
        s_()
    for gt in range(NGT):
        S2(gt, S1(gt + 1) if gt + 1 < NGT else [])
    ar.release()


def pack_wB(w_in_l, j):
    cols = list(range(128 * j, 128 * j + 128))
    cols += list(range(512 + 256 * j, 512 + 256 * j + 256))
    cols += list(range(1536 + 256 * j, 1536 + 256 * j + 256))
    for h in range(2):
        hd = 2 * j + h
        cols += list(range(2560 + 64 * hd, 2560 + 64 * hd + 64)) + [4096 + hd]
    cols += list(range(3072 + 128 * j, 3072 + 128 * j + 128))
    cols += list(range(3584 + 128 * j, 3584 + 128 * j + 128))
    w = w_in_l[:, cols]
    return np.ascontiguousarray(w.reshape(KC, 128, NCB).transpose(1, 0, 2))


def pack_wsB(inp, l, j):
    parts = [inp["w_pool"][l, j]]
    parts += [inp["w_rg_a"][l, 2 * j + hh] for hh in range(2)]
    parts += [inp["w_rg_x"][l, 2 * j + hh] for hh in range(2)]
    return np.ascontiguousarray(np.concatenate(parts, axis=1))


def pack_vecB(inp, l, j):
    v = np.zeros((128, NVB), np.float32)
    v[:, VB_PSCALE] = inp["pool_scale"][l, 128 * j:128 * j + 128]
    for hh in range(2):
        sl = slice(256 * j + 128 * hh, 256 * j + 128 * hh + 128)
        for kk in range(4):
            v[:, VB_CONVW + 4 * hh + kk] = inp["conv_w"][l, kk, sl]
        v[:, VB_CONVB + hh] = inp["conv_b"][l, sl]
        v[:, VB_BA + hh] = inp["b_rg_a"][l, sl]
        v[:, VB_BX + hh] = inp["b_rg_x"][l, sl]
        v[:, VB_LAM + hh] = inp["lru_lambda"][l, sl]
        v[:, VB_BF + hh] = inp["b_f"][l, 2 * j + hh]
    w = 2 ** (j + 1)
    v[:, VB_SEL + j] = np.float32(1.0 / w)
    for t in range(16):
        v[:, VB_CORR + t] = np.float32(w / min(t + 1, w))
    return v


def const_mask():
    jj = np.arange(128)[:, None]
    tt = np.arange(128)[None, :]
    return np.where(jj <= tt, np.float32(3.0e38), np.float32(0.0)).astype(np.float32)


def emit_phaseC(P, cst, x, xres, umine, mix_all_d, wC_d, bgate_d, wo_d):
    k, ar = P.k, P.ar
    ar.mark()
    u = ar.alloc([KC, T], BF16); u_r = [Res() for _ in range(NT)]
    mixs = ar.alloc([16, 2 * TT], BF16); mixs_r = [Res() for _ in range(16)]
    merged = ar.alloc([KC, 2 * TT], BF16); mg_r = [[Res(), Res()] for _ in range(KC)]
    wcb = [ar.alloc([5120], BF16) for _ in range(2)]; wcb_r = [Res(), Res()]
    wob = [ar.alloc([1024], BF16) for _ in range(2)]; wob_r = [Res(), Res()]
    gate = [ar.alloc([TT], F32) for _ in range(3)]; gate_r = [Res(), Res(), Res()]
    acc = [ar.alloc([TT], F32) for _ in range(2)]; acc_r = [Res(), Res()]
    bg = ar.alloc([24], F32); bg_r = Res()
    ds_u = k.dsem("cu")
    for tt in range(NT):
        k.dma("sp", ds_u, out=u[:, :, tt * TT:(tt + 1) * TT], in_=umine(tt), writes=[u_r[tt]])
    k.dma("sp", k.dsem("cbg"), out=bg, in_=bgate_d, writes=[bg_r])
    dsm = k.dsem("cmix")
    dsw = [k.dsem("cw0"), k.dsem("cw1")]
    dso = [k.dsem("co0"), k.dsem("co1")]
    cw = 0
    co = 0
    for st in range(2):
        cs = slice(st * 2 * TT, (st + 1) * 2 * TT)
        for r in range(4):
            if callable(mix_all_d):
                srcA = mix_all_d(r, 0, 128, cs)
                srcB = mix_all_d(r, 128, 256, cs, True)
                srcC = mix_all_d(r, 384, 128, cs)
            else:
                srcA = mix_all_d[r, 0:128, cs]
                srcB = mix_all_d[r, 128:384, cs].rearrange("(hh p) t -> p hh t", hh=2)
                srcC = mix_all_d[r, 384:512, cs]
            k.dma("sp", dsm, out=mixs[:, r, :], in_=srcA, writes=[mixs_r[r]])
            k.dma("sp", dsm, out=mixs[:, 4 + 2 * r:6 + 2 * r, :], in_=srcB,
                  writes=[mixs_r[4 + 2 * r], mixs_r[5 + 2 * r]])
            k.dma("sp", dsm, out=mixs[:, 12 + r, :], in_=srcC, writes=[mixs_r[12 + r]])
        for dc in range(KC):
            s = cw % 2
            cw += 1
            k.dma("pool", dsw[s], out=wcb[s], in_=wC_d[dc], writes=[wcb_r[s]], max_dma_last_dim=1024 * 4)
            for t2 in range(2):
                tt = st * 2 + t2
                sl = slice(tt * TT, (tt + 1) * TT)
                sl2 = slice(t2 * TT, (t2 + 1) * TT)
                for br in range(3):
                    ps, pr = P.bank()
                    for kc in range(KC):
                        c0 = 2048 + kc * 384 + br * 128
                        k.op("pe", lambda e, ps=ps, s=s, kc=kc, c0=c0, sl=sl: e.matmul(
                            ps, lhsT=wcb[s][:, c0:c0 + 128], rhs=u[:, kc, sl], start=(kc == 0), stop=(kc == KC - 1)),
                            reads=[wcb_r[s], u_r[tt]], writes=[pr])
                    k.op("act", lambda e, ps=ps, br=br, dc=dc: e.activation(
                        out=gate[br], in_=ps, func=AF.Sigmoid, bias=bg[:, br * 8 + dc:br * 8 + dc + 1], scale=1.0),
                        reads=[pr, bg_r], writes=[gate_r[br]])
                ys = []
                for (m0, m1) in ((0, 4), (4, 12), (12, 16)):
                    ps, pr = P.bank()
                    for m in range(m0, m1):
                        k.op("pe", lambda e, ps=ps, s=s, m=m, sl2=sl2, m0=m0, m1=m1: e.matmul(
                            ps, lhsT=wcb[s][:, m * 128:(m + 1) * 128], rhs=mixs[:, m, sl2],
                            start=(m == m0), stop=(m == m1 - 1)),
                            reads=[wcb_r[s], mixs_r[m]], writes=[pr])
                    ys.append((ps, pr))
                k.op("dve", lambda e, ys=ys: e.tensor_tensor(out=acc[0], in0=ys[0][0], in1=gate[0], op=ALU.mult),
                     reads=[ys[0][1], gate_r[0]], writes=[acc_r[0]])
                k.op("dve", lambda e, ys=ys: e.tensor_tensor(out=acc[1], in0=ys[1][0], in1=gate[1], op=ALU.mult),
                     reads=[ys[1][1], gate_r[1]], writes=[acc_r[1]])
                k.op("dve", lambda e: e.tensor_tensor(out=acc[0], in0=acc[0], in1=acc[1], op=ALU.add),
                     reads=[acc_r[0], acc_r[1]], writes=[acc_r[0]])
                k.op("dve", lambda e, ys=ys: e.tensor_tensor(out=acc[1], in0=ys[2][0], in1=gate[2], op=ALU.mult),
                     reads=[ys[2][1], gate_r[2]], writes=[acc_r[1]])
                k.op("dve", lambda e, dc=dc, sl2=sl2: e.tensor_tensor(out=merged[:, dc, sl2], in0=acc[0], in1=acc[1],
                                                                      op=ALU.add),
                     reads=[acc_r[0], acc_r[1]], writes=[mg_r[dc][t2]])
        for dco in range(KC):
            s = co % 2
            co += 1
            k.dma("pool", dso[s], out=wob[s], in_=wo_d[dco], writes=[wob_r[s]])
            for t2 in range(2):
                tt = st * 2 + t2
                sl = slice(tt * TT, (tt + 1) * TT)
                sl2 = slice(t2 * TT, (t2 + 1) * TT)
                ps, pr = P.bank()
                for dc in range(KC):
                    k.op("pe", lambda e, ps=ps, s=s, dc=dc, sl2=sl2: e.matmul(
                        ps, lhsT=wob[s][:, dc * 128:(dc + 1) * 128], rhs=merged[:, dc, sl2],
                        start=(dc == 0), stop=(dc == KC - 1)),
                        reads=[wob_r[s], mg_r[dc][t2]], writes=[pr])
                k.op("dve", lambda e, ps=ps, dco=dco, sl=sl: e.tensor_tensor(out=x[:, dco, sl], in0=ps, in1=x[:, dco, sl],
                                                                             op=ALU.add),
                     reads=[pr, xres[tt]], writes=[xres[tt]])
    ar.release()


def emit_cross(P, cst, x, xres, memT_d, gmem, gmem_r, gcross, gcross_r, wxq_d, wxkv_d, wxo_d):
    k, ar = P.k, P.ar
    ar.mark()
    NM = 256
    xn = ar.alloc([KC, T], BF16); xn_r = [Res() for _ in range(NT)]
    QT = ar.alloc([KC, T], BF16); QT_r = [[Res() for _ in range(NT)] for _ in range(KC)]
    memf = ar.alloc([KC, NM], F32); memf_r = [Res()]
    memn = ar.alloc([KC, NM], BF16); memn_r = Res()
    KxT = ar.alloc([KC, NM], BF16); Kx_r = [Res() for _ in range(KC)]
    Vx = ar.alloc([2, D], BF16); Vx_r = [Res() for _ in range(KC)]
    wch = [ar.alloc([1024], BF16) for _ in range(2)]; wch_r = [Res(), Res()]
    Pt = [[ar.alloc([TT], BF16) for _ in range(2)] for _ in range(2)]; Pt_r = [[Res(), Res()] for _ in range(2)]
    rec = ar.alloc([TT], F32); rec_r = Res()
    nt = alloc_normtmp(P)
    dsw = [k.dsem("xw0"), k.dsem("xw1")]
    wc = [0]

    def wload(src):
        s = wc[0] % 2
        wc[0] += 1
        k.dma("pool", dsw[s], out=wch[s], in_=src, writes=[wch_r[s]])
        return s

    k.dma("sp", k.dsem("xmem"), out=memf, in_=memT_d, writes=[memf_r[0]])
    emit_rmsnorm(P, cst, nt, memf, memf_r, gmem, gmem_r, lambda tt: (memn, memn_r), ntiles=1, tw=NM)
    for ck in range(16):
        s = wload(wxkv_d[ck])
        ps, pr = P.bank()
        if ck < 8:
            for kc in range(KC):
                k.op("pe", lambda e, ps=ps, s=s, kc=kc: e.matmul(
                    ps[:, 0:NM], lhsT=wch[s][:, kc * 128:(kc + 1) * 128], rhs=memn[:, kc, :],
                    start=(kc == 0), stop=(kc == KC - 1)),
                    reads=[wch_r[s], memn_r], writes=[pr])
            k.op("act", lambda e, ps=ps, ck=ck: e.activation(out=KxT[:, ck, :], in_=ps[:, 0:NM], func=AF.Copy),
                 reads=[pr], writes=[Kx_r[ck]])
        else:
            c = ck - 8
            for mb in range(2):
                for kc in range(KC):
                    k.op("pe", lambda e, ps=ps, s=s, kc=kc, mb=mb: e.matmul(
                        ps[:, mb * 128:(mb + 1) * 128], lhsT=memn[:, kc, mb * 128:(mb + 1) * 128],
                        rhs=wch[s][:, kc * 128:(kc + 1) * 128], start=(kc == 0), stop=(kc == KC - 1)),
                        reads=[wch_r[s], memn_r], writes=[pr])
            k.op("act", lambda e, ps=ps, c=c: e.activation(
                out=Vx[:, :, c * 128:(c + 1) * 128], in_=ps[:, 0:256].rearrange("p (m j) -> p m j", m=2), func=AF.Copy),
                reads=[pr], writes=[Vx_r[c]])
    emit_rmsnorm(P, cst, nt, x, xres, gcross, gcross_r, lambda tt: (xn[:, :, tt * TT:(tt + 1) * TT], xn_r[tt]))
    for ck in range(KC):
        s = wload(wxq_d[ck])
        for tt in range(NT):
            sl = slice(tt * TT, (tt + 1) * TT)
            ps, pr = P.bank()
            for kc in range(KC):
                k.op("pe", lambda e, ps=ps, s=s, kc=kc, sl=sl: e.matmul(
                    ps, lhsT=wch[s][:, kc * 128:(kc + 1) * 128], rhs=xn[:, kc, sl],
                    start=(kc == 0), stop=(kc == KC - 1)),
                    reads=[wch_r[s], xn_r[tt]], writes=[pr])
            if tt % 2 == 0:
                k.op("act", lambda e, ps=ps, ck=ck, sl=sl: e.activation(out=QT[:, ck, sl], in_=ps, func=AF.Copy),
                     reads=[pr], writes=[QT_r[ck][tt]])
            else:
                k.op("dve", lambda e, ps=ps, ck=ck, sl=sl: e.tensor_copy(out=QT[:, ck, sl], in_=ps),
                     reads=[pr], writes=[QT_r[ck][tt]])
    cnt = 0
    for tt in range(NT):
        sl = slice(tt * TT, (tt + 1) * TT)
        for h in range(4):
            pp = cnt % 2
            cnt += 1
            for mb in range(2):
                ps, pr = P.bank()
                for half in range(2):
                    ck = 2 * h + half
                    k.op("pe", lambda e, ps=ps, ck=ck, mb=mb, sl=sl, half=half: e.matmul(
                        ps, lhsT=KxT[:, ck, mb * 128:(mb + 1) * 128], rhs=QT[:, ck, sl],
                        start=(half == 0), stop=(half == 1)),
                        reads=[Kx_r[ck], QT_r[ck][tt]], writes=[pr])
                k.op("act", lambda e, ps=ps, pp=pp, mb=mb: e.activation(out=Pt[pp][mb], in_=ps, func=AF.Exp,
                                                                       scale=0.0625),
                     reads=[pr], writes=[Pt_r[pp][mb]])
            ps, pr = P.bank()
            for mb in range(2):
                k.op("pe", lambda e, ps=ps, pp=pp, mb=mb: e.matmul(ps, lhsT=cst["ones_bf"], rhs=Pt[pp][mb],
                                                                   start=(mb == 0), stop=(mb == 1)),
                     reads=[cst["ones_bf_r"], Pt_r[pp][mb]], writes=[pr])
            k.op("dve", lambda e, ps=ps: e.reciprocal(out=rec, in_=ps), reads=[pr], writes=[rec_r])
            for half in range(2):
                ck = 2 * h + half
                ps, pr = P.bank()
                for mb in range(2):
                    k.op("pe", lambda e, ps=ps, pp=pp, mb=mb, ck=ck: e.matmul(
                        ps, lhsT=Vx[:, mb, ck * 128:(ck + 1) * 128], rhs=Pt[pp][mb], start=(mb == 0), stop=(mb == 1)),
                        reads=[Vx_r[ck], Pt_r[pp][mb]], writes=[pr])
                k.op("dve", lambda e, ps=ps, ck=ck, sl=sl: e.tensor_tensor(out=QT[:, ck, sl], in0=ps, in1=rec, op=ALU.mult),
                     reads=[pr, rec_r], writes=[QT_r[ck][tt]])
    for dco in range(KC):
        s = wload(wxo_d[dco])
        for tt in range(NT):
            sl = slice(tt * TT, (tt + 1) * TT)
            ps, pr = P.bank()
            for ck in range(KC):
                k.op("pe", lambda e, ps=ps, s=s, ck=ck, sl=sl: e.matmul(
                    ps, lhsT=wch[s][:, ck * 128:(ck + 1) * 128], rhs=QT[:, ck, sl],
                    start=(ck == 0), stop=(ck == KC - 1)),
                    reads=[wch_r[s], QT_r[ck][tt]], writes=[pr])
            k.op("dve", lambda e, ps=ps, dco=dco, sl=sl: e.tensor_tensor(out=x[:, dco, sl], in0=ps, in1=x[:, dco, sl],
                                                                         op=ALU.add),
                 reads=[pr, xres[tt]], writes=[xres[tt]])
    ar.release()


def pack_wC(inp, l):
    ua = inp["w_up_a"][l].reshape(4, 128, KC, 128)
    ub = inp["w_up_b"][l].reshape(8, 128, KC, 128)
    uc = inp["w_up_c"][l].reshape(4, 128, KC, 128)
    up = np.concatenate([ua, ub, uc], axis=0)
    up = up.transpose(2, 1, 0, 3).reshape(KC, 128, 2048)
    wg = inp["w_in"][l][:, 4104:].reshape(KC, 128, 3, KC, 128)
    wg = wg.transpose(3, 1, 0, 2, 4).reshape(KC, 128, 3072)
    return np.ascontiguousarray(np.concatenate([up, wg], axis=2))


def pack_bgate(inp, l):
    return np.ascontiguousarray(inp["b_gate"][l].reshape(3, KC, 128).transpose(2, 0, 1).reshape(128, 24))


def pack_sq(w):
    C = w.shape[1] // 128
    a = w.reshape(KC, 128, C, 128)
    return np.ascontiguousarray(a.transpose(2, 1, 0, 3)).reshape(C, 128, KC * 128)


def pack_memT(m):
    return np.ascontiguousarray(m.reshape(256, KC, 128).transpose(2, 1, 0))


ARENA_BYTES = 205 * 1024
RGROUPS = [[0, 1, 2, 3], [4, 5, 6, 7]]


def build(stages, fused=False):
    nc = bass.Bass("TRN2", target_bir_lowering=False)
    dram = {}

    def dten(name, shape, dt, kind):
        if name not in dram:
            dram[name] = nc.dram_tensor(name, list(shape), dt, kind=kind).ap()
        return dram[name]

    def din(name, shape, dt=F32):
        return dten(name, shape, dt, "ExternalInput")

    def dout(name, shape, dt):
        return dten(name, shape, dt, "ExternalOutput")

    def dint(name, shape, dt):
        return dten(name, shape, dt, "Internal")

    with ExitStack() as stack:
        arena_t = stack.enter_context(nc.sbuf_tensor("arena", [128, ARENA_BYTES // 4], F32))
        psum_t = stack.enter_context(nc.psum_tensor("psum", [128, 8, 512], F32))
        k = K(nc, stack)
        ar = Arena(arena_t[:, :], ARENA_BYTES)
        P = Prog(nc, k, ar, psum_t[:, :, :])
        cst = make_consts(P, din("cmask", [128, 128]))
        x = ar.alloc([KC, T], F32)
        xres = [Res() for _ in range(NT)]
        gvs = ar.alloc([16, KC], F32)
        gv_n = [0]
        pid_cache = {}
        ds_g = k.dsem("gv")

        def load_g(name):
            i = gv_n[0] % 16
            gv_n[0] += 1
            r = Res()
            k.dma("sp", ds_g, out=gvs[:, i, :], in_=din(name, [128, KC]), writes=[r])
            return gvs[:, i, :], r

        def sfx(l):
            return "" if l is None else "_%d" % l

        for stg in stages:
            k.barrier()
            if stg == "load_x":
                xd = din("xT", [128, KC, T])
                for tt in range(NT):
                    sl = slice(tt * TT, (tt + 1) * TT)
                    k.dma("sp", k.dsem("x"), out=x[:, :, sl], in_=xd[:, :, sl], writes=[xres[tt]])
            elif stg == "store_x":
                xo = dout("xT_out", [128, KC, T], F32)
                for tt in range(NT):
                    sl = slice(tt * TT, (tt + 1) * TT)
                    k.dma("sp", k.dsem("xo"), out=xo[:, :, sl], in_=x[:, :, sl], reads=[xres[tt]])
            elif stg[0] == "ffn":
                _, l, i = stg
                ar.mark()
                gv, gr = load_g("gffn%s_%d" % (sfx(l), i))
                xn = ar.alloc([KC, T], BF16); xn_r = [Res() for _ in range(NT)]
                nt = alloc_normtmp(P)
                w1buf = [ar.alloc([KC * 256], BF16) for _ in range(2)]; w1res = [Res(), Res()]
                w2buf = [ar.alloc([NF * 128], BF16) for _ in range(2)]; w2res = [Res(), Res()]
                g = ar.alloc([NF, 2 * TT], BF16); gres2 = [[Res(), Res()] for _ in range(NF)]
                stmp = [ar.alloc([TT], F32) for _ in range(2)]; stres = [Res(), Res()]
                emit_rmsnorm(P, cst, nt, x, xres, gv, gr, lambda tt: (xn[:, :, tt * TT:(tt + 1) * TT], xn_r[tt]))
                emit_ffn(P, x, xres, xn, xn_r, din("w1%s_%d" % (sfx(l), i), [NF, 128, KC * 256]),
                         din("w2%s_%d" % (sfx(l), i), [KC, 128, NF * 128]),
                         (w1buf, w1res, w2buf, w2res, g, gres2, stmp, stres), "f")
                ar.release()
            elif stg[0] == "unorm":
                _, l = stg
                ar.mark()
                gv, gr = load_g("gmix%s" % sfx(l))
                xn = ar.alloc([KC, T], BF16); xn_r = [Res() for _ in range(NT)]
                nt = alloc_normtmp(P)
                dsu = k.dsem("ust")
                if fused:
                    def after(tt, o, ores):
                        um = dint("u_mine_%d" % tt, [128, KC * TT], BF16)
                        ua = dint("u_all_%d" % tt, [4 * 128, KC * TT], BF16)
                        ur = Res()
                        k.dma("sp", dsu, out=um.rearrange("p (k t) -> p k t", k=KC), in_=o, reads=[ores], writes=[ur])
                        k.coll("AllGather", RGROUPS, ins=[um], outs=[ua], reads=[ur])
                else:
                    ud = dout("u_mine%s" % sfx(l), [128, KC, T], BF16)

                    def after(tt, o, ores, ud=ud, dsu=dsu):
                        sl = slice(tt * TT, (tt + 1) * TT)
                        k.dma("sp", dsu, out=ud[:, :, sl], in_=o, reads=[ores], writes=[])

                emit_rmsnorm(P, cst, nt, x, xres, gv, gr, lambda tt: (xn[:, :, tt * TT:(tt + 1) * TT], xn_r[tt]),
                             after=after)
                ar.release()
            elif stg[0] == "B":
                _, l = stg
                if fused:
                    def usrc(r, tt):
                        ua = dint("u_all_%d" % tt, [4 * 128, KC * TT], BF16)
                        return ua[r * 128:(r + 1) * 128, :].rearrange("p (k t) -> p k t", k=KC)

                    def mm(q):
                        return dint("mix_mine_%d" % q, [4 * 64, T], BF16).rearrange("(d c) t -> d c t", d=4)

                    def mdst(kind, r, tsl, h=0):
                        if kind == "A":
                            return [(mm(q)[r, :, tsl], slice(64 * q, 64 * q + 64)) for q in range(2)]
                        if kind == "B":
                            return [(mm(2 + 2 * hh + s_)[r, :, tsl], slice(64 * s_, 64 * s_ + 64), hh)
                                    for hh in range(2) for s_ in range(2)]
                        return [(mm(6 + h)[r, :, tsl], slice(0, 64))]
                else:
                    u_all = din("u_all%s" % sfx(l), [4, 128, KC, T], BF16)
                    mix = dout("mix_mine%s" % sfx(l), [4, 512, T], BF16)

                    def usrc(r, tt, u_all=u_all):
                        return u_all[r, :, :, tt * TT:(tt + 1) * TT]

                    def mdst(kind, r, tsl, h=0, mix=mix):
                        if kind == "A":
                            return [(mix[r, 0:128, tsl], slice(0, 128))]
                        if kind == "B":
                            return [(mix[r, 128 + 128 * hh:256 + 128 * hh, tsl], slice(0, 128), hh) for hh in range(2)]
                        return [(mix[r, 384 + 64 * h:448 + 64 * h, tsl], slice(0, 64))]
                emit_phaseB(P, cst, usrc, din("wB%s" % sfx(l), [128, KC, NCB]), din("wsB%s" % sfx(l), [128, 640]),
                            din("vecB%s" % sfx(l), [128, NVB]), mdst)
            elif stg[0] == "C":
                _, l = stg
                if fused:
                    def umine(tt):
                        return dint("u_mine_%d" % tt, [128, KC * TT], BF16).rearrange("p (k t) -> p k t", k=KC)
                    mix_all = dint("mix_all", [4 * 512, T], BF16).rearrange("(r c) t -> r c t", r=4)
                else:
                    ud = din("u_mine%s" % sfx(l), [128, KC, T], BF16)
                    mix_all = din("mix_all%s" % sfx(l), [4, 512, T], BF16)

                    def umine(tt, ud=ud):
                        return ud[:, :, tt * TT:(tt + 1) * TT]
                emit_phaseC(P, cst, x, xres, umine, mix_all, din("wC%s" % sfx(l), [KC, 128, 5120]),
                            din("bgate%s" % sfx(l), [128, 24]), din("wo%s" % sfx(l), [KC, 128, 1024]))
            elif stg[0] == "cross":
                _, l = stg
                gm, gmr = load_g("gmem%s" % sfx(l))
                gc, gcr = load_g("gcross%s" % sfx(l))
                emit_cross(P, cst, x, xres, din("memT", [128, KC, 256]), gm, gmr, gc, gcr,
                           din("wxq%s" % sfx(l), [KC, 128, 1024]), din("wxkv%s" % sfx(l), [16, 128, 1024]),
                           din("wxo%s" % sfx(l), [KC, 128, 1024]))
            elif stg[0] == "ag_u":
                pass
            elif stg[0] == "ag_mix":
                mall = dint("mix_all", [4 * 512, T], BF16)
                for q in range(8):
                    gq = dint("mix_gath_%d" % q, [16 * 64, T], BF16)
                    gres = Res()
                    k.coll("AllGather", RGROUPS, ins=[dint("mix_mine_%d" % q, [4 * 64, T], BF16)], outs=[gq],
                           writes=[gres])
                    for r in range(4):
                        def srcf(e, r=r, gq=gq):
                            if "c" not in pid_cache:
                                pid_cache["c"] = nc.sync.partition_id() % 4
                            return gq[bass.ds((pid_cache["c"] + 4 * r) * 64, 64), :]
                        k.dma("sp", None, out=mall[r * 512 + 64 * q:r * 512 + 64 * q + 64, :], in_=srcf, reads=[gres])
            elif stg == "final":
                ar.mark()
                gv, gr = load_g("gfin")
                nt = alloc_normtmp(P)
                stg_t = [ar.alloc([KC, TT], F32) for _ in range(2)]; stg_r = [Res(), Res()]
                od = dout("outT", [128, KC, T], F32)
                dso = k.dsem("out")

                def after(tt, o, ores, od=od, dso=dso):
                    sl = slice(tt * TT, (tt + 1) * TT)
                    k.dma("sp", dso, out=od[:, :, sl], in_=o, reads=[ores], writes=[])

                emit_rmsnorm(P, cst, nt, x, xres, gv, gr, lambda tt: (stg_t[tt % 2], stg_r[tt % 2]), after=after)
                ar.release()
            else:
                raise ValueError(stg)
        k.barrier()
        k.final_wait("sp")
        k.emit()
    return nc


_PROGS = {}


def get_prog(key, stages, fused=False):
    if key not in _PROGS:
        _PROGS[key] = build(stages, fused)
    return _PROGS[key]


def _run(nc, maps):
    res = run_bass_kernel_spmd(nc, maps, core_ids=list(range(NCORES)))
    return res.results


def _weights_ffn(inp, l, i, names=None):
    sfx = "_%d_%d" % (l, i)
    if i == 1:
        w_in, w_out, g = inp["w_ffn1_in"], inp["w_ffn1_out"], inp["g_ffn1"]
    else:
        w_in, w_out, g = inp["w_ffn2_in"], inp["w_ffn2_out"], inp["g_ffn2"]
    return {"w1" + sfx: pack_w1(w_in[l]), "w2" + sfx: pack_w2(w_out[l]), "gffn" + sfx: pack_vec(g[l])}


def _weights_C(inp, l):
    s = "_%d" % l
    return {"wC" + s: pack_wC(inp, l), "bgate" + s: pack_bgate(inp, l), "wo" + s: pack_sq(inp["w_o"][l]),
            "gmem" + s: pack_vec(inp["g_mem"][l]), "gcross" + s: pack_vec(inp["g_cross"][l]),
            "wxq" + s: pack_sq(inp["w_xq"][l]), "wxkv" + s: pack_sq(inp["w_xkv"][l]), "wxo" + s: pack_sq(inp["w_xo"][l])}


FUSED = True


def fused_stages():
    st = ["load_x"]
    for l in range(L):
        st += [("ffn", l, 1), ("unorm", l), ("ag_u", l), ("B", l), ("ag_mix", l), ("C", l), ("cross", l), ("ffn", l, 2)]
    st.append("final")
    return st


def kernel_fused(inp):
    cmask = const_mask()
    xs = inp["x"].reshape(NCORES, T, D)
    w = {"cmask": cmask, "gfin": pack_vec(inp["g_final"])}
    wBs = {}
    for l in range(L):
        w.update(_weights_ffn(inp, l, 1))
        w.update(_weights_ffn(inp, l, 2))
        w.update(_weights_C(inp, l))
        w["gmix_%d" % l] = pack_vec(inp["g_mix"][l])
        for j in range(4):
            wBs[(l, j)] = {"wB_%d" % l: pack_wB(inp["w_in"][l], j), "wsB_%d" % l: pack_wsB(inp, l, j),
                           "vecB_%d" % l: pack_vecB(inp, l, j)}
    memT = [pack_memT(inp["mem"][b]) for b in range(2)]
    maps = []
    for c in range(NCORES):
        b, j = c // 4, c % 4
        m = dict(w)
        for l in range(L):
            m.update(wBs[(l, j)])
        m["xT"] = pack_xT(xs[c])
        m["memT"] = memT[b]
        maps.append(m)
    prog = get_prog("fused", fused_stages(), fused=True)
    r = _run(prog, maps)
    out = np.stack([unpack_xT(r[c]["outT"]) for c in range(NCORES)], axis=0)
    return out.reshape(2, 8192, D).astype(np.float32)


def kernel(**inp):
    inp = {k_: np.asarray(v) for k_, v in inp.items()}
    if FUSED:
        return kernel_fused(inp)
    cmask = const_mask()
    xs = inp["x"].reshape(NCORES, T, D)
    p1 = get_prog("p1", ["load_x", ("ffn", 0, 1), ("unorm", 0), "store_x"])
    w = {"cmask": cmask, "gmix_0": pack_vec(inp["g_mix"][0])}
    w.update(_weights_ffn(inp, 0, 1))
    r = _run(p1, [dict(w, xT=pack_xT(xs[c])) for c in range(NCORES)])
    xT = [r[c]["xT_out"] for c in range(NCORES)]
    um = [r[c]["u_mine_0"] for c in range(NCORES)]
    out = None
    for l in range(L):
        pB = get_prog("pB", [("B", None)])
        maps = []
        for c in range(NCORES):
            b, j = c // 4, c % 4
            maps.append({"cmask": cmask, "u_all": np.stack([um[4 * b + rr] for rr in range(4)], axis=0),
                         "wB": pack_wB(inp["w_in"][l], j), "wsB": pack_wsB(inp, l, j), "vecB": pack_vecB(inp, l, j)})
        r = _run(pB, maps)
        mix = [r[c]["mix_mine"] for c in range(NCORES)]
        w = {"cmask": cmask}
        w.update(_weights_C(inp, l))
        w.update(_weights_ffn(inp, l, 2))
        if l + 1 < L:
            stages = ["load_x", ("C", l), ("cross", l), ("ffn", l, 2), ("ffn", l + 1, 1), ("unorm", l + 1), "store_x"]
            w.update(_weights_ffn(inp, l + 1, 1))
            w["gmix_%d" % (l + 1)] = pack_vec(inp["g_mix"][l + 1])
        else:
            stages = ["load_x", ("C", l), ("cross", l), ("ffn", l, 2), "final"]
            w["gfin"] = pack_vec(inp["g_final"])
        pC = get_prog("pC%d" % l, stages)
        maps = []
        for c in range(NCORES):
            b, cc = c // 4, c % 4
            m = dict(w)
            m["xT"] = xT[c]
            m["u_mine_%d" % l] = um[c]
            m["mix_all_%d" % l] = np.stack([mix[4 * b + j][cc] for j in range(4)], axis=0)
            m["memT"] = pack_memT(inp["mem"][b])
            maps.append(m)
        r = _run(pC, maps)
        if l + 1 < L:
            xT = [r[c]["xT_out"] for c in range(NCORES)]
            um = [r[c]["u_mine_%d" % (l + 1)] for c in range(NCORES)]
        else:
            out = np.stack([unpack_xT(r[c]["outT"]) for c in range(NCORES)], axis=0)
    return out.reshape(2, 8192, D).astype(np.float32)
```

```python
import numpy as np
from contextlib import ExitStack
import concourse.bass as bass
import concourse.mybir as mybir
from concourse.bass_utils import run_bass_kernel_spmd

F32 = mybir.dt.float32
BF16 = mybir.dt.bfloat16
AF = mybir.ActivationFunctionType
ALU = mybir.AluOpType
AX = mybir.AxisListType

D = 1024
KC = 8
T = 2048
TT = 512
NT = T // TT
DFF = 2816
NF = DFF // 128
EPS = 1e-6
L = 2
NCORES = 8


class Res:
    __slots__ = ("name", "w", "r")

    def __init__(self, name=""):
        self.name = name
        self.w = None
        self.r = {}


class DSem:
    __slots__ = ("key", "sem", "count")

    def __init__(self, key, sem):
        self.key = key
        self.sem = sem
        self.count = 0


class Eng:
    def __init__(self, name, sem):
        self.name = name
        self.sem = sem
        self.count = 0
        self.seen = {}
        self.q = []


class K:
    def __init__(self, nc, stack):
        self.nc = nc
        self.stack = stack
        self.engs = {}
        for n in ("pe", "act", "dve", "pool", "sp"):
            self.engs[n] = Eng(n, stack.enter_context(nc.semaphore("sem_" + n)))
        self.dsems = {}
        self.ninst = 0
        self.sched = {}

    NPOOL = 64

    def semname(self, sem):
        for e in self.engs.values():
            if e.sem is sem:
                return e.name
        for d in self.dsems.values():
            if d.sem is sem:
                return d.key
        return "?"

    def simulate(self):
        cnt = {}
        pos = {e: 0 for e in self.sched}
        progress = True
        while progress:
            progress = False
            for e, lst in self.sched.items():
                while pos[e] < len(lst):
                    waits, key, inc = lst[pos[e]]
                    if all(cnt.get(kk, 0) >= v for kk, v in waits):
                        cnt[key] = cnt.get(key, 0) + inc
                        pos[e] += 1
                        progress = True
                    else:
                        break
        stuck = {e: (pos[e], len(lst), lst[pos[e]][0] if pos[e] < len(lst) else None) for e, lst in self.sched.items()}
        return stuck, cnt

    def dsem(self, key):
        return None

    def _pool_sem(self):
        if not hasattr(self, "dpool"):
            self.dpool = []
            self.dnext = 0
        if len(self.dpool) < self.NPOOL:
            i = len(self.dpool)
            d = DSem("dp%d" % i, self.stack.enter_context(self.nc.semaphore("dsem_p%d" % i)))
            self.dpool.append(d)
            self.dsems[d.key] = d
            return d
        d = self.dpool[self.dnext % self.NPOOL]
        self.dnext += 1
        return d

    def _waits(self, eng, reads, writes):
        deps = {}

        def add(tok):
            if tok is None:
                return
            key, sem, val = tok
            if key not in deps or deps[key][1] < val:
                deps[key] = (sem, val)

        for r in reads:
            add(r.w)
        for w in writes:
            add(w.w)
            for tok in w.r.values():
                add(tok)
        out = []
        for key, (sem, val) in deps.items():
            if eng.seen.get(key, 0) >= val:
                continue
            if key == "pe" and eng.name == "pe":
                continue
            eng.seen[key] = val
            out.append((sem, val))
        return out

    def op(self, ename, fn, reads=(), writes=()):
        eng = self.engs[ename]
        waits = self._waits(eng, reads, writes)
        sem = eng.sem
        eng.count += 1
        tok = (ename, sem, eng.count)

        def run(e, waits=waits, fn=fn, sem=sem):
            for s, v in waits:
                e.wait_ge(s, v)
            fn(e).then_inc(sem, 1)

        eng.q.append(run)
        self.sched.setdefault(ename, []).append(([(self.semname(s), v) for s, v in waits], ename, 1))
        for r in reads:
            r.r[ename] = tok
        for w in writes:
            w.w = tok
            w.r = {}
        self.ninst += 1

    def dma(self, qname, ds, out, in_, reads=(), writes=(), **kw):
        eng = self.engs[qname]
        ds = self._pool_sem()
        waits = self._waits(eng, reads, writes)
        if ds.count > 0 and eng.seen.get(ds.key, 0) < ds.count:
            eng.seen[ds.key] = ds.count
            waits.append((ds.sem, ds.count))
        ds.count += 16
        tok = (ds.key, ds.sem, ds.count)

        def run(e, waits=waits, out=out, in_=in_, kw=kw, sem=ds.sem, cnt=ds.count):
            for s, v in waits:
                e.wait_ge(s, v)
            e.dma_start(out=out(e) if callable(out) else out, in_=in_(e) if callable(in_) else in_,
                        **kw).then_inc(sem, 16)

        eng.q.append(run)
        self.sched.setdefault(qname, []).append(([(self.semname(s), v) for s, v in waits], ds.key, 16))
        for r in reads:
            r.r[ds.key] = tok
        for w in writes:
            w.w = tok
            w.r = {}
        self.ninst += 1

    def coll(self, kind, groups, ins, outs, reads=(), writes=()):
        eng = self.engs["pool"]
        i = len([k_ for k_ in self.dsems if k_.startswith("cc")])
        ds = DSem("cc%d" % i, self.stack.enter_context(self.nc.semaphore("ccsem_%d" % i)))
        self.dsems[ds.key] = ds
        waits = self._waits(eng, reads, writes)
        ds.count += 1
        tok = (ds.key, ds.sem, ds.count)

        def run(e, waits=waits, sem=ds.sem):
            for s, v in waits:
                e.wait_ge(s, v)
            e.collective_compute(kind, ALU.bypass, replica_groups=groups, ins=[a.opt() for a in ins],
                                 outs=[a.opt() for a in outs]).then_inc(sem, 1)

        eng.q.append(run)
        self.sched.setdefault("pool", []).append(([(self.semname(s), v) for s, v in waits], ds.key, 1))
        for r in reads:
            r.r[ds.key] = tok
        for w in writes:
            w.w = tok
            w.r = {}

    def barrier(self):
        toks = [(e.name, e.sem, e.count) for e in self.engs.values() if e.count > 0]
        toks += [(d.key, d.sem, d.count) for d in self.dsems.values() if d.count > 0]
        for eng in self.engs.values():
            waits = []
            for key, sem, val in toks:
                if eng.seen.get(key, 0) >= val:
                    continue
                eng.seen[key] = val
                waits.append((sem, val))

            def run(e, waits=waits):
                for s, v in waits:
                    e.wait_ge(s, v)

            eng.q.append(run)
            self.sched.setdefault(eng.name, []).append(([(self.semname(s), v) for s, v in waits], "_nop", 0))

    def final_wait(self, qname="sp"):
        eng = self.engs[qname]
        toks = [(d.key, d.sem, d.count) for d in self.dsems.values() if d.count > 0]
        toks += [(e.name, e.sem, e.count) for e in self.engs.values() if e.count > 0]

        def run(e, toks=toks):
            for key, s, v in toks:
                e.wait_ge(s, v)

        eng.q.append(run)

    def emit(self):
        nc = self.nc
        with nc.Block() as block:
            @block.tensor
            def _(e):
                for f in self.engs["pe"].q:
                    f(e)

            @block.scalar
            def _(e):
                for f in self.engs["act"].q:
                    f(e)

            @block.vector
            def _(e):
                for f in self.engs["dve"].q:
                    f(e)

            @block.gpsimd
            def _(e):
                for f in self.engs["pool"].q:
                    f(e)

            @block.sync
            def _(e):
                for f in self.engs["sp"].q:
                    f(e)


class Arena:
    def __init__(self, ap_f32, nbytes):
        self.ap = ap_f32
        self.nbytes = nbytes
        self.off = 0
        self.marks = []

    def alloc(self, shape_free, dtype, parts=128):
        n = int(np.prod(shape_free))
        esz = 2 if dtype == BF16 else 4
        nb = (n * esz + 31) // 32 * 32
        assert self.off + nb <= self.nbytes, ("arena overflow", self.off, nb, self.nbytes)
        a = self.ap[:, self.off // 4:(self.off + nb) // 4]
        self.off += nb
        if dtype == BF16:
            a = a.bitcast(BF16)
        a = a[:, 0:n]
        if len(shape_free) == 2:
            a = a.rearrange("p (a b) -> p a b", a=shape_free[0])
        elif len(shape_free) == 3:
            a = a.rearrange("p (a b c) -> p a b c", a=shape_free[0], b=shape_free[1])
        return a

    def mark(self):
        self.marks.append(self.off)

    def release(self):
        self.off = self.marks.pop()


class Prog:
    def __init__(self, nc, k, arena, psum):
        self.nc = nc
        self.k = k
        self.ar = arena
        self.psum = psum
        self.pres = [Res("ps%d" % i) for i in range(8)]
        self.pidx = 0

    def bank(self):
        i = self.pidx
        self.pidx = (self.pidx + 1) % 8
        return self.psum[:, i, :], self.pres[i]


def alloc_normtmp(P, tw=TT):
    ar = P.ar
    return dict(sq=ar.alloc([KC, tw], F32), sq_r=Res(), ssum=ar.alloc([tw], F32), ssum_r=Res(),
                rstd=ar.alloc([tw], F32), rstd_r=Res())


def emit_rmsnorm(P, cst, nt, x, xres, gvec, gres, dst, ntiles=NT, tw=TT, after=None):
    k = P.k
    ones, ones_res = cst["ones_f"], cst["ones_f_r"]
    sq, sq_res, ssum, ssum_res, rstd, rstd_res = nt["sq"], nt["sq_r"], nt["ssum"], nt["ssum_r"], nt["rstd"], nt["rstd_r"]
    for tt in range(ntiles):
        sl = slice(tt * tw, (tt + 1) * tw)
        o, ores = dst(tt)
        k.op("act", lambda e, sl=sl: e.activation(out=sq[:, :, 0:tw], in_=x[:, :, sl], func=AF.Square),
             reads=[xres[tt]], writes=[sq_res])
        k.op("dve", lambda e: e.tensor_reduce(out=ssum[:, 0:tw], in_=sq[:, :, 0:tw].rearrange("p k t -> p t k"),
                                              axis=AX.X, op=ALU.add),
             reads=[sq_res], writes=[ssum_res])
        ps, pr = P.bank()
        k.op("pe", lambda e, ps=ps: e.matmul(ps[:, 0:tw], lhsT=ones, rhs=ssum[:, 0:tw], start=True, stop=True),
             reads=[ssum_res, ones_res], writes=[pr])
        k.op("act", lambda e, ps=ps: e.activation(out=rstd[:, 0:tw], in_=ps[:, 0:tw], func=AF.Sqrt,
                                                  bias=cst["eps"], scale=1.0 / D),
             reads=[pr, cst["eps_r"]], writes=[rstd_res])
        k.op("dve", lambda e: e.reciprocal(out=rstd[:, 0:tw], in_=rstd[:, 0:tw]),
             reads=[rstd_res], writes=[rstd_res])
        for kc in range(KC):
            k.op("dve", lambda e, kc=kc, sl=sl, o=o: e.scalar_tensor_tensor(
                out=o[:, kc, :], in0=x[:, kc, sl], scalar=gvec[:, kc:kc + 1], in1=rstd[:, 0:tw],
                op0=ALU.mult, op1=ALU.mult),
                reads=[xres[tt], gres, rstd_res], writes=[ores])
        if after is not None:
            after(tt, o, ores)


def consts_eps(P):
    return P.eps_ap


def emit_ffn(P, x, xres, xn, xnres, w1d, w2d, wb, tag):
    k = P.k
    (w1buf, w1res, w2buf, w2res, g, gres, stmp, stres) = wb
    ds1 = [k.dsem("w1_0"), k.dsem("w1_1")]
    ds2 = [k.dsem("w2_0"), k.dsem("w2_1")]
    cnt1 = 0
    cnt2 = 0
    for st in range(2):
        for f in range(NF):
            s = cnt1 % 2
            cnt1 += 1
            k.dma("pool", ds1[s], out=w1buf[s], in_=w1d[f], reads=[], writes=[w1res[s]])
            for t2 in range(2):
                tt = st * 2 + t2
                sl = slice(tt * TT, (tt + 1) * TT)
                pa, par = P.bank()
                pb, pbr = P.bank()
                for kc in range(KC):
                    k.op("pe", lambda e, pa=pa, s=s, kc=kc, sl=sl: e.matmul(
                        pa, lhsT=w1buf[s][:, kc * 256:kc * 256 + 128], rhs=xn[:, kc, sl],
                        start=(kc == 0), stop=(kc == KC - 1)),
                        reads=[w1res[s], xnres[tt]], writes=[par])
                for kc in range(KC):
                    k.op("pe", lambda e, pb=pb, s=s, kc=kc, sl=sl: e.matmul(
                        pb, lhsT=w1buf[s][:, kc * 256 + 128:kc * 256 + 256], rhs=xn[:, kc, sl],
                        start=(kc == 0), stop=(kc == KC - 1)),
                        reads=[w1res[s], xnres[tt]], writes=[pbr])
                ts_ = (f * 2 + t2) % 2
                k.op("act", lambda e, pa=pa, ts_=ts_: e.activation(out=stmp[ts_], in_=pa, func=AF.Silu),
                     reads=[par], writes=[stres[ts_]])
                k.op("dve", lambda e, pb=pb, ts_=ts_, f=f, t2=t2: e.tensor_tensor(
                    out=g[:, f, t2 * TT:(t2 + 1) * TT], in0=stmp[ts_], in1=pb, op=ALU.mult),
                    reads=[stres[ts_], pbr], writes=[gres[f][t2]])
        for dc in range(KC):
            s = cnt2 % 2
            cnt2 += 1
            k.dma("pool", ds2[s], out=w2buf[s], in_=w2d[dc], reads=[], writes=[w2res[s]],
                  max_dma_last_dim=1408 * 4)
            for t2 in range(2):
                tt = st * 2 + t2
                sl = slice(tt * TT, (tt + 1) * TT)
                po, por = P.bank()
                for f in range(NF):
                    k.op("pe", lambda e, po=po, s=s, f=f, t2=t2: e.matmul(
                        po, lhsT=w2buf[s][:, f * 128:(f + 1) * 128], rhs=g[:, f, t2 * TT:(t2 + 1) * TT],
                        start=(f == 0), stop=(f == NF - 1)),
                        reads=[w2res[s], gres[f][t2]], writes=[por])
                k.op("dve", lambda e, po=po, dc=dc, sl=sl: e.scalar_tensor_tensor(
                    out=x[:, dc, sl], in0=po, scalar=0.5, in1=x[:, dc, sl], op0=ALU.mult, op1=ALU.add),
                    reads=[por, xres[tt]], writes=[xres[tt]])


def pack_w1(w):
    a = w[:, :DFF].reshape(KC, 128, NF, 128)
    b = w[:, DFF:].reshape(KC, 128, NF, 128)
    ab = np.stack([a, b], axis=3)
    return np.ascontiguousarray(ab.transpose(2, 1, 0, 3, 4)).reshape(NF, 128, KC * 256)


def pack_w2(w):
    a = w.reshape(NF, 128, KC, 128)
    return np.ascontiguousarray(a.transpose(2, 1, 0, 3)).reshape(KC, 128, NF * 128)


def pack_vec(v):
    return np.ascontiguousarray(v.reshape(KC, 128).T)


def pack_xT(xc):
    return np.ascontiguousarray(xc.reshape(T, KC, 128).transpose(2, 1, 0))


def unpack_xT(a):
    return np.ascontiguousarray(a.transpose(2, 1, 0)).reshape(T, D)


def make_consts(P, mask_d):
    k, ar = P.k, P.ar
    c = {}
    c["ones_f"] = ar.alloc([128], F32); c["ones_f_r"] = Res()
    c["ones_bf"] = ar.alloc([128], BF16); c["ones_bf_r"] = Res()
    c["mask"] = ar.alloc([128], BF16); c["mask_r"] = Res()
    c["eps"] = ar.alloc([1], F32); c["eps_r"] = Res()
    P.eps_ap = c["eps"]; P.eps_res = c["eps_r"]
    k.op("pool", lambda e: e.memset(c["ones_f"], 1.0), writes=[c["ones_f_r"]])
    k.op("pool", lambda e: e.memset(c["ones_bf"], 1.0), writes=[c["ones_bf_r"]])
    k.op("pool", lambda e: e.memset(c["eps"], EPS), writes=[c["eps_r"]])
    k.dma("pool", k.dsem("cmask"), out=c["mask"], in_=mask_d, writes=[c["mask_r"]])
    return c


NGT = 16
NCB = 1026
VB_PSCALE = 0
VB_CONVW = 1
VB_CONVB = 9
VB_BA = 11
VB_BX = 13
VB_LAM = 15
VB_BF = 17
VB_SEL = 19
VB_CORR = 23
NVB = 39


def emit_phaseB(P, cst, usrc, wB_d, wsB_d, vecB_d, mdst, tile_done=None):
    k = P.k
    ar = P.ar
    ar.mark()
    ones_f, ones_f_r, mask, mask_r = cst["ones_f"], cst["ones_f_r"], cst["mask"], cst["mask_r"]
    GB = [5, 6, 7]
    SB = [2, 3, 4]
    OB = [0, 1]
    gcnt = [0]
    scnt = [0]

    def gbank():
        i = GB[gcnt[0] % 3]
        gcnt[0] += 1
        return P.psum[:, i, :], P.pres[i]

    def sbank():
        i = SB[scnt[0] % 3]
        scnt[0] += 1
        return P.psum[:, i, :], P.pres[i]

    utile = ar.alloc([KC, TT], BF16); utile_r = Res()
    wB = ar.alloc([KC, NCB], BF16); wB_r = Res()
    wsB = ar.alloc([640], BF16); wsB_r = Res()
    vec = ar.alloc([NVB], F32); vec_r = Res()
    nbf = ar.alloc([2], F32); nbf_r = Res()
    nsp8 = ar.alloc([2], F32); nsp8_r = Res()
    Kaug = [ar.alloc([NGT * TT], BF16) for _ in range(2)]
    K_r = [[Res() for _ in range(NGT)] for _ in range(2)]
    Vaug = ar.alloc([NGT * 4, 2, 65], BF16)
    V_r = [Res() for _ in range(NGT)]
    negc = ar.alloc([NGT * 4, 2], F32)
    negc_r = [[Res() for _ in range(NGT)] for _ in range(2)]
    xa_buf = ar.alloc([528], F32); xa_r = Res()
    sA = ar.alloc([528], F32); sA_r = Res()
    sBf = ar.alloc([528], F32); sB_r = Res()
    acc = ar.alloc([TT], F32); acc_r = Res()
    d_bf = [ar.alloc([TT], BF16) for _ in range(2)]; d_r = [Res(), Res()]
    xb_buf = [ar.alloc([515], F32) for _ in range(2)]; xb_r = [Res(), Res()]
    xc = [[ar.alloc([TT], F32) for _ in range(2)] for _ in range(2)]
    xc_r = [[Res(), Res()] for _ in range(2)]
    xc_bf = [[ar.alloc([TT], BF16) for _ in range(2)] for _ in range(2)]
    xcb_r = [[Res(), Res()] for _ in range(2)]
    gg = [[ar.alloc([TT], BF16) for _ in range(2)] for _ in range(2)]
    gg_r = [[Res(), Res()] for _ in range(2)]
    t1 = ar.alloc([TT], F32); t1_r = Res()
    Qaug = [[ar.alloc([TT], BF16) for _ in range(2)] for _ in range(2)]
    Q_r = [[Res(), Res()] for _ in range(2)]
    rowe = ar.alloc([TT], F32); rowe_r = Res()
    crow = [ar.alloc([TT], F32) for _ in range(2)]; crow_r = [Res(), Res()]
    clast = ar.alloc([2], F32); clast_r = [Res(), Res()]
    ra = ar.alloc([TT], F32); ra_r = Res()
    tmp = ar.alloc([TT], F32); tmp_r = Res()
    ig = ar.alloc([TT], F32); ig_r = Res()
    hbuf = ar.alloc([TT], F32); hbuf_r = Res()
    hlast = ar.alloc([2], F32); hlast_r = [Res(), Res()]
    Pt = [ar.alloc([TT], BF16) for _ in range(3)]; Pt_r = [Res(), Res(), Res()]
    rec = ar.alloc([TT], F32); rec_r = Res()
    osb = ar.alloc([TT], F32); osb_r = Res()
    outA = [ar.alloc([TT], BF16)] * 2; outA_r = [Res()] * 2
    outB = [ar.alloc([2, TT], BF16) for _ in range(2)]; outB_r = [Res(), Res()]
    outC = [[ar.alloc([TT], BF16)] * 2 for _ in range(2)]
    outC_r = [[Res()] * 2 for _ in range(2)]
    ds_u = k.dsem("bu")
    ds_o = k.dsem("bo")
    store_res = []

    k.dma("pool", k.dsem("bw"), out=wB, in_=wB_d, writes=[wB_r])
    k.dma("pool", k.dsem("bws"), out=wsB, in_=wsB_d, writes=[wsB_r])
    k.dma("sp", k.dsem("bv"), out=vec, in_=vecB_d, writes=[vec_r])
    for h in range(2):
        k.op("pool", lambda e, h=h: e.memset(Kaug[h][64:65, :], 1.0), writes=K_r[h])
    k.op("pool", lambda e: e.memset(Vaug[:, :, :, 64:65], 1.0), writes=V_r)
    k.op("dve", lambda e: e.tensor_scalar(out=nbf, in0=vec[:, VB_BF:VB_BF + 2], scalar1=-1.0, scalar2=None,
                                          op0=ALU.mult), reads=[vec_r], writes=[nbf_r])
    k.op("act", lambda e: e.activation(out=nsp8, in_=vec[:, VB_LAM:VB_LAM + 2], func=AF.Exp, scale=-1.0),
         reads=[vec_r], writes=[nsp8_r])
    k.op("act", lambda e: e.activation(out=nsp8, in_=nsp8, func=AF.Ln, bias=ones_f[:, 0:1], scale=1.0),
         reads=[nsp8_r, ones_f_r], writes=[nsp8_r])
    k.op("dve", lambda e: e.tensor_scalar(out=nsp8, in0=nsp8, scalar1=-8.0, scalar2=None, op0=ALU.mult),
         reads=[nsp8_r], writes=[nsp8_r])

    def bank_any():
        return gbank()

    def S1(gt):
        r, tt = gt // 4, gt % 4
        par = gt % 2
        C = {}

        def proj(c0, ncol, nparts):
            ps, pr = gbank()
            for kc in range(KC):
                k.op("pe", lambda e, ps=ps, kc=kc: e.matmul(ps[0:nparts, :], lhsT=wB[:, kc, c0:c0 + ncol],
                                                          rhs=utile[:, kc, :], start=(kc == 0), stop=(kc == KC - 1)),
                     reads=[wB_r, utile_r], writes=[pr])
            return ps, pr

        def m_load():
            k.dma("sp", ds_u, out=utile, in_=usrc(r, tt), writes=[utile_r])
        C["load"] = [m_load]

        st = {}

        def xa1():
            if gt == 0:
                k.op("dve", lambda e: e.memset(xa_buf[:, 0:16], 0.0), writes=[xa_r])
            else:
                k.op("dve", lambda e: e.tensor_copy(out=xa_buf[:, 0:16], in_=xa_buf[:, 512:528]),
                     reads=[xa_r], writes=[xa_r])
            st["xa"] = proj(0, 128, 128)

        def xa2():
            ps, pr = st["xa"]
            k.op("dve", lambda e: e.tensor_copy(out=xa_buf[:, 16:528], in_=ps), reads=[pr], writes=[xa_r])

            def padd(dst, dr, srcb, sr, lo, sh):
                k.op("dve", lambda e: e.tensor_tensor(out=dst[:, lo:528], in0=srcb[:, lo:528],
                                                      in1=srcb[:, lo - sh:528 - sh], op=ALU.add),
                     reads=[sr], writes=[dr])

            def pacc(srcb, sr, i):
                if i == 0:
                    k.op("dve", lambda e: e.tensor_scalar(out=acc, in0=srcb[:, 16:528],
                                                          scalar1=vec[:, VB_SEL:VB_SEL + 1], scalar2=None, op0=ALU.mult),
                         reads=[sr, vec_r], writes=[acc_r])
                else:
                    k.op("dve", lambda e: e.scalar_tensor_tensor(out=acc, in0=srcb[:, 16:528],
                                                                 scalar=vec[:, VB_SEL + i:VB_SEL + i + 1], in1=acc,
                                                                 op0=ALU.mult, op1=ALU.add),
                         reads=[sr, vec_r, acc_r], writes=[acc_r])

            padd(sA, sA_r, xa_buf, xa_r, 1, 1)
            pacc(sA, sA_r, 0)
            padd(sBf, sB_r, sA, sA_r, 3, 2)
            pacc(sBf, sB_r, 1)
            padd(sA, sA_r, sBf, sB_r, 7, 4)
            pacc(sA, sA_r, 2)
            padd(sBf, sB_r, sA, sA_r, 15, 8)
            pacc(sBf, sB_r, 3)
            if gt == 0:
                k.op("dve", lambda e: e.tensor_tensor(out=acc[:, 0:16], in0=acc[:, 0:16],
                                                      in1=vec[:, VB_CORR:VB_CORR + 16], op=ALU.mult),
                     reads=[acc_r, vec_r], writes=[acc_r])
            k.op("dve", lambda e: e.tensor_tensor(out=d_bf[par], in0=acc, in1=xa_buf[:, 16:528], op=ALU.subtract),
                 reads=[xa_r, acc_r], writes=[d_r[par]])
        C["xa"] = [xa1, xa2]

        def mk_xb(hh):
            def m1():
                if gt == 0:
                    k.op("dve", lambda e: e.memset(xb_buf[hh][:, 0:3], 0.0), writes=[xb_r[hh]])
                else:
                    k.op("dve", lambda e: e.tensor_copy(out=xb_buf[hh][:, 0:3], in_=xb_buf[hh][:, 512:515]),
                         reads=[xb_r[hh]], writes=[xb_r[hh]])
                st["xb%d" % hh] = proj(128 + 128 * hh, 128, 128)

            def m2():
                ps, pr = st["xb%d" % hh]
                k.op("dve", lambda e: e.tensor_copy(out=xb_buf[hh][:, 3:515], in_=ps), reads=[pr], writes=[xb_r[hh]])
                cw = VB_CONVW + 4 * hh
                k.op("dve", lambda e: e.tensor_scalar(
                    out=xc[hh][par], in0=xb_buf[hh][:, 0:512], scalar1=vec[:, cw:cw + 1],
                    scalar2=vec[:, VB_CONVB + hh:VB_CONVB + hh + 1], op0=ALU.mult, op1=ALU.add),
                    reads=[xb_r[hh], vec_r], writes=[xc_r[hh][par]])
                for kk in range(1, 4):
                    k.op("dve", lambda e, kk=kk: e.scalar_tensor_tensor(
                        out=xc[hh][par], in0=xb_buf[hh][:, kk:kk + 512], scalar=vec[:, cw + kk:cw + kk + 1],
                        in1=xc[hh][par], op0=ALU.mult, op1=ALU.add),
                        reads=[xb_r[hh], vec_r, xc_r[hh][par]], writes=[xc_r[hh][par]])
                k.op("dve", lambda e: e.tensor_copy(out=xc_bf[hh][par], in_=xc[hh][par]),
                     reads=[xc_r[hh][par]], writes=[xcb_r[hh][par]])
            return [m1, m2]

        def mk_gb(hh):
            def m1():
                st["gb%d" % hh] = proj(384 + 128 * hh, 128, 128)

            def m2():
                ps, pr = st["gb%d" % hh]
                k.op("act", lambda e: e.activation(out=t1, in_=ps, func=AF.Square), reads=[pr], writes=[t1_r])
                k.op("dve", lambda e: e.tensor_copy(out=gg[hh][par], in_=ps), reads=[pr, t1_r], writes=[gg_r[hh][par]])

            def m3():
                ps, pr = st["gb%d" % hh]
                k.op("dve", lambda e: e.tensor_scalar(out=t1, in0=t1, scalar1=0.044715, scalar2=1.0, op0=ALU.mult,
                                                      op1=ALU.add), reads=[t1_r], writes=[t1_r])
                k.op("dve", lambda e: e.tensor_tensor(out=t1, in0=t1, in1=ps, op=ALU.mult),
                     reads=[t1_r, pr], writes=[t1_r])

            def m4():
                k.op("act", lambda e: e.activation(out=t1, in_=t1, func=AF.Sigmoid, scale=1.5957691216057308),
                     reads=[t1_r], writes=[t1_r])

            def m5():
                k.op("dve", lambda e: e.tensor_tensor(out=gg[hh][par], in0=gg[hh][par], in1=t1, op=ALU.mult),
                     reads=[t1_r, gg_r[hh][par]], writes=[gg_r[hh][par]])
            return [m1, m2, m3, m4, m5]

        def mk_q(h):
            def m1():
                st["q%d" % h] = proj(640 + 65 * h, 65, 65)

            def m2():
                ps, pr = st["q%d" % h]
                k.op("dve", lambda e: e.tensor_copy(out=Qaug[h][par][0:64, :], in_=ps[0:64, :]),
                     reads=[pr], writes=[Q_r[h][par]])
                k.op("act", lambda e: e.activation(out=rowe[64:65, :], in_=ps[64:65, :], func=AF.Exp,
                                                   bias=nbf[64:65, h:h + 1], scale=-1.0),
                     reads=[pr, nbf_r], writes=[rowe_r])
                k.op("act", lambda e: e.activation(out=rowe[64:65, :], in_=rowe[64:65, :], func=AF.Ln,
                                                   bias=ones_f[64:65, 0:1], scale=1.0),
                     reads=[rowe_r, ones_f_r], writes=[rowe_r])

            def m3():
                init = 0.0 if gt == 0 else clast[64:65, h:h + 1]
                k.op("dve", lambda e: e.tensor_tensor_scan(
                    out=crow[h][64:65, :], data0=ones_f[64:65, 0:1].to_broadcast([1, TT]),
                    data1=rowe[64:65, :], initial=init, op0=ALU.mult, op1=ALU.subtract),
                    reads=[rowe_r, clast_r[h], ones_f_r], writes=[crow_r[h]])
                k.op("dve", lambda e: e.tensor_copy(out=clast[64:65, h:h + 1], in_=crow[h][64:65, TT - 1:TT]),
                     reads=[crow_r[h]], writes=[clast_r[h]])

            def m4():
                k.op("act", lambda e: e.activation(out=Qaug[h][par][64:65, :], in_=crow[h][64:65, :],
                                                   func=AF.Copy, scale=8.0),
                     reads=[crow_r[h]], writes=[Q_r[h][par]])

            def m5():
                pt_, ptr_ = gbank()
                st["n%d" % h] = (pt_, ptr_)
                for blk in range(4):
                    k.op("pe", lambda e, blk=blk: e.matmul(
                        pt_[:, blk:blk + 1], lhsT=crow[h][64:65, blk * 128:(blk + 1) * 128], rhs=ones_f[64:65, 0:1],
                        start=True, stop=True),
                        reads=[crow_r[h], ones_f_r], writes=[ptr_])

            def m6():
                pt_, ptr_ = st["n%d" % h]
                k.op("dve", lambda e: e.tensor_scalar(
                    out=negc[:, 4 * gt:4 * gt + 4, h], in0=pt_[:, 0:4], scalar1=-1.0, scalar2=None, op0=ALU.mult),
                    reads=[ptr_], writes=[negc_r[h][gt]])
            return [m1, m2, m3, m4, m5, m6]

        def mk_k(h):
            def m1():
                st["k%d" % h] = proj(770 + 64 * h, 64, 64)

            def m2():
                ps, pr = st["k%d" % h]
                k.op("act", lambda e: e.activation(out=Kaug[h][0:64, gt * TT:(gt + 1) * TT],
                                                   in_=ps[0:64, :], func=AF.Copy),
                     reads=[pr], writes=[K_r[h][gt]])
            return [m1, m2]

        def v1():
            ps, pr = gbank()
            st["v"] = (ps, pr)
            for blk in range(4):
                for kc in range(KC):
                    k.op("pe", lambda e, blk=blk, kc=kc: e.matmul(
                        ps[:, blk * 128:(blk + 1) * 128], lhsT=utile[:, kc, blk * 128:(blk + 1) * 128],
                        rhs=wB[:, kc, 898:1026], start=(kc == 0), stop=(kc == KC - 1)),
                        reads=[wB_r, utile_r], writes=[pr])

        def v2():
            ps, pr = st["v"]
            k.op("dve", lambda e: e.tensor_copy(
                out=Vaug[:, 4 * gt:4 * gt + 4, :, 0:64], in_=ps.rearrange("p (b h d) -> p b h d", b=4, h=2)),
                reads=[pr], writes=[V_r[gt]])
        C["xb0"], C["xb1"] = mk_xb(0), mk_xb(1)
        C["gb0"], C["gb1"] = mk_gb(0), mk_gb(1)
        C["q0"], C["q1"] = mk_q(0), mk_q(1)
        C["k0"], C["k1"] = mk_k(0), mk_k(1)
        C["v"] = [v1, v2]
        return C

    def S2chains(gt):
        r, tt = gt // 4, gt % 4
        par = gt % 2
        tsl = slice(tt * TT, (tt + 1) * TT)
        C = {}
        st = {}

        def pool1():
            ps, pr = gbank()
            st["pool"] = (ps, pr)
            k.op("pe", lambda e: e.matmul(ps, lhsT=wsB[:, 0:128], rhs=d_bf[par], start=True, stop=True),
                 reads=[wsB_r, d_r[par]], writes=[pr])

        def pool2():
            ps, pr = st["pool"]
            k.op("act", lambda e: e.activation(out=outA[par], in_=ps, func=AF.Identity,
                                               scale=vec[:, VB_PSCALE:VB_PSCALE + 1]),
                 reads=[pr, vec_r], writes=[outA_r[par]])
            for dap, ps_ in mdst("A", r, tsl):
                sr = Res(); store_res.append(sr)
                k.dma("sp", ds_o, out=dap, in_=outA[par][ps_, :], reads=[outA_r[par]], writes=[sr])
        C["pool"] = [pool1, pool2]

        def mk_lru(hh):
            def m1():
                ps, pr = gbank()
                k.op("pe", lambda e: e.matmul(ps, lhsT=wsB[:, 128 + 128 * hh:256 + 128 * hh],
                                              rhs=xc_bf[hh][par], start=True, stop=True),
                     reads=[wsB_r, xcb_r[hh][par]], writes=[pr])
                ps2, pr2 = gbank()
                k.op("pe", lambda e: e.matmul(ps2, lhsT=wsB[:, 384 + 128 * hh:512 + 128 * hh],
                                              rhs=xc_bf[hh][par], start=True, stop=True),
                     reads=[wsB_r, xcb_r[hh][par]], writes=[pr2])
                st["l%d" % hh] = (ps, pr, ps2, pr2)

            def m2():
                ps, pr, ps2, pr2 = st["l%d" % hh]
                k.op("act", lambda e: e.activation(out=ra, in_=ps, func=AF.Sigmoid,
                                                   bias=vec[:, VB_BA + hh:VB_BA + hh + 1], scale=1.0),
                     reads=[pr, vec_r], writes=[ra_r])
                k.op("act", lambda e: e.activation(out=ig, in_=ps2, func=AF.Sigmoid,
                                                   bias=vec[:, VB_BX + hh:VB_BX + hh + 1], scale=1.0),
                     reads=[pr2, vec_r], writes=[ig_r])
                k.op("act", lambda e: e.activation(out=ra, in_=ra, func=AF.Exp, scale=nsp8[:, hh:hh + 1]),
                     reads=[ra_r, nsp8_r], writes=[ra_r])

            def m3():
                k.op("dve", lambda e: e.tensor_tensor(out=tmp, in0=ra, in1=ra, op=ALU.mult),
                     reads=[ra_r], writes=[tmp_r])
                k.op("dve", lambda e: e.tensor_tensor(out=ig, in0=ig, in1=xc[hh][par], op=ALU.mult),
                     reads=[ig_r, xc_r[hh][par]], writes=[ig_r])

            def m4():
                k.op("act", lambda e: e.activation(out=tmp, in_=tmp, func=AF.Sqrt, bias=ones_f[:, 0:1], scale=-1.0),
                     reads=[tmp_r, ones_f_r], writes=[tmp_r])

            def m5():
                k.op("dve", lambda e: e.tensor_tensor(out=ig, in0=ig, in1=tmp, op=ALU.mult),
                     reads=[ig_r, tmp_r], writes=[ig_r])
                init = 0.0 if gt == 0 else hlast[:, hh:hh + 1]
                k.op("dve", lambda e: e.tensor_tensor_scan(out=hbuf, data0=ra, data1=ig, initial=init,
                                                           op0=ALU.mult, op1=ALU.add),
                     reads=[ra_r, ig_r, hlast_r[hh]], writes=[hbuf_r])
                k.op("dve", lambda e: e.tensor_copy(out=hlast[:, hh:hh + 1], in_=hbuf[:, TT - 1:TT]),
                     reads=[hbuf_r], writes=[hlast_r[hh]])
                k.op("dve", lambda e: e.tensor_tensor(out=outB[par][:, hh, :], in0=hbuf, in1=gg[hh][par], op=ALU.mult),
                     reads=[hbuf_r, gg_r[hh][par]], writes=[outB_r[par]])
                if hh == 1:
                    for dap, ps_, hx in mdst("B", r, tsl):
                        sr = Res(); store_res.append(sr)
                        k.dma("sp", ds_o, out=dap, in_=outB[par][ps_, hx, :], reads=[outB_r[par]], writes=[sr])
            return [m1, m2, m3, m4, m5]
        C["lru0"], C["lru1"] = mk_lru(0), mk_lru(1)
        return C

    ORDER = [("load", 0), ("pool", 0), ("q0", 0), ("pool", 1), ("q0", 1), ("lru0", 0), ("q0", 2), ("lru0", 1),
             ("q0", 3), ("q1", 0), ("lru0", 2), ("q1", 1), ("lru0", 3), ("q1", 2), ("lru0", 4), ("q1", 3),
             ("lru1", 0), ("q0", 4), ("lru1", 1), ("q0", 5), ("lru1", 2), ("q1", 4), ("lru1", 3), ("q1", 5),
             ("lru1", 4), ("xa", 0), ("xb0", 0), ("xa", 1), ("xb0", 1), ("xb1", 0), ("gb0", 0), ("xb1", 1),
             ("gb0", 1), ("k0", 0), ("gb0", 2), ("k0", 1), ("gb0", 3), ("k1", 0), ("gb0", 4), ("k1", 1),
             ("gb1", 0), ("v", 0), ("gb1", 1), ("v", 1), ("gb1", 2), ("gb1", 3), ("gb1", 4)]

    def flat(c2, c1):
        out = []
        for name, i in ORDER:
            src_ = c2 if name in ("pool", "lru0", "lru1") else c1
            if src_ is not None:
                out.append(src_[name][i])
        return out

    def S2(gt, steps):
        r, tt = gt // 4, gt % 4
        par = gt % 2
        tsl = slice(tt * TT, (tt + 1) * TT)
        nkb = 4 * gt + 4

        def attn(h):
            oi = OB[(2 * gt + h) % 2]
            ob, obr = P.psum[:, oi, :], P.pres[oi]
            sbs = {}

            def emit_S(kb):
                dd = kb - 4 * gt
                q0 = 128 * dd if dd > 0 else 0
                sb, sbr = sbank()
                sbs[kb] = (sb, sbr, q0, dd)
                k.op("pe", lambda e, sb=sb, q0=q0, kb=kb: e.matmul(
                    sb[:, q0:TT], lhsT=Kaug[h][0:65, kb * 128:(kb + 1) * 128], rhs=Qaug[h][par][0:65, q0:TT],
                    start=True, stop=True),
                    reads=[K_r[h][kb // 4], Q_r[h][par]], writes=[sbr])

            emit_S(0)
            if nkb > 1:
                emit_S(1)
            for kb in range(nkb):
                if kb + 2 < nkb:
                    emit_S(kb + 2)
                sb, sbr, q0, dd = sbs.pop(kb)
                pi = kb % 3
                k.op("act", lambda e, sb=sb, q0=q0, kb=kb, pi=pi: e.activation(
                    out=Pt[pi][:, q0:TT], in_=sb[:, q0:TT], func=AF.Exp, bias=negc[:, kb, h:h + 1], scale=0.125),
                    reads=[sbr, negc_r[h][kb // 4]], writes=[Pt_r[pi]])
                if dd >= 0:
                    k.op("dve", lambda e, q0=q0, pi=pi: e.tensor_tensor(
                        out=Pt[pi][:, q0:q0 + 128], in0=Pt[pi][:, q0:q0 + 128], in1=mask, op=ALU.min),
                        reads=[Pt_r[pi], mask_r], writes=[Pt_r[pi]])
                k.op("pe", lambda e, q0=q0, kb=kb, pi=pi: e.matmul(
                    ob[0:65, q0:TT], lhsT=Vaug[:, kb, h, :], rhs=Pt[pi][:, q0:TT],
                    start=(kb == 0), stop=(kb == nkb - 1)),
                    reads=[V_r[kb // 4], Pt_r[pi]], writes=[obr])
                if steps:
                    steps.pop(0)()
            k.op("dve", lambda e: e.reciprocal(out=rec[64:65, :], in_=ob[64:65, :]), reads=[obr], writes=[rec_r])
            bc, bcr = sbank()
            k.op("pe", lambda e, bc=bc: e.matmul(bc[0:64, :], lhsT=ones_f[64:65, 0:64], rhs=rec[64:65, :],
                                                 start=True, stop=True),
                 reads=[rec_r, ones_f_r], writes=[bcr])
            k.op("act", lambda e: e.activation(out=osb[0:64, :], in_=ob[0:64, :], func=AF.Copy),
                 reads=[obr], writes=[osb_r])
            k.op("dve", lambda e, bc=bc: e.tensor_tensor(out=outC[h][par][0:64, :], in0=osb[0:64, :], in1=bc[0:64, :],
                                                         op=ALU.mult),
                 reads=[osb_r, bcr], writes=[outC_r[h][par]])
            for dap, ps_ in mdst("C", r, tsl, h):
                sr = Res(); store_res.append(sr)
                k.dma("sp", ds_o, out=dap, in_=outC[h][par][ps_, :], reads=[outC_r[h][par]], writes=[sr])
        for h in range(2):
            attn(h)
        while steps:
            steps.pop(0)()
        if tile_done is not None:
            tile_done(gt, store_res)

    for s_ in flat(None, S1(0)):
        s_()
    for gt in range(NGT):
        S2(gt, flat(S2chains(gt), S1(gt + 1) if gt + 1 < NGT else None))
    ar.release()


def pack_wB(w_in_l, j):
    cols = list(range(128 * j, 128 * j + 128))
    cols += list(range(512 + 256 * j, 512 + 256 * j + 256))
    cols += list(range(1536 + 256 * j, 1536 + 256 * j + 256))
    for h in range(2):
        hd = 2 * j + h
        cols += list(range(2560 + 64 * hd, 2560 + 64 * hd + 64)) + [4096 + hd]
    cols += list(range(3072 + 128 * j, 3072 + 128 * j + 128))
    cols += list(range(3584 + 128 * j, 3584 + 128 * j + 128))
    w = w_in_l[:, cols]
    return np.ascontiguousarray(w.reshape(KC, 128, NCB).transpose(1, 0, 2))


def pack_wsB(inp, l, j):
    parts = [inp["w_pool"][l, j]]
    parts += [inp["w_rg_a"][l, 2 * j + hh] for hh in range(2)]
    parts += [inp["w_rg_x"][l, 2 * j + hh] for hh in range(2)]
    return np.ascontiguousarray(np.concatenate(parts, axis=1))


def pack_vecB(inp, l, j):
    v = np.zeros((128, NVB), np.float32)
    v[:, VB_PSCALE] = inp["pool_scale"][l, 128 * j:128 * j + 128]
    for hh in range(2):
        sl = slice(256 * j + 128 * hh, 256 * j + 128 * hh + 128)
        for kk in range(4):
            v[:, VB_CONVW + 4 * hh + kk] = inp["conv_w"][l, kk, sl]
        v[:, VB_CONVB + hh] = inp["conv_b"][l, sl]
        v[:, VB_BA + hh] = inp["b_rg_a"][l, sl]
        v[:, VB_BX + hh] = inp["b_rg_x"][l, sl]
        v[:, VB_LAM + hh] = inp["lru_lambda"][l, sl]
        v[:, VB_BF + hh] = inp["b_f"][l, 2 * j + hh]
    w = 2 ** (j + 1)
    v[:, VB_SEL + j] = np.float32(1.0 / w)
    for t in range(16):
        v[:, VB_CORR + t] = np.float32(w / min(t + 1, w))
    return v


def const_mask():
    jj = np.arange(128)[:, None]
    tt = np.arange(128)[None, :]
    return np.where(jj <= tt, np.float32(3.0e38), np.float32(0.0)).astype(np.float32)


def emit_phaseC(P, cst, x, xres, umine, mix_all_d, wC_d, bgate_d, wo_d):
    k, ar = P.k, P.ar
    ar.mark()
    u = ar.alloc([KC, T], BF16); u_r = [Res() for _ in range(NT)]
    mixs = ar.alloc([16, 2 * TT], BF16); mixs_r = [Res() for _ in range(16)]
    merged = ar.alloc([KC, 2 * TT], BF16); mg_r = [[Res(), Res()] for _ in range(KC)]
    wcb = [ar.alloc([5120], BF16) for _ in range(2)]; wcb_r = [Res(), Res()]
    wob = [ar.alloc([1024], BF16) for _ in range(2)]; wob_r = [Res(), Res()]
    gate = [ar.alloc([TT], F32) for _ in range(3)]; gate_r = [Res(), Res(), Res()]
    acc = [ar.alloc([TT], F32) for _ in range(2)]; acc_r = [Res(), Res()]
    bg = ar.alloc([24], F32); bg_r = Res()
    ds_u = k.dsem("cu")
    for tt in range(NT):
        k.dma("sp", ds_u, out=u[:, :, tt * TT:(tt + 1) * TT], in_=umine(tt), writes=[u_r[tt]])
    k.dma("sp", k.dsem("cbg"), out=bg, in_=bgate_d, writes=[bg_r])
    dsm = k.dsem("cmix")
    dsw = [k.dsem("cw0"), k.dsem("cw1")]
    dso = [k.dsem("co0"), k.dsem("co1")]
    cw = 0
    co = 0
    for st in range(2):
        cs = slice(st * 2 * TT, (st + 1) * 2 * TT)
        for r in range(4):
            if callable(mix_all_d):
                srcs = [(r, mix_all_d(r, 0, cs)), (4 + 2 * r, mix_all_d(r, 1, cs)), (5 + 2 * r, mix_all_d(r, 2, cs)),
                        (12 + r, mix_all_d(r, 3, cs))]
            else:
                srcs = [(r, mix_all_d[r, 0:128, cs]), (4 + 2 * r, mix_all_d[r, 128:256, cs]),
                        (5 + 2 * r, mix_all_d[r, 256:384, cs]), (12 + r, mix_all_d[r, 384:512, cs])]
            for m_, s_ in srcs:
                k.dma("sp", dsm, out=mixs[:, m_, :], in_=s_, writes=[mixs_r[m_]])
        for dc in range(KC):
            s = cw % 2
            cw += 1
            k.dma("pool", dsw[s], out=wcb[s], in_=wC_d[dc], writes=[wcb_r[s]], max_dma_last_dim=1024 * 4)
            for t2 in range(2):
                tt = st * 2 + t2
                sl = slice(tt * TT, (tt + 1) * TT)
                sl2 = slice(t2 * TT, (t2 + 1) * TT)
                for br in range(3):
                    ps, pr = P.bank()
                    for kc in range(KC):
                        c0 = 2048 + kc * 384 + br * 128
                        k.op("pe", lambda e, ps=ps, s=s, kc=kc, c0=c0, sl=sl: e.matmul(
                            ps, lhsT=wcb[s][:, c0:c0 + 128], rhs=u[:, kc, sl], start=(kc == 0), stop=(kc == KC - 1)),
                            reads=[wcb_r[s], u_r[tt]], writes=[pr])
                    k.op("act", lambda e, ps=ps, br=br, dc=dc: e.activation(
                        out=gate[br], in_=ps, func=AF.Sigmoid, bias=bg[:, br * 8 + dc:br * 8 + dc + 1], scale=1.0),
                        reads=[pr, bg_r], writes=[gate_r[br]])
                ys = []
                for (m0, m1) in ((0, 4), (4, 12), (12, 16)):
                    ps, pr = P.bank()
                    for m in range(m0, m1):
                        k.op("pe", lambda e, ps=ps, s=s, m=m, sl2=sl2, m0=m0, m1=m1: e.matmul(
                            ps, lhsT=wcb[s][:, m * 128:(m + 1) * 128], rhs=mixs[:, m, sl2],
                            start=(m == m0), stop=(m == m1 - 1)),
                            reads=[wcb_r[s], mixs_r[m]], writes=[pr])
                    ys.append((ps, pr))
                k.op("dve", lambda e, ys=ys: e.tensor_tensor(out=acc[0], in0=ys[0][0], in1=gate[0], op=ALU.mult),
                     reads=[ys[0][1], gate_r[0]], writes=[acc_r[0]])
                k.op("dve", lambda e, ys=ys: e.tensor_tensor(out=acc[1], in0=ys[1][0], in1=gate[1], op=ALU.mult),
                     reads=[ys[1][1], gate_r[1]], writes=[acc_r[1]])
                k.op("dve", lambda e: e.tensor_tensor(out=acc[0], in0=acc[0], in1=acc[1], op=ALU.add),
                     reads=[acc_r[0], acc_r[1]], writes=[acc_r[0]])
                k.op("dve", lambda e, ys=ys: e.tensor_tensor(out=acc[1], in0=ys[2][0], in1=gate[2], op=ALU.mult),
                     reads=[ys[2][1], gate_r[2]], writes=[acc_r[1]])
                k.op("dve", lambda e, dc=dc, sl2=sl2: e.tensor_tensor(out=merged[:, dc, sl2], in0=acc[0], in1=acc[1],
                                                                      op=ALU.add),
                     reads=[acc_r[0], acc_r[1]], writes=[mg_r[dc][t2]])
        for dco in range(KC):
            s = co % 2
            co += 1
            k.dma("pool", dso[s], out=wob[s], in_=wo_d[dco], writes=[wob_r[s]])
            for t2 in range(2):
                tt = st * 2 + t2
                sl = slice(tt * TT, (tt + 1) * TT)
                sl2 = slice(t2 * TT, (t2 + 1) * TT)
                ps, pr = P.bank()
                for dc in range(KC):
                    k.op("pe", lambda e, ps=ps, s=s, dc=dc, sl2=sl2: e.matmul(
                        ps, lhsT=wob[s][:, dc * 128:(dc + 1) * 128], rhs=merged[:, dc, sl2],
                        start=(dc == 0), stop=(dc == KC - 1)),
                        reads=[wob_r[s], mg_r[dc][t2]], writes=[pr])
                k.op("dve", lambda e, ps=ps, dco=dco, sl=sl: e.tensor_tensor(out=x[:, dco, sl], in0=ps, in1=x[:, dco, sl],
                                                                             op=ALU.add),
                     reads=[pr, xres[tt]], writes=[xres[tt]])
    ar.release()


def emit_cross(P, cst, x, xres, memT_d, gmem, gmem_r, gcross, gcross_r, wxq_d, wxkv_d, wxo_d):
    k, ar = P.k, P.ar
    ar.mark()
    NM = 256
    xn = ar.alloc([KC, T], BF16); xn_r = [Res() for _ in range(NT)]
    QT = ar.alloc([KC, T], BF16); QT_r = [[Res() for _ in range(NT)] for _ in range(KC)]
    memf = ar.alloc([KC, NM], F32); memf_r = [Res()]
    memn = ar.alloc([KC, NM], BF16); memn_r = Res()
    KxT = ar.alloc([KC, NM], BF16); Kx_r = [Res() for _ in range(KC)]
    Vx = ar.alloc([2, D], BF16); Vx_r = [Res() for _ in range(KC)]
    wch = [ar.alloc([1024], BF16) for _ in range(2)]; wch_r = [Res(), Res()]
    Pt = [[ar.alloc([TT], BF16) for _ in range(2)] for _ in range(2)]; Pt_r = [[Res(), Res()] for _ in range(2)]
    rec = ar.alloc([TT], F32); rec_r = Res()
    nt = alloc_normtmp(P)
    dsw = [k.dsem("xw0"), k.dsem("xw1")]
    wc = [0]

    def wload(src):
        s = wc[0] % 2
        wc[0] += 1
        k.dma("pool", dsw[s], out=wch[s], in_=src, writes=[wch_r[s]])
        return s

    k.dma("sp", k.dsem("xmem"), out=memf, in_=memT_d, writes=[memf_r[0]])
    emit_rmsnorm(P, cst, nt, memf, memf_r, gmem, gmem_r, lambda tt: (memn, memn_r), ntiles=1, tw=NM)
    for ck in range(16):
        s = wload(wxkv_d[ck])
        ps, pr = P.bank()
        if ck < 8:
            for kc in range(KC):
                k.op("pe", lambda e, ps=ps, s=s, kc=kc: e.matmul(
                    ps[:, 0:NM], lhsT=wch[s][:, kc * 128:(kc + 1) * 128], rhs=memn[:, kc, :],
                    start=(kc == 0), stop=(kc == KC - 1)),
                    reads=[wch_r[s], memn_r], writes=[pr])
            k.op("act", lambda e, ps=ps, ck=ck: e.activation(out=KxT[:, ck, :], in_=ps[:, 0:NM], func=AF.Copy),
                 reads=[pr], writes=[Kx_r[ck]])
        else:
            c = ck - 8
            for mb in range(2):
                for kc in range(KC):
                    k.op("pe", lambda e, ps=ps, s=s, kc=kc, mb=mb: e.matmul(
                        ps[:, mb * 128:(mb + 1) * 128], lhsT=memn[:, kc, mb * 128:(mb + 1) * 128],
                        rhs=wch[s][:, kc * 128:(kc + 1) * 128], start=(kc == 0), stop=(kc == KC - 1)),
                        reads=[wch_r[s], memn_r], writes=[pr])
            k.op("act", lambda e, ps=ps, c=c: e.activation(
                out=Vx[:, :, c * 128:(c + 1) * 128], in_=ps[:, 0:256].rearrange("p (m j) -> p m j", m=2), func=AF.Copy),
                reads=[pr], writes=[Vx_r[c]])
    emit_rmsnorm(P, cst, nt, x, xres, gcross, gcross_r, lambda tt: (xn[:, :, tt * TT:(tt + 1) * TT], xn_r[tt]))
    for ck in range(KC):
        s = wload(wxq_d[ck])
        for tt in range(NT):
            sl = slice(tt * TT, (tt + 1) * TT)
            ps, pr = P.bank()
            for kc in range(KC):
                k.op("pe", lambda e, ps=ps, s=s, kc=kc, sl=sl: e.matmul(
                    ps, lhsT=wch[s][:, kc * 128:(kc + 1) * 128], rhs=xn[:, kc, sl],
                    start=(kc == 0), stop=(kc == KC - 1)),
                    reads=[wch_r[s], xn_r[tt]], writes=[pr])
            if tt % 2 == 0:
                k.op("act", lambda e, ps=ps, ck=ck, sl=sl: e.activation(out=QT[:, ck, sl], in_=ps, func=AF.Copy),
                     reads=[pr], writes=[QT_r[ck][tt]])
            else:
                k.op("dve", lambda e, ps=ps, ck=ck, sl=sl: e.tensor_copy(out=QT[:, ck, sl], in_=ps),
                     reads=[pr], writes=[QT_r[ck][tt]])
    cnt = 0
    for tt in range(NT):
        sl = slice(tt * TT, (tt + 1) * TT)
        for h in range(4):
            pp = cnt % 2
            cnt += 1
            for mb in range(2):
                ps, pr = P.bank()
                for half in range(2):
                    ck = 2 * h + half
                    k.op("pe", lambda e, ps=ps, ck=ck, mb=mb, sl=sl, half=half: e.matmul(
                        ps, lhsT=KxT[:, ck, mb * 128:(mb + 1) * 128], rhs=QT[:, ck, sl],
                        start=(half == 0), stop=(half == 1)),
                        reads=[Kx_r[ck], QT_r[ck][tt]], writes=[pr])
                k.op("act", lambda e, ps=ps, pp=pp, mb=mb: e.activation(out=Pt[pp][mb], in_=ps, func=AF.Exp,
                                                                       scale=0.0625),
                     reads=[pr], writes=[Pt_r[pp][mb]])
            ps, pr = P.bank()
            for mb in range(2):
                k.op("pe", lambda e, ps=ps, pp=pp, mb=mb: e.matmul(ps, lhsT=cst["ones_bf"], rhs=Pt[pp][mb],
                                                                   start=(mb == 0), stop=(mb == 1)),
                     reads=[cst["ones_bf_r"], Pt_r[pp][mb]], writes=[pr])
            k.op("dve", lambda e, ps=ps: e.reciprocal(out=rec, in_=ps), reads=[pr], writes=[rec_r])
            for half in range(2):
                ck = 2 * h + half
                ps, pr = P.bank()
                for mb in range(2):
                    k.op("pe", lambda e, ps=ps, pp=pp, mb=mb, ck=ck: e.matmul(
                        ps, lhsT=Vx[:, mb, ck * 128:(ck + 1) * 128], rhs=Pt[pp][mb], start=(mb == 0), stop=(mb == 1)),
                        reads=[Vx_r[ck], Pt_r[pp][mb]], writes=[pr])
                k.op("dve", lambda e, ps=ps, ck=ck, sl=sl: e.tensor_tensor(out=QT[:, ck, sl], in0=ps, in1=rec, op=ALU.mult),
                     reads=[pr, rec_r], writes=[QT_r[ck][tt]])
    for dco in range(KC):
        s = wload(wxo_d[dco])
        for tt in range(NT):
            sl = slice(tt * TT, (tt + 1) * TT)
            ps, pr = P.bank()
            for ck in range(KC):
                k.op("pe", lambda e, ps=ps, s=s, ck=ck, sl=sl: e.matmul(
                    ps, lhsT=wch[s][:, ck * 128:(ck + 1) * 128], rhs=QT[:, ck, sl],
                    start=(ck == 0), stop=(ck == KC - 1)),
                    reads=[wch_r[s], QT_r[ck][tt]], writes=[pr])
            k.op("dve", lambda e, ps=ps, dco=dco, sl=sl: e.tensor_tensor(out=x[:, dco, sl], in0=ps, in1=x[:, dco, sl],
                                                                         op=ALU.add),
                 reads=[pr, xres[tt]], writes=[xres[tt]])
    ar.release()


def pack_wC(inp, l):
    ua = inp["w_up_a"][l].reshape(4, 128, KC, 128)
    ub = inp["w_up_b"][l].reshape(8, 128, KC, 128)
    uc = inp["w_up_c"][l].reshape(4, 128, KC, 128)
    up = np.concatenate([ua, ub, uc], axis=0)
    up = up.transpose(2, 1, 0, 3).reshape(KC, 128, 2048)
    wg = inp["w_in"][l][:, 4104:].reshape(KC, 128, 3, KC, 128)
    wg = wg.transpose(3, 1, 0, 2, 4).reshape(KC, 128, 3072)
    return np.ascontiguousarray(np.concatenate([up, wg], axis=2))


def pack_bgate(inp, l):
    return np.ascontiguousarray(inp["b_gate"][l].reshape(3, KC, 128).transpose(2, 0, 1).reshape(128, 24))


def pack_sq(w):
    C = w.shape[1] // 128
    a = w.reshape(KC, 128, C, 128)
    return np.ascontiguousarray(a.transpose(2, 1, 0, 3)).reshape(C, 128, KC * 128)


def pack_memT(m):
    return np.ascontiguousarray(m.reshape(256, KC, 128).transpose(2, 1, 0))


ARENA_BYTES = 205 * 1024
RGROUPS = [[0, 1, 2, 3], [4, 5, 6, 7]]


def build(stages, fused=False):
    nc = bass.Bass("TRN2", target_bir_lowering=False)
    dram = {}

    def dten(name, shape, dt, kind):
        if name not in dram:
            dram[name] = nc.dram_tensor(name, list(shape), dt, kind=kind).ap()
        return dram[name]

    def din(name, shape, dt=F32):
        return dten(name, shape, dt, "ExternalInput")

    def dout(name, shape, dt):
        return dten(name, shape, dt, "ExternalOutput")

    def dint(name, shape, dt):
        return dten(name, shape, dt, "Internal")

    with ExitStack() as stack:
        arena_t = stack.enter_context(nc.sbuf_tensor("arena", [128, ARENA_BYTES // 4], F32))
        psum_t = stack.enter_context(nc.psum_tensor("psum", [128, 8, 512], F32))
        k = K(nc, stack)
        ar = Arena(arena_t[:, :], ARENA_BYTES)
        P = Prog(nc, k, ar, psum_t[:, :, :])
        cst = make_consts(P, din("cmask", [128, 128]))
        x = ar.alloc([KC, T], F32)
        xres = [Res() for _ in range(NT)]
        gvs = ar.alloc([16, KC], F32)
        gv_n = [0]
        pid_cache = {}
        gath_res = {}
        ds_g = k.dsem("gv")

        def load_g(name):
            i = gv_n[0] % 16
            gv_n[0] += 1
            r = Res()
            k.dma("sp", ds_g, out=gvs[:, i, :], in_=din(name, [128, KC]), writes=[r])
            return gvs[:, i, :], r

        def sfx(l):
            return "" if l is None else "_%d" % l

        for stg in stages:
            k.barrier()
            if stg == "load_x":
                xd = din("xT", [128, KC, T])
                for tt in range(NT):
                    sl = slice(tt * TT, (tt + 1) * TT)
                    k.dma("sp", k.dsem("x"), out=x[:, :, sl], in_=xd[:, :, sl], writes=[xres[tt]])
            elif stg == "store_x":
                xo = dout("xT_out", [128, KC, T], F32)
                for tt in range(NT):
                    sl = slice(tt * TT, (tt + 1) * TT)
                    k.dma("sp", k.dsem("xo"), out=xo[:, :, sl], in_=x[:, :, sl], reads=[xres[tt]])
            elif stg[0] == "ffn":
                _, l, i = stg
                ar.mark()
                gv, gr = load_g("gffn%s_%d" % (sfx(l), i))
                xn = ar.alloc([KC, T], BF16); xn_r = [Res() for _ in range(NT)]
                nt = alloc_normtmp(P)
                w1buf = [ar.alloc([KC * 256], BF16) for _ in range(2)]; w1res = [Res(), Res()]
                w2buf = [ar.alloc([NF * 128], BF16) for _ in range(2)]; w2res = [Res(), Res()]
                g = ar.alloc([NF, 2 * TT], BF16); gres2 = [[Res(), Res()] for _ in range(NF)]
                stmp = [ar.alloc([TT], F32) for _ in range(2)]; stres = [Res(), Res()]
                emit_rmsnorm(P, cst, nt, x, xres, gv, gr, lambda tt: (xn[:, :, tt * TT:(tt + 1) * TT], xn_r[tt]))
                emit_ffn(P, x, xres, xn, xn_r, din("w1%s_%d" % (sfx(l), i), [NF, 128, KC * 256]),
                         din("w2%s_%d" % (sfx(l), i), [KC, 128, NF * 128]),
                         (w1buf, w1res, w2buf, w2res, g, gres2, stmp, stres), "f")
                ar.release()
            elif stg[0] == "unorm":
                _, l = stg
                ar.mark()
                gv, gr = load_g("gmix%s" % sfx(l))
                xn = ar.alloc([KC, T], BF16); xn_r = [Res() for _ in range(NT)]
                nt = alloc_normtmp(P)
                dsu = k.dsem("ust")
                if fused:
                    def after(tt, o, ores):
                        um = dint("u_mine_%d" % tt, [128, KC * TT], BF16)
                        ua = dint("u_all_%d" % tt, [4 * 128, KC * TT], BF16)
                        ur = Res()
                        k.dma("sp", dsu, out=um.rearrange("p (k t) -> p k t", k=KC), in_=o, reads=[ores], writes=[ur])
                        k.coll("AllGather", RGROUPS, ins=[um], outs=[ua], reads=[ur])
                else:
                    ud = dout("u_mine%s" % sfx(l), [128, KC, T], BF16)

                    def after(tt, o, ores, ud=ud, dsu=dsu):
                        sl = slice(tt * TT, (tt + 1) * TT)
                        k.dma("sp", dsu, out=ud[:, :, sl], in_=o, reads=[ores], writes=[])

                emit_rmsnorm(P, cst, nt, x, xres, gv, gr, lambda tt: (xn[:, :, tt * TT:(tt + 1) * TT], xn_r[tt]),
                             after=after)
                ar.release()
            elif stg[0] == "B":
                _, l = stg
                if fused:
                    def usrc(r, tt):
                        ua = dint("u_all_%d" % tt, [4 * 128, KC * TT], BF16)
                        return ua[r * 128:(r + 1) * 128, :].rearrange("p (k t) -> p k t", k=KC)

                    def mm(d, half):
                        return dint("mix_mine_%d_%d" % (d, half), [256, T], BF16)

                    def mdst(kind, r, tsl, h=0):
                        if kind == "A":
                            return [(mm(r, 0)[0:128, tsl], slice(0, 128))]
                        if kind == "B":
                            return [(mm(r, 0)[128:256, tsl], slice(0, 128), 0), (mm(r, 1)[0:128, tsl], slice(0, 128), 1)]
                        return [(mm(r, 1)[128 + 64 * h:192 + 64 * h, tsl], slice(0, 64))]

                    def tile_done(gt, store_res):
                        if gt % 4 == 3:
                            d = gt // 4
                            for half in range(2):
                                gq = dint("mix_gath_%d_%d" % (d, half), [4 * 256, T], BF16)
                                gres = Res()
                                gath_res[(d, half)] = gres
                                k.coll("AllGather", RGROUPS, ins=[mm(d, half)], outs=[gq], reads=list(store_res),
                                       writes=[gres])
                            del store_res[:]
                else:
                    u_all = din("u_all%s" % sfx(l), [4, 128, KC, T], BF16)
                    mix = dout("mix_mine%s" % sfx(l), [4, 512, T], BF16)

                    def usrc(r, tt, u_all=u_all):
                        return u_all[r, :, :, tt * TT:(tt + 1) * TT]

                    def mdst(kind, r, tsl, h=0, mix=mix):
                        if kind == "A":
                            return [(mix[r, 0:128, tsl], slice(0, 128))]
                        if kind == "B":
                            return [(mix[r, 128 + 128 * hh:256 + 128 * hh, tsl], slice(0, 128), hh) for hh in range(2)]
                        return [(mix[r, 384 + 64 * h:448 + 64 * h, tsl], slice(0, 64))]
                emit_phaseB(P, cst, usrc, din("wB%s" % sfx(l), [128, KC, NCB]), din("wsB%s" % sfx(l), [128, 640]),
                            din("vecB%s" % sfx(l), [128, NVB]), mdst, tile_done if fused else None)
            elif stg[0] == "C":
                _, l = stg
                if fused:
                    def umine(tt):
                        return dint("u_mine_%d" % tt, [128, KC * TT], BF16).rearrange("p (k t) -> p k t", k=KC)

                    def mix_all(r, piece, cs):
                        mb = dint("mix_big_%d_%d" % (r, piece // 2), [4 * 256, T], BF16)
                        return mb[128 * (piece % 2):128 * (piece % 2) + 128, cs]
                else:
                    ud = din("u_mine%s" % sfx(l), [128, KC, T], BF16)
                    mix_all = din("mix_all%s" % sfx(l), [4, 512, T], BF16)

                    def umine(tt, ud=ud):
                        return ud[:, :, tt * TT:(tt + 1) * TT]
                emit_phaseC(P, cst, x, xres, umine, mix_all, din("wC%s" % sfx(l), [KC, 128, 5120]),
                            din("bgate%s" % sfx(l), [128, 24]), din("wo%s" % sfx(l), [KC, 128, 1024]))
            elif stg[0] == "cross":
                _, l = stg
                gm, gmr = load_g("gmem%s" % sfx(l))
                gc, gcr = load_g("gcross%s" % sfx(l))
                emit_cross(P, cst, x, xres, din("memT", [128, KC, 256]), gm, gmr, gc, gcr,
                           din("wxq%s" % sfx(l), [KC, 128, 1024]), din("wxkv%s" % sfx(l), [16, 128, 1024]),
                           din("wxo%s" % sfx(l), [KC, 128, 1024]))
            elif stg[0] == "ag_u":
                pass
            elif stg[0] == "ag_mix":
                for d in range(4):
                    for half in range(2):
                        gq = dint("mix_gath_%d_%d" % (d, half), [4 * 256, T], BF16)
                        for r in range(4):
                            mb = dint("mix_big_%d_%d" % (r, half), [4 * 256, T], BF16)

                            def dstf(e, d=d, mb=mb):
                                if "c" not in pid_cache:
                                    pid_cache["c"] = nc.sync.partition_id() % 4
                                return mb[bass.ds(((d + 4 - pid_cache["c"]) % 4) * 256, 256), :]
                            k.dma("sp", None, out=dstf, in_=gq[r * 256:(r + 1) * 256, :], reads=[gath_res[(d, half)]])
            elif stg == "final":
                ar.mark()
                gv, gr = load_g("gfin")
                nt = alloc_normtmp(P)
                stg_t = [ar.alloc([KC, TT], F32) for _ in range(2)]; stg_r = [Res(), Res()]
                od = dout("outT", [128, KC, T], F32)
                dso = k.dsem("out")

                def after(tt, o, ores, od=od, dso=dso):
                    sl = slice(tt * TT, (tt + 1) * TT)
                    k.dma("sp", dso, out=od[:, :, sl], in_=o, reads=[ores], writes=[])

                emit_rmsnorm(P, cst, nt, x, xres, gv, gr, lambda tt: (stg_t[tt % 2], stg_r[tt % 2]), after=after)
                ar.release()
            else:
                raise ValueError(stg)
        k.barrier()
        k.final_wait("sp")
        k.emit()
    return nc


_PROGS = {}


def get_prog(key, stages, fused=False):
    if key not in _PROGS:
        _PROGS[key] = build(stages, fused)
    return _PROGS[key]


def _run(nc, maps):
    res = run_bass_kernel_spmd(nc, maps, core_ids=list(range(NCORES)))
    return res.results


def _weights_ffn(inp, l, i, names=None):
    sfx = "_%d_%d" % (l, i)
    if i == 1:
        w_in, w_out, g = inp["w_ffn1_in"], inp["w_ffn1_out"], inp["g_ffn1"]
    else:
        w_in, w_out, g = inp["w_ffn2_in"], inp["w_ffn2_out"], inp["g_ffn2"]
    return {"w1" + sfx: pack_w1(w_in[l]), "w2" + sfx: pack_w2(w_out[l]), "gffn" + sfx: pack_vec(g[l])}


def _weights_C(inp, l):
    s = "_%d" % l
    return {"wC" + s: pack_wC(inp, l), "bgate" + s: pack_bgate(inp, l), "wo" + s: pack_sq(inp["w_o"][l]),
            "gmem" + s: pack_vec(inp["g_mem"][l]), "gcross" + s: pack_vec(inp["g_cross"][l]),
            "wxq" + s: pack_sq(inp["w_xq"][l]), "wxkv" + s: pack_sq(inp["w_xkv"][l]), "wxo" + s: pack_sq(inp["w_xo"][l])}


FUSED = True


def fused_stages():
    st = ["load_x"]
    for l in range(L):
        st += [("ffn", l, 1), ("unorm", l), ("ag_u", l), ("B", l), ("ag_mix", l), ("C", l), ("cross", l), ("ffn", l, 2)]
    st.append("final")
    return st


def kernel_fused(inp):
    cmask = const_mask()
    xs = inp["x"].reshape(NCORES, T, D)
    w = {"cmask": cmask, "gfin": pack_vec(inp["g_final"])}
    wBs = {}
    for l in range(L):
        w.update(_weights_ffn(inp, l, 1))
        w.update(_weights_ffn(inp, l, 2))
        w.update(_weights_C(inp, l))
        w["gmix_%d" % l] = pack_vec(inp["g_mix"][l])
        for j in range(4):
            wBs[(l, j)] = {"wB_%d" % l: pack_wB(inp["w_in"][l], j), "wsB_%d" % l: pack_wsB(inp, l, j),
                           "vecB_%d" % l: pack_vecB(inp, l, j)}
    memT = [pack_memT(inp["mem"][b]) for b in range(2)]
    maps = []
    for c in range(NCORES):
        b, j = c // 4, c % 4
        m = dict(w)
        for l in range(L):
            m.update(wBs[(l, j)])
        m["xT"] = pack_xT(xs[c])
        m["memT"] = memT[b]
        maps.append(m)
    prog = get_prog("fused", fused_stages(), fused=True)
    r = _run(prog, maps)
    out = np.stack([unpack_xT(r[c]["outT"]) for c in range(NCORES)], axis=0)
    return out.reshape(2, 8192, D).astype(np.float32)


def kernel(**inp):
    inp = {k_: np.asarray(v) for k_, v in inp.items()}
    if FUSED:
        return kernel_fused(inp)
    cmask = const_mask()
    xs = inp["x"].reshape(NCORES, T, D)
    p1 = get_prog("p1", ["load_x", ("ffn", 0, 1), ("unorm", 0), "store_x"])
    w = {"cmask": cmask, "gmix_0": pack_vec(inp["g_mix"][0])}
    w.update(_weights_ffn(inp, 0, 1))
    r = _run(p1, [dict(w, xT=pack_xT(xs[c])) for c in range(NCORES)])
    xT = [r[c]["xT_out"] for c in range(NCORES)]
    um = [r[c]["u_mine_0"] for c in range(NCORES)]
    out = None
    for l in range(L):
        pB = get_prog("pB", [("B", None)])
        maps = []
        for c in range(NCORES):
            b, j = c // 4, c % 4
            maps.append({"cmask": cmask, "u_all": np.stack([um[4 * b + rr] for rr in range(4)], axis=0),
                         "wB": pack_wB(inp["w_in"][l], j), "wsB": pack_wsB(inp, l, j), "vecB": pack_vecB(inp, l, j)})
        r = _run(pB, maps)
        mix = [r[c]["mix_mine"] for c in range(NCORES)]
        w = {"cmask": cmask}
        w.update(_weights_C(inp, l))
        w.update(_weights_ffn(inp, l, 2))
        if l + 1 < L:
            stages = ["load_x", ("C", l), ("cross", l), ("ffn", l, 2), ("ffn", l + 1, 1), ("unorm", l + 1), "store_x"]
            w.update(_weights_ffn(inp, l + 1, 1))
            w["gmix_%d" % (l + 1)] = pack_vec(inp["g_mix"][l + 1])
        else:
            stages = ["load_x", ("C", l), ("cross", l), ("ffn", l, 2), "final"]
            w["gfin"] = pack_vec(inp["g_final"])
        pC = get_prog("pC%d" % l, stages)
        maps = []
        for c in range(NCORES):
            b, cc = c // 4, c % 4
            m = dict(w)
            m["xT"] = xT[c]
            m["u_mine_%d" % l] = um[c]
            m["mix_all_%d" % l] = np.stack([mix[4 * b + j][cc] for j in range(4)], axis=0)
            m["memT"] = pack_memT(inp["mem"][b])
            maps.append(m)
        r = _run(pC, maps)
        if l + 1 < L:
            xT = [r[c]["xT_out"] for c in range(NCORES)]
            um = [r[c]["u_mine_%d" % (l + 1)] for c in range(NCORES)]
        else:
            out = np.stack([unpack_xT(r[c]["outT"]) for c in range(NCORES)], axis=0)
    return out.reshape(2, 8192, D).astype(np.float32)
```

```python
import numpy as np
from contextlib import ExitStack
import concourse.bass as bass
import concourse.mybir as mybir
from concourse.bass_utils import run_bass_kernel_spmd

F32 = mybir.dt.float32
BF16 = mybir.dt.bfloat16
AF = mybir.ActivationFunctionType
ALU = mybir.AluOpType
AX = mybir.AxisListType

D = 1024
KC = 8
T = 2048
TT = 512
NT = T // TT
DFF = 2816
NF = DFF // 128
EPS = 1e-6
L = 2
NCORES = 8


class Res:
    __slots__ = ("name", "w", "r")

    def __init__(self, name=""):
        self.name = name
        self.w = None
        self.r = {}


class DSem:
    __slots__ = ("key", "sem", "count")

    def __init__(self, key, sem):
        self.key = key
        self.sem = sem
        self.count = 0


class Eng:
    def __init__(self, name, sem):
        self.name = name
        self.sem = sem
        self.count = 0
        self.seen = {}
        self.q = []


class K:
    def __init__(self, nc, stack):
        self.nc = nc
        self.stack = stack
        self.engs = {}
        for n in ("pe", "act", "dve", "pool", "sp"):
            self.engs[n] = Eng(n, stack.enter_context(nc.semaphore("sem_" + n)))
        self.dsems = {}
        self.ninst = 0
        self.sched = {}

    NPOOL = 64

    def semname(self, sem):
        for e in self.engs.values():
            if e.sem is sem:
                return e.name
        for d in self.dsems.values():
            if d.sem is sem:
                return d.key
        return "?"

    def simulate(self):
        cnt = {}
        pos = {e: 0 for e in self.sched}
        progress = True
        while progress:
            progress = False
            for e, lst in self.sched.items():
                while pos[e] < len(lst):
                    waits, key, inc = lst[pos[e]]
                    if all(cnt.get(kk, 0) >= v for kk, v in waits):
                        cnt[key] = cnt.get(key, 0) + inc
                        pos[e] += 1
                        progress = True
                    else:
                        break
        stuck = {e: (pos[e], len(lst), lst[pos[e]][0] if pos[e] < len(lst) else None) for e, lst in self.sched.items()}
        return stuck, cnt

    def dsem(self, key):
        return None

    def _pool_sem(self):
        if not hasattr(self, "dpool"):
            self.dpool = []
            self.dnext = 0
        if len(self.dpool) < self.NPOOL:
            i = len(self.dpool)
            d = DSem("dp%d" % i, self.stack.enter_context(self.nc.semaphore("dsem_p%d" % i)))
            self.dpool.append(d)
            self.dsems[d.key] = d
            return d
        d = self.dpool[self.dnext % self.NPOOL]
        self.dnext += 1
        return d

    def _waits(self, eng, reads, writes):
        deps = {}

        def add(tok):
            if tok is None:
                return
            key, sem, val = tok
            if key not in deps or deps[key][1] < val:
                deps[key] = (sem, val)

        for r in reads:
            add(r.w)
        for w in writes:
            add(w.w)
            for tok in w.r.values():
                add(tok)
        out = []
        for key, (sem, val) in deps.items():
            if eng.seen.get(key, 0) >= val:
                continue
            if key == "pe" and eng.name == "pe":
                continue
            eng.seen[key] = val
            out.append((sem, val))
        return out

    def op(self, ename, fn, reads=(), writes=()):
        eng = self.engs[ename]
        waits = self._waits(eng, reads, writes)
        sem = eng.sem
        eng.count += 1
        tok = (ename, sem, eng.count)

        def run(e, waits=waits, fn=fn, sem=sem):
            for s, v in waits:
                e.wait_ge(s, v)
            fn(e).then_inc(sem, 1)

        eng.q.append(run)
        self.sched.setdefault(ename, []).append(([(self.semname(s), v) for s, v in waits], ename, 1))
        for r in reads:
            r.r[ename] = tok
        for w in writes:
            w.w = tok
            w.r = {}
        self.ninst += 1

    def dma(self, qname, ds, out, in_, reads=(), writes=(), **kw):
        eng = self.engs[qname]
        ds = self._pool_sem()
        waits = self._waits(eng, reads, writes)
        if ds.count > 0 and eng.seen.get(ds.key, 0) < ds.count:
            eng.seen[ds.key] = ds.count
            waits.append((ds.sem, ds.count))
        ds.count += 16
        tok = (ds.key, ds.sem, ds.count)

        def run(e, waits=waits, out=out, in_=in_, kw=kw, sem=ds.sem, cnt=ds.count):
            for s, v in waits:
                e.wait_ge(s, v)
            e.dma_start(out=out(e) if callable(out) else out, in_=in_(e) if callable(in_) else in_,
                        **kw).then_inc(sem, 16)

        eng.q.append(run)
        self.sched.setdefault(qname, []).append(([(self.semname(s), v) for s, v in waits], ds.key, 16))
        for r in reads:
            r.r[ds.key] = tok
        for w in writes:
            w.w = tok
            w.r = {}
        self.ninst += 1

    def coll(self, kind, groups, ins, outs, reads=(), writes=()):
        eng = self.engs["pool"]
        i = len([k_ for k_ in self.dsems if k_.startswith("cc")])
        ds = DSem("cc%d" % i, self.stack.enter_context(self.nc.semaphore("ccsem_%d" % i)))
        self.dsems[ds.key] = ds
        waits = self._waits(eng, reads, writes)
        ds.count += 1
        tok = (ds.key, ds.sem, ds.count)

        def run(e, waits=waits, sem=ds.sem):
            for s, v in waits:
                e.wait_ge(s, v)
            e.collective_compute(kind, ALU.bypass, replica_groups=groups, ins=[a.opt() for a in ins],
                                 outs=[a.opt() for a in outs]).then_inc(sem, 1)

        eng.q.append(run)
        self.sched.setdefault("pool", []).append(([(self.semname(s), v) for s, v in waits], ds.key, 1))
        for r in reads:
            r.r[ds.key] = tok
        for w in writes:
            w.w = tok
            w.r = {}

    def barrier(self, skip_cc=False):
        toks = [(e.name, e.sem, e.count) for e in self.engs.values() if e.count > 0]
        toks += [(d.key, d.sem, d.count) for d in self.dsems.values()
                 if d.count > 0 and not (skip_cc and d.key.startswith("cc"))]
        for eng in self.engs.values():
            waits = []
            for key, sem, val in toks:
                if eng.seen.get(key, 0) >= val:
                    continue
                eng.seen[key] = val
                waits.append((sem, val))

            def run(e, waits=waits):
                for s, v in waits:
                    e.wait_ge(s, v)

            eng.q.append(run)
            self.sched.setdefault(eng.name, []).append(([(self.semname(s), v) for s, v in waits], "_nop", 0))

    def final_wait(self, qname="sp"):
        eng = self.engs[qname]
        toks = [(d.key, d.sem, d.count) for d in self.dsems.values() if d.count > 0]
        toks += [(e.name, e.sem, e.count) for e in self.engs.values() if e.count > 0]

        def run(e, toks=toks):
            for key, s, v in toks:
                e.wait_ge(s, v)

        eng.q.append(run)

    def emit(self):
        nc = self.nc
        with nc.Block() as block:
            @block.tensor
            def _(e):
                for f in self.engs["pe"].q:
                    f(e)

            @block.scalar
            def _(e):
                for f in self.engs["act"].q:
                    f(e)

            @block.vector
            def _(e):
                for f in self.engs["dve"].q:
                    f(e)

            @block.gpsimd
            def _(e):
                for f in self.engs["pool"].q:
                    f(e)

            @block.sync
            def _(e):
                for f in self.engs["sp"].q:
                    f(e)


class Arena:
    def __init__(self, ap_f32, nbytes):
        self.ap = ap_f32
        self.nbytes = nbytes
        self.off = 0
        self.marks = []

    def alloc(self, shape_free, dtype, parts=128):
        n = int(np.prod(shape_free))
        esz = 2 if dtype == BF16 else 4
        nb = (n * esz + 31) // 32 * 32
        assert self.off + nb <= self.nbytes, ("arena overflow", self.off, nb, self.nbytes)
        a = self.ap[:, self.off // 4:(self.off + nb) // 4]
        self.off += nb
        if dtype == BF16:
            a = a.bitcast(BF16)
        a = a[:, 0:n]
        if len(shape_free) == 2:
            a = a.rearrange("p (a b) -> p a b", a=shape_free[0])
        elif len(shape_free) == 3:
            a = a.rearrange("p (a b c) -> p a b c", a=shape_free[0], b=shape_free[1])
        return a

    def mark(self):
        self.marks.append(self.off)

    def release(self):
        self.off = self.marks.pop()


class Prog:
    def __init__(self, nc, k, arena, psum):
        self.nc = nc
        self.k = k
        self.ar = arena
        self.psum = psum
        self.pres = [Res("ps%d" % i) for i in range(8)]
        self.pidx = 0

    def bank(self):
        i = self.pidx
        self.pidx = (self.pidx + 1) % 8
        return self.psum[:, i, :], self.pres[i]


def alloc_normtmp(P, tw=TT):
    ar = P.ar
    return dict(sq=ar.alloc([KC, tw], F32), sq_r=Res(), ssum=ar.alloc([tw], F32), ssum_r=Res(),
                rstd=ar.alloc([tw], F32), rstd_r=Res())


def emit_rmsnorm(P, cst, nt, x, xres, gvec, gres, dst, ntiles=NT, tw=TT, after=None):
    k = P.k
    ones, ones_res = cst["ones_f"], cst["ones_f_r"]
    sq, sq_res, ssum, ssum_res, rstd, rstd_res = nt["sq"], nt["sq_r"], nt["ssum"], nt["ssum_r"], nt["rstd"], nt["rstd_r"]
    for tt in range(ntiles):
        sl = slice(tt * tw, (tt + 1) * tw)
        o, ores = dst(tt)
        k.op("act", lambda e, sl=sl: e.activation(out=sq[:, :, 0:tw], in_=x[:, :, sl], func=AF.Square),
             reads=[xres[tt]], writes=[sq_res])
        k.op("dve", lambda e: e.tensor_reduce(out=ssum[:, 0:tw], in_=sq[:, :, 0:tw].rearrange("p k t -> p t k"),
                                              axis=AX.X, op=ALU.add),
             reads=[sq_res], writes=[ssum_res])
        ps, pr = P.bank()
        k.op("pe", lambda e, ps=ps: e.matmul(ps[:, 0:tw], lhsT=ones, rhs=ssum[:, 0:tw], start=True, stop=True),
             reads=[ssum_res, ones_res], writes=[pr])
        k.op("act", lambda e, ps=ps: e.activation(out=rstd[:, 0:tw], in_=ps[:, 0:tw], func=AF.Sqrt,
                                                  bias=cst["eps"], scale=1.0 / D),
             reads=[pr, cst["eps_r"]], writes=[rstd_res])
        k.op("dve", lambda e: e.reciprocal(out=rstd[:, 0:tw], in_=rstd[:, 0:tw]),
             reads=[rstd_res], writes=[rstd_res])
        for kc in range(KC):
            k.op("dve", lambda e, kc=kc, sl=sl, o=o: e.scalar_tensor_tensor(
                out=o[:, kc, :], in0=x[:, kc, sl], scalar=gvec[:, kc:kc + 1], in1=rstd[:, 0:tw],
                op0=ALU.mult, op1=ALU.mult),
                reads=[xres[tt], gres, rstd_res], writes=[ores])
        if after is not None:
            after(tt, o, ores)


def consts_eps(P):
    return P.eps_ap


def emit_ffn(P, x, xres, xn, xnres, w1d, w2d, wb, tag):
    k = P.k
    (w1buf, w1res, w2buf, w2res, g, gres, stmp, stres) = wb
    ds1 = [k.dsem("w1_0"), k.dsem("w1_1")]
    ds2 = [k.dsem("w2_0"), k.dsem("w2_1")]
    cnt1 = 0
    cnt2 = 0
    for st in range(2):
        for f in range(NF):
            s = cnt1 % 2
            cnt1 += 1
            k.dma("pool", ds1[s], out=w1buf[s], in_=w1d[f], reads=[], writes=[w1res[s]])
            for t2 in range(2):
                tt = st * 2 + t2
                sl = slice(tt * TT, (tt + 1) * TT)
                pa, par = P.bank()
                pb, pbr = P.bank()
                for kc in range(KC):
                    k.op("pe", lambda e, pa=pa, s=s, kc=kc, sl=sl: e.matmul(
                        pa, lhsT=w1buf[s][:, kc * 256:kc * 256 + 128], rhs=xn[:, kc, sl],
                        start=(kc == 0), stop=(kc == KC - 1)),
                        reads=[w1res[s], xnres[tt]], writes=[par])
                for kc in range(KC):
                    k.op("pe", lambda e, pb=pb, s=s, kc=kc, sl=sl: e.matmul(
                        pb, lhsT=w1buf[s][:, kc * 256 + 128:kc * 256 + 256], rhs=xn[:, kc, sl],
                        start=(kc == 0), stop=(kc == KC - 1)),
                        reads=[w1res[s], xnres[tt]], writes=[pbr])
                ts_ = (f * 2 + t2) % 2
                k.op("act", lambda e, pa=pa, ts_=ts_: e.activation(out=stmp[ts_], in_=pa, func=AF.Silu),
                     reads=[par], writes=[stres[ts_]])
                k.op("dve", lambda e, pb=pb, ts_=ts_, f=f, t2=t2: e.tensor_tensor(
                    out=g[:, f, t2 * TT:(t2 + 1) * TT], in0=stmp[ts_], in1=pb, op=ALU.mult),
                    reads=[stres[ts_], pbr], writes=[gres[f][t2]])
        for dc in range(KC):
            s = cnt2 % 2
            cnt2 += 1
            k.dma("pool", ds2[s], out=w2buf[s], in_=w2d[dc], reads=[], writes=[w2res[s]],
                  max_dma_last_dim=1408 * 4)
            for t2 in range(2):
                tt = st * 2 + t2
                sl = slice(tt * TT, (tt + 1) * TT)
                po, por = P.bank()
                for f in range(NF):
                    k.op("pe", lambda e, po=po, s=s, f=f, t2=t2: e.matmul(
                        po, lhsT=w2buf[s][:, f * 128:(f + 1) * 128], rhs=g[:, f, t2 * TT:(t2 + 1) * TT],
                        start=(f == 0), stop=(f == NF - 1)),
                        reads=[w2res[s], gres[f][t2]], writes=[por])
                k.op("dve", lambda e, po=po, dc=dc, sl=sl: e.scalar_tensor_tensor(
                    out=x[:, dc, sl], in0=po, scalar=0.5, in1=x[:, dc, sl], op0=ALU.mult, op1=ALU.add),
                    reads=[por, xres[tt]], writes=[xres[tt]])


def pack_w1(w):
    a = w[:, :DFF].reshape(KC, 128, NF, 128)
    b = w[:, DFF:].reshape(KC, 128, NF, 128)
    ab = np.stack([a, b], axis=3)
    return np.ascontiguousarray(ab.transpose(2, 1, 0, 3, 4)).reshape(NF, 128, KC * 256)


def pack_w2(w):
    a = w.reshape(NF, 128, KC, 128)
    return np.ascontiguousarray(a.transpose(2, 1, 0, 3)).reshape(KC, 128, NF * 128)


def pack_vec(v):
    return np.ascontiguousarray(v.reshape(KC, 128).T)


def pack_xT(xc):
    return np.ascontiguousarray(xc.reshape(T, KC, 128).transpose(2, 1, 0))


def unpack_xT(a):
    return np.ascontiguousarray(a.transpose(2, 1, 0)).reshape(T, D)


def make_consts(P, mask_d):
    k, ar = P.k, P.ar
    c = {}
    c["ones_f"] = ar.alloc([128], F32); c["ones_f_r"] = Res()
    c["ones_bf"] = ar.alloc([128], BF16); c["ones_bf_r"] = Res()
    c["mask"] = ar.alloc([128], BF16); c["mask_r"] = Res()
    c["eps"] = ar.alloc([1], F32); c["eps_r"] = Res()
    P.eps_ap = c["eps"]; P.eps_res = c["eps_r"]
    k.op("pool", lambda e: e.memset(c["ones_f"], 1.0), writes=[c["ones_f_r"]])
    k.op("pool", lambda e: e.memset(c["ones_bf"], 1.0), writes=[c["ones_bf_r"]])
    k.op("pool", lambda e: e.memset(c["eps"], EPS), writes=[c["eps_r"]])
    k.dma("pool", k.dsem("cmask"), out=c["mask"], in_=mask_d, writes=[c["mask_r"]])
    return c


NGT = 16
NCB = 1026
VB_PSCALE = 0
VB_CONVW = 1
VB_CONVB = 9
VB_BA = 11
VB_BX = 13
VB_LAM = 15
VB_BF = 17
VB_SEL = 19
VB_CORR = 23
NVB = 39


def emit_phaseB(P, cst, usrc, wB_d, wsB_d, vecB_d, mdst, tile_done=None, ures=None):
    k = P.k
    ar = P.ar
    ar.mark()
    ones_f, ones_f_r, mask, mask_r = cst["ones_f"], cst["ones_f_r"], cst["mask"], cst["mask_r"]
    GB = [5, 6, 7]
    SB = [2, 3, 4]
    OB = [0, 1]
    gcnt = [0]
    scnt = [0]

    def gbank():
        i = GB[gcnt[0] % 3]
        gcnt[0] += 1
        return P.psum[:, i, :], P.pres[i]

    def sbank():
        i = SB[scnt[0] % 3]
        scnt[0] += 1
        return P.psum[:, i, :], P.pres[i]

    utile = ar.alloc([KC, TT], BF16); utile_r = Res()
    wB = ar.alloc([KC, NCB], BF16); wB_r = Res()
    wsB = ar.alloc([640], BF16); wsB_r = Res()
    vec = ar.alloc([NVB], F32); vec_r = Res()
    nbf = ar.alloc([2], F32); nbf_r = Res()
    nsp8 = ar.alloc([2], F32); nsp8_r = Res()
    Kaug = [ar.alloc([NGT * TT], BF16) for _ in range(2)]
    K_r = [[Res() for _ in range(NGT)] for _ in range(2)]
    Vaug = ar.alloc([NGT * 4, 2, 65], BF16)
    V_r = [Res() for _ in range(NGT)]
    negc = ar.alloc([NGT * 4, 2], F32)
    negc_r = [[Res() for _ in range(NGT)] for _ in range(2)]
    xa_buf = ar.alloc([528], F32); xa_r = Res()
    sA = ar.alloc([528], F32); sA_r = Res()
    sBf = ar.alloc([528], F32); sB_r = Res()
    acc = ar.alloc([TT], F32); acc_r = Res()
    d_bf = [ar.alloc([TT], BF16) for _ in range(2)]; d_r = [Res(), Res()]
    xb_buf = [ar.alloc([515], F32) for _ in range(2)]; xb_r = [Res(), Res()]
    xc = [[ar.alloc([TT], F32) for _ in range(2)] for _ in range(2)]
    xc_r = [[Res(), Res()] for _ in range(2)]
    xc_bf = [[ar.alloc([TT], BF16) for _ in range(2)] for _ in range(2)]
    xcb_r = [[Res(), Res()] for _ in range(2)]
    gg = [[ar.alloc([TT], BF16) for _ in range(2)] for _ in range(2)]
    gg_r = [[Res(), Res()] for _ in range(2)]
    t1 = ar.alloc([TT], F32); t1_r = Res()
    Qaug = [[ar.alloc([TT], BF16) for _ in range(2)] for _ in range(2)]
    Q_r = [[Res(), Res()] for _ in range(2)]
    rowe = ar.alloc([TT], F32); rowe_r = Res()
    crow = [ar.alloc([TT], F32) for _ in range(2)]; crow_r = [Res(), Res()]
    clast = ar.alloc([2], F32); clast_r = [Res(), Res()]
    ra = ar.alloc([TT], F32); ra_r = Res()
    tmp = ar.alloc([TT], F32); tmp_r = Res()
    ig = ar.alloc([TT], F32); ig_r = Res()
    hbuf = ar.alloc([TT], F32); hbuf_r = Res()
    hlast = ar.alloc([2], F32); hlast_r = [Res(), Res()]
    Pt = [ar.alloc([TT], BF16) for _ in range(3)]; Pt_r = [Res(), Res(), Res()]
    rec = ar.alloc([TT], F32); rec_r = Res()
    osb = ar.alloc([TT], F32); osb_r = Res()
    outA = [ar.alloc([TT], BF16)] * 2; outA_r = [Res()] * 2
    outB = [ar.alloc([2, TT], BF16) for _ in range(2)]; outB_r = [Res(), Res()]
    outC = [[ar.alloc([TT], BF16)] * 2 for _ in range(2)]
    outC_r = [[Res()] * 2 for _ in range(2)]
    ds_u = k.dsem("bu")
    ds_o = k.dsem("bo")
    store_res = []

    k.dma("pool", k.dsem("bw"), out=wB, in_=wB_d, writes=[wB_r])
    k.dma("pool", k.dsem("bws"), out=wsB, in_=wsB_d, writes=[wsB_r])
    k.dma("sp", k.dsem("bv"), out=vec, in_=vecB_d, writes=[vec_r])
    for h in range(2):
        k.op("pool", lambda e, h=h: e.memset(Kaug[h][64:65, :], 1.0), writes=K_r[h])
    k.op("pool", lambda e: e.memset(Vaug[:, :, :, 64:65], 1.0), writes=V_r)
    k.op("dve", lambda e: e.tensor_scalar(out=nbf, in0=vec[:, VB_BF:VB_BF + 2], scalar1=-1.0, scalar2=None,
                                          op0=ALU.mult), reads=[vec_r], writes=[nbf_r])
    k.op("act", lambda e: e.activation(out=nsp8, in_=vec[:, VB_LAM:VB_LAM + 2], func=AF.Exp, scale=-1.0),
         reads=[vec_r], writes=[nsp8_r])
    k.op("act", lambda e: e.activation(out=nsp8, in_=nsp8, func=AF.Ln, bias=ones_f[:, 0:1], scale=1.0),
         reads=[nsp8_r, ones_f_r], writes=[nsp8_r])
    k.op("dve", lambda e: e.tensor_scalar(out=nsp8, in0=nsp8, scalar1=-8.0, scalar2=None, op0=ALU.mult),
         reads=[nsp8_r], writes=[nsp8_r])

    def bank_any():
        return gbank()

    def S1(gt):
        r, tt = gt // 4, gt % 4
        par = gt % 2
        C = {}

        def proj(c0, ncol, nparts):
            ps, pr = gbank()
            for kc in range(KC):
                k.op("pe", lambda e, ps=ps, kc=kc: e.matmul(ps[0:nparts, :], lhsT=wB[:, kc, c0:c0 + ncol],
                                                          rhs=utile[:, kc, :], start=(kc == 0), stop=(kc == KC - 1)),
                     reads=[wB_r, utile_r], writes=[pr])
            return ps, pr

        def m_load():
            k.dma("sp", ds_u, out=utile, in_=usrc(r, tt), reads=(ures(tt) if ures else []), writes=[utile_r])
        C["load"] = [m_load]

        st = {}

        def xa1():
            if gt == 0:
                k.op("dve", lambda e: e.memset(xa_buf[:, 0:16], 0.0), writes=[xa_r])
            else:
                k.op("dve", lambda e: e.tensor_copy(out=xa_buf[:, 0:16], in_=xa_buf[:, 512:528]),
                     reads=[xa_r], writes=[xa_r])
            st["xa"] = proj(0, 128, 128)

        def xa2():
            ps, pr = st["xa"]
            k.op("dve", lambda e: e.tensor_copy(out=xa_buf[:, 16:528], in_=ps), reads=[pr], writes=[xa_r])

            def padd(dst, dr, srcb, sr, lo, sh):
                k.op("dve", lambda e: e.tensor_tensor(out=dst[:, lo:528], in0=srcb[:, lo:528],
                                                      in1=srcb[:, lo - sh:528 - sh], op=ALU.add),
                     reads=[sr], writes=[dr])

            def pacc(srcb, sr, i):
                if i == 0:
                    k.op("dve", lambda e: e.tensor_scalar(out=acc, in0=srcb[:, 16:528],
                                                          scalar1=vec[:, VB_SEL:VB_SEL + 1], scalar2=None, op0=ALU.mult),
                         reads=[sr, vec_r], writes=[acc_r])
                else:
                    k.op("dve", lambda e: e.scalar_tensor_tensor(out=acc, in0=srcb[:, 16:528],
                                                                 scalar=vec[:, VB_SEL + i:VB_SEL + i + 1], in1=acc,
                                                                 op0=ALU.mult, op1=ALU.add),
                         reads=[sr, vec_r, acc_r], writes=[acc_r])

            padd(sA, sA_r, xa_buf, xa_r, 1, 1)
            pacc(sA, sA_r, 0)
            padd(sBf, sB_r, sA, sA_r, 3, 2)
            pacc(sBf, sB_r, 1)
            padd(sA, sA_r, sBf, sB_r, 7, 4)
            pacc(sA, sA_r, 2)
            padd(sBf, sB_r, sA, sA_r, 15, 8)
            pacc(sBf, sB_r, 3)
            if gt == 0:
                k.op("dve", lambda e: e.tensor_tensor(out=acc[:, 0:16], in0=acc[:, 0:16],
                                                      in1=vec[:, VB_CORR:VB_CORR + 16], op=ALU.mult),
                     reads=[acc_r, vec_r], writes=[acc_r])
            k.op("dve", lambda e: e.tensor_tensor(out=d_bf[par], in0=acc, in1=xa_buf[:, 16:528], op=ALU.subtract),
                 reads=[xa_r, acc_r], writes=[d_r[par]])
        C["xa"] = [xa1, xa2]

        def mk_xb(hh):
            def m1():
                if gt == 0:
                    k.op("dve", lambda e: e.memset(xb_buf[hh][:, 0:3], 0.0), writes=[xb_r[hh]])
                else:
                    k.op("dve", lambda e: e.tensor_copy(out=xb_buf[hh][:, 0:3], in_=xb_buf[hh][:, 512:515]),
                         reads=[xb_r[hh]], writes=[xb_r[hh]])
                st["xb%d" % hh] = proj(128 + 128 * hh, 128, 128)

            def m2():
                ps, pr = st["xb%d" % hh]
                k.op("dve", lambda e: e.tensor_copy(out=xb_buf[hh][:, 3:515], in_=ps), reads=[pr], writes=[xb_r[hh]])
                cw = VB_CONVW + 4 * hh
                k.op("dve", lambda e: e.tensor_scalar(
                    out=xc[hh][par], in0=xb_buf[hh][:, 0:512], scalar1=vec[:, cw:cw + 1],
                    scalar2=vec[:, VB_CONVB + hh:VB_CONVB + hh + 1], op0=ALU.mult, op1=ALU.add),
                    reads=[xb_r[hh], vec_r], writes=[xc_r[hh][par]])
                for kk in range(1, 4):
                    k.op("dve", lambda e, kk=kk: e.scalar_tensor_tensor(
                        out=xc[hh][par], in0=xb_buf[hh][:, kk:kk + 512], scalar=vec[:, cw + kk:cw + kk + 1],
                        in1=xc[hh][par], op0=ALU.mult, op1=ALU.add),
                        reads=[xb_r[hh], vec_r, xc_r[hh][par]], writes=[xc_r[hh][par]])
                k.op("dve", lambda e: e.tensor_copy(out=xc_bf[hh][par], in_=xc[hh][par]),
                     reads=[xc_r[hh][par]], writes=[xcb_r[hh][par]])
            return [m1, m2]

        def mk_gb(hh):
            def m1():
                st["gb%d" % hh] = proj(384 + 128 * hh, 128, 128)

            def m2():
                ps, pr = st["gb%d" % hh]
                k.op("act", lambda e: e.activation(out=t1, in_=ps, func=AF.Square), reads=[pr], writes=[t1_r])
                k.op("dve", lambda e: e.tensor_copy(out=gg[hh][par], in_=ps), reads=[pr, t1_r], writes=[gg_r[hh][par]])

            def m3():
                ps, pr = st["gb%d" % hh]
                k.op("dve", lambda e: e.tensor_scalar(out=t1, in0=t1, scalar1=0.044715, scalar2=1.0, op0=ALU.mult,
                                                      op1=ALU.add), reads=[t1_r], writes=[t1_r])
                k.op("dve", lambda e: e.tensor_tensor(out=t1, in0=t1, in1=ps, op=ALU.mult),
                     reads=[t1_r, pr], writes=[t1_r])

            def m4():
                k.op("act", lambda e: e.activation(out=t1, in_=t1, func=AF.Sigmoid, scale=1.5957691216057308),
                     reads=[t1_r], writes=[t1_r])

            def m5():
                k.op("dve", lambda e: e.tensor_tensor(out=gg[hh][par], in0=gg[hh][par], in1=t1, op=ALU.mult),
                     reads=[t1_r, gg_r[hh][par]], writes=[gg_r[hh][par]])
            return [m1, m2, m3, m4, m5]

        def mk_q(h):
            def m1():
                st["q%d" % h] = proj(640 + 65 * h, 65, 65)

            def m2():
                ps, pr = st["q%d" % h]
                k.op("dve", lambda e: e.tensor_copy(out=Qaug[h][par][0:64, :], in_=ps[0:64, :]),
                     reads=[pr], writes=[Q_r[h][par]])
                k.op("act", lambda e: e.activation(out=rowe[64:65, :], in_=ps[64:65, :], func=AF.Exp,
                                                   bias=nbf[64:65, h:h + 1], scale=-1.0),
                     reads=[pr, nbf_r], writes=[rowe_r])
                k.op("act", lambda e: e.activation(out=rowe[64:65, :], in_=rowe[64:65, :], func=AF.Ln,
                                                   bias=ones_f[64:65, 0:1], scale=1.0),
                     reads=[rowe_r, ones_f_r], writes=[rowe_r])

            def m3():
                init = 0.0 if gt == 0 else clast[64:65, h:h + 1]
                k.op("dve", lambda e: e.tensor_tensor_scan(
                    out=crow[h][64:65, :], data0=ones_f[64:65, 0:1].to_broadcast([1, TT]),
                    data1=rowe[64:65, :], initial=init, op0=ALU.mult, op1=ALU.subtract),
                    reads=[rowe_r, clast_r[h], ones_f_r], writes=[crow_r[h]])
                k.op("dve", lambda e: e.tensor_copy(out=clast[64:65, h:h + 1], in_=crow[h][64:65, TT - 1:TT]),
                     reads=[crow_r[h]], writes=[clast_r[h]])

            def m4():
                k.op("act", lambda e: e.activation(out=Qaug[h][par][64:65, :], in_=crow[h][64:65, :],
                                                   func=AF.Copy, scale=8.0),
                     reads=[crow_r[h]], writes=[Q_r[h][par]])

            def m5():
                pt_, ptr_ = gbank()
                st["n%d" % h] = (pt_, ptr_)
                for blk in range(4):
                    k.op("pe", lambda e, blk=blk: e.matmul(
                        pt_[:, blk:blk + 1], lhsT=crow[h][64:65, blk * 128:(blk + 1) * 128], rhs=ones_f[64:65, 0:1],
                        start=True, stop=True),
                        reads=[crow_r[h], ones_f_r], writes=[ptr_])

            def m6():
                pt_, ptr_ = st["n%d" % h]
                k.op("dve", lambda e: e.tensor_scalar(
                    out=negc[:, 4 * gt:4 * gt + 4, h], in0=pt_[:, 0:4], scalar1=-1.0, scalar2=None, op0=ALU.mult),
                    reads=[ptr_], writes=[negc_r[h][gt]])
            return [m1, m2, m3, m4, m5, m6]

        def mk_k(h):
            def m1():
                st["k%d" % h] = proj(770 + 64 * h, 64, 64)

            def m2():
                ps, pr = st["k%d" % h]
                k.op("act", lambda e: e.activation(out=Kaug[h][0:64, gt * TT:(gt + 1) * TT],
                                                   in_=ps[0:64, :], func=AF.Copy),
                     reads=[pr], writes=[K_r[h][gt]])
            return [m1, m2]

        def v1():
            ps, pr = gbank()
            st["v"] = (ps, pr)
            for blk in range(4):
                for kc in range(KC):
                    k.op("pe", lambda e, blk=blk, kc=kc: e.matmul(
                        ps[:, blk * 128:(blk + 1) * 128], lhsT=utile[:, kc, blk * 128:(blk + 1) * 128],
                        rhs=wB[:, kc, 898:1026], start=(kc == 0), stop=(kc == KC - 1)),
                        reads=[wB_r, utile_r], writes=[pr])

        def v2():
            ps, pr = st["v"]
            k.op("dve", lambda e: e.tensor_copy(
                out=Vaug[:, 4 * gt:4 * gt + 4, :, 0:64], in_=ps.rearrange("p (b h d) -> p b h d", b=4, h=2)),
                reads=[pr], writes=[V_r[gt]])
        C["xb0"], C["xb1"] = mk_xb(0), mk_xb(1)
        C["gb0"], C["gb1"] = mk_gb(0), mk_gb(1)
        C["q0"], C["q1"] = mk_q(0), mk_q(1)
        C["k0"], C["k1"] = mk_k(0), mk_k(1)
        C["v"] = [v1, v2]
        return C

    def S2chains(gt):
        r, tt = gt // 4, gt % 4
        par = gt % 2
        tsl = slice(tt * TT, (tt + 1) * TT)
        C = {}
        st = {}

        def pool1():
            ps, pr = gbank()
            st["pool"] = (ps, pr)
            k.op("pe", lambda e: e.matmul(ps, lhsT=wsB[:, 0:128], rhs=d_bf[par], start=True, stop=True),
                 reads=[wsB_r, d_r[par]], writes=[pr])

        def pool2():
            ps, pr = st["pool"]
            k.op("act", lambda e: e.activation(out=outA[par], in_=ps, func=AF.Identity,
                                               scale=vec[:, VB_PSCALE:VB_PSCALE + 1]),
                 reads=[pr, vec_r], writes=[outA_r[par]])
            for dap, ps_ in mdst("A", r, tsl):
                sr = Res(); store_res.append(sr)
                k.dma("sp", ds_o, out=dap, in_=outA[par][ps_, :], reads=[outA_r[par]], writes=[sr])
        C["pool"] = [pool1, pool2]

        def mk_lru(hh):
            def m1():
                ps, pr = gbank()
                k.op("pe", lambda e: e.matmul(ps, lhsT=wsB[:, 128 + 128 * hh:256 + 128 * hh],
                                              rhs=xc_bf[hh][par], start=True, stop=True),
                     reads=[wsB_r, xcb_r[hh][par]], writes=[pr])
                ps2, pr2 = gbank()
                k.op("pe", lambda e: e.matmul(ps2, lhsT=wsB[:, 384 + 128 * hh:512 + 128 * hh],
                                              rhs=xc_bf[hh][par], start=True, stop=True),
                     reads=[wsB_r, xcb_r[hh][par]], writes=[pr2])
                st["l%d" % hh] = (ps, pr, ps2, pr2)

            def m2():
                ps, pr, ps2, pr2 = st["l%d" % hh]
                k.op("act", lambda e: e.activation(out=ra, in_=ps, func=AF.Sigmoid,
                                                   bias=vec[:, VB_BA + hh:VB_BA + hh + 1], scale=1.0),
                     reads=[pr, vec_r], writes=[ra_r])
                k.op("act", lambda e: e.activation(out=ig, in_=ps2, func=AF.Sigmoid,
                                                   bias=vec[:, VB_BX + hh:VB_BX + hh + 1], scale=1.0),
                     reads=[pr2, vec_r], writes=[ig_r])
                k.op("act", lambda e: e.activation(out=ra, in_=ra, func=AF.Exp, scale=nsp8[:, hh:hh + 1]),
                     reads=[ra_r, nsp8_r], writes=[ra_r])

            def m3():
                k.op("dve", lambda e: e.tensor_tensor(out=tmp, in0=ra, in1=ra, op=ALU.mult),
                     reads=[ra_r], writes=[tmp_r])
                k.op("dve", lambda e: e.tensor_tensor(out=ig, in0=ig, in1=xc[hh][par], op=ALU.mult),
                     reads=[ig_r, xc_r[hh][par]], writes=[ig_r])

            def m4():
                k.op("act", lambda e: e.activation(out=tmp, in_=tmp, func=AF.Sqrt, bias=ones_f[:, 0:1], scale=-1.0),
                     reads=[tmp_r, ones_f_r], writes=[tmp_r])

            def m5():
                k.op("dve", lambda e: e.tensor_tensor(out=ig, in0=ig, in1=tmp, op=ALU.mult),
                     reads=[ig_r, tmp_r], writes=[ig_r])
                init = 0.0 if gt == 0 else hlast[:, hh:hh + 1]
                k.op("dve", lambda e: e.tensor_tensor_scan(out=hbuf, data0=ra, data1=ig, initial=init,
                                                           op0=ALU.mult, op1=ALU.add),
                     reads=[ra_r, ig_r, hlast_r[hh]], writes=[hbuf_r])
                k.op("dve", lambda e: e.tensor_copy(out=hlast[:, hh:hh + 1], in_=hbuf[:, TT - 1:TT]),
                     reads=[hbuf_r], writes=[hlast_r[hh]])
                k.op("dve", lambda e: e.tensor_tensor(out=outB[par][:, hh, :], in0=hbuf, in1=gg[hh][par], op=ALU.mult),
                     reads=[hbuf_r, gg_r[hh][par]], writes=[outB_r[par]])
                if hh == 1:
                    for dap, ps_, hx in mdst("B", r, tsl):
                        sr = Res(); store_res.append(sr)
                        k.dma("sp", ds_o, out=dap, in_=outB[par][ps_, hx, :], reads=[outB_r[par]], writes=[sr])
            return [m1, m2, m3, m4, m5]
        C["lru0"], C["lru1"] = mk_lru(0), mk_lru(1)
        return C

    ORDER = [("load", 0), ("pool", 0), ("q0", 0), ("pool", 1), ("q0", 1), ("lru0", 0), ("q0", 2), ("lru0", 1),
             ("q0", 3), ("q1", 0), ("lru0", 2), ("q1", 1), ("lru0", 3), ("q1", 2), ("lru0", 4), ("q1", 3),
             ("lru1", 0), ("q0", 4), ("lru1", 1), ("q0", 5), ("lru1", 2), ("q1", 4), ("lru1", 3), ("q1", 5),
             ("lru1", 4), ("xa", 0), ("xb0", 0), ("xa", 1), ("xb0", 1), ("xb1", 0), ("gb0", 0), ("xb1", 1),
             ("gb0", 1), ("k0", 0), ("gb0", 2), ("k0", 1), ("gb0", 3), ("k1", 0), ("gb0", 4), ("k1", 1),
             ("gb1", 0), ("v", 0), ("gb1", 1), ("v", 1), ("gb1", 2), ("gb1", 3), ("gb1", 4)]

    def flat(c2, c1):
        out = []
        for name, i in ORDER:
            src_ = c2 if name in ("pool", "lru0", "lru1") else c1
            if src_ is not None:
                out.append(src_[name][i])
        return out

    def S2(gt, steps):
        r, tt = gt // 4, gt % 4
        par = gt % 2
        tsl = slice(tt * TT, (tt + 1) * TT)
        nkb = 4 * gt + 4

        def attn(h):
            oi = OB[(2 * gt + h) % 2]
            ob, obr = P.psum[:, oi, :], P.pres[oi]
            sbs = {}

            def emit_S(kb):
                dd = kb - 4 * gt
                q0 = 128 * dd if dd > 0 else 0
                sb, sbr = sbank()
                sbs[kb] = (sb, sbr, q0, dd)
                k.op("pe", lambda e, sb=sb, q0=q0, kb=kb: e.matmul(
                    sb[:, q0:TT], lhsT=Kaug[h][0:65, kb * 128:(kb + 1) * 128], rhs=Qaug[h][par][0:65, q0:TT],
                    start=True, stop=True),
                    reads=[K_r[h][kb // 4], Q_r[h][par]], writes=[sbr])

            emit_S(0)
            if nkb > 1:
                emit_S(1)
            for kb in range(nkb):
                if kb + 2 < nkb:
                    emit_S(kb + 2)
                sb, sbr, q0, dd = sbs.pop(kb)
                pi = kb % 3
                k.op("act", lambda e, sb=sb, q0=q0, kb=kb, pi=pi: e.activation(
                    out=Pt[pi][:, q0:TT], in_=sb[:, q0:TT], func=AF.Exp, bias=negc[:, kb, h:h + 1], scale=0.125),
                    reads=[sbr, negc_r[h][kb // 4]], writes=[Pt_r[pi]])
                if dd >= 0:
                    k.op("dve", lambda e, q0=q0, pi=pi: e.tensor_tensor(
                        out=Pt[pi][:, q0:q0 + 128], in0=Pt[pi][:, q0:q0 + 128], in1=mask, op=ALU.min),
                        reads=[Pt_r[pi], mask_r], writes=[Pt_r[pi]])
                k.op("pe", lambda e, q0=q0, kb=kb, pi=pi: e.matmul(
                    ob[0:65, q0:TT], lhsT=Vaug[:, kb, h, :], rhs=Pt[pi][:, q0:TT],
                    start=(kb == 0), stop=(kb == nkb - 1)),
                    reads=[V_r[kb // 4], Pt_r[pi]], writes=[obr])
                if steps:
                    steps.pop(0)()
            k.op("dve", lambda e: e.reciprocal(out=rec[64:65, :], in_=ob[64:65, :]), reads=[obr], writes=[rec_r])
            bc, bcr = sbank()
            k.op("pe", lambda e, bc=bc: e.matmul(bc[0:64, :], lhsT=ones_f[64:65, 0:64], rhs=rec[64:65, :],
                                                 start=True, stop=True),
                 reads=[rec_r, ones_f_r], writes=[bcr])
            k.op("act", lambda e: e.activation(out=osb[0:64, :], in_=ob[0:64, :], func=AF.Copy),
                 reads=[obr], writes=[osb_r])
            k.op("dve", lambda e, bc=bc: e.tensor_tensor(out=outC[h][par][0:64, :], in0=osb[0:64, :], in1=bc[0:64, :],
                                                         op=ALU.mult),
                 reads=[osb_r, bcr], writes=[outC_r[h][par]])
            for dap, ps_ in mdst("C", r, tsl, h):
                sr = Res(); store_res.append(sr)
                k.dma("sp", ds_o, out=dap, in_=outC[h][par][ps_, :], reads=[outC_r[h][par]], writes=[sr])
        for h in range(2):
            attn(h)
        while steps:
            steps.pop(0)()
        if tile_done is not None:
            tile_done(gt, store_res)

    for s_ in flat(None, S1(0)):
        s_()
    for gt in range(NGT):
        S2(gt, flat(S2chains(gt), S1(gt + 1) if gt + 1 < NGT else None))
    ar.release()


def pack_wB(w_in_l, j):
    cols = list(range(128 * j, 128 * j + 128))
    cols += list(range(512 + 256 * j, 512 + 256 * j + 256))
    cols += list(range(1536 + 256 * j, 1536 + 256 * j + 256))
    for h in range(2):
        hd = 2 * j + h
        cols += list(range(2560 + 64 * hd, 2560 + 64 * hd + 64)) + [4096 + hd]
    cols += list(range(3072 + 128 * j, 3072 + 128 * j + 128))
    cols += list(range(3584 + 128 * j, 3584 + 128 * j + 128))
    w = w_in_l[:, cols]
    return np.ascontiguousarray(w.reshape(KC, 128, NCB).transpose(1, 0, 2))


def pack_wsB(inp, l, j):
    parts = [inp["w_pool"][l, j]]
    parts += [inp["w_rg_a"][l, 2 * j + hh] for hh in range(2)]
    parts += [inp["w_rg_x"][l, 2 * j + hh] for hh in range(2)]
    return np.ascontiguousarray(np.concatenate(parts, axis=1))


def pack_vecB(inp, l, j):
    v = np.zeros((128, NVB), np.float32)
    v[:, VB_PSCALE] = inp["pool_scale"][l, 128 * j:128 * j + 128]
    for hh in range(2):
        sl = slice(256 * j + 128 * hh, 256 * j + 128 * hh + 128)
        for kk in range(4):
            v[:, VB_CONVW + 4 * hh + kk] = inp["conv_w"][l, kk, sl]
        v[:, VB_CONVB + hh] = inp["conv_b"][l, sl]
        v[:, VB_BA + hh] = inp["b_rg_a"][l, sl]
        v[:, VB_BX + hh] = inp["b_rg_x"][l, sl]
        v[:, VB_LAM + hh] = inp["lru_lambda"][l, sl]
        v[:, VB_BF + hh] = inp["b_f"][l, 2 * j + hh]
    w = 2 ** (j + 1)
    v[:, VB_SEL + j] = np.float32(1.0 / w)
    for t in range(16):
        v[:, VB_CORR + t] = np.float32(w / min(t + 1, w))
    return v


def const_mask():
    jj = np.arange(128)[:, None]
    tt = np.arange(128)[None, :]
    return np.where(jj <= tt, np.float32(3.0e38), np.float32(0.0)).astype(np.float32)


def emit_phaseC(P, cst, x, xres, umine, mix_all_d, wC_d, bgate_d, wo_d):
    k, ar = P.k, P.ar
    ar.mark()
    u = ar.alloc([KC, T], BF16); u_r = [Res() for _ in range(NT)]
    mixs = ar.alloc([16, 2 * TT], BF16); mixs_r = [Res() for _ in range(16)]
    merged = ar.alloc([KC, 2 * TT], BF16); mg_r = [[Res(), Res()] for _ in range(KC)]
    wcb = [ar.alloc([5120], BF16) for _ in range(2)]; wcb_r = [Res(), Res()]
    wob = [ar.alloc([1024], BF16) for _ in range(2)]; wob_r = [Res(), Res()]
    gate = [ar.alloc([TT], F32) for _ in range(3)]; gate_r = [Res(), Res(), Res()]
    acc = [ar.alloc([TT], F32) for _ in range(2)]; acc_r = [Res(), Res()]
    bg = ar.alloc([24], F32); bg_r = Res()
    ds_u = k.dsem("cu")
    for tt in range(NT):
        k.dma("sp", ds_u, out=u[:, :, tt * TT:(tt + 1) * TT], in_=umine(tt), writes=[u_r[tt]])
    k.dma("sp", k.dsem("cbg"), out=bg, in_=bgate_d, writes=[bg_r])
    dsm = k.dsem("cmix")
    dsw = [k.dsem("cw0"), k.dsem("cw1")]
    dso = [k.dsem("co0"), k.dsem("co1")]
    cw = 0
    co = 0
    for st in range(2):
        cs = slice(st * 2 * TT, (st + 1) * 2 * TT)
        for r in range(4):
            if callable(mix_all_d):
                srcs = [(r, mix_all_d(r, 0, cs)), (4 + 2 * r, mix_all_d(r, 1, cs)), (5 + 2 * r, mix_all_d(r, 2, cs)),
                        (12 + r, mix_all_d(r, 3, cs))]
            else:
                srcs = [(r, mix_all_d[r, 0:128, cs]), (4 + 2 * r, mix_all_d[r, 128:256, cs]),
                        (5 + 2 * r, mix_all_d[r, 256:384, cs]), (12 + r, mix_all_d[r, 384:512, cs])]
            for m_, s_ in srcs:
                k.dma("sp", dsm, out=mixs[:, m_, :], in_=s_, writes=[mixs_r[m_]])
        for dc in range(KC):
            s = cw % 2
            cw += 1
            k.dma("pool", dsw[s], out=wcb[s], in_=wC_d[dc], writes=[wcb_r[s]], max_dma_last_dim=1024 * 4)
            for t2 in range(2):
                tt = st * 2 + t2
                sl = slice(tt * TT, (tt + 1) * TT)
                sl2 = slice(t2 * TT, (t2 + 1) * TT)
                for br in range(3):
                    ps, pr = P.bank()
                    for kc in range(KC):
                        c0 = 2048 + kc * 384 + br * 128
                        k.op("pe", lambda e, ps=ps, s=s, kc=kc, c0=c0, sl=sl: e.matmul(
                            ps, lhsT=wcb[s][:, c0:c0 + 128], rhs=u[:, kc, sl], start=(kc == 0), stop=(kc == KC - 1)),
                            reads=[wcb_r[s], u_r[tt]], writes=[pr])
                    k.op("act", lambda e, ps=ps, br=br, dc=dc: e.activation(
                        out=gate[br], in_=ps, func=AF.Sigmoid, bias=bg[:, br * 8 + dc:br * 8 + dc + 1], scale=1.0),
                        reads=[pr, bg_r], writes=[gate_r[br]])
                ys = []
                for (m0, m1) in ((0, 4), (4, 12), (12, 16)):
                    ps, pr = P.bank()
                    for m in range(m0, m1):
                        k.op("pe", lambda e, ps=ps, s=s, m=m, sl2=sl2, m0=m0, m1=m1: e.matmul(
                            ps, lhsT=wcb[s][:, m * 128:(m + 1) * 128], rhs=mixs[:, m, sl2],
                            start=(m == m0), stop=(m == m1 - 1)),
                            reads=[wcb_r[s], mixs_r[m]], writes=[pr])
                    ys.append((ps, pr))
                k.op("dve", lambda e, ys=ys: e.tensor_tensor(out=acc[0], in0=ys[0][0], in1=gate[0], op=ALU.mult),
                     reads=[ys[0][1], gate_r[0]], writes=[acc_r[0]])
                k.op("dve", lambda e, ys=ys: e.tensor_tensor(out=acc[1], in0=ys[1][0], in1=gate[1], op=ALU.mult),
                     reads=[ys[1][1], gate_r[1]], writes=[acc_r[1]])
                k.op("dve", lambda e: e.tensor_tensor(out=acc[0], in0=acc[0], in1=acc[1], op=ALU.add),
                     reads=[acc_r[0], acc_r[1]], writes=[acc_r[0]])
                k.op("dve", lambda e, ys=ys: e.tensor_tensor(out=acc[1], in0=ys[2][0], in1=gate[2], op=ALU.mult),
                     reads=[ys[2][1], gate_r[2]], writes=[acc_r[1]])
                k.op("dve", lambda e, dc=dc, sl2=sl2: e.tensor_tensor(out=merged[:, dc, sl2], in0=acc[0], in1=acc[1],
                                                                      op=ALU.add),
                     reads=[acc_r[0], acc_r[1]], writes=[mg_r[dc][t2]])
        for dco in range(KC):
            s = co % 2
            co += 1
            k.dma("pool", dso[s], out=wob[s], in_=wo_d[dco], writes=[wob_r[s]])
            for t2 in range(2):
                tt = st * 2 + t2
                sl = slice(tt * TT, (tt + 1) * TT)
                sl2 = slice(t2 * TT, (t2 + 1) * TT)
                ps, pr = P.bank()
                for dc in range(KC):
                    k.op("pe", lambda e, ps=ps, s=s, dc=dc, sl2=sl2: e.matmul(
                        ps, lhsT=wob[s][:, dc * 128:(dc + 1) * 128], rhs=merged[:, dc, sl2],
                        start=(dc == 0), stop=(dc == KC - 1)),
                        reads=[wob_r[s], mg_r[dc][t2]], writes=[pr])
                k.op("dve", lambda e, ps=ps, dco=dco, sl=sl: e.tensor_tensor(out=x[:, dco, sl], in0=ps, in1=x[:, dco, sl],
                                                                             op=ALU.add),
                     reads=[pr, xres[tt]], writes=[xres[tt]])
    ar.release()


def emit_cross(P, cst, x, xres, memT_d, gmem, gmem_r, gcross, gcross_r, wxq_d, wxkv_d, wxo_d):
    k, ar = P.k, P.ar
    ar.mark()
    NM = 256
    xn = ar.alloc([KC, T], BF16); xn_r = [Res() for _ in range(NT)]
    QT = ar.alloc([KC, T], BF16); QT_r = [[Res() for _ in range(NT)] for _ in range(KC)]
    memf = ar.alloc([KC, NM], F32); memf_r = [Res()]
    memn = ar.alloc([KC, NM], BF16); memn_r = Res()
    KxT = ar.alloc([KC, NM], BF16); Kx_r = [Res() for _ in range(KC)]
    Vx = ar.alloc([2, D], BF16); Vx_r = [Res() for _ in range(KC)]
    wch = [ar.alloc([1024], BF16) for _ in range(2)]; wch_r = [Res(), Res()]
    Pt = [[ar.alloc([TT], BF16) for _ in range(2)] for _ in range(2)]; Pt_r = [[Res(), Res()] for _ in range(2)]
    rec = ar.alloc([TT], F32); rec_r = Res()
    nt = alloc_normtmp(P)
    dsw = [k.dsem("xw0"), k.dsem("xw1")]
    wc = [0]

    def wload(src):
        s = wc[0] % 2
        wc[0] += 1
        k.dma("pool", dsw[s], out=wch[s], in_=src, writes=[wch_r[s]])
        return s

    k.dma("sp", k.dsem("xmem"), out=memf, in_=memT_d, writes=[memf_r[0]])
    emit_rmsnorm(P, cst, nt, memf, memf_r, gmem, gmem_r, lambda tt: (memn, memn_r), ntiles=1, tw=NM)
    for ck in range(16):
        s = wload(wxkv_d[ck])
        ps, pr = P.bank()
        if ck < 8:
            for kc in range(KC):
                k.op("pe", lambda e, ps=ps, s=s, kc=kc: e.matmul(
                    ps[:, 0:NM], lhsT=wch[s][:, kc * 128:(kc + 1) * 128], rhs=memn[:, kc, :],
                    start=(kc == 0), stop=(kc == KC - 1)),
                    reads=[wch_r[s], memn_r], writes=[pr])
            k.op("act", lambda e, ps=ps, ck=ck: e.activation(out=KxT[:, ck, :], in_=ps[:, 0:NM], func=AF.Copy),
                 reads=[pr], writes=[Kx_r[ck]])
        else:
            c = ck - 8
            for mb in range(2):
                for kc in range(KC):
                    k.op("pe", lambda e, ps=ps, s=s, kc=kc, mb=mb: e.matmul(
                        ps[:, mb * 128:(mb + 1) * 128], lhsT=memn[:, kc, mb * 128:(mb + 1) * 128],
                        rhs=wch[s][:, kc * 128:(kc + 1) * 128], start=(kc == 0), stop=(kc == KC - 1)),
                        reads=[wch_r[s], memn_r], writes=[pr])
            k.op("act", lambda e, ps=ps, c=c: e.activation(
                out=Vx[:, :, c * 128:(c + 1) * 128], in_=ps[:, 0:256].rearrange("p (m j) -> p m j", m=2), func=AF.Copy),
                reads=[pr], writes=[Vx_r[c]])
    emit_rmsnorm(P, cst, nt, x, xres, gcross, gcross_r, lambda tt: (xn[:, :, tt * TT:(tt + 1) * TT], xn_r[tt]))
    for ck in range(KC):
        s = wload(wxq_d[ck])
        for tt in range(NT):
            sl = slice(tt * TT, (tt + 1) * TT)
            ps, pr = P.bank()
            for kc in range(KC):
                k.op("pe", lambda e, ps=ps, s=s, kc=kc, sl=sl: e.matmul(
                    ps, lhsT=wch[s][:, kc * 128:(kc + 1) * 128], rhs=xn[:, kc, sl],
                    start=(kc == 0), stop=(kc == KC - 1)),
                    reads=[wch_r[s], xn_r[tt]], writes=[pr])
            if tt % 2 == 0:
                k.op("act", lambda e, ps=ps, ck=ck, sl=sl: e.activation(out=QT[:, ck, sl], in_=ps, func=AF.Copy),
                     reads=[pr], writes=[QT_r[ck][tt]])
            else:
                k.op("dve", lambda e, ps=ps, ck=ck, sl=sl: e.tensor_copy(out=QT[:, ck, sl], in_=ps),
                     reads=[pr], writes=[QT_r[ck][tt]])
    cnt = 0
    for tt in range(NT):
        sl = slice(tt * TT, (tt + 1) * TT)
        for h in range(4):
            pp = cnt % 2
            cnt += 1
            for mb in range(2):
                ps, pr = P.bank()
                for half in range(2):
                    ck = 2 * h + half
                    k.op("pe", lambda e, ps=ps, ck=ck, mb=mb, sl=sl, half=half: e.matmul(
                        ps, lhsT=KxT[:, ck, mb * 128:(mb + 1) * 128], rhs=QT[:, ck, sl],
                        start=(half == 0), stop=(half == 1)),
                        reads=[Kx_r[ck], QT_r[ck][tt]], writes=[pr])
                k.op("act", lambda e, ps=ps, pp=pp, mb=mb: e.activation(out=Pt[pp][mb], in_=ps, func=AF.Exp,
                                                                       scale=0.0625),
                     reads=[pr], writes=[Pt_r[pp][mb]])
            ps, pr = P.bank()
            for mb in range(2):
                k.op("pe", lambda e, ps=ps, pp=pp, mb=mb: e.matmul(ps, lhsT=cst["ones_bf"], rhs=Pt[pp][mb],
                                                                   start=(mb == 0), stop=(mb == 1)),
                     reads=[cst["ones_bf_r"], Pt_r[pp][mb]], writes=[pr])
            k.op("dve", lambda e, ps=ps: e.reciprocal(out=rec, in_=ps), reads=[pr], writes=[rec_r])
            for half in range(2):
                ck = 2 * h + half
                ps, pr = P.bank()
                for mb in range(2):
                    k.op("pe", lambda e, ps=ps, pp=pp, mb=mb, ck=ck: e.matmul(
                        ps, lhsT=Vx[:, mb, ck * 128:(ck + 1) * 128], rhs=Pt[pp][mb], start=(mb == 0), stop=(mb == 1)),
                        reads=[Vx_r[ck], Pt_r[pp][mb]], writes=[pr])
                k.op("dve", lambda e, ps=ps, ck=ck, sl=sl: e.tensor_tensor(out=QT[:, ck, sl], in0=ps, in1=rec, op=ALU.mult),
                     reads=[pr, rec_r], writes=[QT_r[ck][tt]])
    for dco in range(KC):
        s = wload(wxo_d[dco])
        for tt in range(NT):
            sl = slice(tt * TT, (tt + 1) * TT)
            ps, pr = P.bank()
            for ck in range(KC):
                k.op("pe", lambda e, ps=ps, s=s, ck=ck, sl=sl: e.matmul(
                    ps, lhsT=wch[s][:, ck * 128:(ck + 1) * 128], rhs=QT[:, ck, sl],
                    start=(ck == 0), stop=(ck == KC - 1)),
                    reads=[wch_r[s], QT_r[ck][tt]], writes=[pr])
            k.op("dve", lambda e, ps=ps, dco=dco, sl=sl: e.tensor_tensor(out=x[:, dco, sl], in0=ps, in1=x[:, dco, sl],
                                                                         op=ALU.add),
                 reads=[pr, xres[tt]], writes=[xres[tt]])
    ar.release()


def pack_wC(inp, l):
    ua = inp["w_up_a"][l].reshape(4, 128, KC, 128)
    ub = inp["w_up_b"][l].reshape(8, 128, KC, 128)
    uc = inp["w_up_c"][l].reshape(4, 128, KC, 128)
    up = np.concatenate([ua, ub, uc], axis=0)
    up = up.transpose(2, 1, 0, 3).reshape(KC, 128, 2048)
    wg = inp["w_in"][l][:, 4104:].reshape(KC, 128, 3, KC, 128)
    wg = wg.transpose(3, 1, 0, 2, 4).reshape(KC, 128, 3072)
    return np.ascontiguousarray(np.concatenate([up, wg], axis=2))


def pack_bgate(inp, l):
    return np.ascontiguousarray(inp["b_gate"][l].reshape(3, KC, 128).transpose(2, 0, 1).reshape(128, 24))


def pack_sq(w):
    C = w.shape[1] // 128
    a = w.reshape(KC, 128, C, 128)
    return np.ascontiguousarray(a.transpose(2, 1, 0, 3)).reshape(C, 128, KC * 128)


def pack_memT(m):
    return np.ascontiguousarray(m.reshape(256, KC, 128).transpose(2, 1, 0))


ARENA_BYTES = 205 * 1024
RGROUPS = [[0, 1, 2, 3], [4, 5, 6, 7]]


def build(stages, fused=False):
    nc = bass.Bass("TRN2", target_bir_lowering=False)
    dram = {}

    def dten(name, shape, dt, kind):
        if name not in dram:
            dram[name] = nc.dram_tensor(name, list(shape), dt, kind=kind).ap()
        return dram[name]

    def din(name, shape, dt=F32):
        return dten(name, shape, dt, "ExternalInput")

    def dout(name, shape, dt):
        return dten(name, shape, dt, "ExternalOutput")

    def dint(name, shape, dt):
        return dten(name, shape, dt, "Internal")

    with ExitStack() as stack:
        arena_t = stack.enter_context(nc.sbuf_tensor("arena", [128, ARENA_BYTES // 4], F32))
        psum_t = stack.enter_context(nc.psum_tensor("psum", [128, 8, 512], F32))
        k = K(nc, stack)
        ar = Arena(arena_t[:, :], ARENA_BYTES)
        P = Prog(nc, k, ar, psum_t[:, :, :])
        cst = make_consts(P, din("cmask", [128, 128]))
        x = ar.alloc([KC, T], F32)
        xres = [Res() for _ in range(NT)]
        gvs = ar.alloc([16, KC], F32)
        gv_n = [0]
        pid_cache = {}
        gath_res = {}
        uall_res = {}
        ds_g = k.dsem("gv")

        def load_g(name):
            i = gv_n[0] % 16
            gv_n[0] += 1
            r = Res()
            k.dma("sp", ds_g, out=gvs[:, i, :], in_=din(name, [128, KC]), writes=[r])
            return gvs[:, i, :], r

        def sfx(l):
            return "" if l is None else "_%d" % l

        def slab_copies(d):
            for half in range(2):
                gq = dint("mix_gath_%d_%d" % (d, half), [4 * 256, T], BF16)
                for r in range(4):
                    mb = dint("mix_big_%d_%d" % (r, half), [4 * 256, T], BF16)

                    def dstf(e, d=d, mb=mb):
                        if "c" not in pid_cache:
                            pid_cache["c"] = nc.sync.partition_id() % 4
                        return mb[bass.ds(((d + 4 - pid_cache["c"]) % 4) * 256, 256), :]
                    k.dma("sp", None, out=dstf, in_=gq[r * 256:(r + 1) * 256, :], reads=[gath_res[(d, half)]])

        for stg in stages:
            k.barrier(skip_cc=(fused and stg[0] in ("ag_u", "B")))
            if stg == "load_x":
                xd = din("xT", [128, KC, T])
                for tt in range(NT):
                    sl = slice(tt * TT, (tt + 1) * TT)
                    k.dma("sp", k.dsem("x"), out=x[:, :, sl], in_=xd[:, :, sl], writes=[xres[tt]])
            elif stg == "store_x":
                xo = dout("xT_out", [128, KC, T], F32)
                for tt in range(NT):
                    sl = slice(tt * TT, (tt + 1) * TT)
                    k.dma("sp", k.dsem("xo"), out=xo[:, :, sl], in_=x[:, :, sl], reads=[xres[tt]])
            elif stg[0] == "ffn":
                _, l, i = stg
                ar.mark()
                gv, gr = load_g("gffn%s_%d" % (sfx(l), i))
                xn = ar.alloc([KC, T], BF16); xn_r = [Res() for _ in range(NT)]
                nt = alloc_normtmp(P)
                w1buf = [ar.alloc([KC * 256], BF16) for _ in range(2)]; w1res = [Res(), Res()]
                w2buf = [ar.alloc([NF * 128], BF16) for _ in range(2)]; w2res = [Res(), Res()]
                g = ar.alloc([NF, 2 * TT], BF16); gres2 = [[Res(), Res()] for _ in range(NF)]
                stmp = [ar.alloc([TT], F32) for _ in range(2)]; stres = [Res(), Res()]
                emit_rmsnorm(P, cst, nt, x, xres, gv, gr, lambda tt: (xn[:, :, tt * TT:(tt + 1) * TT], xn_r[tt]))
                emit_ffn(P, x, xres, xn, xn_r, din("w1%s_%d" % (sfx(l), i), [NF, 128, KC * 256]),
                         din("w2%s_%d" % (sfx(l), i), [KC, 128, NF * 128]),
                         (w1buf, w1res, w2buf, w2res, g, gres2, stmp, stres), "f")
                ar.release()
            elif stg[0] == "unorm":
                _, l = stg
                ar.mark()
                gv, gr = load_g("gmix%s" % sfx(l))
                xn = ar.alloc([KC, T], BF16); xn_r = [Res() for _ in range(NT)]
                nt = alloc_normtmp(P)
                dsu = k.dsem("ust")
                if fused:
                    def after(tt, o, ores):
                        um = dint("u_mine_%d" % tt, [128, KC * TT], BF16)
                        ua = dint("u_all_%d" % tt, [4 * 128, KC * TT], BF16)
                        ur = Res()
                        k.dma("sp", dsu, out=um.rearrange("p (k t) -> p k t", k=KC), in_=o, reads=[ores], writes=[ur])
                        uall_res[tt] = Res()
                        k.coll("AllGather", RGROUPS, ins=[um], outs=[ua], reads=[ur], writes=[uall_res[tt]])
                else:
                    ud = dout("u_mine%s" % sfx(l), [128, KC, T], BF16)

                    def after(tt, o, ores, ud=ud, dsu=dsu):
                        sl = slice(tt * TT, (tt + 1) * TT)
                        k.dma("sp", dsu, out=ud[:, :, sl], in_=o, reads=[ores], writes=[])

                emit_rmsnorm(P, cst, nt, x, xres, gv, gr, lambda tt: (xn[:, :, tt * TT:(tt + 1) * TT], xn_r[tt]),
                             after=after)
                ar.release()
            elif stg[0] == "B":
                _, l = stg
                if fused:
                    def usrc(r, tt):
                        ua = dint("u_all_%d" % tt, [4 * 128, KC * TT], BF16)
                        return ua[r * 128:(r + 1) * 128, :].rearrange("p (k t) -> p k t", k=KC)

                    def mm(d, half):
                        return dint("mix_mine_%d_%d" % (d, half), [256, T], BF16)

                    def mdst(kind, r, tsl, h=0):
                        if kind == "A":
                            return [(mm(r, 0)[0:128, tsl], slice(0, 128))]
                        if kind == "B":
                            return [(mm(r, 0)[128:256, tsl], slice(0, 128), 0), (mm(r, 1)[0:128, tsl], slice(0, 128), 1)]
                        return [(mm(r, 1)[128 + 64 * h:192 + 64 * h, tsl], slice(0, 64))]

                    def tile_done(gt, store_res):
                        if gt % 4 == 3:
                            d = gt // 4
                            for half in range(2):
                                gq = dint("mix_gath_%d_%d" % (d, half), [4 * 256, T], BF16)
                                gres = Res()
                                gath_res[(d, half)] = gres
                                k.coll("AllGather", RGROUPS, ins=[mm(d, half)], outs=[gq], reads=list(store_res),
                                       writes=[gres])
                            del store_res[:]
                        if gt % 4 == 1 and gt >= 5:
                            slab_copies(gt // 4 - 1)
                else:
                    u_all = din("u_all%s" % sfx(l), [4, 128, KC, T], BF16)
                    mix = dout("mix_mine%s" % sfx(l), [4, 512, T], BF16)

                    def usrc(r, tt, u_all=u_all):
                        return u_all[r, :, :, tt * TT:(tt + 1) * TT]

                    def mdst(kind, r, tsl, h=0, mix=mix):
                        if kind == "A":
                            return [(mix[r, 0:128, tsl], slice(0, 128))]
                        if kind == "B":
                            return [(mix[r, 128 + 128 * hh:256 + 128 * hh, tsl], slice(0, 128), hh) for hh in range(2)]
                        return [(mix[r, 384 + 64 * h:448 + 64 * h, tsl], slice(0, 64))]
                emit_phaseB(P, cst, usrc, din("wB%s" % sfx(l), [128, KC, NCB]), din("wsB%s" % sfx(l), [128, 640]),
                            din("vecB%s" % sfx(l), [128, NVB]), mdst, tile_done if fused else None,
                            (lambda tt: [uall_res[tt]]) if fused else None)
            elif stg[0] == "C":
                _, l = stg
                if fused:
                    def umine(tt):
                        return dint("u_mine_%d" % tt, [128, KC * TT], BF16).rearrange("p (k t) -> p k t", k=KC)

                    def mix_all(r, piece, cs):
                        mb = dint("mix_big_%d_%d" % (r, piece // 2), [4 * 256, T], BF16)
                        return mb[128 * (piece % 2):128 * (piece % 2) + 128, cs]
                else:
                    ud = din("u_mine%s" % sfx(l), [128, KC, T], BF16)
                    mix_all = din("mix_all%s" % sfx(l), [4, 512, T], BF16)

                    def umine(tt, ud=ud):
                        return ud[:, :, tt * TT:(tt + 1) * TT]
                emit_phaseC(P, cst, x, xres, umine, mix_all, din("wC%s" % sfx(l), [KC, 128, 5120]),
                            din("bgate%s" % sfx(l), [128, 24]), din("wo%s" % sfx(l), [KC, 128, 1024]))
            elif stg[0] == "cross":
                _, l = stg
                gm, gmr = load_g("gmem%s" % sfx(l))
                gc, gcr = load_g("gcross%s" % sfx(l))
                emit_cross(P, cst, x, xres, din("memT", [128, KC, 256]), gm, gmr, gc, gcr,
                           din("wxq%s" % sfx(l), [KC, 128, 1024]), din("wxkv%s" % sfx(l), [16, 128, 1024]),
                           din("wxo%s" % sfx(l), [KC, 128, 1024]))
            elif stg[0] == "ag_u":
                pass
            elif stg[0] == "ag_mix":
                slab_copies(3)
            elif stg == "final":
                ar.mark()
                gv, gr = load_g("gfin")
                nt = alloc_normtmp(P)
                stg_t = [ar.alloc([KC, TT], F32) for _ in range(2)]; stg_r = [Res(), Res()]
                od = dout("outT", [128, KC, T], F32)
                dso = k.dsem("out")

                def after(tt, o, ores, od=od, dso=dso):
                    sl = slice(tt * TT, (tt + 1) * TT)
                    k.dma("sp", dso, out=od[:, :, sl], in_=o, reads=[ores], writes=[])

                emit_rmsnorm(P, cst, nt, x, xres, gv, gr, lambda tt: (stg_t[tt % 2], stg_r[tt % 2]), after=after)
                ar.release()
            else:
                raise ValueError(stg)
        k.barrier()
        k.final_wait("sp")
        k.emit()
    return nc


_PROGS = {}


def get_prog(key, stages, fused=False):
    if key not in _PROGS:
        _PROGS[key] = build(stages, fused)
    return _PROGS[key]


def _run(nc, maps):
    res = run_bass_kernel_spmd(nc, maps, core_ids=list(range(NCORES)))
    return res.results


def _weights_ffn(inp, l, i, names=None):
    sfx = "_%d_%d" % (l, i)
    if i == 1:
        w_in, w_out, g = inp["w_ffn1_in"], inp["w_ffn1_out"], inp["g_ffn1"]
    else:
        w_in, w_out, g = inp["w_ffn2_in"], inp["w_ffn2_out"], inp["g_ffn2"]
    return {"w1" + sfx: pack_w1(w_in[l]), "w2" + sfx: pack_w2(w_out[l]), "gffn" + sfx: pack_vec(g[l])}


def _weights_C(inp, l):
    s = "_%d" % l
    return {"wC" + s: pack_wC(inp, l), "bgate" + s: pack_bgate(inp, l), "wo" + s: pack_sq(inp["w_o"][l]),
            "gmem" + s: pack_vec(inp["g_mem"][l]), "gcross" + s: pack_vec(inp["g_cross"][l]),
            "wxq" + s: pack_sq(inp["w_xq"][l]), "wxkv" + s: pack_sq(inp["w_xkv"][l]), "wxo" + s: pack_sq(inp["w_xo"][l])}


FUSED = True


def fused_stages():
    st = ["load_x"]
    for l in range(L):
        st += [("ffn", l, 1), ("unorm", l), ("ag_u", l), ("B", l), ("ag_mix", l), ("C", l), ("cross", l), ("ffn", l, 2)]
    st.append("final")
    return st


def kernel_fused(inp):
    cmask = const_mask()
    xs = inp["x"].reshape(NCORES, T, D)
    w = {"cmask": cmask, "gfin": pack_vec(inp["g_final"])}
    wBs = {}
    for l in range(L):
        w.update(_weights_ffn(inp, l, 1))
        w.update(_weights_ffn(inp, l, 2))
        w.update(_weights_C(inp, l))
        w["gmix_%d" % l] = pack_vec(inp["g_mix"][l])
        for j in range(4):
            wBs[(l, j)] = {"wB_%d" % l: pack_wB(inp["w_in"][l], j), "wsB_%d" % l: pack_wsB(inp, l, j),
                           "vecB_%d" % l: pack_vecB(inp, l, j)}
    memT = [pack_memT(inp["mem"][b]) for b in range(2)]
    maps = []
    for c in range(NCORES):
        b, j = c // 4, c % 4
        m = dict(w)
        for l in range(L):
            m.update(wBs[(l, j)])
        m["xT"] = pack_xT(xs[c])
        m["memT"] = memT[b]
        maps.append(m)
    prog = get_prog("fused", fused_stages(), fused=True)
    r = _run(prog, maps)
    out = np.stack([unpack_xT(r[c]["outT"]) for c in range(NCORES)], axis=0)
    return out.reshape(2, 8192, D).astype(np.float32)


def kernel(**inp):
    inp = {k_: np.asarray(v) for k_, v in inp.items()}
    if FUSED:
        return kernel_fused(inp)
    cmask = const_mask()
    xs = inp["x"].reshape(NCORES, T, D)
    p1 = get_prog("p1", ["load_x", ("ffn", 0, 1), ("unorm", 0), "store_x"])
    w = {"cmask": cmask, "gmix_0": pack_vec(inp["g_mix"][0])}
    w.update(_weights_ffn(inp, 0, 1))
    r = _run(p1, [dict(w, xT=pack_xT(xs[c])) for c in range(NCORES)])
    xT = [r[c]["xT_out"] for c in range(NCORES)]
    um = [r[c]["u_mine_0"] for c in range(NCORES)]
    out = None
    for l in range(L):
        pB = get_prog("pB", [("B", None)])
        maps = []
        for c in range(NCORES):
            b, j = c // 4, c % 4
            maps.append({"cmask": cmask, "u_all": np.stack([um[4 * b + rr] for rr in range(4)], axis=0),
                         "wB": pack_wB(inp["w_in"][l], j), "wsB": pack_wsB(inp, l, j), "vecB": pack_vecB(inp, l, j)})
        r = _run(pB, maps)
        mix = [r[c]["mix_mine"] for c in range(NCORES)]
        w = {"cmask": cmask}
        w.update(_weights_C(inp, l))
        w.update(_weights_ffn(inp, l, 2))
        if l + 1 < L:
            stages = ["load_x", ("C", l), ("cross", l), ("ffn", l, 2), ("ffn", l + 1, 1), ("unorm", l + 1), "store_x"]
            w.update(_weights_ffn(inp, l + 1, 1))
            w["gmix_%d" % (l + 1)] = pack_vec(inp["g_mix"][l + 1])
        else:
            stages = ["load_x", ("C", l), ("cross", l), ("ffn", l, 2), "final"]
            w["gfin"] = pack_vec(inp["g_final"])
        pC = get_prog("pC%d" % l, stages)
        maps = []
        for c in range(NCORES):
            b, cc = c // 4, c % 4
            m = dict(w)
            m["xT"] = xT[c]
            m["u_mine_%d" % l] = um[c]
            m["mix_all_%d" % l] = np.stack([mix[4 * b + j][cc] for j in range(4)], axis=0)
            m["memT"] = pack_memT(inp["mem"][b])
            maps.append(m)
        r = _run(pC, maps)
        if l + 1 < L:
            xT = [r[c]["xT_out"] for c in range(NCORES)]
            um = [r[c]["u_mine_%d" % (l + 1)] for c in range(NCORES)]
        else:
            out = np.stack([unpack_xT(r[c]["outT"]) for c in range(NCORES)], axis=0)
    return out.reshape(2, 8192, D).astype(np.float32)
```

```python
import numpy as np
from contextlib import ExitStack
import concourse.bass as bass
import concourse.mybir as mybir
from concourse.bass_utils import run_bass_kernel_spmd

F32 = mybir.dt.float32
BF16 = mybir.dt.bfloat16
AF = mybir.ActivationFunctionType
ALU = mybir.AluOpType
AX = mybir.AxisListType

D = 1024
KC = 8
T = 2048
TT = 512
NT = T // TT
DFF = 2816
NF = DFF // 128
EPS = 1e-6
L = 2
NCORES = 8


class Res:
    __slots__ = ("name", "w", "r")

    def __init__(self, name=""):
        self.name = name
        self.w = None
        self.r = {}


class DSem:
    __slots__ = ("key", "sem", "count")

    def __init__(self, key, sem):
        self.key = key
        self.sem = sem
        self.count = 0


class Eng:
    def __init__(self, name, sem):
        self.name = name
        self.sem = sem
        self.count = 0
        self.seen = {}
        self.q = []


class K:
    def __init__(self, nc, stack):
        self.nc = nc
        self.stack = stack
        self.engs = {}
        for n in ("pe", "act", "dve", "pool", "sp"):
            self.engs[n] = Eng(n, stack.enter_context(nc.semaphore("sem_" + n)))
        self.dsems = {}
        self.ninst = 0
        self.sched = {}

    NPOOL = 64

    def semname(self, sem):
        for e in self.engs.values():
            if e.sem is sem:
                return e.name
        for d in self.dsems.values():
            if d.sem is sem:
                return d.key
        return "?"

    def simulate(self):
        cnt = {}
        pos = {e: 0 for e in self.sched}
        progress = True
        while progress:
            progress = False
            for e, lst in self.sched.items():
                while pos[e] < len(lst):
                    waits, key, inc = lst[pos[e]]
                    if all(cnt.get(kk, 0) >= v for kk, v in waits):
                        cnt[key] = cnt.get(key, 0) + inc
                        pos[e] += 1
                        progress = True
                    else:
                        break
        stuck = {e: (pos[e], len(lst), lst[pos[e]][0] if pos[e] < len(lst) else None) for e, lst in self.sched.items()}
        return stuck, cnt

    def dsem(self, key):
        return None

    def _pool_sem(self):
        if not hasattr(self, "dpool"):
            self.dpool = []
            self.dnext = 0
        if len(self.dpool) < self.NPOOL:
            i = len(self.dpool)
            d = DSem("dp%d" % i, self.stack.enter_context(self.nc.semaphore("dsem_p%d" % i)))
            self.dpool.append(d)
            self.dsems[d.key] = d
            return d
        d = self.dpool[self.dnext % self.NPOOL]
        self.dnext += 1
        return d

    def _waits(self, eng, reads, writes):
        deps = {}

        def add(tok):
            if tok is None:
                return
            key, sem, val = tok
            if key not in deps or deps[key][1] < val:
                deps[key] = (sem, val)

        for r in reads:
            add(r.w)
        for w in writes:
            add(w.w)
            for tok in w.r.values():
                add(tok)
        out = []
        for key, (sem, val) in deps.items():
            if eng.seen.get(key, 0) >= val:
                continue
            if key == "pe" and eng.name == "pe":
                continue
            eng.seen[key] = val
            out.append((sem, val))
        return out

    def op(self, ename, fn, reads=(), writes=()):
        eng = self.engs[ename]
        waits = self._waits(eng, reads, writes)
        sem = eng.sem
        eng.count += 1
        tok = (ename, sem, eng.count)

        def run(e, waits=waits, fn=fn, sem=sem):
            for s, v in waits:
                e.wait_ge(s, v)
            fn(e).then_inc(sem, 1)

        eng.q.append(run)
        self.sched.setdefault(ename, []).append(([(self.semname(s), v) for s, v in waits], ename, 1))
        for r in reads:
            r.r[ename] = tok
        for w in writes:
            w.w = tok
            w.r = {}
        self.ninst += 1

    def dma(self, qname, ds, out, in_, reads=(), writes=(), **kw):
        eng = self.engs[qname]
        ds = self._pool_sem()
        waits = self._waits(eng, reads, writes)
        if ds.count > 0 and eng.seen.get(ds.key, 0) < ds.count:
            eng.seen[ds.key] = ds.count
            waits.append((ds.sem, ds.count))
        ds.count += 16
        tok = (ds.key, ds.sem, ds.count)

        def run(e, waits=waits, out=out, in_=in_, kw=kw, sem=ds.sem, cnt=ds.count):
            for s, v in waits:
                e.wait_ge(s, v)
            e.dma_start(out=out(e) if callable(out) else out, in_=in_(e) if callable(in_) else in_,
                        **kw).then_inc(sem, 16)

        eng.q.append(run)
        self.sched.setdefault(qname, []).append(([(self.semname(s), v) for s, v in waits], ds.key, 16))
        for r in reads:
            r.r[ds.key] = tok
        for w in writes:
            w.w = tok
            w.r = {}
        self.ninst += 1

    def coll(self, kind, groups, ins, outs, reads=(), writes=()):
        eng = self.engs["pool"]
        i = len([k_ for k_ in self.dsems if k_.startswith("cc")])
        ds = DSem("cc%d" % i, self.stack.enter_context(self.nc.semaphore("ccsem_%d" % i)))
        self.dsems[ds.key] = ds
        waits = self._waits(eng, reads, writes)
        ds.count += 1
        tok = (ds.key, ds.sem, ds.count)

        def run(e, waits=waits, sem=ds.sem):
            for s, v in waits:
                e.wait_ge(s, v)
            e.collective_compute(kind, ALU.bypass, replica_groups=groups, ins=[a.opt() for a in ins],
                                 outs=[a.opt() for a in outs]).then_inc(sem, 1)

        eng.q.append(run)
        self.sched.setdefault("pool", []).append(([(self.semname(s), v) for s, v in waits], ds.key, 1))
        for r in reads:
            r.r[ds.key] = tok
        for w in writes:
            w.w = tok
            w.r = {}

    def barrier(self, skip_cc=False):
        toks = [(e.name, e.sem, e.count) for e in self.engs.values() if e.count > 0]
        toks += [(d.key, d.sem, d.count) for d in self.dsems.values()
                 if d.count > 0 and not (skip_cc and d.key.startswith("cc"))]
        for eng in self.engs.values():
            waits = []
            for key, sem, val in toks:
                if eng.seen.get(key, 0) >= val:
                    continue
                eng.seen[key] = val
                waits.append((sem, val))

            def run(e, waits=waits):
                for s, v in waits:
                    e.wait_ge(s, v)

            eng.q.append(run)
            self.sched.setdefault(eng.name, []).append(([(self.semname(s), v) for s, v in waits], "_nop", 0))

    def final_wait(self, qname="sp"):
        eng = self.engs[qname]
        toks = [(d.key, d.sem, d.count) for d in self.dsems.values() if d.count > 0]
        toks += [(e.name, e.sem, e.count) for e in self.engs.values() if e.count > 0]

        def run(e, toks=toks):
            for key, s, v in toks:
                e.wait_ge(s, v)

        eng.q.append(run)

    def emit(self):
        nc = self.nc
        with nc.Block() as block:
            @block.tensor
            def _(e):
                for f in self.engs["pe"].q:
                    f(e)

            @block.scalar
            def _(e):
                for f in self.engs["act"].q:
                    f(e)

            @block.vector
            def _(e):
                for f in self.engs["dve"].q:
                    f(e)

            @block.gpsimd
            def _(e):
                for f in self.engs["pool"].q:
                    f(e)

            @block.sync
            def _(e):
                for f in self.engs["sp"].q:
                    f(e)


class Arena:
    def __init__(self, ap_f32, nbytes):
        self.ap = ap_f32
        self.nbytes = nbytes
        self.off = 0
        self.marks = []

    def alloc(self, shape_free, dtype, parts=128):
        n = int(np.prod(shape_free))
        esz = 2 if dtype == BF16 else 4
        nb = (n * esz + 31) // 32 * 32
        assert self.off + nb <= self.nbytes, ("arena overflow", self.off, nb, self.nbytes)
        a = self.ap[:, self.off // 4:(self.off + nb) // 4]
        self.off += nb
        if dtype == BF16:
            a = a.bitcast(BF16)
        a = a[:, 0:n]
        if len(shape_free) == 2:
            a = a.rearrange("p (a b) -> p a b", a=shape_free[0])
        elif len(shape_free) == 3:
            a = a.rearrange("p (a b c) -> p a b c", a=shape_free[0], b=shape_free[1])
        return a

    def mark(self):
        self.marks.append(self.off)

    def release(self):
        self.off = self.marks.pop()


class Prog:
    def __init__(self, nc, k, arena, psum):
        self.nc = nc
        self.k = k
        self.ar = arena
        self.psum = psum
        self.pres = [Res("ps%d" % i) for i in range(8)]
        self.pidx = 0

    def bank(self):
        i = self.pidx
        self.pidx = (self.pidx + 1) % 8
        return self.psum[:, i, :], self.pres[i]


def alloc_normtmp(P, tw=TT):
    ar = P.ar
    return dict(sq=ar.alloc([KC, tw], F32), sq_r=Res(), ssum=ar.alloc([tw], F32), ssum_r=Res(),
                rstd=ar.alloc([tw], F32), rstd_r=Res())


def emit_rmsnorm(P, cst, nt, x, xres, gvec, gres, dst, ntiles=NT, tw=TT, after=None):
    k = P.k
    ones, ones_res = cst["ones_f"], cst["ones_f_r"]
    sq, sq_res, ssum, ssum_res, rstd, rstd_res = nt["sq"], nt["sq_r"], nt["ssum"], nt["ssum_r"], nt["rstd"], nt["rstd_r"]
    for tt in range(ntiles):
        sl = slice(tt * tw, (tt + 1) * tw)
        o, ores = dst(tt)
        k.op("act", lambda e, sl=sl: e.activation(out=sq[:, :, 0:tw], in_=x[:, :, sl], func=AF.Square),
             reads=[xres[tt]], writes=[sq_res])
        k.op("dve", lambda e: e.tensor_reduce(out=ssum[:, 0:tw], in_=sq[:, :, 0:tw].rearrange("p k t -> p t k"),
                                              axis=AX.X, op=ALU.add),
             reads=[sq_res], writes=[ssum_res])
        ps, pr = P.bank()
        k.op("pe", lambda e, ps=ps: e.matmul(ps[:, 0:tw], lhsT=ones, rhs=ssum[:, 0:tw], start=True, stop=True),
             reads=[ssum_res, ones_res], writes=[pr])
        k.op("act", lambda e, ps=ps: e.activation(out=rstd[:, 0:tw], in_=ps[:, 0:tw], func=AF.Sqrt,
                                                  bias=cst["eps"], scale=1.0 / D),
             reads=[pr, cst["eps_r"]], writes=[rstd_res])
        k.op("dve", lambda e: e.reciprocal(out=rstd[:, 0:tw], in_=rstd[:, 0:tw]),
             reads=[rstd_res], writes=[rstd_res])
        for kc in range(KC):
            k.op("dve", lambda e, kc=kc, sl=sl, o=o: e.scalar_tensor_tensor(
                out=o[:, kc, :], in0=x[:, kc, sl], scalar=gvec[:, kc:kc + 1], in1=rstd[:, 0:tw],
                op0=ALU.mult, op1=ALU.mult),
                reads=[xres[tt], gres, rstd_res], writes=[ores])
        if after is not None:
            after(tt, o, ores)


def consts_eps(P):
    return P.eps_ap


def emit_ffn(P, x, xres, xn, xnres, w1d, w2d, wb, tag):
    k = P.k
    (w1buf, w1res, w2buf, w2res, g, gres, stmp, stres) = wb
    ds1 = [None] * 3
    ds2 = [None] * 3
    NB1 = len(w1buf)
    NB2 = len(w2buf)
    cnt1 = 0
    cnt2 = 0
    for st in range(2):
        for f in range(NF):
            s = cnt1 % NB1
            cnt1 += 1
            k.dma("pool", ds1[s], out=w1buf[s], in_=w1d[f], reads=[], writes=[w1res[s]])
            for t2 in range(2):
                tt = st * 2 + t2
                sl = slice(tt * TT, (tt + 1) * TT)
                pa, par = P.bank()
                pb, pbr = P.bank()
                for kc in range(KC):
                    k.op("pe", lambda e, pa=pa, s=s, kc=kc, sl=sl: e.matmul(
                        pa, lhsT=w1buf[s][:, kc * 256:kc * 256 + 128], rhs=xn[:, kc, sl],
                        start=(kc == 0), stop=(kc == KC - 1)),
                        reads=[w1res[s], xnres[tt]], writes=[par])
                for kc in range(KC):
                    k.op("pe", lambda e, pb=pb, s=s, kc=kc, sl=sl: e.matmul(
                        pb, lhsT=w1buf[s][:, kc * 256 + 128:kc * 256 + 256], rhs=xn[:, kc, sl],
                        start=(kc == 0), stop=(kc == KC - 1)),
                        reads=[w1res[s], xnres[tt]], writes=[pbr])
                ts_ = (f * 2 + t2) % 2
                k.op("act", lambda e, pa=pa, ts_=ts_: e.activation(out=stmp[ts_], in_=pa, func=AF.Silu),
                     reads=[par], writes=[stres[ts_]])
                k.op("dve", lambda e, pb=pb, ts_=ts_, f=f, t2=t2: e.tensor_tensor(
                    out=g[:, f, t2 * TT:(t2 + 1) * TT], in0=stmp[ts_], in1=pb, op=ALU.mult),
                    reads=[stres[ts_], pbr], writes=[gres[f][t2]])
        for dc in range(KC):
            s = cnt2 % NB2
            cnt2 += 1
            k.dma("pool", ds2[s], out=w2buf[s], in_=w2d[dc], reads=[], writes=[w2res[s]],
                  max_dma_last_dim=1408 * 4)
            for t2 in range(2):
                tt = st * 2 + t2
                sl = slice(tt * TT, (tt + 1) * TT)
                po, por = P.bank()
                for f in range(NF):
                    k.op("pe", lambda e, po=po, s=s, f=f, t2=t2: e.matmul(
                        po, lhsT=w2buf[s][:, f * 128:(f + 1) * 128], rhs=g[:, f, t2 * TT:(t2 + 1) * TT],
                        start=(f == 0), stop=(f == NF - 1)),
                        reads=[w2res[s], gres[f][t2]], writes=[por])
                k.op("dve", lambda e, po=po, dc=dc, sl=sl: e.scalar_tensor_tensor(
                    out=x[:, dc, sl], in0=po, scalar=0.5, in1=x[:, dc, sl], op0=ALU.mult, op1=ALU.add),
                    reads=[por, xres[tt]], writes=[xres[tt]])


def pack_w1(w):
    a = w[:, :DFF].reshape(KC, 128, NF, 128)
    b = w[:, DFF:].reshape(KC, 128, NF, 128)
    ab = np.stack([a, b], axis=3)
    return np.ascontiguousarray(ab.transpose(2, 1, 0, 3, 4)).reshape(NF, 128, KC * 256)


def pack_w2(w):
    a = w.reshape(NF, 128, KC, 128)
    return np.ascontiguousarray(a.transpose(2, 1, 0, 3)).reshape(KC, 128, NF * 128)


def pack_vec(v):
    return np.ascontiguousarray(v.reshape(KC, 128).T)


def pack_xT(xc):
    return np.ascontiguousarray(xc.reshape(T, KC, 128).transpose(2, 1, 0))


def unpack_xT(a):
    return np.ascontiguousarray(a.transpose(2, 1, 0)).reshape(T, D)


def make_consts(P, mask_d):
    k, ar = P.k, P.ar
    c = {}
    c["ones_f"] = ar.alloc([128], F32); c["ones_f_r"] = Res()
    c["ones_bf"] = ar.alloc([128], BF16); c["ones_bf_r"] = Res()
    c["mask"] = ar.alloc([128], BF16); c["mask_r"] = Res()
    c["eps"] = ar.alloc([1], F32); c["eps_r"] = Res()
    P.eps_ap = c["eps"]; P.eps_res = c["eps_r"]
    k.op("pool", lambda e: e.memset(c["ones_f"], 1.0), writes=[c["ones_f_r"]])
    k.op("pool", lambda e: e.memset(c["ones_bf"], 1.0), writes=[c["ones_bf_r"]])
    k.op("pool", lambda e: e.memset(c["eps"], EPS), writes=[c["eps_r"]])
    k.dma("pool", k.dsem("cmask"), out=c["mask"], in_=mask_d, writes=[c["mask_r"]])
    return c


NGT = 16
NCB = 1026
VB_PSCALE = 0
VB_CONVW = 1
VB_CONVB = 9
VB_BA = 11
VB_BX = 13
VB_LAM = 15
VB_BF = 17
VB_SEL = 19
VB_CORR = 23
NVB = 39


def emit_phaseB(P, cst, usrc, wB_d, wsB_d, vecB_d, mdst, tile_done=None, ures=None):
    k = P.k
    ar = P.ar
    ar.mark()
    ones_f, ones_f_r, mask, mask_r = cst["ones_f"], cst["ones_f_r"], cst["mask"], cst["mask_r"]
    GB = [5, 6, 7]
    SB = [2, 3, 4]
    OB = [0, 1]
    gcnt = [0]
    scnt = [0]

    def gbank():
        i = GB[gcnt[0] % 3]
        gcnt[0] += 1
        return P.psum[:, i, :], P.pres[i]

    def sbank():
        i = SB[scnt[0] % 3]
        scnt[0] += 1
        return P.psum[:, i, :], P.pres[i]

    utile = ar.alloc([KC, TT], BF16); utile_r = Res()
    wB = ar.alloc([KC, NCB], BF16); wB_r = Res()
    wsB = ar.alloc([640], BF16); wsB_r = Res()
    vec = ar.alloc([NVB], F32); vec_r = Res()
    nbf = ar.alloc([2], F32); nbf_r = Res()
    nsp8 = ar.alloc([2], F32); nsp8_r = Res()
    Kaug = [ar.alloc([NGT * TT], BF16) for _ in range(2)]
    K_r = [[Res() for _ in range(NGT)] for _ in range(2)]
    Vaug = ar.alloc([NGT * 4, 2, 65], BF16)
    V_r = [Res() for _ in range(NGT)]
    negc = ar.alloc([NGT * 4, 2], F32)
    negc_r = [[Res() for _ in range(NGT)] for _ in range(2)]
    xa_buf = ar.alloc([528], F32); xa_r = Res()
    sA = ar.alloc([528], F32); sA_r = Res()
    sBf = ar.alloc([528], F32); sB_r = Res()
    acc = ar.alloc([TT], F32); acc_r = Res()
    d_bf = [ar.alloc([TT], BF16) for _ in range(2)]; d_r = [Res(), Res()]
    xb_buf = [ar.alloc([515], F32) for _ in range(2)]; xb_r = [Res(), Res()]
    xc = [[ar.alloc([TT], F32) for _ in range(2)] for _ in range(2)]
    xc_r = [[Res(), Res()] for _ in range(2)]
    xc_bf = [[ar.alloc([TT], BF16) for _ in range(2)] for _ in range(2)]
    xcb_r = [[Res(), Res()] for _ in range(2)]
    gg = [[ar.alloc([TT], BF16) for _ in range(2)] for _ in range(2)]
    gg_r = [[Res(), Res()] for _ in range(2)]
    t1 = ar.alloc([TT], F32); t1_r = Res()
    Qaug = [[ar.alloc([TT], BF16) for _ in range(2)] for _ in range(2)]
    Q_r = [[Res(), Res()] for _ in range(2)]
    rowe = ar.alloc([TT], F32); rowe_r = Res()
    crow = [ar.alloc([TT], F32) for _ in range(2)]; crow_r = [Res(), Res()]
    clast = ar.alloc([2], F32); clast_r = [Res(), Res()]
    ra = ar.alloc([TT], F32); ra_r = Res()
    tmp = ar.alloc([TT], F32); tmp_r = Res()
    ig = ar.alloc([TT], F32); ig_r = Res()
    hbuf = ar.alloc([TT], F32); hbuf_r = Res()
    hlast = ar.alloc([2], F32); hlast_r = [Res(), Res()]
    Pt = [ar.alloc([TT], BF16) for _ in range(3)]; Pt_r = [Res(), Res(), Res()]
    rec = ar.alloc([TT], F32); rec_r = Res()
    osb = ar.alloc([TT], F32); osb_r = Res()
    outA = [ar.alloc([TT], BF16)] * 2; outA_r = [Res()] * 2
    outB = [ar.alloc([2, TT], BF16) for _ in range(2)]; outB_r = [Res(), Res()]
    outC = [[ar.alloc([TT], BF16)] * 2 for _ in range(2)]
    outC_r = [[Res()] * 2 for _ in range(2)]
    ds_u = k.dsem("bu")
    ds_o = k.dsem("bo")
    store_res = []

    k.dma("pool", k.dsem("bw"), out=wB, in_=wB_d, writes=[wB_r])
    k.dma("pool", k.dsem("bws"), out=wsB, in_=wsB_d, writes=[wsB_r])
    k.dma("sp", k.dsem("bv"), out=vec, in_=vecB_d, writes=[vec_r])
    for h in range(2):
        k.op("pool", lambda e, h=h: e.memset(Kaug[h][64:65, :], 1.0), writes=K_r[h])
    k.op("pool", lambda e: e.memset(Vaug[:, :, :, 64:65], 1.0), writes=V_r)
    k.op("dve", lambda e: e.tensor_scalar(out=nbf, in0=vec[:, VB_BF:VB_BF + 2], scalar1=-1.0, scalar2=None,
                                          op0=ALU.mult), reads=[vec_r], writes=[nbf_r])
    k.op("act", lambda e: e.activation(out=nsp8, in_=vec[:, VB_LAM:VB_LAM + 2], func=AF.Exp, scale=-1.0),
         reads=[vec_r], writes=[nsp8_r])
    k.op("act", lambda e: e.activation(out=nsp8, in_=nsp8, func=AF.Ln, bias=ones_f[:, 0:1], scale=1.0),
         reads=[nsp8_r, ones_f_r], writes=[nsp8_r])
    k.op("dve", lambda e: e.tensor_scalar(out=nsp8, in0=nsp8, scalar1=-8.0, scalar2=None, op0=ALU.mult),
         reads=[nsp8_r], writes=[nsp8_r])

    def bank_any():
        return gbank()

    def S1(gt):
        r, tt = gt // 4, gt % 4
        par = gt % 2
        C = {}

        def proj(c0, ncol, nparts):
            ps, pr = gbank()
            for kc in range(KC):
                k.op("pe", lambda e, ps=ps, kc=kc: e.matmul(ps[0:nparts, :], lhsT=wB[:, kc, c0:c0 + ncol],
                                                          rhs=utile[:, kc, :], start=(kc == 0), stop=(kc == KC - 1)),
                     reads=[wB_r, utile_r], writes=[pr])
            return ps, pr

        def m_load():
            k.dma("sp", ds_u, out=utile, in_=usrc(r, tt), reads=(ures(tt) if ures else []), writes=[utile_r])
        C["load"] = [m_load]

        st = {}

        def xa1():
            if gt == 0:
                k.op("dve", lambda e: e.memset(xa_buf[:, 0:16], 0.0), writes=[xa_r])
            else:
                k.op("dve", lambda e: e.tensor_copy(out=xa_buf[:, 0:16], in_=xa_buf[:, 512:528]),
                     reads=[xa_r], writes=[xa_r])
            st["xa"] = proj(0, 128, 128)

        def xa2():
            ps, pr = st["xa"]
            k.op("dve", lambda e: e.tensor_copy(out=xa_buf[:, 16:528], in_=ps), reads=[pr], writes=[xa_r])

            def padd(dst, dr, srcb, sr, lo, sh):
                k.op("dve", lambda e: e.tensor_tensor(out=dst[:, lo:528], in0=srcb[:, lo:528],
                                                      in1=srcb[:, lo - sh:528 - sh], op=ALU.add),
                     reads=[sr], writes=[dr])

            def pacc(srcb, sr, i):
                if i == 0:
                    k.op("dve", lambda e: e.tensor_scalar(out=acc, in0=srcb[:, 16:528],
                                                          scalar1=vec[:, VB_SEL:VB_SEL + 1], scalar2=None, op0=ALU.mult),
                         reads=[sr, vec_r], writes=[acc_r])
                else:
                    k.op("dve", lambda e: e.scalar_tensor_tensor(out=acc, in0=srcb[:, 16:528],
                                                                 scalar=vec[:, VB_SEL + i:VB_SEL + i + 1], in1=acc,
                                                                 op0=ALU.mult, op1=ALU.add),
                         reads=[sr, vec_r, acc_r], writes=[acc_r])

            padd(sA, sA_r, xa_buf, xa_r, 1, 1)
            pacc(sA, sA_r, 0)
            padd(sBf, sB_r, sA, sA_r, 3, 2)
            pacc(sBf, sB_r, 1)
            padd(sA, sA_r, sBf, sB_r, 7, 4)
            pacc(sA, sA_r, 2)
            padd(sBf, sB_r, sA, sA_r, 15, 8)
            pacc(sBf, sB_r, 3)
            if gt == 0:
                k.op("dve", lambda e: e.tensor_tensor(out=acc[:, 0:16], in0=acc[:, 0:16],
                                                      in1=vec[:, VB_CORR:VB_CORR + 16], op=ALU.mult),
                     reads=[acc_r, vec_r], writes=[acc_r])
            k.op("dve", lambda e: e.tensor_tensor(out=d_bf[par], in0=acc, in1=xa_buf[:, 16:528], op=ALU.subtract),
                 reads=[xa_r, acc_r], writes=[d_r[par]])
        C["xa"] = [xa1, xa2]

        def mk_xb(hh):
            def m1():
                if gt == 0:
                    k.op("dve", lambda e: e.memset(xb_buf[hh][:, 0:3], 0.0), writes=[xb_r[hh]])
                else:
                    k.op("dve", lambda e: e.tensor_copy(out=xb_buf[hh][:, 0:3], in_=xb_buf[hh][:, 512:515]),
                         reads=[xb_r[hh]], writes=[xb_r[hh]])
                st["xb%d" % hh] = proj(128 + 128 * hh, 128, 128)

            def m2():
                ps, pr = st["xb%d" % hh]
                k.op("dve", lambda e: e.tensor_copy(out=xb_buf[hh][:, 3:515], in_=ps), reads=[pr], writes=[xb_r[hh]])
                cw = VB_CONVW + 4 * hh
                k.op("dve", lambda e: e.tensor_scalar(
                    out=xc[hh][par], in0=xb_buf[hh][:, 0:512], scalar1=vec[:, cw:cw + 1],
                    scalar2=vec[:, VB_CONVB + hh:VB_CONVB + hh + 1], op0=ALU.mult, op1=ALU.add),
                    reads=[xb_r[hh], vec_r], writes=[xc_r[hh][par]])
                for kk in range(1, 4):
                    k.op("dve", lambda e, kk=kk: e.scalar_tensor_tensor(
                        out=xc[hh][par], in0=xb_buf[hh][:, kk:kk + 512], scalar=vec[:, cw + kk:cw + kk + 1],
                        in1=xc[hh][par], op0=ALU.mult, op1=ALU.add),
                        reads=[xb_r[hh], vec_r, xc_r[hh][par]], writes=[xc_r[hh][par]])
                k.op("dve", lambda e: e.tensor_copy(out=xc_bf[hh][par], in_=xc[hh][par]),
                     reads=[xc_r[hh][par]], writes=[xcb_r[hh][par]])
            return [m1, m2]

        def mk_gb(hh):
            def m1():
                st["gb%d" % hh] = proj(384 + 128 * hh, 128, 128)

            def m2():
                ps, pr = st["gb%d" % hh]
                k.op("act", lambda e: e.activation(out=t1, in_=ps, func=AF.Square), reads=[pr], writes=[t1_r])
                k.op("dve", lambda e: e.tensor_copy(out=gg[hh][par], in_=ps), reads=[pr, t1_r], writes=[gg_r[hh][par]])

            def m3():
                ps, pr = st["gb%d" % hh]
                k.op("dve", lambda e: e.tensor_scalar(out=t1, in0=t1, scalar1=0.044715, scalar2=1.0, op0=ALU.mult,
                                                      op1=ALU.add), reads=[t1_r], writes=[t1_r])
                k.op("dve", lambda e: e.tensor_tensor(out=t1, in0=t1, in1=ps, op=ALU.mult),
                     reads=[t1_r, pr], writes=[t1_r])

            def m4():
                k.op("act", lambda e: e.activation(out=t1, in_=t1, func=AF.Sigmoid, scale=1.5957691216057308),
                     reads=[t1_r], writes=[t1_r])

            def m5():
                k.op("dve", lambda e: e.tensor_tensor(out=gg[hh][par], in0=gg[hh][par], in1=t1, op=ALU.mult),
                     reads=[t1_r, gg_r[hh][par]], writes=[gg_r[hh][par]])
            return [m1, m2, m3, m4, m5]

        def mk_q(h):
            def m1():
                st["q%d" % h] = proj(640 + 65 * h, 65, 65)

            def m2():
                ps, pr = st["q%d" % h]
                k.op("dve", lambda e: e.tensor_copy(out=Qaug[h][par][0:64, :], in_=ps[0:64, :]),
                     reads=[pr], writes=[Q_r[h][par]])
                k.op("act", lambda e: e.activation(out=rowe[64:65, :], in_=ps[64:65, :], func=AF.Exp,
                                                   bias=nbf[64:65, h:h + 1], scale=-1.0),
                     reads=[pr, nbf_r], writes=[rowe_r])
                k.op("act", lambda e: e.activation(out=rowe[64:65, :], in_=rowe[64:65, :], func=AF.Ln,
                                                   bias=ones_f[64:65, 0:1], scale=1.0),
                     reads=[rowe_r, ones_f_r], writes=[rowe_r])

            def m3():
                init = 0.0 if gt == 0 else clast[64:65, h:h + 1]
                k.op("dve", lambda e: e.tensor_tensor_scan(
                    out=crow[h][64:65, :], data0=ones_f[64:65, 0:1].to_broadcast([1, TT]),
                    data1=rowe[64:65, :], initial=init, op0=ALU.mult, op1=ALU.subtract),
                    reads=[rowe_r, clast_r[h], ones_f_r], writes=[crow_r[h]])
                k.op("dve", lambda e: e.tensor_copy(out=clast[64:65, h:h + 1], in_=crow[h][64:65, TT - 1:TT]),
                     reads=[crow_r[h]], writes=[clast_r[h]])

            def m4():
                k.op("act", lambda e: e.activation(out=Qaug[h][par][64:65, :], in_=crow[h][64:65, :],
                                                   func=AF.Copy, scale=8.0),
                     reads=[crow_r[h]], writes=[Q_r[h][par]])

            def m5():
                pt_, ptr_ = gbank()
                st["n%d" % h] = (pt_, ptr_)
                for blk in range(4):
                    k.op("pe", lambda e, blk=blk: e.matmul(
                        pt_[:, blk:blk + 1], lhsT=crow[h][64:65, blk * 128:(blk + 1) * 128], rhs=ones_f[64:65, 0:1],
                        start=True, stop=True),
                        reads=[crow_r[h], ones_f_r], writes=[ptr_])

            def m6():
                pt_, ptr_ = st["n%d" % h]
                k.op("dve", lambda e: e.tensor_scalar(
                    out=negc[:, 4 * gt:4 * gt + 4, h], in0=pt_[:, 0:4], scalar1=-1.0, scalar2=None, op0=ALU.mult),
                    reads=[ptr_], writes=[negc_r[h][gt]])
            return [m1, m2, m3, m4, m5, m6]

        def mk_k(h):
            def m1():
                st["k%d" % h] = proj(770 + 64 * h, 64, 64)

            def m2():
                ps, pr = st["k%d" % h]
                k.op("act", lambda e: e.activation(out=Kaug[h][0:64, gt * TT:(gt + 1) * TT],
                                                   in_=ps[0:64, :], func=AF.Copy),
                     reads=[pr], writes=[K_r[h][gt]])
            return [m1, m2]

        def v1():
            ps, pr = gbank()
            st["v"] = (ps, pr)
            for blk in range(4):
                for kc in range(KC):
                    k.op("pe", lambda e, blk=blk, kc=kc: e.matmul(
                        ps[:, blk * 128:(blk + 1) * 128], lhsT=utile[:, kc, blk * 128:(blk + 1) * 128],
                        rhs=wB[:, kc, 898:1026], start=(kc == 0), stop=(kc == KC - 1)),
                        reads=[wB_r, utile_r], writes=[pr])

        def v2():
            ps, pr = st["v"]
            k.op("dve", lambda e: e.tensor_copy(
                out=Vaug[:, 4 * gt:4 * gt + 4, :, 0:64], in_=ps.rearrange("p (b h d) -> p b h d", b=4, h=2)),
                reads=[pr], writes=[V_r[gt]])
        C["xb0"], C["xb1"] = mk_xb(0), mk_xb(1)
        C["gb0"], C["gb1"] = mk_gb(0), mk_gb(1)
        C["q0"], C["q1"] = mk_q(0), mk_q(1)
        C["k0"], C["k1"] = mk_k(0), mk_k(1)
        C["v"] = [v1, v2]
        return C

    def S2chains(gt):
        r, tt = gt // 4, gt % 4
        par = gt % 2
        tsl = slice(tt * TT, (tt + 1) * TT)
        C = {}
        st = {}

        def pool1():
            ps, pr = gbank()
            st["pool"] = (ps, pr)
            k.op("pe", lambda e: e.matmul(ps, lhsT=wsB[:, 0:128], rhs=d_bf[par], start=True, stop=True),
                 reads=[wsB_r, d_r[par]], writes=[pr])

        def pool2():
            ps, pr = st["pool"]
            k.op("act", lambda e: e.activation(out=outA[par], in_=ps, func=AF.Identity,
                                               scale=vec[:, VB_PSCALE:VB_PSCALE + 1]),
                 reads=[pr, vec_r], writes=[outA_r[par]])
            for dap, ps_ in mdst("A", r, tsl):
                sr = Res(); store_res.append(sr)
                k.dma("sp", ds_o, out=dap, in_=outA[par][ps_, :], reads=[outA_r[par]], writes=[sr])
        C["pool"] = [pool1, pool2]

        def mk_lru(hh):
            def m1():
                ps, pr = gbank()
                k.op("pe", lambda e: e.matmul(ps, lhsT=wsB[:, 128 + 128 * hh:256 + 128 * hh],
                                              rhs=xc_bf[hh][par], start=True, stop=True),
                     reads=[wsB_r, xcb_r[hh][par]], writes=[pr])
                ps2, pr2 = gbank()
                k.op("pe", lambda e: e.matmul(ps2, lhsT=wsB[:, 384 + 128 * hh:512 + 128 * hh],
                                              rhs=xc_bf[hh][par], start=True, stop=True),
                     reads=[wsB_r, xcb_r[hh][par]], writes=[pr2])
                st["l%d" % hh] = (ps, pr, ps2, pr2)

            def m2():
                ps, pr, ps2, pr2 = st["l%d" % hh]
                k.op("act", lambda e: e.activation(out=ra, in_=ps, func=AF.Sigmoid,
                                                   bias=vec[:, VB_BA + hh:VB_BA + hh + 1], scale=1.0),
                     reads=[pr, vec_r], writes=[ra_r])
                k.op("act", lambda e: e.activation(out=ig, in_=ps2, func=AF.Sigmoid,
                                                   bias=vec[:, VB_BX + hh:VB_BX + hh + 1], scale=1.0),
                     reads=[pr2, vec_r], writes=[ig_r])
                k.op("act", lambda e: e.activation(out=ra, in_=ra, func=AF.Exp, scale=nsp8[:, hh:hh + 1]),
                     reads=[ra_r, nsp8_r], writes=[ra_r])

            def m3():
                k.op("dve", lambda e: e.tensor_tensor(out=tmp, in0=ra, in1=ra, op=ALU.mult),
                     reads=[ra_r], writes=[tmp_r])
                k.op("dve", lambda e: e.tensor_tensor(out=ig, in0=ig, in1=xc[hh][par], op=ALU.mult),
                     reads=[ig_r, xc_r[hh][par]], writes=[ig_r])

            def m4():
                k.op("act", lambda e: e.activation(out=tmp, in_=tmp, func=AF.Sqrt, bias=ones_f[:, 0:1], scale=-1.0),
                     reads=[tmp_r, ones_f_r], writes=[tmp_r])

            def m5():
                k.op("dve", lambda e: e.tensor_tensor(out=ig, in0=ig, in1=tmp, op=ALU.mult),
                     reads=[ig_r, tmp_r], writes=[ig_r])
                init = 0.0 if gt == 0 else hlast[:, hh:hh + 1]
                k.op("dve", lambda e: e.tensor_tensor_scan(out=hbuf, data0=ra, data1=ig, initial=init,
                                                           op0=ALU.mult, op1=ALU.add),
                     reads=[ra_r, ig_r, hlast_r[hh]], writes=[hbuf_r])
                k.op("dve", lambda e: e.tensor_copy(out=hlast[:, hh:hh + 1], in_=hbuf[:, TT - 1:TT]),
                     reads=[hbuf_r], writes=[hlast_r[hh]])
                k.op("dve", lambda e: e.tensor_tensor(out=outB[par][:, hh, :], in0=hbuf, in1=gg[hh][par], op=ALU.mult),
                     reads=[hbuf_r, gg_r[hh][par]], writes=[outB_r[par]])
                if hh == 1:
                    for dap, ps_, hx in mdst("B", r, tsl):
                        sr = Res(); store_res.append(sr)
                        k.dma("sp", ds_o, out=dap, in_=outB[par][ps_, hx, :], reads=[outB_r[par]], writes=[sr])
            return [m1, m2, m3, m4, m5]
        C["lru0"], C["lru1"] = mk_lru(0), mk_lru(1)
        return C

    ORDER = [("load", 0), ("pool", 0), ("q0", 0), ("pool", 1), ("q0", 1), ("lru0", 0), ("q0", 2), ("lru0", 1),
             ("q0", 3), ("q1", 0), ("lru0", 2), ("q1", 1), ("lru0", 3), ("q1", 2), ("lru0", 4), ("q1", 3),
             ("lru1", 0), ("q0", 4), ("lru1", 1), ("q0", 5), ("lru1", 2), ("q1", 4), ("lru1", 3), ("q1", 5),
             ("lru1", 4), ("xa", 0), ("xb0", 0), ("xa", 1), ("xb0", 1), ("xb1", 0), ("gb0", 0), ("xb1", 1),
             ("gb0", 1), ("k0", 0), ("gb0", 2), ("k0", 1), ("gb0", 3), ("k1", 0), ("gb0", 4), ("k1", 1),
             ("gb1", 0), ("v", 0), ("gb1", 1), ("v", 1), ("gb1", 2), ("gb1", 3), ("gb1", 4)]

    def flat(c2, c1):
        out = []
        for name, i in ORDER:
            src_ = c2 if name in ("pool", "lru0", "lru1") else c1
            if src_ is not None:
                out.append(src_[name][i])
        return out

    def S2(gt, steps):
        r, tt = gt // 4, gt % 4
        par = gt % 2
        tsl = slice(tt * TT, (tt + 1) * TT)
        nkb = 4 * gt + 4

        def attn(h):
            oi = OB[(2 * gt + h) % 2]
            ob, obr = P.psum[:, oi, :], P.pres[oi]
            sbs = {}

            def emit_S(kb):
                dd = kb - 4 * gt
                q0 = 128 * dd if dd > 0 else 0
                sb, sbr = sbank()
                sbs[kb] = (sb, sbr, q0, dd)
                k.op("pe", lambda e, sb=sb, q0=q0, kb=kb: e.matmul(
                    sb[:, q0:TT], lhsT=Kaug[h][0:65, kb * 128:(kb + 1) * 128], rhs=Qaug[h][par][0:65, q0:TT],
                    start=True, stop=True),
                    reads=[K_r[h][kb // 4], Q_r[h][par]], writes=[sbr])

            emit_S(0)
            if nkb > 1:
                emit_S(1)
            for kb in range(nkb):
                if kb + 2 < nkb:
                    emit_S(kb + 2)
                sb, sbr, q0, dd = sbs.pop(kb)
                pi = kb % 3
                k.op("act", lambda e, sb=sb, q0=q0, kb=kb, pi=pi: e.activation(
                    out=Pt[pi][:, q0:TT], in_=sb[:, q0:TT], func=AF.Exp, bias=negc[:, kb, h:h + 1], scale=0.125),
                    reads=[sbr, negc_r[h][kb // 4]], writes=[Pt_r[pi]])
                if dd >= 0:
                    k.op("dve", lambda e, q0=q0, pi=pi: e.tensor_tensor(
                        out=Pt[pi][:, q0:q0 + 128], in0=Pt[pi][:, q0:q0 + 128], in1=mask, op=ALU.min),
                        reads=[Pt_r[pi], mask_r], writes=[Pt_r[pi]])
                k.op("pe", lambda e, q0=q0, kb=kb, pi=pi: e.matmul(
                    ob[0:65, q0:TT], lhsT=Vaug[:, kb, h, :], rhs=Pt[pi][:, q0:TT],
                    start=(kb == 0), stop=(kb == nkb - 1)),
                    reads=[V_r[kb // 4], Pt_r[pi]], writes=[obr])
                if steps:
                    steps.pop(0)()
            k.op("dve", lambda e: e.reciprocal(out=rec[64:65, :], in_=ob[64:65, :]), reads=[obr], writes=[rec_r])
            bc, bcr = sbank()
            k.op("pe", lambda e, bc=bc: e.matmul(bc[0:64, :], lhsT=ones_f[64:65, 0:64], rhs=rec[64:65, :],
                                                 start=True, stop=True),
                 reads=[rec_r, ones_f_r], writes=[bcr])
            k.op("act", lambda e: e.activation(out=osb[0:64, :], in_=ob[0:64, :], func=AF.Copy),
                 reads=[obr], writes=[osb_r])
            k.op("dve", lambda e, bc=bc: e.tensor_tensor(out=outC[h][par][0:64, :], in0=osb[0:64, :], in1=bc[0:64, :],
                                                         op=ALU.mult),
                 reads=[osb_r, bcr], writes=[outC_r[h][par]])
            for dap, ps_ in mdst("C", r, tsl, h):
                sr = Res(); store_res.append(sr)
                k.dma("sp", ds_o, out=dap, in_=outC[h][par][ps_, :], reads=[outC_r[h][par]], writes=[sr])
        for h in range(2):
            attn(h)
        while steps:
            steps.pop(0)()
        if tile_done is not None:
            tile_done(gt, store_res)

    for s_ in flat(None, S1(0)):
        s_()
    for gt in range(NGT):
        S2(gt, flat(S2chains(gt), S1(gt + 1) if gt + 1 < NGT else None))
    ar.release()


def pack_wB(w_in_l, j):
    cols = list(range(128 * j, 128 * j + 128))
    cols += list(range(512 + 256 * j, 512 + 256 * j + 256))
    cols += list(range(1536 + 256 * j, 1536 + 256 * j + 256))
    for h in range(2):
        hd = 2 * j + h
        cols += list(range(2560 + 64 * hd, 2560 + 64 * hd + 64)) + [4096 + hd]
    cols += list(range(3072 + 128 * j, 3072 + 128 * j + 128))
    cols += list(range(3584 + 128 * j, 3584 + 128 * j + 128))
    w = w_in_l[:, cols]
    return np.ascontiguousarray(w.reshape(KC, 128, NCB).transpose(1, 0, 2))


def pack_wsB(inp, l, j):
    parts = [inp["w_pool"][l, j]]
    parts += [inp["w_rg_a"][l, 2 * j + hh] for hh in range(2)]
    parts += [inp["w_rg_x"][l, 2 * j + hh] for hh in range(2)]
    return np.ascontiguousarray(np.concatenate(parts, axis=1))


def pack_vecB(inp, l, j):
    v = np.zeros((128, NVB), np.float32)
    v[:, VB_PSCALE] = inp["pool_scale"][l, 128 * j:128 * j + 128]
    for hh in range(2):
        sl = slice(256 * j + 128 * hh, 256 * j + 128 * hh + 128)
        for kk in range(4):
            v[:, VB_CONVW + 4 * hh + kk] = inp["conv_w"][l, kk, sl]
        v[:, VB_CONVB + hh] = inp["conv_b"][l, sl]
        v[:, VB_BA + hh] = inp["b_rg_a"][l, sl]
        v[:, VB_BX + hh] = inp["b_rg_x"][l, sl]
        v[:, VB_LAM + hh] = inp["lru_lambda"][l, sl]
        v[:, VB_BF + hh] = inp["b_f"][l, 2 * j + hh]
    w = 2 ** (j + 1)
    v[:, VB_SEL + j] = np.float32(1.0 / w)
    for t in range(16):
        v[:, VB_CORR + t] = np.float32(w / min(t + 1, w))
    return v


def const_mask():
    jj = np.arange(128)[:, None]
    tt = np.arange(128)[None, :]
    return np.where(jj <= tt, np.float32(3.0e38), np.float32(0.0)).astype(np.float32)


def emit_phaseC(P, cst, x, xres, umine, mix_all_d, wC_d, bgate_d, wo_d):
    k, ar = P.k, P.ar
    ar.mark()
    u = ar.alloc([KC, T], BF16); u_r = [Res() for _ in range(NT)]
    mixs = ar.alloc([16, 2 * TT], BF16); mixs_r = [Res() for _ in range(16)]
    merged = ar.alloc([KC, 2 * TT], BF16); mg_r = [[Res(), Res()] for _ in range(KC)]
    wcb = [ar.alloc([5120], BF16) for _ in range(3)]; wcb_r = [Res(), Res(), Res()]
    wob = [ar.alloc([1024], BF16) for _ in range(2)]; wob_r = [Res(), Res()]
    gate = [ar.alloc([TT], F32) for _ in range(3)]; gate_r = [Res(), Res(), Res()]
    acc = [ar.alloc([TT], F32) for _ in range(2)]; acc_r = [Res(), Res()]
    bg = ar.alloc([24], F32); bg_r = Res()
    ds_u = k.dsem("cu")
    for tt in range(NT):
        k.dma("sp", ds_u, out=u[:, :, tt * TT:(tt + 1) * TT], in_=umine(tt), writes=[u_r[tt]])
    k.dma("sp", k.dsem("cbg"), out=bg, in_=bgate_d, writes=[bg_r])
    dsm = k.dsem("cmix")
    dsw = [None] * 3
    dso = [None] * 2
    cw = 0
    co = 0
    for st in range(2):
        cs = slice(st * 2 * TT, (st + 1) * 2 * TT)
        for r in range(4):
            if callable(mix_all_d):
                srcs = [(r, mix_all_d(r, 0, cs)), (4 + 2 * r, mix_all_d(r, 1, cs)), (5 + 2 * r, mix_all_d(r, 2, cs)),
                        (12 + r, mix_all_d(r, 3, cs))]
            else:
                srcs = [(r, mix_all_d[r, 0:128, cs]), (4 + 2 * r, mix_all_d[r, 128:256, cs]),
                        (5 + 2 * r, mix_all_d[r, 256:384, cs]), (12 + r, mix_all_d[r, 384:512, cs])]
            for m_, s_ in srcs:
                k.dma("sp", dsm, out=mixs[:, m_, :], in_=s_, writes=[mixs_r[m_]])
        for dc in range(KC):
            s = cw % 3
            cw += 1
            k.dma("pool", dsw[s], out=wcb[s], in_=wC_d[dc], writes=[wcb_r[s]], max_dma_last_dim=1024 * 4)
            for t2 in range(2):
                tt = st * 2 + t2
                sl = slice(tt * TT, (tt + 1) * TT)
                sl2 = slice(t2 * TT, (t2 + 1) * TT)
                for br in range(3):
                    ps, pr = P.bank()
                    for kc in range(KC):
                        c0 = 2048 + kc * 384 + br * 128
                        k.op("pe", lambda e, ps=ps, s=s, kc=kc, c0=c0, sl=sl: e.matmul(
                            ps, lhsT=wcb[s][:, c0:c0 + 128], rhs=u[:, kc, sl], start=(kc == 0), stop=(kc == KC - 1)),
                            reads=[wcb_r[s], u_r[tt]], writes=[pr])
                    k.op("act", lambda e, ps=ps, br=br, dc=dc: e.activation(
                        out=gate[br], in_=ps, func=AF.Sigmoid, bias=bg[:, br * 8 + dc:br * 8 + dc + 1], scale=1.0),
                        reads=[pr, bg_r], writes=[gate_r[br]])
                ys = []
                for (m0, m1) in ((0, 4), (4, 12), (12, 16)):
                    ps, pr = P.bank()
                    for m in range(m0, m1):
                        k.op("pe", lambda e, ps=ps, s=s, m=m, sl2=sl2, m0=m0, m1=m1: e.matmul(
                            ps, lhsT=wcb[s][:, m * 128:(m + 1) * 128], rhs=mixs[:, m, sl2],
                            start=(m == m0), stop=(m == m1 - 1)),
                            reads=[wcb_r[s], mixs_r[m]], writes=[pr])
                    ys.append((ps, pr))
                k.op("dve", lambda e, ys=ys: e.tensor_tensor(out=acc[0], in0=ys[0][0], in1=gate[0], op=ALU.mult),
                     reads=[ys[0][1], gate_r[0]], writes=[acc_r[0]])
                k.op("dve", lambda e, ys=ys: e.tensor_tensor(out=acc[1], in0=ys[1][0], in1=gate[1], op=ALU.mult),
                     reads=[ys[1][1], gate_r[1]], writes=[acc_r[1]])
                k.op("dve", lambda e: e.tensor_tensor(out=acc[0], in0=acc[0], in1=acc[1], op=ALU.add),
                     reads=[acc_r[0], acc_r[1]], writes=[acc_r[0]])
                k.op("dve", lambda e, ys=ys: e.tensor_tensor(out=acc[1], in0=ys[2][0], in1=gate[2], op=ALU.mult),
                     reads=[ys[2][1], gate_r[2]], writes=[acc_r[1]])
                k.op("dve", lambda e, dc=dc, sl2=sl2: e.tensor_tensor(out=merged[:, dc, sl2], in0=acc[0], in1=acc[1],
                                                                      op=ALU.add),
                     reads=[acc_r[0], acc_r[1]], writes=[mg_r[dc][t2]])
        for dco in range(KC):
            s = co % 2
            co += 1
            k.dma("pool", dso[s], out=wob[s], in_=wo_d[dco], writes=[wob_r[s]])
            for t2 in range(2):
                tt = st * 2 + t2
                sl = slice(tt * TT, (tt + 1) * TT)
                sl2 = slice(t2 * TT, (t2 + 1) * TT)
                ps, pr = P.bank()
                for dc in range(KC):
                    k.op("pe", lambda e, ps=ps, s=s, dc=dc, sl2=sl2: e.matmul(
                        ps, lhsT=wob[s][:, dc * 128:(dc + 1) * 128], rhs=merged[:, dc, sl2],
                        start=(dc == 0), stop=(dc == KC - 1)),
                        reads=[wob_r[s], mg_r[dc][t2]], writes=[pr])
                k.op("dve", lambda e, ps=ps, dco=dco, sl=sl: e.tensor_tensor(out=x[:, dco, sl], in0=ps, in1=x[:, dco, sl],
                                                                             op=ALU.add),
                     reads=[pr, xres[tt]], writes=[xres[tt]])
    ar.release()


def emit_cross(P, cst, x, xres, memT_d, gmem, gmem_r, gcross, gcross_r, wxq_d, wxkv_d, wxo_d):
    k, ar = P.k, P.ar
    ar.mark()
    NM = 256
    xn = ar.alloc([KC, T], BF16); xn_r = [Res() for _ in range(NT)]
    QT = ar.alloc([KC, T], BF16); QT_r = [[Res() for _ in range(NT)] for _ in range(KC)]
    memf = ar.alloc([KC, NM], F32); memf_r = [Res()]
    memn = ar.alloc([KC, NM], BF16); memn_r = Res()
    KxT = ar.alloc([KC, NM], BF16); Kx_r = [Res() for _ in range(KC)]
    Vx = ar.alloc([2, D], BF16); Vx_r = [Res() for _ in range(KC)]
    wch = [ar.alloc([1024], BF16) for _ in range(2)]; wch_r = [Res(), Res()]
    Pt = [[ar.alloc([TT], BF16) for _ in range(2)] for _ in range(2)]; Pt_r = [[Res(), Res()] for _ in range(2)]
    rec = ar.alloc([TT], F32); rec_r = Res()
    nt = alloc_normtmp(P)
    dsw = [k.dsem("xw0"), k.dsem("xw1")]
    wc = [0]

    def wload(src):
        s = wc[0] % 2
        wc[0] += 1
        k.dma("pool", dsw[s], out=wch[s], in_=src, writes=[wch_r[s]])
        return s

    k.dma("sp", k.dsem("xmem"), out=memf, in_=memT_d, writes=[memf_r[0]])
    emit_rmsnorm(P, cst, nt, memf, memf_r, gmem, gmem_r, lambda tt: (memn, memn_r), ntiles=1, tw=NM)
    for ck in range(16):
        s = wload(wxkv_d[ck])
        ps, pr = P.bank()
        if ck < 8:
            for kc in range(KC):
                k.op("pe", lambda e, ps=ps, s=s, kc=kc: e.matmul(
                    ps[:, 0:NM], lhsT=wch[s][:, kc * 128:(kc + 1) * 128], rhs=memn[:, kc, :],
                    start=(kc == 0), stop=(kc == KC - 1)),
                    reads=[wch_r[s], memn_r], writes=[pr])
            k.op("act", lambda e, ps=ps, ck=ck: e.activation(out=KxT[:, ck, :], in_=ps[:, 0:NM], func=AF.Copy),
                 reads=[pr], writes=[Kx_r[ck]])
        else:
            c = ck - 8
            for mb in range(2):
                for kc in range(KC):
                    k.op("pe", lambda e, ps=ps, s=s, kc=kc, mb=mb: e.matmul(
                        ps[:, mb * 128:(mb + 1) * 128], lhsT=memn[:, kc, mb * 128:(mb + 1) * 128],
                        rhs=wch[s][:, kc * 128:(kc + 1) * 128], start=(kc == 0), stop=(kc == KC - 1)),
                        reads=[wch_r[s], memn_r], writes=[pr])
            k.op("act", lambda e, ps=ps, c=c: e.activation(
                out=Vx[:, :, c * 128:(c + 1) * 128], in_=ps[:, 0:256].rearrange("p (m j) -> p m j", m=2), func=AF.Copy),
                reads=[pr], writes=[Vx_r[c]])
    emit_rmsnorm(P, cst, nt, x, xres, gcross, gcross_r, lambda tt: (xn[:, :, tt * TT:(tt + 1) * TT], xn_r[tt]))
    for ck in range(KC):
        s = wload(wxq_d[ck])
        for tt in range(NT):
            sl = slice(tt * TT, (tt + 1) * TT)
            ps, pr = P.bank()
            for kc in range(KC):
                k.op("pe", lambda e, ps=ps, s=s, kc=kc, sl=sl: e.matmul(
                    ps, lhsT=wch[s][:, kc * 128:(kc + 1) * 128], rhs=xn[:, kc, sl],
                    start=(kc == 0), stop=(kc == KC - 1)),
                    reads=[wch_r[s], xn_r[tt]], writes=[pr])
            if tt % 2 == 0:
                k.op("act", lambda e, ps=ps, ck=ck, sl=sl: e.activation(out=QT[:, ck, sl], in_=ps, func=AF.Copy),
                     reads=[pr], writes=[QT_r[ck][tt]])
            else:
                k.op("dve", lambda e, ps=ps, ck=ck, sl=sl: e.tensor_copy(out=QT[:, ck, sl], in_=ps),
                     reads=[pr], writes=[QT_r[ck][tt]])
    cnt = 0
    for tt in range(NT):
        sl = slice(tt * TT, (tt + 1) * TT)
        for h in range(4):
            pp = cnt % 2
            cnt += 1
            for mb in range(2):
                ps, pr = P.bank()
                for half in range(2):
                    ck = 2 * h + half
                    k.op("pe", lambda e, ps=ps, ck=ck, mb=mb, sl=sl, half=half: e.matmul(
                        ps, lhsT=KxT[:, ck, mb * 128:(mb + 1) * 128], rhs=QT[:, ck, sl],
                        start=(half == 0), stop=(half == 1)),
                        reads=[Kx_r[ck], QT_r[ck][tt]], writes=[pr])
                k.op("act", lambda e, ps=ps, pp=pp, mb=mb: e.activation(out=Pt[pp][mb], in_=ps, func=AF.Exp,
                                                                       scale=0.0625),
                     reads=[pr], writes=[Pt_r[pp][mb]])
            ps, pr = P.bank()
            for mb in range(2):
                k.op("pe", lambda e, ps=ps, pp=pp, mb=mb: e.matmul(ps, lhsT=cst["ones_bf"], rhs=Pt[pp][mb],
                                                                   start=(mb == 0), stop=(mb == 1)),
                     reads=[cst["ones_bf_r"], Pt_r[pp][mb]], writes=[pr])
            k.op("dve", lambda e, ps=ps: e.reciprocal(out=rec, in_=ps), reads=[pr], writes=[rec_r])
            for half in range(2):
                ck = 2 * h + half
                ps, pr = P.bank()
                for mb in range(2):
                    k.op("pe", lambda e, ps=ps, pp=pp, mb=mb, ck=ck: e.matmul(
                        ps, lhsT=Vx[:, mb, ck * 128:(ck + 1) * 128], rhs=Pt[pp][mb], start=(mb == 0), stop=(mb == 1)),
                        reads=[Vx_r[ck], Pt_r[pp][mb]], writes=[pr])
                k.op("dve", lambda e, ps=ps, ck=ck, sl=sl: e.tensor_tensor(out=QT[:, ck, sl], in0=ps, in1=rec, op=ALU.mult),
                     reads=[pr, rec_r], writes=[QT_r[ck][tt]])
    for dco in range(KC):
        s = wload(wxo_d[dco])
        for tt in range(NT):
            sl = slice(tt * TT, (tt + 1) * TT)
            ps, pr = P.bank()
            for ck in range(KC):
                k.op("pe", lambda e, ps=ps, s=s, ck=ck, sl=sl: e.matmul(
                    ps, lhsT=wch[s][:, ck * 128:(ck + 1) * 128], rhs=QT[:, ck, sl],
                    start=(ck == 0), stop=(ck == KC - 1)),
                    reads=[wch_r[s], QT_r[ck][tt]], writes=[pr])
            k.op("dve", lambda e, ps=ps, dco=dco, sl=sl: e.tensor_tensor(out=x[:, dco, sl], in0=ps, in1=x[:, dco, sl],
                                                                         op=ALU.add),
                 reads=[pr, xres[tt]], writes=[xres[tt]])
    ar.release()


def pack_wC(inp, l):
    ua = inp["w_up_a"][l].reshape(4, 128, KC, 128)
    ub = inp["w_up_b"][l].reshape(8, 128, KC, 128)
    uc = inp["w_up_c"][l].reshape(4, 128, KC, 128)
    up = np.concatenate([ua, ub, uc], axis=0)
    up = up.transpose(2, 1, 0, 3).reshape(KC, 128, 2048)
    wg = inp["w_in"][l][:, 4104:].reshape(KC, 128, 3, KC, 128)
    wg = wg.transpose(3, 1, 0, 2, 4).reshape(KC, 128, 3072)
    return np.ascontiguousarray(np.concatenate([up, wg], axis=2))


def pack_bgate(inp, l):
    return np.ascontiguousarray(inp["b_gate"][l].reshape(3, KC, 128).transpose(2, 0, 1).reshape(128, 24))


def pack_sq(w):
    C = w.shape[1] // 128
    a = w.reshape(KC, 128, C, 128)
    return np.ascontiguousarray(a.transpose(2, 1, 0, 3)).reshape(C, 128, KC * 128)


def pack_memT(m):
    return np.ascontiguousarray(m.reshape(256, KC, 128).transpose(2, 1, 0))


ARENA_BYTES = 205 * 1024
RGROUPS = [[0, 1, 2, 3], [4, 5, 6, 7]]


def build(stages, fused=False):
    nc = bass.Bass("TRN2", target_bir_lowering=False)
    dram = {}

    def dten(name, shape, dt, kind):
        if name not in dram:
            dram[name] = nc.dram_tensor(name, list(shape), dt, kind=kind).ap()
        return dram[name]

    def din(name, shape, dt=F32):
        return dten(name, shape, dt, "ExternalInput")

    def dout(name, shape, dt):
        return dten(name, shape, dt, "ExternalOutput")

    def dint(name, shape, dt):
        return dten(name, shape, dt, "Internal")

    with ExitStack() as stack:
        arena_t = stack.enter_context(nc.sbuf_tensor("arena", [128, ARENA_BYTES // 4], F32))
        psum_t = stack.enter_context(nc.psum_tensor("psum", [128, 8, 512], F32))
        k = K(nc, stack)
        ar = Arena(arena_t[:, :], ARENA_BYTES)
        P = Prog(nc, k, ar, psum_t[:, :, :])
        cst = make_consts(P, din("cmask", [128, 128]))
        x = ar.alloc([KC, T], F32)
        xres = [Res() for _ in range(NT)]
        gvs = ar.alloc([16, KC], F32)
        gv_n = [0]
        pid_cache = {}
        gath_res = {}
        uall_res = {}
        ds_g = k.dsem("gv")

        def load_g(name):
            i = gv_n[0] % 16
            gv_n[0] += 1
            r = Res()
            k.dma("sp", ds_g, out=gvs[:, i, :], in_=din(name, [128, KC]), writes=[r])
            return gvs[:, i, :], r

        def sfx(l):
            return "" if l is None else "_%d" % l

        def slab_copies(d, q="sp"):
            for half in range(2):
                gq = dint("mix_gath_%d_%d" % (d, half), [4 * 256, T], BF16)
                for r in range(4):
                    mb = dint("mix_big_%d_%d" % (r, half), [4 * 256, T], BF16)

                    def dstf(e, d=d, mb=mb, q=q):
                        if q not in pid_cache:
                            pid_cache[q] = (nc.sync if q == "sp" else nc.scalar).partition_id() % 4
                        return mb[bass.ds(((d + 4 - pid_cache[q]) % 4) * 256, 256), :]
                    k.dma(q, None, out=dstf, in_=gq[r * 256:(r + 1) * 256, :], reads=[gath_res[(d, half)]])

        for stg in stages:
            k.barrier(skip_cc=(fused and stg[0] in ("ag_u", "B")))
            if stg == "load_x":
                xd = din("xT", [128, KC, T])
                for tt in range(NT):
                    sl = slice(tt * TT, (tt + 1) * TT)
                    k.dma("sp", k.dsem("x"), out=x[:, :, sl], in_=xd[:, :, sl], writes=[xres[tt]])
            elif stg == "store_x":
                xo = dout("xT_out", [128, KC, T], F32)
                for tt in range(NT):
                    sl = slice(tt * TT, (tt + 1) * TT)
                    k.dma("sp", k.dsem("xo"), out=xo[:, :, sl], in_=x[:, :, sl], reads=[xres[tt]])
            elif stg[0] == "ffn":
                _, l, i = stg
                ar.mark()
                gv, gr = load_g("gffn%s_%d" % (sfx(l), i))
                xn = ar.alloc([KC, T], BF16); xn_r = [Res() for _ in range(NT)]
                nt = alloc_normtmp(P)
                w1buf = [ar.alloc([KC * 256], BF16) for _ in range(3)]; w1res = [Res(), Res(), Res()]
                w2buf = [ar.alloc([NF * 128], BF16) for _ in range(3)]; w2res = [Res(), Res(), Res()]
                g = ar.alloc([NF, 2 * TT], BF16); gres2 = [[Res(), Res()] for _ in range(NF)]
                stmp = [ar.alloc([TT], F32) for _ in range(2)]; stres = [Res(), Res()]
                emit_rmsnorm(P, cst, nt, x, xres, gv, gr, lambda tt: (xn[:, :, tt * TT:(tt + 1) * TT], xn_r[tt]))
                emit_ffn(P, x, xres, xn, xn_r, din("w1%s_%d" % (sfx(l), i), [NF, 128, KC * 256]),
                         din("w2%s_%d" % (sfx(l), i), [KC, 128, NF * 128]),
                         (w1buf, w1res, w2buf, w2res, g, gres2, stmp, stres), "f")
                ar.release()
            elif stg[0] == "unorm":
                _, l = stg
                ar.mark()
                gv, gr = load_g("gmix%s" % sfx(l))
                xn = ar.alloc([KC, T], BF16); xn_r = [Res() for _ in range(NT)]
                nt = alloc_normtmp(P)
                dsu = k.dsem("ust")
                if fused:
                    def after(tt, o, ores):
                        um = dint("u_mine_%d" % tt, [128, KC * TT], BF16)
                        ua = dint("u_all_%d" % tt, [4 * 128, KC * TT], BF16)
                        ur = Res()
                        k.dma("sp", dsu, out=um.rearrange("p (k t) -> p k t", k=KC), in_=o, reads=[ores], writes=[ur])
                        uall_res[tt] = Res()
                        k.coll("AllGather", RGROUPS, ins=[um], outs=[ua], reads=[ur], writes=[uall_res[tt]])
                else:
                    ud = dout("u_mine%s" % sfx(l), [128, KC, T], BF16)

                    def after(tt, o, ores, ud=ud, dsu=dsu):
                        sl = slice(tt * TT, (tt + 1) * TT)
                        k.dma("sp", dsu, out=ud[:, :, sl], in_=o, reads=[ores], writes=[])

                emit_rmsnorm(P, cst, nt, x, xres, gv, gr, lambda tt: (xn[:, :, tt * TT:(tt + 1) * TT], xn_r[tt]),
                             after=after)
                ar.release()
            elif stg[0] == "B":
                _, l = stg
                if fused:
                    def usrc(r, tt):
                        ua = dint("u_all_%d" % tt, [4 * 128, KC * TT], BF16)
                        return ua[r * 128:(r + 1) * 128, :].rearrange("p (k t) -> p k t", k=KC)

                    def mm(d, half):
                        return dint("mix_mine_%d_%d" % (d, half), [256, T], BF16)

                    def mdst(kind, r, tsl, h=0):
                        if kind == "A":
                            return [(mm(r, 0)[0:128, tsl], slice(0, 128))]
                        if kind == "B":
                            return [(mm(r, 0)[128:256, tsl], slice(0, 128), 0), (mm(r, 1)[0:128, tsl], slice(0, 128), 1)]
                        return [(mm(r, 1)[128 + 64 * h:192 + 64 * h, tsl], slice(0, 64))]

                    def tile_done(gt, store_res):
                        if gt % 4 == 3:
                            d = gt // 4
                            for half in range(2):
                                gq = dint("mix_gath_%d_%d" % (d, half), [4 * 256, T], BF16)
                                gres = Res()
                                gath_res[(d, half)] = gres
                                k.coll("AllGather", RGROUPS, ins=[mm(d, half)], outs=[gq], reads=list(store_res),
                                       writes=[gres])
                            del store_res[:]
                        if gt % 4 == 1 and gt >= 5:
                            slab_copies(gt // 4 - 1, "act")
                else:
                    u_all = din("u_all%s" % sfx(l), [4, 128, KC, T], BF16)
                    mix = dout("mix_mine%s" % sfx(l), [4, 512, T], BF16)

                    def usrc(r, tt, u_all=u_all):
                        return u_all[r, :, :, tt * TT:(tt + 1) * TT]

                    def mdst(kind, r, tsl, h=0, mix=mix):
                        if kind == "A":
                            return [(mix[r, 0:128, tsl], slice(0, 128))]
                        if kind == "B":
                            return [(mix[r, 128 + 128 * hh:256 + 128 * hh, tsl], slice(0, 128), hh) for hh in range(2)]
                        return [(mix[r, 384 + 64 * h:448 + 64 * h, tsl], slice(0, 64))]
                emit_phaseB(P, cst, usrc, din("wB%s" % sfx(l), [128, KC, NCB]), din("wsB%s" % sfx(l), [128, 640]),
                            din("vecB%s" % sfx(l), [128, NVB]), mdst, tile_done if fused else None,
                            (lambda tt: [uall_res[tt]]) if fused else None)
            elif stg[0] == "C":
                _, l = stg
                if fused:
                    def umine(tt):
                        return dint("u_mine_%d" % tt, [128, KC * TT], BF16).rearrange("p (k t) -> p k t", k=KC)

                    def mix_all(r, piece, cs):
                        mb = dint("mix_big_%d_%d" % (r, piece // 2), [4 * 256, T], BF16)
                        return mb[128 * (piece % 2):128 * (piece % 2) + 128, cs]
                else:
                    ud = din("u_mine%s" % sfx(l), [128, KC, T], BF16)
                    mix_all = din("mix_all%s" % sfx(l), [4, 512, T], BF16)

                    def umine(tt, ud=ud):
                        return ud[:, :, tt * TT:(tt + 1) * TT]
                emit_phaseC(P, cst, x, xres, umine, mix_all, din("wC%s" % sfx(l), [KC, 128, 5120]),
                            din("bgate%s" % sfx(l), [128, 24]), din("wo%s" % sfx(l), [KC, 128, 1024]))
            elif stg[0] == "cross":
                _, l = stg
                gm, gmr = load_g("gmem%s" % sfx(l))
                gc, gcr = load_g("gcross%s" % sfx(l))
                emit_cross(P, cst, x, xres, din("memT", [128, KC, 256]), gm, gmr, gc, gcr,
                           din("wxq%s" % sfx(l), [KC, 128, 1024]), din("wxkv%s" % sfx(l), [16, 128, 1024]),
                           din("wxo%s" % sfx(l), [KC, 128, 1024]))
            elif stg[0] == "ag_u":
                pass
            elif stg[0] == "ag_mix":
                slab_copies(3)
            elif stg == "final":
                ar.mark()
                gv, gr = load_g("gfin")
                nt = alloc_normtmp(P)
                stg_t = [ar.alloc([KC, TT], F32) for _ in range(2)]; stg_r = [Res(), Res()]
                od = dout("outT", [128, KC, T], F32)
                dso = k.dsem("out")

                def after(tt, o, ores, od=od, dso=dso):
                    sl = slice(tt * TT, (tt + 1) * TT)
                    k.dma("sp", dso, out=od[:, :, sl], in_=o, reads=[ores], writes=[])

                emit_rmsnorm(P, cst, nt, x, xres, gv, gr, lambda tt: (stg_t[tt % 2], stg_r[tt % 2]), after=after)
                ar.release()
            else:
                raise ValueError(stg)
        k.barrier()
        k.final_wait("sp")
        k.emit()
    return nc


_PROGS = {}


def get_prog(key, stages, fused=False):
    if key not in _PROGS:
        _PROGS[key] = build(stages, fused)
    return _PROGS[key]


def _run(nc, maps):
    res = run_bass_kernel_spmd(nc, maps, core_ids=list(range(NCORES)))
    return res.results


def _weights_ffn(inp, l, i, names=None):
    sfx = "_%d_%d" % (l, i)
    if i == 1:
        w_in, w_out, g = inp["w_ffn1_in"], inp["w_ffn1_out"], inp["g_ffn1"]
    else:
        w_in, w_out, g = inp["w_ffn2_in"], inp["w_ffn2_out"], inp["g_ffn2"]
    return {"w1" + sfx: pack_w1(w_in[l]), "w2" + sfx: pack_w2(w_out[l]), "gffn" + sfx: pack_vec(g[l])}


def _weights_C(inp, l):
    s = "_%d" % l
    return {"wC" + s: pack_wC(inp, l), "bgate" + s: pack_bgate(inp, l), "wo" + s: pack_sq(inp["w_o"][l]),
            "gmem" + s: pack_vec(inp["g_mem"][l]), "gcross" + s: pack_vec(inp["g_cross"][l]),
            "wxq" + s: pack_sq(inp["w_xq"][l]), "wxkv" + s: pack_sq(inp["w_xkv"][l]), "wxo" + s: pack_sq(inp["w_xo"][l])}


FUSED = True


def fused_stages():
    st = ["load_x"]
    for l in range(L):
        st += [("ffn", l, 1), ("unorm", l), ("ag_u", l), ("B", l), ("ag_mix", l), ("C", l), ("cross", l), ("ffn", l, 2)]
    st.append("final")
    return st


def kernel_fused(inp):
    cmask = const_mask()
    xs = inp["x"].reshape(NCORES, T, D)
    w = {"cmask": cmask, "gfin": pack_vec(inp["g_final"])}
    wBs = {}
    for l in range(L):
        w.update(_weights_ffn(inp, l, 1))
        w.update(_weights_ffn(inp, l, 2))
        w.update(_weights_C(inp, l))
        w["gmix_%d" % l] = pack_vec(inp["g_mix"][l])
        for j in range(4):
            wBs[(l, j)] = {"wB_%d" % l: pack_wB(inp["w_in"][l], j), "wsB_%d" % l: pack_wsB(inp, l, j),
                           "vecB_%d" % l: pack_vecB(inp, l, j)}
    memT = [pack_memT(inp["mem"][b]) for b in range(2)]
    maps = []
    for c in range(NCORES):
        b, j = c // 4, c % 4
        m = dict(w)
        for l in range(L):
            m.update(wBs[(l, j)])
        m["xT"] = pack_xT(xs[c])
        m["memT"] = memT[b]
        maps.append(m)
    prog = get_prog("fused", fused_stages(), fused=True)
    r = _run(prog, maps)
    out = np.stack([unpack_xT(r[c]["outT"]) for c in range(NCORES)], axis=0)
    return out.reshape(2, 8192, D).astype(np.float32)


def kernel(**inp):
    inp = {k_: np.asarray(v) for k_, v in inp.items()}
    if FUSED:
        return kernel_fused(inp)
    cmask = const_mask()
    xs = inp["x"].reshape(NCORES, T, D)
    p1 = get_prog("p1", ["load_x", ("ffn", 0, 1), ("unorm", 0), "store_x"])
    w = {"cmask": cmask, "gmix_0": pack_vec(inp["g_mix"][0])}
    w.update(_weights_ffn(inp, 0, 1))
    r = _run(p1, [dict(w, xT=pack_xT(xs[c])) for c in range(NCORES)])
    xT = [r[c]["xT_out"] for c in range(NCORES)]
    um = [r[c]["u_mine_0"] for c in range(NCORES)]
    out = None
    for l in range(L):
        pB = get_prog("pB", [("B", None)])
        maps = []
        for c in range(NCORES):
            b, j = c // 4, c % 4
            maps.append({"cmask": cmask, "u_all": np.stack([um[4 * b + rr] for rr in range(4)], axis=0),
                         "wB": pack_wB(inp["w_in"][l], j), "wsB": pack_wsB(inp, l, j), "vecB": pack_vecB(inp, l, j)})
        r = _run(pB, maps)
        mix = [r[c]["mix_mine"] for c in range(NCORES)]
        w = {"cmask": cmask}
        w.update(_weights_C(inp, l))
        w.update(_weights_ffn(inp, l, 2))
        if l + 1 < L:
            stages = ["load_x", ("C", l), ("cross", l), ("ffn", l, 2), ("ffn", l + 1, 1), ("unorm", l + 1), "store_x"]
            w.update(_weights_ffn(inp, l + 1, 1))
            w["gmix_%d" % (l + 1)] = pack_vec(inp["g_mix"][l + 1])
        else:
            stages = ["load_x", ("C", l), ("cross", l), ("ffn", l, 2), "final"]
            w["gfin"] = pack_vec(inp["g_final"])
        pC = get_prog("pC%d" % l, stages)
        maps = []
        for c in range(NCORES):
            b, cc = c // 4, c % 4
            m = dict(w)
            m["xT"] = xT[c]
            m["u_mine_%d" % l] = um[c]
            m["mix_all_%d" % l] = np.stack([mix[4 * b + j][cc] for j in range(4)], axis=0)
            m["memT"] = pack_memT(inp["mem"][b])
            maps.append(m)
        r = _run(pC, maps)
        if l + 1 < L:
            xT = [r[c]["xT_out"] for c in range(NCORES)]
            um = [r[c]["u_mine_%d" % (l + 1)] for c in range(NCORES)]
        else:
            out = np.stack([unpack_xT(r[c]["outT"]) for c in range(NCORES)], axis=0)
    return out.reshape(2, 8192, D).astype(np.float32)
```

```python
import numpy as np
from contextlib import ExitStack
import concourse.bass as bass
import concourse.mybir as mybir
from concourse.bass_utils import run_bass_kernel_spmd

F32 = mybir.dt.float32
BF16 = mybir.dt.bfloat16
AF = mybir.ActivationFunctionType
ALU = mybir.AluOpType
AX = mybir.AxisListType

D = 1024
KC = 8
T = 2048
TT = 512
NT = T // TT
DFF = 2816
NF = DFF // 128
EPS = 1e-6
L = 2
NCORES = 8


class Res:
    __slots__ = ("name", "w", "r")

    def __init__(self, name=""):
        self.name = name
        self.w = None
        self.r = {}


class DSem:
    __slots__ = ("key", "sem", "count")

    def __init__(self, key, sem):
        self.key = key
        self.sem = sem
        self.count = 0


class Eng:
    def __init__(self, name, sem):
        self.name = name
        self.sem = sem
        self.count = 0
        self.seen = {}
        self.q = []


class K:
    def __init__(self, nc, stack):
        self.nc = nc
        self.stack = stack
        self.engs = {}
        for n in ("pe", "act", "dve", "pool", "sp"):
            self.engs[n] = Eng(n, stack.enter_context(nc.semaphore("sem_" + n)))
        self.dsems = {}
        self.ninst = 0
        self.sched = {}

    NPOOL = 64

    def semname(self, sem):
        for e in self.engs.values():
            if e.sem is sem:
                return e.name
        for d in self.dsems.values():
            if d.sem is sem:
                return d.key
        return "?"

    def simulate(self):
        cnt = {}
        pos = {e: 0 for e in self.sched}
        progress = True
        while progress:
            progress = False
            for e, lst in self.sched.items():
                while pos[e] < len(lst):
                    waits, key, inc = lst[pos[e]]
                    if all(cnt.get(kk, 0) >= v for kk, v in waits):
                        cnt[key] = cnt.get(key, 0) + inc
                        pos[e] += 1
                        progress = True
                    else:
                        break
        stuck = {e: (pos[e], len(lst), lst[pos[e]][0] if pos[e] < len(lst) else None) for e, lst in self.sched.items()}
        return stuck, cnt

    def dsem(self, key):
        return None

    def _pool_sem(self):
        if not hasattr(self, "dpool"):
            self.dpool = []
            self.dnext = 0
        if len(self.dpool) < self.NPOOL:
            i = len(self.dpool)
            d = DSem("dp%d" % i, self.stack.enter_context(self.nc.semaphore("dsem_p%d" % i)))
            self.dpool.append(d)
            self.dsems[d.key] = d
            return d
        d = self.dpool[self.dnext % self.NPOOL]
        self.dnext += 1
        return d

    def _waits(self, eng, reads, writes):
        deps = {}

        def add(tok):
            if tok is None:
                return
            key, sem, val = tok
            if key not in deps or deps[key][1] < val:
                deps[key] = (sem, val)

        for r in reads:
            add(r.w)
        for w in writes:
            add(w.w)
            for tok in w.r.values():
                add(tok)
        out = []
        for key, (sem, val) in deps.items():
            if eng.seen.get(key, 0) >= val:
                continue
            if key == "pe" and eng.name == "pe":
                continue
            eng.seen[key] = val
            out.append((sem, val))
        return out

    def op(self, ename, fn, reads=(), writes=()):
        eng = self.engs[ename]
        waits = self._waits(eng, reads, writes)
        sem = eng.sem
        eng.count += 1
        tok = (ename, sem, eng.count)

        def run(e, waits=waits, fn=fn, sem=sem):
            for s, v in waits:
                e.wait_ge(s, v)
            fn(e).then_inc(sem, 1)

        eng.q.append(run)
        self.sched.setdefault(ename, []).append(([(self.semname(s), v) for s, v in waits], ename, 1))
        for r in reads:
            r.r[ename] = tok
        for w in writes:
            w.w = tok
            w.r = {}
        self.ninst += 1

    def dma(self, qname, ds, out, in_, reads=(), writes=(), **kw):
        eng = self.engs[qname]
        ds = self._pool_sem()
        waits = self._waits(eng, reads, writes)
        if ds.count > 0 and eng.seen.get(ds.key, 0) < ds.count:
            eng.seen[ds.key] = ds.count
            waits.append((ds.sem, ds.count))
        ds.count += 16
        tok = (ds.key, ds.sem, ds.count)

        def run(e, waits=waits, out=out, in_=in_, kw=kw, sem=ds.sem, cnt=ds.count):
            for s, v in waits:
                e.wait_ge(s, v)
            e.dma_start(out=out(e) if callable(out) else out, in_=in_(e) if callable(in_) else in_,
                        **kw).then_inc(sem, 16)

        eng.q.append(run)
        self.sched.setdefault(qname, []).append(([(self.semname(s), v) for s, v in waits], ds.key, 16))
        for r in reads:
            r.r[ds.key] = tok
        for w in writes:
            w.w = tok
            w.r = {}
        self.ninst += 1

    def coll(self, kind, groups, ins, outs, reads=(), writes=()):
        eng = self.engs["pool"]
        i = len([k_ for k_ in self.dsems if k_.startswith("cc")])
        ds = DSem("cc%d" % i, self.stack.enter_context(self.nc.semaphore("ccsem_%d" % i)))
        self.dsems[ds.key] = ds
        waits = self._waits(eng, reads, writes)
        ds.count += 1
        tok = (ds.key, ds.sem, ds.count)

        def run(e, waits=waits, sem=ds.sem):
            for s, v in waits:
                e.wait_ge(s, v)
            e.collective_compute(kind, ALU.bypass, replica_groups=groups, ins=[a.opt() for a in ins],
                                 outs=[a.opt() for a in outs], dma_qos="P2").then_inc(sem, 1)

        eng.q.append(run)
        self.sched.setdefault("pool", []).append(([(self.semname(s), v) for s, v in waits], ds.key, 1))
        for r in reads:
            r.r[ds.key] = tok
        for w in writes:
            w.w = tok
            w.r = {}

    def barrier(self, skip_cc=False):
        toks = [(e.name, e.sem, e.count) for e in self.engs.values() if e.count > 0]
        toks += [(d.key, d.sem, d.count) for d in self.dsems.values()
                 if d.count > 0 and not (skip_cc and d.key.startswith("cc"))]
        for eng in self.engs.values():
            waits = []
            for key, sem, val in toks:
                if eng.seen.get(key, 0) >= val:
                    continue
                eng.seen[key] = val
                waits.append((sem, val))

            def run(e, waits=waits):
                for s, v in waits:
                    e.wait_ge(s, v)

            eng.q.append(run)
            self.sched.setdefault(eng.name, []).append(([(self.semname(s), v) for s, v in waits], "_nop", 0))

    def final_wait(self, qname="sp"):
        eng = self.engs[qname]
        toks = [(d.key, d.sem, d.count) for d in self.dsems.values() if d.count > 0]
        toks += [(e.name, e.sem, e.count) for e in self.engs.values() if e.count > 0]

        def run(e, toks=toks):
            for key, s, v in toks:
                e.wait_ge(s, v)

        eng.q.append(run)

    def emit(self):
        nc = self.nc
        with nc.Block() as block:
            @block.tensor
            def _(e):
                for f in self.engs["pe"].q:
                    f(e)

            @block.scalar
            def _(e):
                for f in self.engs["act"].q:
                    f(e)

            @block.vector
            def _(e):
                for f in self.engs["dve"].q:
                    f(e)

            @block.gpsimd
            def _(e):
                for f in self.engs["pool"].q:
                    f(e)

            @block.sync
            def _(e):
                for f in self.engs["sp"].q:
                    f(e)


class Arena:
    def __init__(self, ap_f32, nbytes):
        self.ap = ap_f32
        self.nbytes = nbytes
        self.off = 0
        self.marks = []

    def alloc(self, shape_free, dtype, parts=128):
        n = int(np.prod(shape_free))
        esz = 2 if dtype == BF16 else 4
        nb = (n * esz + 31) // 32 * 32
        assert self.off + nb <= self.nbytes, ("arena overflow", self.off, nb, self.nbytes)
        a = self.ap[:, self.off // 4:(self.off + nb) // 4]
        self.off += nb
        if dtype == BF16:
            a = a.bitcast(BF16)
        a = a[:, 0:n]
        if len(shape_free) == 2:
            a = a.rearrange("p (a b) -> p a b", a=shape_free[0])
        elif len(shape_free) == 3:
            a = a.rearrange("p (a b c) -> p a b c", a=shape_free[0], b=shape_free[1])
        return a

    def mark(self):
        self.marks.append(self.off)

    def release(self):
        self.off = self.marks.pop()


class Prog:
    def __init__(self, nc, k, arena, psum):
        self.nc = nc
        self.k = k
        self.ar = arena
        self.psum = psum
        self.pres = [Res("ps%d" % i) for i in range(8)]
        self.pidx = 0

    def bank(self):
        i = self.pidx
        self.pidx = (self.pidx + 1) % 8
        return self.psum[:, i, :], self.pres[i]


def alloc_normtmp(P, tw=TT):
    ar = P.ar
    return dict(sq=ar.alloc([KC, tw], F32), sq_r=[Res() for _ in range(KC)],
                rstd=ar.alloc([tw], F32), rstd_r=Res())


def emit_rmsnorm(P, cst, nt, x, xres, gvec, gres, dst, ntiles=NT, tw=TT, after=None):
    k = P.k
    ones, ones_res = cst["ones_f"], cst["ones_f_r"]
    sq, sq_res, rstd, rstd_res = nt["sq"], nt["sq_r"], nt["rstd"], nt["rstd_r"]
    for tt in range(ntiles):
        sl = slice(tt * tw, (tt + 1) * tw)
        o, ores = dst(tt)
        ps, pr = P.bank()
        for kc in range(KC):
            k.op("act", lambda e, sl=sl, kc=kc: e.activation(out=sq[:, kc, 0:tw], in_=x[:, kc, sl], func=AF.Square),
                 reads=[xres[tt]], writes=[sq_res[kc]])
            k.op("pe", lambda e, ps=ps, kc=kc: e.matmul(ps[:, 0:tw], lhsT=ones, rhs=sq[:, kc, 0:tw],
                                                        start=(kc == 0), stop=(kc == KC - 1)),
                 reads=[sq_res[kc], ones_res], writes=[pr])
        k.op("act", lambda e, ps=ps: e.activation(out=rstd[:, 0:tw], in_=ps[:, 0:tw], func=AF.Ln,
                                                  bias=cst["eps"], scale=1.0 / D),
             reads=[pr, cst["eps_r"]], writes=[rstd_res])
        k.op("act", lambda e: e.activation(out=rstd[:, 0:tw], in_=rstd[:, 0:tw], func=AF.Exp, scale=-0.5),
             reads=[rstd_res], writes=[rstd_res])
        for kc in range(KC):
            k.op("dve", lambda e, kc=kc, sl=sl, o=o: e.scalar_tensor_tensor(
                out=o[:, kc, :], in0=x[:, kc, sl], scalar=gvec[:, kc:kc + 1], in1=rstd[:, 0:tw],
                op0=ALU.mult, op1=ALU.mult),
                reads=[xres[tt], gres, rstd_res], writes=[ores])
        if after is not None:
            after(tt, o, ores)


def consts_eps(P):
    return P.eps_ap


def emit_ffn(P, x, xres, xn, xnres, w1d, w2d, wb, tag):
    k = P.k
    (w1buf, w1res, w2buf, w2res, g, gres, stmp, stres) = wb
    ds1 = [None] * 3
    ds2 = [None] * 3
    NB1 = len(w1buf)
    NB2 = len(w2buf)
    cnt1 = 0
    cnt2 = 0
    for st in range(2):
        for f in range(NF):
            s = cnt1 % NB1
            cnt1 += 1
            k.dma("pool", ds1[s], out=w1buf[s], in_=w1d[f], reads=[], writes=[w1res[s]])
            for t2 in range(2):
                tt = st * 2 + t2
                sl = slice(tt * TT, (tt + 1) * TT)
                pa, par = P.bank()
                pb, pbr = P.bank()
                for kc in range(KC):
                    k.op("pe", lambda e, pa=pa, s=s, kc=kc, sl=sl: e.matmul(
                        pa, lhsT=w1buf[s][:, kc * 256:kc * 256 + 128], rhs=xn[:, kc, sl],
                        start=(kc == 0), stop=(kc == KC - 1)),
                        reads=[w1res[s], xnres[tt]], writes=[par])
                for kc in range(KC):
                    k.op("pe", lambda e, pb=pb, s=s, kc=kc, sl=sl: e.matmul(
                        pb, lhsT=w1buf[s][:, kc * 256 + 128:kc * 256 + 256], rhs=xn[:, kc, sl],
                        start=(kc == 0), stop=(kc == KC - 1)),
                        reads=[w1res[s], xnres[tt]], writes=[pbr])
                ts_ = (f * 2 + t2) % 2
                k.op("act", lambda e, pa=pa, ts_=ts_: e.activation(out=stmp[ts_], in_=pa, func=AF.Silu),
                     reads=[par], writes=[stres[ts_]])
                k.op("dve", lambda e, pb=pb, ts_=ts_, f=f, t2=t2: e.tensor_tensor(
                    out=g[:, f, t2 * TT:(t2 + 1) * TT], in0=stmp[ts_], in1=pb, op=ALU.mult),
                    reads=[stres[ts_], pbr], writes=[gres[f][t2]])
        for dc in range(KC):
            s = cnt2 % NB2
            cnt2 += 1
            k.dma("pool", ds2[s], out=w2buf[s], in_=w2d[dc], reads=[], writes=[w2res[s]],
                  max_dma_last_dim=1408 * 4)
            for t2 in range(2):
                tt = st * 2 + t2
                sl = slice(tt * TT, (tt + 1) * TT)
                po, por = P.bank()
                for f in range(NF):
                    k.op("pe", lambda e, po=po, s=s, f=f, t2=t2: e.matmul(
                        po, lhsT=w2buf[s][:, f * 128:(f + 1) * 128], rhs=g[:, f, t2 * TT:(t2 + 1) * TT],
                        start=(f == 0), stop=(f == NF - 1)),
                        reads=[w2res[s], gres[f][t2]], writes=[por])
                k.op("dve", lambda e, po=po, dc=dc, sl=sl: e.scalar_tensor_tensor(
                    out=x[:, dc, sl], in0=po, scalar=0.5, in1=x[:, dc, sl], op0=ALU.mult, op1=ALU.add),
                    reads=[por, xres[tt]], writes=[xres[tt]])


def pack_w1(w):
    a = w[:, :DFF].reshape(KC, 128, NF, 128)
    b = w[:, DFF:].reshape(KC, 128, NF, 128)
    ab = np.stack([a, b], axis=3)
    return np.ascontiguousarray(ab.transpose(2, 1, 0, 3, 4)).reshape(NF, 128, KC * 256)


def pack_w2(w):
    a = w.reshape(NF, 128, KC, 128)
    return np.ascontiguousarray(a.transpose(2, 1, 0, 3)).reshape(KC, 128, NF * 128)


def pack_vec(v):
    return np.ascontiguousarray(v.reshape(KC, 128).T)


def pack_xT(xc):
    return np.ascontiguousarray(xc.reshape(T, KC, 128).transpose(2, 1, 0))


def unpack_xT(a):
    return np.ascontiguousarray(a.transpose(2, 1, 0)).reshape(T, D)


def make_consts(P, mask_d):
    k, ar = P.k, P.ar
    c = {}
    c["ones_f"] = ar.alloc([128], F32); c["ones_f_r"] = Res()
    c["ones_bf"] = ar.alloc([128], BF16); c["ones_bf_r"] = Res()
    c["mask"] = ar.alloc([128], BF16); c["mask_r"] = Res()
    c["eps"] = ar.alloc([1], F32); c["eps_r"] = Res()
    P.eps_ap = c["eps"]; P.eps_res = c["eps_r"]
    k.op("pool", lambda e: e.memset(c["ones_f"], 1.0), writes=[c["ones_f_r"]])
    k.op("pool", lambda e: e.memset(c["ones_bf"], 1.0), writes=[c["ones_bf_r"]])
    k.op("pool", lambda e: e.memset(c["eps"], EPS), writes=[c["eps_r"]])
    k.dma("pool", k.dsem("cmask"), out=c["mask"], in_=mask_d, writes=[c["mask_r"]])
    return c


NGT = 16
NCB = 1026
VB_PSCALE = 0
VB_CONVW = 1
VB_CONVB = 9
VB_BA = 11
VB_BX = 13
VB_LAM = 15
VB_BF = 17
VB_SEL = 19
VB_CORR = 23
NVB = 39


def emit_phaseB(P, cst, usrc, wB_d, wsB_d, vecB_d, mdst, tile_done=None, ures=None):
    k = P.k
    ar = P.ar
    ar.mark()
    ones_f, ones_f_r, mask, mask_r = cst["ones_f"], cst["ones_f_r"], cst["mask"], cst["mask_r"]
    GB = [5, 6, 7]
    SB = [2, 3, 4]
    OB = [0, 1]
    gcnt = [0]
    scnt = [0]

    def gbank():
        i = GB[gcnt[0] % 3]
        gcnt[0] += 1
        return P.psum[:, i, :], P.pres[i]

    def sbank():
        i = SB[scnt[0] % 3]
        scnt[0] += 1
        return P.psum[:, i, :], P.pres[i]

    utile = ar.alloc([KC, TT], BF16); utile_r = Res()
    wB = ar.alloc([KC, NCB], BF16); wB_r = Res()
    wsB = ar.alloc([640], BF16); wsB_r = Res()
    vec = ar.alloc([NVB], F32); vec_r = Res()
    nbf = ar.alloc([2], F32); nbf_r = Res()
    nsp8 = ar.alloc([2], F32); nsp8_r = Res()
    Kaug = [ar.alloc([NGT * TT], BF16) for _ in range(2)]
    K_r = [[Res() for _ in range(NGT)] for _ in range(2)]
    Vaug = ar.alloc([NGT * 4, 2, 65], BF16)
    V_r = [Res() for _ in range(NGT)]
    negc = ar.alloc([NGT * 4, 2], F32)
    negc_r = [[Res() for _ in range(NGT)] for _ in range(2)]
    xa_buf = ar.alloc([528], F32); xa_r = Res()
    sA = ar.alloc([528], F32); sA_r = Res()
    sBf = ar.alloc([528], F32); sB_r = Res()
    acc = ar.alloc([TT], F32); acc_r = Res()
    d_bf = [ar.alloc([TT], BF16) for _ in range(2)]; d_r = [Res(), Res()]
    xb_buf = [ar.alloc([515], F32) for _ in range(2)]; xb_r = [Res(), Res()]
    xc = [[ar.alloc([TT], F32) for _ in range(2)] for _ in range(2)]
    xc_r = [[Res(), Res()] for _ in range(2)]
    xc_bf = [[ar.alloc([TT], BF16) for _ in range(2)] for _ in range(2)]
    xcb_r = [[Res(), Res()] for _ in range(2)]
    gg = [[ar.alloc([TT], BF16) for _ in range(2)] for _ in range(2)]
    gg_r = [[Res(), Res()] for _ in range(2)]
    t1 = ar.alloc([TT], F32); t1_r = Res()
    Qaug = [[ar.alloc([TT], BF16) for _ in range(2)] for _ in range(2)]
    Q_r = [[Res(), Res()] for _ in range(2)]
    rowe = ar.alloc([TT], F32); rowe_r = Res()
    crow = [ar.alloc([TT], F32) for _ in range(2)]; crow_r = [Res(), Res()]
    clast = ar.alloc([2], F32); clast_r = [Res(), Res()]
    ra = ar.alloc([TT], F32); ra_r = Res()
    tmp = ar.alloc([TT], F32); tmp_r = Res()
    ig = ar.alloc([TT], F32); ig_r = Res()
    hbuf = ar.alloc([TT], F32); hbuf_r = Res()
    hlast = ar.alloc([2], F32); hlast_r = [Res(), Res()]
    Pt = [ar.alloc([TT], BF16) for _ in range(3)]; Pt_r = [Res(), Res(), Res()]
    rec = ar.alloc([TT], F32); rec_r = Res()
    osb = ar.alloc([TT], F32); osb_r = Res()
    outA = [ar.alloc([TT], BF16)] * 2; outA_r = [Res()] * 2
    outB = [ar.alloc([2, TT], BF16) for _ in range(2)]; outB_r = [Res(), Res()]
    outC = [[ar.alloc([TT], BF16)] * 2 for _ in range(2)]
    outC_r = [[Res()] * 2 for _ in range(2)]
    ds_u = k.dsem("bu")
    ds_o = k.dsem("bo")
    store_res = []

    k.dma("pool", k.dsem("bw"), out=wB, in_=wB_d, writes=[wB_r])
    k.dma("pool", k.dsem("bws"), out=wsB, in_=wsB_d, writes=[wsB_r])
    k.dma("sp", k.dsem("bv"), out=vec, in_=vecB_d, writes=[vec_r])
    for h in range(2):
        k.op("pool", lambda e, h=h: e.memset(Kaug[h][64:65, :], 1.0), writes=K_r[h])
    k.op("pool", lambda e: e.memset(Vaug[:, :, :, 64:65], 1.0), writes=V_r)
    k.op("dve", lambda e: e.tensor_scalar(out=nbf, in0=vec[:, VB_BF:VB_BF + 2], scalar1=-1.0, scalar2=None,
                                          op0=ALU.mult), reads=[vec_r], writes=[nbf_r])
    k.op("act", lambda e: e.activation(out=nsp8, in_=vec[:, VB_LAM:VB_LAM + 2], func=AF.Exp, scale=-1.0),
         reads=[vec_r], writes=[nsp8_r])
    k.op("act", lambda e: e.activation(out=nsp8, in_=nsp8, func=AF.Ln, bias=ones_f[:, 0:1], scale=1.0),
         reads=[nsp8_r, ones_f_r], writes=[nsp8_r])
    k.op("dve", lambda e: e.tensor_scalar(out=nsp8, in0=nsp8, scalar1=-8.0, scalar2=None, op0=ALU.mult),
         reads=[nsp8_r], writes=[nsp8_r])

    def bank_any():
        return gbank()

    def S1(gt):
        r, tt = gt // 4, gt % 4
        par = gt % 2
        C = {}

        def proj(c0, ncol, nparts):
            ps, pr = gbank()
            for kc in range(KC):
                k.op("pe", lambda e, ps=ps, kc=kc: e.matmul(ps[0:nparts, :], lhsT=wB[:, kc, c0:c0 + ncol],
                                                          rhs=utile[:, kc, :], start=(kc == 0), stop=(kc == KC - 1)),
                     reads=[wB_r, utile_r], writes=[pr])
            return ps, pr

        def m_loadnext():
            if gt + 1 < NGT:
                r2, tt2 = (gt + 1) // 4, (gt + 1) % 4
                k.dma("sp", ds_u, out=utile, in_=usrc(r2, tt2), reads=(ures(tt2) if ures else []), writes=[utile_r])
        C["loadnext"] = [m_loadnext]

        st = {}

        def xa1():
            if gt == 0:
                k.op("dve", lambda e: e.memset(xa_buf[:, 0:16], 0.0), writes=[xa_r])
            else:
                k.op("dve", lambda e: e.tensor_copy(out=xa_buf[:, 0:16], in_=xa_buf[:, 512:528]),
                     reads=[xa_r], writes=[xa_r])
            st["xa"] = proj(0, 128, 128)

        def xa2():
            ps, pr = st["xa"]
            k.op("dve", lambda e: e.tensor_copy(out=xa_buf[:, 16:528], in_=ps), reads=[pr], writes=[xa_r])

            def padd(dst, dr, srcb, sr, lo, sh):
                k.op("dve", lambda e: e.tensor_tensor(out=dst[:, lo:528], in0=srcb[:, lo:528],
                                                      in1=srcb[:, lo - sh:528 - sh], op=ALU.add),
                     reads=[sr], writes=[dr])

            def pacc(srcb, sr, i):
                if i == 0:
                    k.op("dve", lambda e: e.tensor_scalar(out=acc, in0=srcb[:, 16:528],
                                                          scalar1=vec[:, VB_SEL:VB_SEL + 1], scalar2=None, op0=ALU.mult),
                         reads=[sr, vec_r], writes=[acc_r])
                else:
                    k.op("dve", lambda e: e.scalar_tensor_tensor(out=acc, in0=srcb[:, 16:528],
                                                                 scalar=vec[:, VB_SEL + i:VB_SEL + i + 1], in1=acc,
                                                                 op0=ALU.mult, op1=ALU.add),
                         reads=[sr, vec_r, acc_r], writes=[acc_r])

            padd(sA, sA_r, xa_buf, xa_r, 1, 1)
            pacc(sA, sA_r, 0)
            padd(sBf, sB_r, sA, sA_r, 3, 2)
            pacc(sBf, sB_r, 1)
            padd(sA, sA_r, sBf, sB_r, 7, 4)
            pacc(sA, sA_r, 2)
            padd(sBf, sB_r, sA, sA_r, 15, 8)
            pacc(sBf, sB_r, 3)
            if gt == 0:
                k.op("dve", lambda e: e.tensor_tensor(out=acc[:, 0:16], in0=acc[:, 0:16],
                                                      in1=vec[:, VB_CORR:VB_CORR + 16], op=ALU.mult),
                     reads=[acc_r, vec_r], writes=[acc_r])
            k.op("dve", lambda e: e.tensor_tensor(out=d_bf[par], in0=acc, in1=xa_buf[:, 16:528], op=ALU.subtract),
                 reads=[xa_r, acc_r], writes=[d_r[par]])
        C["xa"] = [xa1, xa2]

        def mk_xb(hh):
            def m1():
                if gt == 0:
                    k.op("dve", lambda e: e.memset(xb_buf[hh][:, 0:3], 0.0), writes=[xb_r[hh]])
                else:
                    k.op("dve", lambda e: e.tensor_copy(out=xb_buf[hh][:, 0:3], in_=xb_buf[hh][:, 512:515]),
                         reads=[xb_r[hh]], writes=[xb_r[hh]])
                st["xb%d" % hh] = proj(128 + 128 * hh, 128, 128)

            def m2():
                ps, pr = st["xb%d" % hh]
                k.op("dve", lambda e: e.tensor_copy(out=xb_buf[hh][:, 3:515], in_=ps), reads=[pr], writes=[xb_r[hh]])
                cw = VB_CONVW + 4 * hh
                k.op("dve", lambda e: e.tensor_scalar(
                    out=xc[hh][par], in0=xb_buf[hh][:, 0:512], scalar1=vec[:, cw:cw + 1],
                    scalar2=vec[:, VB_CONVB + hh:VB_CONVB + hh + 1], op0=ALU.mult, op1=ALU.add),
                    reads=[xb_r[hh], vec_r], writes=[xc_r[hh][par]])
                for kk in range(1, 4):
                    k.op("dve", lambda e, kk=kk: e.scalar_tensor_tensor(
                        out=xc[hh][par], in0=xb_buf[hh][:, kk:kk + 512], scalar=vec[:, cw + kk:cw + kk + 1],
                        in1=xc[hh][par], op0=ALU.mult, op1=ALU.add),
                        reads=[xb_r[hh], vec_r, xc_r[hh][par]], writes=[xc_r[hh][par]])
                k.op("dve", lambda e: e.tensor_copy(out=xc_bf[hh][par], in_=xc[hh][par]),
                     reads=[xc_r[hh][par]], writes=[xcb_r[hh][par]])
            return [m1, m2]

        def mk_gb(hh):
            def m1():
                st["gb%d" % hh] = proj(384 + 128 * hh, 128, 128)

            def m2():
                ps, pr = st["gb%d" % hh]
                k.op("act", lambda e: e.activation(out=t1, in_=ps, func=AF.Square), reads=[pr], writes=[t1_r])
                k.op("dve", lambda e: e.tensor_copy(out=gg[hh][par], in_=ps), reads=[pr, t1_r], writes=[gg_r[hh][par]])

            def m3():
                ps, pr = st["gb%d" % hh]
                k.op("dve", lambda e: e.tensor_scalar(out=t1, in0=t1, scalar1=0.044715, scalar2=1.0, op0=ALU.mult,
                                                      op1=ALU.add), reads=[t1_r], writes=[t1_r])
                k.op("dve", lambda e: e.tensor_tensor(out=t1, in0=t1, in1=ps, op=ALU.mult),
                     reads=[t1_r, pr], writes=[t1_r])

            def m4():
                k.op("act", lambda e: e.activation(out=t1, in_=t1, func=AF.Sigmoid, scale=1.5957691216057308),
                     reads=[t1_r], writes=[t1_r])

            def m5():
                k.op("dve", lambda e: e.tensor_tensor(out=gg[hh][par], in0=gg[hh][par], in1=t1, op=ALU.mult),
                     reads=[t1_r, gg_r[hh][par]], writes=[gg_r[hh][par]])
            return [m1, m2, m3, m4, m5]

        def mk_q(h):
            def m1():
                st["q%d" % h] = proj(640 + 65 * h, 65, 65)

            def m2():
                ps, pr = st["q%d" % h]
                k.op("dve", lambda e: e.tensor_copy(out=Qaug[h][par][0:64, :], in_=ps[0:64, :]),
                     reads=[pr], writes=[Q_r[h][par]])
                k.op("act", lambda e: e.activation(out=rowe[64:65, :], in_=ps[64:65, :], func=AF.Exp,
                                                   bias=nbf[64:65, h:h + 1], scale=-1.0),
                     reads=[pr, nbf_r], writes=[rowe_r])
                k.op("act", lambda e: e.activation(out=rowe[64:65, :], in_=rowe[64:65, :], func=AF.Ln,
                                                   bias=ones_f[64:65, 0:1], scale=1.0),
                     reads=[rowe_r, ones_f_r], writes=[rowe_r])

            def m3():
                init = 0.0 if gt == 0 else clast[64:65, h:h + 1]
                k.op("dve", lambda e: e.tensor_tensor_scan(
                    out=crow[h][64:65, :], data0=ones_f[64:65, 0:1].to_broadcast([1, TT]),
                    data1=rowe[64:65, :], initial=init, op0=ALU.mult, op1=ALU.subtract),
                    reads=[rowe_r, clast_r[h], ones_f_r], writes=[crow_r[h]])
                k.op("dve", lambda e: e.tensor_copy(out=clast[64:65, h:h + 1], in_=crow[h][64:65, TT - 1:TT]),
                     reads=[crow_r[h]], writes=[clast_r[h]])

            def m4():
                k.op("act", lambda e: e.activation(out=Qaug[h][par][64:65, :], in_=crow[h][64:65, :],
                                                   func=AF.Copy, scale=8.0),
                     reads=[crow_r[h]], writes=[Q_r[h][par]])

            def m5():
                pt_, ptr_ = gbank()
                st["n%d" % h] = (pt_, ptr_)
                for blk in range(4):
                    k.op("pe", lambda e, blk=blk: e.matmul(
                        pt_[:, blk:blk + 1], lhsT=crow[h][64:65, blk * 128:(blk + 1) * 128], rhs=ones_f[64:65, 0:1],
                        start=True, stop=True),
                        reads=[crow_r[h], ones_f_r], writes=[ptr_])

            def m6():
                pt_, ptr_ = st["n%d" % h]
                k.op("dve", lambda e: e.tensor_scalar(
                    out=negc[:, 4 * gt:4 * gt + 4, h], in0=pt_[:, 0:4], scalar1=-1.0, scalar2=None, op0=ALU.mult),
                    reads=[ptr_], writes=[negc_r[h][gt]])
            return [m1, m2, m3, m4, m5, m6]

        def mk_k(h):
            def m1():
                st["k%d" % h] = proj(770 + 64 * h, 64, 64)

            def m2():
                ps, pr = st["k%d" % h]
                k.op("act", lambda e: e.activation(out=Kaug[h][0:64, gt * TT:(gt + 1) * TT],
                                                   in_=ps[0:64, :], func=AF.Copy),
                     reads=[pr], writes=[K_r[h][gt]])
            return [m1, m2]

        def v1():
            ps, pr = gbank()
            st["v"] = (ps, pr)
            for blk in range(4):
                for kc in range(KC):
                    k.op("pe", lambda e, blk=blk, kc=kc: e.matmul(
                        ps[:, blk * 128:(blk + 1) * 128], lhsT=utile[:, kc, blk * 128:(blk + 1) * 128],
                        rhs=wB[:, kc, 898:1026], start=(kc == 0), stop=(kc == KC - 1)),
                        reads=[wB_r, utile_r], writes=[pr])

        def v2():
            ps, pr = st["v"]
            k.op("dve", lambda e: e.tensor_copy(
                out=Vaug[:, 4 * gt:4 * gt + 4, :, 0:64], in_=ps.rearrange("p (b h d) -> p b h d", b=4, h=2)),
                reads=[pr], writes=[V_r[gt]])
        C["xb0"], C["xb1"] = mk_xb(0), mk_xb(1)
        C["gb0"], C["gb1"] = mk_gb(0), mk_gb(1)
        C["q0"], C["q1"] = mk_q(0), mk_q(1)
        C["k0"], C["k1"] = mk_k(0), mk_k(1)
        C["v"] = [v1, v2]
        return C

    def S2chains(gt):
        r, tt = gt // 4, gt % 4
        par = gt % 2
        tsl = slice(tt * TT, (tt + 1) * TT)
        C = {}
        st = {}

        def pool1():
            ps, pr = gbank()
            st["pool"] = (ps, pr)
            k.op("pe", lambda e: e.matmul(ps, lhsT=wsB[:, 0:128], rhs=d_bf[par], start=True, stop=True),
                 reads=[wsB_r, d_r[par]], writes=[pr])

        def pool2():
            ps, pr = st["pool"]
            k.op("act", lambda e: e.activation(out=outA[par], in_=ps, func=AF.Identity,
                                               scale=vec[:, VB_PSCALE:VB_PSCALE + 1]),
                 reads=[pr, vec_r], writes=[outA_r[par]])
            for dap, ps_ in mdst("A", r, tsl):
                sr = Res(); store_res.append(sr)
                k.dma("sp", ds_o, out=dap, in_=outA[par][ps_, :], reads=[outA_r[par]], writes=[sr])
        C["pool"] = [pool1, pool2]

        def mk_lru(hh):
            def m1():
                ps, pr = gbank()
                k.op("pe", lambda e: e.matmul(ps, lhsT=wsB[:, 128 + 128 * hh:256 + 128 * hh],
                                              rhs=xc_bf[hh][par], start=True, stop=True),
                     reads=[wsB_r, xcb_r[hh][par]], writes=[pr])
                ps2, pr2 = gbank()
                k.op("pe", lambda e: e.matmul(ps2, lhsT=wsB[:, 384 + 128 * hh:512 + 128 * hh],
                                              rhs=xc_bf[hh][par], start=True, stop=True),
                     reads=[wsB_r, xcb_r[hh][par]], writes=[pr2])
                st["l%d" % hh] = (ps, pr, ps2, pr2)

            def m2():
                ps, pr, ps2, pr2 = st["l%d" % hh]
                k.op("act", lambda e: e.activation(out=ra, in_=ps, func=AF.Sigmoid,
                                                   bias=vec[:, VB_BA + hh:VB_BA + hh + 1], scale=1.0),
                     reads=[pr, vec_r], writes=[ra_r])
                k.op("act", lambda e: e.activation(out=ig, in_=ps2, func=AF.Sigmoid,
                                                   bias=vec[:, VB_BX + hh:VB_BX + hh + 1], scale=1.0),
                     reads=[pr2, vec_r], writes=[ig_r])
                k.op("act", lambda e: e.activation(out=ra, in_=ra, func=AF.Exp, scale=nsp8[:, hh:hh + 1]),
                     reads=[ra_r, nsp8_r], writes=[ra_r])

            def m3():
                k.op("dve", lambda e: e.tensor_tensor(out=tmp, in0=ra, in1=ra, op=ALU.mult),
                     reads=[ra_r], writes=[tmp_r])
                k.op("dve", lambda e: e.tensor_tensor(out=ig, in0=ig, in1=xc[hh][par], op=ALU.mult),
                     reads=[ig_r, xc_r[hh][par]], writes=[ig_r])

            def m4():
                k.op("act", lambda e: e.activation(out=tmp, in_=tmp, func=AF.Sqrt, bias=ones_f[:, 0:1], scale=-1.0),
                     reads=[tmp_r, ones_f_r], writes=[tmp_r])

            def m5():
                k.op("dve", lambda e: e.tensor_tensor(out=ig, in0=ig, in1=tmp, op=ALU.mult),
                     reads=[ig_r, tmp_r], writes=[ig_r])
                init = 0.0 if gt == 0 else hlast[:, hh:hh + 1]
                k.op("dve", lambda e: e.tensor_tensor_scan(out=hbuf, data0=ra, data1=ig, initial=init,
                                                           op0=ALU.mult, op1=ALU.add),
                     reads=[ra_r, ig_r, hlast_r[hh]], writes=[hbuf_r])
                k.op("dve", lambda e: e.tensor_copy(out=hlast[:, hh:hh + 1], in_=hbuf[:, TT - 1:TT]),
                     reads=[hbuf_r], writes=[hlast_r[hh]])
                k.op("dve", lambda e: e.tensor_tensor(out=outB[par][:, hh, :], in0=hbuf, in1=gg[hh][par], op=ALU.mult),
                     reads=[hbuf_r, gg_r[hh][par]], writes=[outB_r[par]])
                if hh == 1:
                    for dap, ps_, hx in mdst("B", r, tsl):
                        sr = Res(); store_res.append(sr)
                        k.dma("sp", ds_o, out=dap, in_=outB[par][ps_, hx, :], reads=[outB_r[par]], writes=[sr])
            return [m1, m2, m3, m4, m5]
        C["lru0"], C["lru1"] = mk_lru(0), mk_lru(1)
        return C

    ORDER = [("pool", 0), ("q0", 0), ("pool", 1), ("q0", 1), ("lru0", 0), ("q0", 2), ("lru0", 1),
             ("q0", 3), ("q1", 0), ("lru0", 2), ("q1", 1), ("lru0", 3), ("q1", 2), ("lru0", 4), ("q1", 3),
             ("lru1", 0), ("q0", 4), ("lru1", 1), ("q0", 5), ("lru1", 2), ("q1", 4), ("lru1", 3), ("q1", 5),
             ("lru1", 4), ("xa", 0), ("xb0", 0), ("xa", 1), ("xb0", 1), ("xb1", 0), ("gb0", 0), ("xb1", 1),
             ("gb0", 1), ("k0", 0), ("gb0", 2), ("k0", 1), ("gb0", 3), ("k1", 0), ("gb0", 4), ("k1", 1),
             ("gb1", 0), ("v", 0), ("loadnext", 0), ("gb1", 1), ("v", 1), ("gb1", 2), ("gb1", 3), ("gb1", 4)]

    def flat(c2, c1):
        out = []
        for name, i in ORDER:
            src_ = c2 if name in ("pool", "lru0", "lru1") else c1
            if src_ is not None:
                out.append(src_[name][i])
        return out

    def S2(gt, steps):
        r, tt = gt // 4, gt % 4
        par = gt % 2
        tsl = slice(tt * TT, (tt + 1) * TT)
        nkb = 4 * gt + 4

        def attn(h):
            oi = OB[(2 * gt + h) % 2]
            ob, obr = P.psum[:, oi, :], P.pres[oi]
            sbs = {}

            def emit_S(kb):
                dd = kb - 4 * gt
                q0 = 128 * dd if dd > 0 else 0
                sb, sbr = sbank()
                sbs[kb] = (sb, sbr, q0, dd)
                k.op("pe", lambda e, sb=sb, q0=q0, kb=kb: e.matmul(
                    sb[:, q0:TT], lhsT=Kaug[h][0:65, kb * 128:(kb + 1) * 128], rhs=Qaug[h][par][0:65, q0:TT],
                    start=True, stop=True),
                    reads=[K_r[h][kb // 4], Q_r[h][par]], writes=[sbr])

            emit_S(0)
            if nkb > 1:
                emit_S(1)
            for kb in range(nkb):
                if kb + 2 < nkb:
                    emit_S(kb + 2)
                sb, sbr, q0, dd = sbs.pop(kb)
                pi = kb % 3
                k.op("act", lambda e, sb=sb, q0=q0, kb=kb, pi=pi: e.activation(
                    out=Pt[pi][:, q0:TT], in_=sb[:, q0:TT], func=AF.Exp, bias=negc[:, kb, h:h + 1], scale=0.125),
                    reads=[sbr, negc_r[h][kb // 4]], writes=[Pt_r[pi]])
                if dd >= 0:
                    k.op("dve", lambda e, q0=q0, pi=pi: e.tensor_tensor(
                        out=Pt[pi][:, q0:q0 + 128], in0=Pt[pi][:, q0:q0 + 128], in1=mask, op=ALU.min),
                        reads=[Pt_r[pi], mask_r], writes=[Pt_r[pi]])
                k.op("pe", lambda e, q0=q0, kb=kb, pi=pi: e.matmul(
                    ob[0:65, q0:TT], lhsT=Vaug[:, kb, h, :], rhs=Pt[pi][:, q0:TT],
                    start=(kb == 0), stop=(kb == nkb - 1)),
                    reads=[V_r[kb // 4], Pt_r[pi]], writes=[obr])
                if steps:
                    steps.pop(0)()
            k.op("dve", lambda e: e.reciprocal(out=rec[64:65, :], in_=ob[64:65, :]), reads=[obr], writes=[rec_r])
            bc, bcr = sbank()
            k.op("pe", lambda e, bc=bc: e.matmul(bc[0:64, :], lhsT=ones_f[64:65, 0:64], rhs=rec[64:65, :],
                                                 start=True, stop=True),
                 reads=[rec_r, ones_f_r], writes=[bcr])
            k.op("act", lambda e: e.activation(out=osb[0:64, :], in_=ob[0:64, :], func=AF.Copy),
                 reads=[obr], writes=[osb_r])
            k.op("dve", lambda e, bc=bc: e.tensor_tensor(out=outC[h][par][0:64, :], in0=osb[0:64, :], in1=bc[0:64, :],
                                                         op=ALU.mult),
                 reads=[osb_r, bcr], writes=[outC_r[h][par]])
            for dap, ps_ in mdst("C", r, tsl, h):
                sr = Res(); store_res.append(sr)
                k.dma("sp", ds_o, out=dap, in_=outC[h][par][ps_, :], reads=[outC_r[h][par]], writes=[sr])
        for h in range(2):
            attn(h)
        while steps:
            steps.pop(0)()
        if tile_done is not None:
            tile_done(gt, store_res)

    k.dma("sp", ds_u, out=utile, in_=usrc(0, 0), reads=(ures(0) if ures else []), writes=[utile_r])
    for s_ in flat(None, S1(0)):
        s_()
    for gt in range(NGT):
        S2(gt, flat(S2chains(gt), S1(gt + 1) if gt + 1 < NGT else None))
    ar.release()


def pack_wB(w_in_l, j):
    cols = list(range(128 * j, 128 * j + 128))
    cols += list(range(512 + 256 * j, 512 + 256 * j + 256))
    cols += list(range(1536 + 256 * j, 1536 + 256 * j + 256))
    for h in range(2):
        hd = 2 * j + h
        cols += list(range(2560 + 64 * hd, 2560 + 64 * hd + 64)) + [4096 + hd]
    cols += list(range(3072 + 128 * j, 3072 + 128 * j + 128))
    cols += list(range(3584 + 128 * j, 3584 + 128 * j + 128))
    w = w_in_l[:, cols]
    return np.ascontiguousarray(w.reshape(KC, 128, NCB).transpose(1, 0, 2))


def pack_wsB(inp, l, j):
    parts = [inp["w_pool"][l, j]]
    parts += [inp["w_rg_a"][l, 2 * j + hh] for hh in range(2)]
    parts += [inp["w_rg_x"][l, 2 * j + hh] for hh in range(2)]
    return np.ascontiguousarray(np.concatenate(parts, axis=1))


def pack_vecB(inp, l, j):
    v = np.zeros((128, NVB), np.float32)
    v[:, VB_PSCALE] = inp["pool_scale"][l, 128 * j:128 * j + 128]
    for hh in range(2):
        sl = slice(256 * j + 128 * hh, 256 * j + 128 * hh + 128)
        for kk in range(4):
            v[:, VB_CONVW + 4 * hh + kk] = inp["conv_w"][l, kk, sl]
        v[:, VB_CONVB + hh] = inp["conv_b"][l, sl]
        v[:, VB_BA + hh] = inp["b_rg_a"][l, sl]
        v[:, VB_BX + hh] = inp["b_rg_x"][l, sl]
        v[:, VB_LAM + hh] = inp["lru_lambda"][l, sl]
        v[:, VB_BF + hh] = inp["b_f"][l, 2 * j + hh]
    w = 2 ** (j + 1)
    v[:, VB_SEL + j] = np.float32(1.0 / w)
    for t in range(16):
        v[:, VB_CORR + t] = np.float32(w / min(t + 1, w))
    return v


def const_mask():
    jj = np.arange(128)[:, None]
    tt = np.arange(128)[None, :]
    return np.where(jj <= tt, np.float32(3.0e38), np.float32(0.0)).astype(np.float32)


def emit_phaseC(P, cst, x, xres, umine, mix_all_d, wC_d, bgate_d, wo_d):
    k, ar = P.k, P.ar
    ar.mark()
    u = ar.alloc([KC, T], BF16); u_r = [Res() for _ in range(NT)]
    mixs = ar.alloc([16, 2 * TT], BF16); mixs_r = [Res() for _ in range(16)]
    merged = ar.alloc([KC, 2 * TT], BF16); mg_r = [[Res(), Res()] for _ in range(KC)]
    wcb = [ar.alloc([5120], BF16) for _ in range(3)]; wcb_r = [Res(), Res(), Res()]
    wob = [ar.alloc([1024], BF16) for _ in range(2)]; wob_r = [Res(), Res()]
    gate = [ar.alloc([TT], F32) for _ in range(3)]; gate_r = [Res(), Res(), Res()]
    acc = [ar.alloc([TT], F32) for _ in range(2)]; acc_r = [Res(), Res()]
    bg = ar.alloc([24], F32); bg_r = Res()
    ds_u = k.dsem("cu")
    for tt in range(NT):
        k.dma("sp", ds_u, out=u[:, :, tt * TT:(tt + 1) * TT], in_=umine(tt), writes=[u_r[tt]])
    k.dma("sp", k.dsem("cbg"), out=bg, in_=bgate_d, writes=[bg_r])
    dsm = k.dsem("cmix")
    dsw = [None] * 3
    dso = [None] * 2
    cw = 0
    co = 0
    for st in range(2):
        cs = slice(st * 2 * TT, (st + 1) * 2 * TT)
        for r in range(4):
            if callable(mix_all_d):
                srcs = [(r, mix_all_d(r, 0, cs)), (4 + 2 * r, mix_all_d(r, 1, cs)), (5 + 2 * r, mix_all_d(r, 2, cs)),
                        (12 + r, mix_all_d(r, 3, cs))]
            else:
                srcs = [(r, mix_all_d[r, 0:128, cs]), (4 + 2 * r, mix_all_d[r, 128:256, cs]),
                        (5 + 2 * r, mix_all_d[r, 256:384, cs]), (12 + r, mix_all_d[r, 384:512, cs])]
            for m_, s_ in srcs:
                k.dma("sp", dsm, out=mixs[:, m_, :], in_=s_, writes=[mixs_r[m_]])
        for dc in range(KC):
            s = cw % 3
            cw += 1
            k.dma("pool", dsw[s], out=wcb[s], in_=wC_d[dc], writes=[wcb_r[s]], max_dma_last_dim=1024 * 4)
            for t2 in range(2):
                tt = st * 2 + t2
                sl = slice(tt * TT, (tt + 1) * TT)
                sl2 = slice(t2 * TT, (t2 + 1) * TT)
                for br in range(3):
                    ps, pr = P.bank()
                    for kc in range(KC):
                        c0 = 2048 + kc * 384 + br * 128
                        k.op("pe", lambda e, ps=ps, s=s, kc=kc, c0=c0, sl=sl: e.matmul(
                            ps, lhsT=wcb[s][:, c0:c0 + 128], rhs=u[:, kc, sl], start=(kc == 0), stop=(kc == KC - 1)),
                            reads=[wcb_r[s], u_r[tt]], writes=[pr])
                    k.op("act", lambda e, ps=ps, br=br, dc=dc: e.activation(
                        out=gate[br], in_=ps, func=AF.Sigmoid, bias=bg[:, br * 8 + dc:br * 8 + dc + 1], scale=1.0),
                        reads=[pr, bg_r], writes=[gate_r[br]])
                ys = []
                for (m0, m1) in ((0, 4), (4, 12), (12, 16)):
                    ps, pr = P.bank()
                    for m in range(m0, m1):
                        k.op("pe", lambda e, ps=ps, s=s, m=m, sl2=sl2, m0=m0, m1=m1: e.matmul(
                            ps, lhsT=wcb[s][:, m * 128:(m + 1) * 128], rhs=mixs[:, m, sl2],
                            start=(m == m0), stop=(m == m1 - 1)),
                            reads=[wcb_r[s], mixs_r[m]], writes=[pr])
                    ys.append((ps, pr))
                k.op("dve", lambda e, ys=ys: e.tensor_tensor(out=acc[0], in0=ys[0][0], in1=gate[0], op=ALU.mult),
                     reads=[ys[0][1], gate_r[0]], writes=[acc_r[0]])
                k.op("dve", lambda e, ys=ys: e.tensor_tensor(out=acc[1], in0=ys[1][0], in1=gate[1], op=ALU.mult),
                     reads=[ys[1][1], gate_r[1]], writes=[acc_r[1]])
                k.op("dve", lambda e: e.tensor_tensor(out=acc[0], in0=acc[0], in1=acc[1], op=ALU.add),
                     reads=[acc_r[0], acc_r[1]], writes=[acc_r[0]])
                k.op("dve", lambda e, ys=ys: e.tensor_tensor(out=acc[1], in0=ys[2][0], in1=gate[2], op=ALU.mult),
                     reads=[ys[2][1], gate_r[2]], writes=[acc_r[1]])
                k.op("dve", lambda e, dc=dc, sl2=sl2: e.tensor_tensor(out=merged[:, dc, sl2], in0=acc[0], in1=acc[1],
                                                                      op=ALU.add),
                     reads=[acc_r[0], acc_r[1]], writes=[mg_r[dc][t2]])
        for dco in range(KC):
            s = co % 2
            co += 1
            k.dma("pool", dso[s], out=wob[s], in_=wo_d[dco], writes=[wob_r[s]])
            for t2 in range(2):
                tt = st * 2 + t2
                sl = slice(tt * TT, (tt + 1) * TT)
                sl2 = slice(t2 * TT, (t2 + 1) * TT)
                ps, pr = P.bank()
                for dc in range(KC):
                    k.op("pe", lambda e, ps=ps, s=s, dc=dc, sl2=sl2: e.matmul(
                        ps, lhsT=wob[s][:, dc * 128:(dc + 1) * 128], rhs=merged[:, dc, sl2],
                        start=(dc == 0), stop=(dc == KC - 1)),
                        reads=[wob_r[s], mg_r[dc][t2]], writes=[pr])
                k.op("dve", lambda e, ps=ps, dco=dco, sl=sl: e.tensor_tensor(out=x[:, dco, sl], in0=ps, in1=x[:, dco, sl],
                                                                             op=ALU.add),
                     reads=[pr, xres[tt]], writes=[xres[tt]])
    ar.release()


def emit_cross(P, cst, x, xres, memT_d, gmem, gmem_r, gcross, gcross_r, wxq_d, wxkv_d, wxo_d):
    k, ar = P.k, P.ar
    ar.mark()
    NM = 256
    xn = ar.alloc([KC, T], BF16); xn_r = [Res() for _ in range(NT)]
    QT = ar.alloc([KC, T], BF16); QT_r = [[Res() for _ in range(NT)] for _ in range(KC)]
    memf = ar.alloc([KC, NM], F32); memf_r = [Res()]
    memn = ar.alloc([KC, NM], BF16); memn_r = Res()
    KxT = ar.alloc([KC, NM], BF16); Kx_r = [Res() for _ in range(KC)]
    Vx = ar.alloc([2, D], BF16); Vx_r = [Res() for _ in range(KC)]
    wch = [ar.alloc([1024], BF16) for _ in range(2)]; wch_r = [Res(), Res()]
    Pt = [[ar.alloc([TT], BF16) for _ in range(2)] for _ in range(2)]; Pt_r = [[Res(), Res()] for _ in range(2)]
    rec = ar.alloc([TT], F32); rec_r = Res()
    nt = alloc_normtmp(P)
    dsw = [k.dsem("xw0"), k.dsem("xw1")]
    wc = [0]

    def wload(src):
        s = wc[0] % 2
        wc[0] += 1
        k.dma("pool", dsw[s], out=wch[s], in_=src, writes=[wch_r[s]])
        return s

    k.dma("sp", k.dsem("xmem"), out=memf, in_=memT_d, writes=[memf_r[0]])
    emit_rmsnorm(P, cst, nt, memf, memf_r, gmem, gmem_r, lambda tt: (memn, memn_r), ntiles=1, tw=NM)
    for ck in range(16):
        s = wload(wxkv_d[ck])
        ps, pr = P.bank()
        if ck < 8:
            for kc in range(KC):
                k.op("pe", lambda e, ps=ps, s=s, kc=kc: e.matmul(
                    ps[:, 0:NM], lhsT=wch[s][:, kc * 128:(kc + 1) * 128], rhs=memn[:, kc, :],
                    start=(kc == 0), stop=(kc == KC - 1)),
                    reads=[wch_r[s], memn_r], writes=[pr])
            k.op("act", lambda e, ps=ps, ck=ck: e.activation(out=KxT[:, ck, :], in_=ps[:, 0:NM], func=AF.Copy),
                 reads=[pr], writes=[Kx_r[ck]])
        else:
            c = ck - 8
            for mb in range(2):
                for kc in range(KC):
                    k.op("pe", lambda e, ps=ps, s=s, kc=kc, mb=mb: e.matmul(
                        ps[:, mb * 128:(mb + 1) * 128], lhsT=memn[:, kc, mb * 128:(mb + 1) * 128],
                        rhs=wch[s][:, kc * 128:(kc + 1) * 128], start=(kc == 0), stop=(kc == KC - 1)),
                        reads=[wch_r[s], memn_r], writes=[pr])
            k.op("act", lambda e, ps=ps, c=c: e.activation(
                out=Vx[:, :, c * 128:(c + 1) * 128], in_=ps[:, 0:256].rearrange("p (m j) -> p m j", m=2), func=AF.Copy),
                reads=[pr], writes=[Vx_r[c]])
    emit_rmsnorm(P, cst, nt, x, xres, gcross, gcross_r, lambda tt: (xn[:, :, tt * TT:(tt + 1) * TT], xn_r[tt]))
    for ck in range(KC):
        s = wload(wxq_d[ck])
        for tt in range(NT):
            sl = slice(tt * TT, (tt + 1) * TT)
            ps, pr = P.bank()
            for kc in range(KC):
                k.op("pe", lambda e, ps=ps, s=s, kc=kc, sl=sl: e.matmul(
                    ps, lhsT=wch[s][:, kc * 128:(kc + 1) * 128], rhs=xn[:, kc, sl],
                    start=(kc == 0), stop=(kc == KC - 1)),
                    reads=[wch_r[s], xn_r[tt]], writes=[pr])
            if tt % 2 == 0:
                k.op("act", lambda e, ps=ps, ck=ck, sl=sl: e.activation(out=QT[:, ck, sl], in_=ps, func=AF.Copy),
                     reads=[pr], writes=[QT_r[ck][tt]])
            else:
                k.op("dve", lambda e, ps=ps, ck=ck, sl=sl: e.tensor_copy(out=QT[:, ck, sl], in_=ps),
                     reads=[pr], writes=[QT_r[ck][tt]])
    cnt = 0
    for tt in range(NT):
        sl = slice(tt * TT, (tt + 1) * TT)
        for h in range(4):
            pp = cnt % 2
            cnt += 1
            for mb in range(2):
                ps, pr = P.bank()
                for half in range(2):
                    ck = 2 * h + half
                    k.op("pe", lambda e, ps=ps, ck=ck, mb=mb, sl=sl, half=half: e.matmul(
                        ps, lhsT=KxT[:, ck, mb * 128:(mb + 1) * 128], rhs=QT[:, ck, sl],
                        start=(half == 0), stop=(half == 1)),
                        reads=[Kx_r[ck], QT_r[ck][tt]], writes=[pr])
                k.op("act", lambda e, ps=ps, pp=pp, mb=mb: e.activation(out=Pt[pp][mb], in_=ps, func=AF.Exp,
                                                                       scale=0.0625),
                     reads=[pr], writes=[Pt_r[pp][mb]])
            ps, pr = P.bank()
            for mb in range(2):
                k.op("pe", lambda e, ps=ps, pp=pp, mb=mb: e.matmul(ps, lhsT=cst["ones_bf"], rhs=Pt[pp][mb],
                                                                   start=(mb == 0), stop=(mb == 1)),
                     reads=[cst["ones_bf_r"], Pt_r[pp][mb]], writes=[pr])
            k.op("dve", lambda e, ps=ps: e.reciprocal(out=rec, in_=ps), reads=[pr], writes=[rec_r])
            for half in range(2):
                ck = 2 * h + half
                ps, pr = P.bank()
                for mb in range(2):
                    k.op("pe", lambda e, ps=ps, pp=pp, mb=mb, ck=ck: e.matmul(
                        ps, lhsT=Vx[:, mb, ck * 128:(ck + 1) * 128], rhs=Pt[pp][mb], start=(mb == 0), stop=(mb == 1)),
                        reads=[Vx_r[ck], Pt_r[pp][mb]], writes=[pr])
                k.op("dve", lambda e, ps=ps, ck=ck, sl=sl: e.tensor_tensor(out=QT[:, ck, sl], in0=ps, in1=rec, op=ALU.mult),
                     reads=[pr, rec_r], writes=[QT_r[ck][tt]])
    for dco in range(KC):
        s = wload(wxo_d[dco])
        for tt in range(NT):
            sl = slice(tt * TT, (tt + 1) * TT)
            ps, pr = P.bank()
            for ck in range(KC):
                k.op("pe", lambda e, ps=ps, s=s, ck=ck, sl=sl: e.matmul(
                    ps, lhsT=wch[s][:, ck * 128:(ck + 1) * 128], rhs=QT[:, ck, sl],
                    start=(ck == 0), stop=(ck == KC - 1)),
                    reads=[wch_r[s], QT_r[ck][tt]], writes=[pr])
            k.op("dve", lambda e, ps=ps, dco=dco, sl=sl: e.tensor_tensor(out=x[:, dco, sl], in0=ps, in1=x[:, dco, sl],
                                                                         op=ALU.add),
                 reads=[pr, xres[tt]], writes=[xres[tt]])
    ar.release()


def pack_wC(inp, l):
    ua = inp["w_up_a"][l].reshape(4, 128, KC, 128)
    ub = inp["w_up_b"][l].reshape(8, 128, KC, 128)
    uc = inp["w_up_c"][l].reshape(4, 128, KC, 128)
    up = np.concatenate([ua, ub, uc], axis=0)
    up = up.transpose(2, 1, 0, 3).reshape(KC, 128, 2048)
    wg = inp["w_in"][l][:, 4104:].reshape(KC, 128, 3, KC, 128)
    wg = wg.transpose(3, 1, 0, 2, 4).reshape(KC, 128, 3072)
    return np.ascontiguousarray(np.concatenate([up, wg], axis=2))


def pack_bgate(inp, l):
    return np.ascontiguousarray(inp["b_gate"][l].reshape(3, KC, 128).transpose(2, 0, 1).reshape(128, 24))


def pack_sq(w):
    C = w.shape[1] // 128
    a = w.reshape(KC, 128, C, 128)
    return np.ascontiguousarray(a.transpose(2, 1, 0, 3)).reshape(C, 128, KC * 128)


def pack_memT(m):
    return np.ascontiguousarray(m.reshape(256, KC, 128).transpose(2, 1, 0))


ARENA_BYTES = 205 * 1024
RGROUPS = [[0, 1, 2, 3], [4, 5, 6, 7]]


def build(stages, fused=False):
    nc = bass.Bass("TRN2", target_bir_lowering=False)
    dram = {}

    def dten(name, shape, dt, kind):
        if name not in dram:
            dram[name] = nc.dram_tensor(name, list(shape), dt, kind=kind).ap()
        return dram[name]

    def din(name, shape, dt=F32):
        return dten(name, shape, dt, "ExternalInput")

    def dout(name, shape, dt):
        return dten(name, shape, dt, "ExternalOutput")

    def dint(name, shape, dt):
        return dten(name, shape, dt, "Internal")

    with ExitStack() as stack:
        arena_t = stack.enter_context(nc.sbuf_tensor("arena", [128, ARENA_BYTES // 4], F32))
        psum_t = stack.enter_context(nc.psum_tensor("psum", [128, 8, 512], F32))
        k = K(nc, stack)
        ar = Arena(arena_t[:, :], ARENA_BYTES)
        P = Prog(nc, k, ar, psum_t[:, :, :])
        cst = make_consts(P, din("cmask", [128, 128]))
        x = ar.alloc([KC, T], F32)
        xres = [Res() for _ in range(NT)]
        gvs = ar.alloc([16, KC], F32)
        gv_n = [0]
        pid_cache = {}
        gath_res = {}
        uall_res = {}
        ds_g = k.dsem("gv")

        def load_g(name):
            i = gv_n[0] % 16
            gv_n[0] += 1
            r = Res()
            k.dma("sp", ds_g, out=gvs[:, i, :], in_=din(name, [128, KC]), writes=[r])
            return gvs[:, i, :], r

        def sfx(l):
            return "" if l is None else "_%d" % l

        def slab_copies(d, q="sp"):
            for half in range(2):
                gq = dint("mix_gath_%d_%d" % (d, half), [4 * 256, T], BF16)
                for r in range(4):
                    mb = dint("mix_big_%d_%d" % (r, half), [4 * 256, T], BF16)

                    def dstf(e, d=d, mb=mb, q=q):
                        if q not in pid_cache:
                            pid_cache[q] = (nc.sync if q == "sp" else nc.scalar).partition_id() % 4
                        return mb[bass.ds(((d + 4 - pid_cache[q]) % 4) * 256, 256), :]
                    k.dma(q, None, out=dstf, in_=gq[r * 256:(r + 1) * 256, :], reads=[gath_res[(d, half)]])

        for stg in stages:
            k.barrier(skip_cc=(fused and stg[0] in ("ag_u", "B")))
            if stg == "load_x":
                xd = din("xT", [128, KC, T])
                for tt in range(NT):
                    sl = slice(tt * TT, (tt + 1) * TT)
                    k.dma("sp", k.dsem("x"), out=x[:, :, sl], in_=xd[:, :, sl], writes=[xres[tt]])
            elif stg == "store_x":
                xo = dout("xT_out", [128, KC, T], F32)
                for tt in range(NT):
                    sl = slice(tt * TT, (tt + 1) * TT)
                    k.dma("sp", k.dsem("xo"), out=xo[:, :, sl], in_=x[:, :, sl], reads=[xres[tt]])
            elif stg[0] == "ffn":
                _, l, i = stg
                ar.mark()
                gv, gr = load_g("gffn%s_%d" % (sfx(l), i))
                xn = ar.alloc([KC, T], BF16); xn_r = [Res() for _ in range(NT)]
                nt = alloc_normtmp(P)
                w1buf = [ar.alloc([KC * 256], BF16) for _ in range(3)]; w1res = [Res(), Res(), Res()]
                w2buf = [ar.alloc([NF * 128], BF16) for _ in range(3)]; w2res = [Res(), Res(), Res()]
                g = ar.alloc([NF, 2 * TT], BF16); gres2 = [[Res(), Res()] for _ in range(NF)]
                stmp = [ar.alloc([TT], F32) for _ in range(2)]; stres = [Res(), Res()]
                emit_rmsnorm(P, cst, nt, x, xres, gv, gr, lambda tt: (xn[:, :, tt * TT:(tt + 1) * TT], xn_r[tt]))
                emit_ffn(P, x, xres, xn, xn_r, din("w1%s_%d" % (sfx(l), i), [NF, 128, KC * 256]),
                         din("w2%s_%d" % (sfx(l), i), [KC, 128, NF * 128]),
                         (w1buf, w1res, w2buf, w2res, g, gres2, stmp, stres), "f")
                ar.release()
            elif stg[0] == "unorm":
                _, l = stg
                ar.mark()
                gv, gr = load_g("gmix%s" % sfx(l))
                xn = ar.alloc([KC, T], BF16); xn_r = [Res() for _ in range(NT)]
                nt = alloc_normtmp(P)
                dsu = k.dsem("ust")
                if fused:
                    def after(tt, o, ores):
                        um = dint("u_mine_%d" % tt, [128, KC * TT], BF16)
                        ua = dint("u_all_%d" % tt, [4 * 128, KC * TT], BF16)
                        ur = Res()
                        k.dma("sp", dsu, out=um.rearrange("p (k t) -> p k t", k=KC), in_=o, reads=[ores], writes=[ur])
                        uall_res[tt] = Res()
                        k.coll("AllGather", RGROUPS, ins=[um], outs=[ua], reads=[ur], writes=[uall_res[tt]])
                else:
                    ud = dout("u_mine%s" % sfx(l), [128, KC, T], BF16)

                    def after(tt, o, ores, ud=ud, dsu=dsu):
                        sl = slice(tt * TT, (tt + 1) * TT)
                        k.dma("sp", dsu, out=ud[:, :, sl], in_=o, reads=[ores], writes=[])

                emit_rmsnorm(P, cst, nt, x, xres, gv, gr, lambda tt: (xn[:, :, tt * TT:(tt + 1) * TT], xn_r[tt]),
                             after=after)
                ar.release()
            elif stg[0] == "B":
                _, l = stg
                if fused:
                    def usrc(r, tt):
                        ua = dint("u_all_%d" % tt, [4 * 128, KC * TT], BF16)
                        return ua[r * 128:(r + 1) * 128, :].rearrange("p (k t) -> p k t", k=KC)

                    def mm(d, half):
                        return dint("mix_mine_%d_%d" % (d, half), [256, T], BF16)

                    def mdst(kind, r, tsl, h=0):
                        if kind == "A":
                            return [(mm(r, 0)[0:128, tsl], slice(0, 128))]
                        if kind == "B":
                            return [(mm(r, 0)[128:256, tsl], slice(0, 128), 0), (mm(r, 1)[0:128, tsl], slice(0, 128), 1)]
                        return [(mm(r, 1)[128 + 64 * h:192 + 64 * h, tsl], slice(0, 64))]

                    def tile_done(gt, store_res):
                        if gt % 4 == 3:
                            d = gt // 4
                            for half in range(2):
                                gq = dint("mix_gath_%d_%d" % (d, half), [4 * 256, T], BF16)
                                gres = Res()
                                gath_res[(d, half)] = gres
                                k.coll("AllGather", RGROUPS, ins=[mm(d, half)], outs=[gq], reads=list(store_res),
                                       writes=[gres])
                            del store_res[:]
                        if gt % 4 == 1 and gt >= 5:
                            slab_copies(gt // 4 - 1, "act")
                else:
                    u_all = din("u_all%s" % sfx(l), [4, 128, KC, T], BF16)
                    mix = dout("mix_mine%s" % sfx(l), [4, 512, T], BF16)

                    def usrc(r, tt, u_all=u_all):
                        return u_all[r, :, :, tt * TT:(tt + 1) * TT]

                    def mdst(kind, r, tsl, h=0, mix=mix):
                        if kind == "A":
                            return [(mix[r, 0:128, tsl], slice(0, 128))]
                        if kind == "B":
                            return [(mix[r, 128 + 128 * hh:256 + 128 * hh, tsl], slice(0, 128), hh) for hh in range(2)]
                        return [(mix[r, 384 + 64 * h:448 + 64 * h, tsl], slice(0, 64))]
                emit_phaseB(P, cst, usrc, din("wB%s" % sfx(l), [128, KC, NCB]), din("wsB%s" % sfx(l), [128, 640]),
                            din("vecB%s" % sfx(l), [128, NVB]), mdst, tile_done if fused else None,
                            (lambda tt: [uall_res[tt]]) if fused else None)
            elif stg[0] == "C":
                _, l = stg
                if fused:
                    def umine(tt):
                        return dint("u_mine_%d" % tt, [128, KC * TT], BF16).rearrange("p (k t) -> p k t", k=KC)

                    def mix_all(r, piece, cs):
                        mb = dint("mix_big_%d_%d" % (r, piece // 2), [4 * 256, T], BF16)
                        return mb[128 * (piece % 2):128 * (piece % 2) + 128, cs]
                else:
                    ud = din("u_mine%s" % sfx(l), [128, KC, T], BF16)
                    mix_all = din("mix_all%s" % sfx(l), [4, 512, T], BF16)

                    def umine(tt, ud=ud):
                        return ud[:, :, tt * TT:(tt + 1) * TT]
                emit_phaseC(P, cst, x, xres, umine, mix_all, din("wC%s" % sfx(l), [KC, 128, 5120]),
                            din("bgate%s" % sfx(l), [128, 24]), din("wo%s" % sfx(l), [KC, 128, 1024]))
            elif stg[0] == "cross":
                _, l = stg
                gm, gmr = load_g("gmem%s" % sfx(l))
                gc, gcr = load_g("gcross%s" % sfx(l))
                emit_cross(P, cst, x, xres, din("memT", [128, KC, 256]), gm, gmr, gc, gcr,
                           din("wxq%s" % sfx(l), [KC, 128, 1024]), din("wxkv%s" % sfx(l), [16, 128, 1024]),
                           din("wxo%s" % sfx(l), [KC, 128, 1024]))
            elif stg[0] == "ag_u":
                pass
            elif stg[0] == "ag_mix":
                slab_copies(3)
            elif stg == "final":
                ar.mark()
                gv, gr = load_g("gfin")
                nt = alloc_normtmp(P)
                stg_t = [ar.alloc([KC, TT], F32) for _ in range(2)]; stg_r = [Res(), Res()]
                od = dout("outT", [128, KC, T], F32)
                dso = k.dsem("out")

                def after(tt, o, ores, od=od, dso=dso):
                    sl = slice(tt * TT, (tt + 1) * TT)
                    k.dma("sp", dso, out=od[:, :, sl], in_=o, reads=[ores], writes=[])

                emit_rmsnorm(P, cst, nt, x, xres, gv, gr, lambda tt: (stg_t[tt % 2], stg_r[tt % 2]), after=after)
                ar.release()
            else:
                raise ValueError(stg)
        k.barrier()
        k.final_wait("sp")
        k.emit()
    return nc


_PROGS = {}


def get_prog(key, stages, fused=False):
    if key not in _PROGS:
        _PROGS[key] = build(stages, fused)
    return _PROGS[key]


def _run(nc, maps):
    res = run_bass_kernel_spmd(nc, maps, core_ids=list(range(NCORES)))
    return res.results


def _weights_ffn(inp, l, i, names=None):
    sfx = "_%d_%d" % (l, i)
    if i == 1:
        w_in, w_out, g = inp["w_ffn1_in"], inp["w_ffn1_out"], inp["g_ffn1"]
    else:
        w_in, w_out, g = inp["w_ffn2_in"], inp["w_ffn2_out"], inp["g_ffn2"]
    return {"w1" + sfx: pack_w1(w_in[l]), "w2" + sfx: pack_w2(w_out[l]), "gffn" + sfx: pack_vec(g[l])}


def _weights_C(inp, l):
    s = "_%d" % l
    return {"wC" + s: pack_wC(inp, l), "bgate" + s: pack_bgate(inp, l), "wo" + s: pack_sq(inp["w_o"][l]),
            "gmem" + s: pack_vec(inp["g_mem"][l]), "gcross" + s: pack_vec(inp["g_cross"][l]),
            "wxq" + s: pack_sq(inp["w_xq"][l]), "wxkv" + s: pack_sq(inp["w_xkv"][l]), "wxo" + s: pack_sq(inp["w_xo"][l])}


FUSED = True


def fused_stages():
    st = ["load_x"]
    for l in range(L):
        st += [("ffn", l, 1), ("unorm", l), ("ag_u", l), ("B", l), ("ag_mix", l), ("C", l), ("cross", l), ("ffn", l, 2)]
    st.append("final")
    return st


def kernel_fused(inp):
    cmask = const_mask()
    xs = inp["x"].reshape(NCORES, T, D)
    w = {"cmask": cmask, "gfin": pack_vec(inp["g_final"])}
    wBs = {}
    for l in range(L):
        w.update(_weights_ffn(inp, l, 1))
        w.update(_weights_ffn(inp, l, 2))
        w.update(_weights_C(inp, l))
        w["gmix_%d" % l] = pack_vec(inp["g_mix"][l])
        for j in range(4):
            wBs[(l, j)] = {"wB_%d" % l: pack_wB(inp["w_in"][l], j), "wsB_%d" % l: pack_wsB(inp, l, j),
                           "vecB_%d" % l: pack_vecB(inp, l, j)}
    memT = [pack_memT(inp["mem"][b]) for b in range(2)]
    maps = []
    for c in range(NCORES):
        b, j = c // 4, c % 4
        m = dict(w)
        for l in range(L):
            m.update(wBs[(l, j)])
        m["xT"] = pack_xT(xs[c])
        m["memT"] = memT[b]
        maps.append(m)
    prog = get_prog("fused", fused_stages(), fused=True)
    r = _run(prog, maps)
    out = np.stack([unpack_xT(r[c]["outT"]) for c in range(NCORES)], axis=0)
    return out.reshape(2, 8192, D).astype(np.float32)


def kernel(**inp):
    inp = {k_: np.asarray(v) for k_, v in inp.items()}
    if FUSED:
        return kernel_fused(inp)
    cmask = const_mask()
    xs = inp["x"].reshape(NCORES, T, D)
    p1 = get_prog("p1", ["load_x", ("ffn", 0, 1), ("unorm", 0), "store_x"])
    w = {"cmask": cmask, "gmix_0": pack_vec(inp["g_mix"][0])}
    w.update(_weights_ffn(inp, 0, 1))
    r = _run(p1, [dict(w, xT=pack_xT(xs[c])) for c in range(NCORES)])
    xT = [r[c]["xT_out"] for c in range(NCORES)]
    um = [r[c]["u_mine_0"] for c in range(NCORES)]
    out = None
    for l in range(L):
        pB = get_prog("pB", [("B", None)])
        maps = []
        for c in range(NCORES):
            b, j = c // 4, c % 4
            maps.append({"cmask": cmask, "u_all": np.stack([um[4 * b + rr] for rr in range(4)], axis=0),
                         "wB": pack_wB(inp["w_in"][l], j), "wsB": pack_wsB(inp, l, j), "vecB": pack_vecB(inp, l, j)})
        r = _run(pB, maps)
        mix = [r[c]["mix_mine"] for c in range(NCORES)]
        w = {"cmask": cmask}
        w.update(_weights_C(inp, l))
        w.update(_weights_ffn(inp, l, 2))
        if l + 1 < L:
            stages = ["load_x", ("C", l), ("cross", l), ("ffn", l, 2), ("ffn", l + 1, 1), ("unorm", l + 1), "store_x"]
            w.update(_weights_ffn(inp, l + 1, 1))
            w["gmix_%d" % (l + 1)] = pack_vec(inp["g_mix"][l + 1])
        else:
            stages = ["load_x", ("C", l), ("cross", l), ("ffn", l, 2), "final"]
            w["gfin"] = pack_vec(inp["g_final"])
        pC = get_prog("pC%d" % l, stages)
        maps = []
        for c in range(NCORES):
            b, cc = c // 4, c % 4
            m = dict(w)
            m["xT"] = xT[c]
            m["u_mine_%d" % l] = um[c]
            m["mix_all_%d" % l] = np.stack([mix[4 * b + j][cc] for j in range(4)], axis=0)
            m["memT"] = pack_memT(inp["mem"][b])
            maps.append(m)
        r = _run(pC, maps)
        if l + 1 < L:
            xT = [r[c]["xT_out"] for c in range(NCORES)]
            um = [r[c]["u_mine_%d" % (l + 1)] for c in range(NCORES)]
        else:
            out = np.stack([unpack_xT(r[c]["outT"]) for c in range(NCORES)], axis=0)
    return out.reshape(2, 8192, D).astype(np.float32)
```

```python
import numpy as np
from contextlib import ExitStack
import concourse.bass as bass
import concourse.mybir as mybir
from concourse.bass_utils import run_bass_kernel_spmd

F32 = mybir.dt.float32
BF16 = mybir.dt.bfloat16
AF = mybir.ActivationFunctionType
ALU = mybir.AluOpType
AX = mybir.AxisListType

D = 1024
KC = 8
T = 2048
TT = 512
NT = T // TT
DFF = 2816
NF = DFF // 128
EPS = 1e-6
L = 2
NCORES = 8


class Res:
    __slots__ = ("name", "w", "r")

    def __init__(self, name=""):
        self.name = name
        self.w = None
        self.r = {}


class DSem:
    __slots__ = ("key", "sem", "count")

    def __init__(self, key, sem):
        self.key = key
        self.sem = sem
        self.count = 0


class Eng:
    def __init__(self, name, sem):
        self.name = name
        self.sem = sem
        self.count = 0
        self.seen = {}
        self.q = []


class K:
    def __init__(self, nc, stack):
        self.nc = nc
        self.stack = stack
        self.engs = {}
        for n in ("pe", "act", "dve", "pool", "sp"):
            self.engs[n] = Eng(n, stack.enter_context(nc.semaphore("sem_" + n)))
        self.dsems = {}
        self.ninst = 0
        self.sched = {}

    NPOOL = 64

    def semname(self, sem):
        for e in self.engs.values():
            if e.sem is sem:
                return e.name
        for d in self.dsems.values():
            if d.sem is sem:
                return d.key
        return "?"

    def simulate(self):
        cnt = {}
        pos = {e: 0 for e in self.sched}
        progress = True
        while progress:
            progress = False
            for e, lst in self.sched.items():
                while pos[e] < len(lst):
                    waits, key, inc = lst[pos[e]]
                    if all(cnt.get(kk, 0) >= v for kk, v in waits):
                        cnt[key] = cnt.get(key, 0) + inc
                        pos[e] += 1
                        progress = True
                    else:
                        break
        stuck = {e: (pos[e], len(lst), lst[pos[e]][0] if pos[e] < len(lst) else None) for e, lst in self.sched.items()}
        return stuck, cnt

    def dsem(self, key):
        return None

    def _pool_sem(self):
        if not hasattr(self, "dpool"):
            self.dpool = []
            self.dnext = 0
        if len(self.dpool) < self.NPOOL:
            i = len(self.dpool)
            d = DSem("dp%d" % i, self.stack.enter_context(self.nc.semaphore("dsem_p%d" % i)))
            self.dpool.append(d)
            self.dsems[d.key] = d
            return d
        d = self.dpool[self.dnext % self.NPOOL]
        self.dnext += 1
        return d

    def _waits(self, eng, reads, writes):
        deps = {}

        def add(tok):
            if tok is None:
                return
            key, sem, val = tok
            if key not in deps or deps[key][1] < val:
                deps[key] = (sem, val)

        for r in reads:
            add(r.w)
        for w in writes:
            add(w.w)
            for tok in w.r.values():
                add(tok)
        out = []
        for key, (sem, val) in deps.items():
            if eng.seen.get(key, 0) >= val:
                continue
            if key == "pe" and eng.name == "pe":
                continue
            eng.seen[key] = val
            out.append((sem, val))
        return out

    def op(self, ename, fn, reads=(), writes=()):
        eng = self.engs[ename]
        waits = self._waits(eng, reads, writes)
        sem = eng.sem
        eng.count += 1
        tok = (ename, sem, eng.count)

        def run(e, waits=waits, fn=fn, sem=sem):
            for s, v in waits:
                e.wait_ge(s, v)
            fn(e).then_inc(sem, 1)

        eng.q.append(run)
        self.sched.setdefault(ename, []).append(([(self.semname(s), v) for s, v in waits], ename, 1))
        for r in reads:
            r.r[ename] = tok
        for w in writes:
            w.w = tok
            w.r = {}
        self.ninst += 1

    def dma(self, qname, ds, out, in_, reads=(), writes=(), **kw):
        eng = self.engs[qname]
        ds = self._pool_sem()
        waits = self._waits(eng, reads, writes)
        if ds.count > 0 and eng.seen.get(ds.key, 0) < ds.count:
            eng.seen[ds.key] = ds.count
            waits.append((ds.sem, ds.count))
        ds.count += 16
        tok = (ds.key, ds.sem, ds.count)

        def run(e, waits=waits, out=out, in_=in_, kw=kw, sem=ds.sem, cnt=ds.count):
            for s, v in waits:
                e.wait_ge(s, v)
            e.dma_start(out=out(e) if callable(out) else out, in_=in_(e) if callable(in_) else in_,
                        **kw).then_inc(sem, 16)

        eng.q.append(run)
        self.sched.setdefault(qname, []).append(([(self.semname(s), v) for s, v in waits], ds.key, 16))
        for r in reads:
            r.r[ds.key] = tok
        for w in writes:
            w.w = tok
            w.r = {}
        self.ninst += 1

    def coll(self, kind, groups, ins, outs, reads=(), writes=()):
        eng = self.engs["pool"]
        i = len([k_ for k_ in self.dsems if k_.startswith("cc")])
        ds = DSem("cc%d" % i, self.stack.enter_context(self.nc.semaphore("ccsem_%d" % i)))
        self.dsems[ds.key] = ds
        waits = self._waits(eng, reads, writes)
        ds.count += 1
        tok = (ds.key, ds.sem, ds.count)

        def run(e, waits=waits, sem=ds.sem):
            for s, v in waits:
                e.wait_ge(s, v)
            e.collective_compute(kind, ALU.bypass, replica_groups=groups, ins=[a.opt() for a in ins],
                                 outs=[a.opt() for a in outs], dma_qos="P2").then_inc(sem, 1)

        eng.q.append(run)
        self.sched.setdefault("pool", []).append(([(self.semname(s), v) for s, v in waits], ds.key, 1))
        for r in reads:
            r.r[ds.key] = tok
        for w in writes:
            w.w = tok
            w.r = {}

    def barrier(self, skip_cc=False):
        toks = [(e.name, e.sem, e.count) for e in self.engs.values() if e.count > 0]
        toks += [(d.key, d.sem, d.count) for d in self.dsems.values()
                 if d.count > 0 and not (skip_cc and d.key.startswith("cc"))]
        for eng in self.engs.values():
            waits = []
            for key, sem, val in toks:
                if eng.seen.get(key, 0) >= val:
                    continue
                eng.seen[key] = val
                waits.append((sem, val))

            def run(e, waits=waits):
                for s, v in waits:
                    e.wait_ge(s, v)

            eng.q.append(run)
            self.sched.setdefault(eng.name, []).append(([(self.semname(s), v) for s, v in waits], "_nop", 0))

    def final_wait(self, qname="sp"):
        eng = self.engs[qname]
        toks = [(d.key, d.sem, d.count) for d in self.dsems.values() if d.count > 0]
        toks += [(e.name, e.sem, e.count) for e in self.engs.values() if e.count > 0]

        def run(e, toks=toks):
            for key, s, v in toks:
                e.wait_ge(s, v)

        eng.q.append(run)

    def emit(self):
        nc = self.nc
        with nc.Block() as block:
            @block.tensor
            def _(e):
                for f in self.engs["pe"].q:
                    f(e)

            @block.scalar
            def _(e):
                for f in self.engs["act"].q:
                    f(e)

            @block.vector
            def _(e):
                for f in self.engs["dve"].q:
                    f(e)

            @block.gpsimd
            def _(e):
                for f in self.engs["pool"].q:
                    f(e)

            @block.sync
            def _(e):
                for f in self.engs["sp"].q:
                    f(e)


class Arena:
    def __init__(self, ap_f32, nbytes):
        self.ap = ap_f32
        self.nbytes = nbytes
        self.off = 0
        self.marks = []

    def alloc(self, shape_free, dtype, parts=128):
        n = int(np.prod(shape_free))
        esz = 2 if dtype == BF16 else 4
        nb = (n * esz + 31) // 32 * 32
        assert self.off + nb <= self.nbytes, ("arena overflow", self.off, nb, self.nbytes)
        a = self.ap[:, self.off // 4:(self.off + nb) // 4]
        self.off += nb
        if dtype == BF16:
            a = a.bitcast(BF16)
        a = a[:, 0:n]
        if len(shape_free) == 2:
            a = a.rearrange("p (a b) -> p a b", a=shape_free[0])
        elif len(shape_free) == 3:
            a = a.rearrange("p (a b c) -> p a b c", a=shape_free[0], b=shape_free[1])
        return a

    def mark(self):
        self.marks.append(self.off)

    def release(self):
        self.off = self.marks.pop()


class Prog:
    def __init__(self, nc, k, arena, psum):
        self.nc = nc
        self.k = k
        self.ar = arena
        self.psum = psum
        self.pres = [Res("ps%d" % i) for i in range(8)]
        self.pidx = 0

    def bank(self):
        i = self.pidx
        self.pidx = (self.pidx + 1) % 8
        return self.psum[:, i, :], self.pres[i]


def alloc_normtmp(P, tw=TT):
    ar = P.ar
    return dict(sq=ar.alloc([KC, tw], F32), sq_r=[Res() for _ in range(KC)],
                rstd=ar.alloc([tw], F32), rstd_r=Res())


def emit_rmsnorm(P, cst, nt, x, xres, gvec, gres, dst, ntiles=NT, tw=TT, after=None):
    k = P.k
    ones, ones_res = cst["ones_f"], cst["ones_f_r"]
    sq, sq_res, rstd, rstd_res = nt["sq"], nt["sq_r"], nt["rstd"], nt["rstd_r"]
    for tt in range(ntiles):
        sl = slice(tt * tw, (tt + 1) * tw)
        o, ores = dst(tt)
        ps, pr = P.bank()
        for kc in range(KC):
            k.op("act", lambda e, sl=sl, kc=kc: e.activation(out=sq[:, kc, 0:tw], in_=x[:, kc, sl], func=AF.Square),
                 reads=[xres[tt]], writes=[sq_res[kc]])
            k.op("pe", lambda e, ps=ps, kc=kc: e.matmul(ps[:, 0:tw], lhsT=ones, rhs=sq[:, kc, 0:tw],
                                                        start=(kc == 0), stop=(kc == KC - 1)),
                 reads=[sq_res[kc], ones_res], writes=[pr])
        k.op("act", lambda e, ps=ps: e.activation(out=rstd[:, 0:tw], in_=ps[:, 0:tw], func=AF.Ln,
                                                  bias=cst["eps"], scale=1.0 / D),
             reads=[pr, cst["eps_r"]], writes=[rstd_res])
        k.op("act", lambda e: e.activation(out=rstd[:, 0:tw], in_=rstd[:, 0:tw], func=AF.Exp, scale=-0.5),
             reads=[rstd_res], writes=[rstd_res])
        for kc in range(KC):
            k.op("dve", lambda e, kc=kc, sl=sl, o=o: e.scalar_tensor_tensor(
                out=o[:, kc, :], in0=x[:, kc, sl], scalar=gvec[:, kc:kc + 1], in1=rstd[:, 0:tw],
                op0=ALU.mult, op1=ALU.mult),
                reads=[xres[tt], gres, rstd_res], writes=[ores])
        if after is not None:
            after(tt, o, ores)


def consts_eps(P):
    return P.eps_ap


def emit_ffn(P, x, xres, xn, xnres, w1d, w2d, wb, tag):
    k = P.k
    (w1buf, w1res, w2buf, w2res, g, gres, stmp, stres) = wb
    ds1 = [None] * 3
    ds2 = [None] * 3
    NB1 = len(w1buf)
    NB2 = len(w2buf)
    cnt1 = 0
    cnt2 = 0
    for st in range(2):
        for f in range(NF):
            s = cnt1 % NB1
            cnt1 += 1
            k.dma("pool", ds1[s], out=w1buf[s], in_=w1d[f], reads=[], writes=[w1res[s]])
            for t2 in range(2):
                tt = st * 2 + t2
                sl = slice(tt * TT, (tt + 1) * TT)
                pa, par = P.bank()
                pb, pbr = P.bank()
                for kc in range(KC):
                    k.op("pe", lambda e, pa=pa, s=s, kc=kc, sl=sl: e.matmul(
                        pa, lhsT=w1buf[s][:, kc * 256:kc * 256 + 128], rhs=xn[:, kc, sl],
                        start=(kc == 0), stop=(kc == KC - 1)),
                        reads=[w1res[s], xnres[tt]], writes=[par])
                for kc in range(KC):
                    k.op("pe", lambda e, pb=pb, s=s, kc=kc, sl=sl: e.matmul(
                        pb, lhsT=w1buf[s][:, kc * 256 + 128:kc * 256 + 256], rhs=xn[:, kc, sl],
                        start=(kc == 0), stop=(kc == KC - 1)),
                        reads=[w1res[s], xnres[tt]], writes=[pbr])
                ts_ = (f * 2 + t2) % 2
                k.op("act", lambda e, pa=pa, ts_=ts_: e.activation(out=stmp[ts_], in_=pa, func=AF.Silu),
                     reads=[par], writes=[stres[ts_]])
                k.op("dve", lambda e, pb=pb, ts_=ts_, f=f, t2=t2: e.tensor_tensor(
                    out=g[:, f, t2 * TT:(t2 + 1) * TT], in0=stmp[ts_], in1=pb, op=ALU.mult),
                    reads=[stres[ts_], pbr], writes=[gres[f][t2]])
        for dc in range(KC):
            s = cnt2 % NB2
            cnt2 += 1
            k.dma("pool", ds2[s], out=w2buf[s], in_=w2d[dc], reads=[], writes=[w2res[s]],
                  max_dma_last_dim=1408 * 4)
            for t2 in range(2):
                tt = st * 2 + t2
                sl = slice(tt * TT, (tt + 1) * TT)
                po, por = P.bank()
                for f in range(NF):
                    k.op("pe", lambda e, po=po, s=s, f=f, t2=t2: e.matmul(
                        po, lhsT=w2buf[s][:, f * 128:(f + 1) * 128], rhs=g[:, f, t2 * TT:(t2 + 1) * TT],
                        start=(f == 0), stop=(f == NF - 1)),
                        reads=[w2res[s], gres[f][t2]], writes=[por])
                k.op("dve", lambda e, po=po, dc=dc, sl=sl: e.scalar_tensor_tensor(
                    out=x[:, dc, sl], in0=po, scalar=0.5, in1=x[:, dc, sl], op0=ALU.mult, op1=ALU.add),
                    reads=[por, xres[tt]], writes=[xres[tt]])


def pack_w1(w):
    a = w[:, :DFF].reshape(KC, 128, NF, 128)
    b = w[:, DFF:].reshape(KC, 128, NF, 128)
    ab = np.stack([a, b], axis=3)
    return np.ascontiguousarray(ab.transpose(2, 1, 0, 3, 4)).reshape(NF, 128, KC * 256)


def pack_w2(w):
    a = w.reshape(NF, 128, KC, 128)
    return np.ascontiguousarray(a.transpose(2, 1, 0, 3)).reshape(KC, 128, NF * 128)


def pack_vec(v):
    return np.ascontiguousarray(v.reshape(KC, 128).T)


def pack_xT(xc):
    return np.ascontiguousarray(xc.reshape(T, KC, 128).transpose(2, 1, 0))


def unpack_xT(a):
    return np.ascontiguousarray(a.transpose(2, 1, 0)).reshape(T, D)


def make_consts(P, mask_d):
    k, ar = P.k, P.ar
    c = {}
    c["ones_f"] = ar.alloc([128], F32); c["ones_f_r"] = Res()
    c["ones_bf"] = ar.alloc([128], BF16); c["ones_bf_r"] = Res()
    c["mask"] = ar.alloc([128], BF16); c["mask_r"] = Res()
    c["eps"] = ar.alloc([1], F32); c["eps_r"] = Res()
    P.eps_ap = c["eps"]; P.eps_res = c["eps_r"]
    k.op("pool", lambda e: e.memset(c["ones_f"], 1.0), writes=[c["ones_f_r"]])
    k.op("pool", lambda e: e.memset(c["ones_bf"], 1.0), writes=[c["ones_bf_r"]])
    k.op("pool", lambda e: e.memset(c["eps"], EPS), writes=[c["eps_r"]])
    k.dma("pool", k.dsem("cmask"), out=c["mask"], in_=mask_d, writes=[c["mask_r"]])
    return c


NGT = 16
NCB = 1026
VB_PSCALE = 0
VB_CONVW = 1
VB_CONVB = 9
VB_BA = 11
VB_BX = 13
VB_LAM = 15
VB_BF = 17
VB_SEL = 19
VB_CORR = 23
NVB = 39


def emit_phaseB(P, cst, usrc, wB_d, wsB_d, vecB_d, mdst, tile_done=None, ures=None, mid_hook=None):
    k = P.k
    ar = P.ar
    ar.mark()
    ones_f, ones_f_r, mask, mask_r = cst["ones_f"], cst["ones_f_r"], cst["mask"], cst["mask_r"]
    GB = [5, 6, 7]
    SB = [2, 3, 4]
    OB = [0, 1]
    gcnt = [0]
    scnt = [0]

    def gbank():
        i = GB[gcnt[0] % 3]
        gcnt[0] += 1
        return P.psum[:, i, :], P.pres[i]

    def sbank():
        i = SB[scnt[0] % 3]
        scnt[0] += 1
        return P.psum[:, i, :], P.pres[i]

    utile = ar.alloc([KC, TT], BF16); utile_r = Res()
    wB = ar.alloc([KC, NCB], BF16); wB_r = Res()
    wsB = ar.alloc([640], BF16); wsB_r = Res()
    vec = ar.alloc([NVB], F32); vec_r = Res()
    nbf = ar.alloc([2], F32); nbf_r = Res()
    nsp8 = ar.alloc([2], F32); nsp8_r = Res()
    Kaug = [ar.alloc([NGT * TT], BF16) for _ in range(2)]
    K_r = [[Res() for _ in range(NGT)] for _ in range(2)]
    Vaug = ar.alloc([NGT * 4, 2, 65], BF16)
    V_r = [Res() for _ in range(NGT)]
    negc = ar.alloc([NGT * 4, 2], F32)
    negc_r = [[Res() for _ in range(NGT)] for _ in range(2)]
    xa_buf = ar.alloc([528], F32); xa_r = Res()
    sA = ar.alloc([528], F32); sA_r = Res()
    sBf = ar.alloc([528], F32); sB_r = Res()
    acc = ar.alloc([TT], F32); acc_r = Res()
    d_bf = [ar.alloc([TT], BF16) for _ in range(2)]; d_r = [Res(), Res()]
    xb_buf = [ar.alloc([515], F32) for _ in range(2)]; xb_r = [Res(), Res()]
    xc = [[ar.alloc([TT], F32) for _ in range(2)] for _ in range(2)]
    xc_r = [[Res(), Res()] for _ in range(2)]
    xc_bf = [[ar.alloc([TT], BF16) for _ in range(2)] for _ in range(2)]
    xcb_r = [[Res(), Res()] for _ in range(2)]
    gg = [[ar.alloc([TT], BF16) for _ in range(2)] for _ in range(2)]
    gg_r = [[Res(), Res()] for _ in range(2)]
    t1 = ar.alloc([TT], F32); t1_r = Res()
    Qaug = [[ar.alloc([TT], BF16) for _ in range(2)] for _ in range(2)]
    Q_r = [[Res(), Res()] for _ in range(2)]
    rowe = ar.alloc([TT], F32); rowe_r = Res()
    crow = [ar.alloc([TT], F32) for _ in range(2)]; crow_r = [Res(), Res()]
    clast = ar.alloc([2], F32); clast_r = [Res(), Res()]
    ra = ar.alloc([TT], F32); ra_r = Res()
    tmp = ar.alloc([TT], F32); tmp_r = Res()
    ig = ar.alloc([TT], F32); ig_r = Res()
    hbuf = ar.alloc([TT], F32); hbuf_r = Res()
    hlast = ar.alloc([2], F32); hlast_r = [Res(), Res()]
    Pt = [ar.alloc([TT], BF16) for _ in range(3)]; Pt_r = [Res(), Res(), Res()]
    rec = ar.alloc([TT], F32); rec_r = Res()
    osb = ar.alloc([TT], F32); osb_r = Res()
    outA = [ar.alloc([TT], BF16)] * 2; outA_r = [Res()] * 2
    outB = [ar.alloc([2, TT], BF16) for _ in range(2)]; outB_r = [Res(), Res()]
    outC = [[ar.alloc([TT], BF16)] * 2 for _ in range(2)]
    outC_r = [[Res()] * 2 for _ in range(2)]
    ds_u = k.dsem("bu")
    ds_o = k.dsem("bo")
    store_res = {0: [], 1: []}

    k.dma("pool", k.dsem("bw"), out=wB, in_=wB_d, writes=[wB_r])
    k.dma("pool", k.dsem("bws"), out=wsB, in_=wsB_d, writes=[wsB_r])
    k.dma("sp", k.dsem("bv"), out=vec, in_=vecB_d, writes=[vec_r])
    for h in range(2):
        k.op("pool", lambda e, h=h: e.memset(Kaug[h][64:65, :], 1.0), writes=K_r[h])
    k.op("pool", lambda e: e.memset(Vaug[:, :, :, 64:65], 1.0), writes=V_r)
    k.op("dve", lambda e: e.tensor_scalar(out=nbf, in0=vec[:, VB_BF:VB_BF + 2], scalar1=-1.0, scalar2=None,
                                          op0=ALU.mult), reads=[vec_r], writes=[nbf_r])
    k.op("act", lambda e: e.activation(out=nsp8, in_=vec[:, VB_LAM:VB_LAM + 2], func=AF.Exp, scale=-1.0),
         reads=[vec_r], writes=[nsp8_r])
    k.op("act", lambda e: e.activation(out=nsp8, in_=nsp8, func=AF.Ln, bias=ones_f[:, 0:1], scale=1.0),
         reads=[nsp8_r, ones_f_r], writes=[nsp8_r])
    k.op("dve", lambda e: e.tensor_scalar(out=nsp8, in0=nsp8, scalar1=-8.0, scalar2=None, op0=ALU.mult),
         reads=[nsp8_r], writes=[nsp8_r])

    def bank_any():
        return gbank()

    def S1(gt):
        r, tt = gt // 4, gt % 4
        par = gt % 2
        C = {}

        def proj(c0, ncol, nparts):
            ps, pr = gbank()
            for kc in range(KC):
                k.op("pe", lambda e, ps=ps, kc=kc: e.matmul(ps[0:nparts, :], lhsT=wB[:, kc, c0:c0 + ncol],
                                                          rhs=utile[:, kc, :], start=(kc == 0), stop=(kc == KC - 1)),
                     reads=[wB_r, utile_r], writes=[pr])
            return ps, pr

        def m_loadnext():
            if gt + 1 < NGT:
                r2, tt2 = (gt + 1) // 4, (gt + 1) % 4
                k.dma("sp", ds_u, out=utile, in_=usrc(r2, tt2), reads=(ures(tt2) if ures else []), writes=[utile_r])
        C["loadnext"] = [m_loadnext]

        st = {}

        def xa1():
            if gt == 0:
                k.op("dve", lambda e: e.memset(xa_buf[:, 0:16], 0.0), writes=[xa_r])
            else:
                k.op("dve", lambda e: e.tensor_copy(out=xa_buf[:, 0:16], in_=xa_buf[:, 512:528]),
                     reads=[xa_r], writes=[xa_r])
            st["xa"] = proj(0, 128, 128)

        def xa2():
            ps, pr = st["xa"]
            k.op("dve", lambda e: e.tensor_copy(out=xa_buf[:, 16:528], in_=ps), reads=[pr], writes=[xa_r])

            def padd(dst, dr, srcb, sr, lo, sh):
                k.op("dve", lambda e: e.tensor_tensor(out=dst[:, lo:528], in0=srcb[:, lo:528],
                                                      in1=srcb[:, lo - sh:528 - sh], op=ALU.add),
                     reads=[sr], writes=[dr])

            def pacc(srcb, sr, i):
                if i == 0:
                    k.op("dve", lambda e: e.tensor_scalar(out=acc, in0=srcb[:, 16:528],
                                                          scalar1=vec[:, VB_SEL:VB_SEL + 1], scalar2=None, op0=ALU.mult),
                         reads=[sr, vec_r], writes=[acc_r])
                else:
                    k.op("dve", lambda e: e.scalar_tensor_tensor(out=acc, in0=srcb[:, 16:528],
                                                                 scalar=vec[:, VB_SEL + i:VB_SEL + i + 1], in1=acc,
                                                                 op0=ALU.mult, op1=ALU.add),
                         reads=[sr, vec_r, acc_r], writes=[acc_r])

            padd(sA, sA_r, xa_buf, xa_r, 1, 1)
            pacc(sA, sA_r, 0)
            padd(sBf, sB_r, sA, sA_r, 3, 2)
            pacc(sBf, sB_r, 1)
            padd(sA, sA_r, sBf, sB_r, 7, 4)
            pacc(sA, sA_r, 2)
            padd(sBf, sB_r, sA, sA_r, 15, 8)
            pacc(sBf, sB_r, 3)
            if gt == 0:
                k.op("dve", lambda e: e.tensor_tensor(out=acc[:, 0:16], in0=acc[:, 0:16],
                                                      in1=vec[:, VB_CORR:VB_CORR + 16], op=ALU.mult),
                     reads=[acc_r, vec_r], writes=[acc_r])
            k.op("dve", lambda e: e.tensor_tensor(out=d_bf[par], in0=acc, in1=xa_buf[:, 16:528], op=ALU.subtract),
                 reads=[xa_r, acc_r], writes=[d_r[par]])
        C["xa"] = [xa1, xa2]

        def mk_xb(hh):
            def m1():
                if gt == 0:
                    k.op("dve", lambda e: e.memset(xb_buf[hh][:, 0:3], 0.0), writes=[xb_r[hh]])
                else:
                    k.op("dve", lambda e: e.tensor_copy(out=xb_buf[hh][:, 0:3], in_=xb_buf[hh][:, 512:515]),
                         reads=[xb_r[hh]], writes=[xb_r[hh]])
                st["xb%d" % hh] = proj(128 + 128 * hh, 128, 128)

            def m2():
                ps, pr = st["xb%d" % hh]
                k.op("dve", lambda e: e.tensor_copy(out=xb_buf[hh][:, 3:515], in_=ps), reads=[pr], writes=[xb_r[hh]])
                cw = VB_CONVW + 4 * hh
                k.op("dve", lambda e: e.tensor_scalar(
                    out=xc[hh][par], in0=xb_buf[hh][:, 0:512], scalar1=vec[:, cw:cw + 1],
                    scalar2=vec[:, VB_CONVB + hh:VB_CONVB + hh + 1], op0=ALU.mult, op1=ALU.add),
                    reads=[xb_r[hh], vec_r], writes=[xc_r[hh][par]])
                for kk in range(1, 4):
                    k.op("dve", lambda e, kk=kk: e.scalar_tensor_tensor(
                        out=xc[hh][par], in0=xb_buf[hh][:, kk:kk + 512], scalar=vec[:, cw + kk:cw + kk + 1],
                        in1=xc[hh][par], op0=ALU.mult, op1=ALU.add),
                        reads=[xb_r[hh], vec_r, xc_r[hh][par]], writes=[xc_r[hh][par]])
                k.op("dve", lambda e: e.tensor_copy(out=xc_bf[hh][par], in_=xc[hh][par]),
                     reads=[xc_r[hh][par]], writes=[xcb_r[hh][par]])
            return [m1, m2]

        def mk_gb(hh):
            def m1():
                st["gb%d" % hh] = proj(384 + 128 * hh, 128, 128)

            def m2():
                ps, pr = st["gb%d" % hh]
                k.op("act", lambda e: e.activation(out=t1, in_=ps, func=AF.Square), reads=[pr], writes=[t1_r])
                k.op("dve", lambda e: e.tensor_copy(out=gg[hh][par], in_=ps), reads=[pr, t1_r], writes=[gg_r[hh][par]])

            def m3():
                ps, pr = st["gb%d" % hh]
                k.op("dve", lambda e: e.tensor_scalar(out=t1, in0=t1, scalar1=0.044715, scalar2=1.0, op0=ALU.mult,
                                                      op1=ALU.add), reads=[t1_r], writes=[t1_r])
                k.op("dve", lambda e: e.tensor_tensor(out=t1, in0=t1, in1=ps, op=ALU.mult),
                     reads=[t1_r, pr], writes=[t1_r])

            def m4():
                k.op("act", lambda e: e.activation(out=t1, in_=t1, func=AF.Sigmoid, scale=1.5957691216057308),
                     reads=[t1_r], writes=[t1_r])

            def m5():
                k.op("dve", lambda e: e.tensor_tensor(out=gg[hh][par], in0=gg[hh][par], in1=t1, op=ALU.mult),
                     reads=[t1_r, gg_r[hh][par]], writes=[gg_r[hh][par]])
            return [m1, m2, m3, m4, m5]

        def mk_q(h):
            def m1():
                st["q%d" % h] = proj(640 + 65 * h, 65, 65)

            def m2():
                ps, pr = st["q%d" % h]
                k.op("dve", lambda e: e.tensor_copy(out=Qaug[h][par][0:64, :], in_=ps[0:64, :]),
                     reads=[pr], writes=[Q_r[h][par]])
                k.op("act", lambda e: e.activation(out=rowe[64:65, :], in_=ps[64:65, :], func=AF.Exp,
                                                   bias=nbf[64:65, h:h + 1], scale=-1.0),
                     reads=[pr, nbf_r], writes=[rowe_r])
                k.op("act", lambda e: e.activation(out=rowe[64:65, :], in_=rowe[64:65, :], func=AF.Ln,
                                                   bias=ones_f[64:65, 0:1], scale=1.0),
                     reads=[rowe_r, ones_f_r], writes=[rowe_r])

            def m3():
                init = 0.0 if gt == 0 else clast[64:65, h:h + 1]
                k.op("dve", lambda e: e.tensor_tensor_scan(
                    out=crow[h][64:65, :], data0=ones_f[64:65, 0:1].to_broadcast([1, TT]),
                    data1=rowe[64:65, :], initial=init, op0=ALU.mult, op1=ALU.subtract),
                    reads=[rowe_r, clast_r[h], ones_f_r], writes=[crow_r[h]])
                k.op("dve", lambda e: e.tensor_copy(out=clast[64:65, h:h + 1], in_=crow[h][64:65, TT - 1:TT]),
                     reads=[crow_r[h]], writes=[clast_r[h]])

            def m4():
                k.op("act", lambda e: e.activation(out=Qaug[h][par][64:65, :], in_=crow[h][64:65, :],
                                                   func=AF.Copy, scale=8.0),
                     reads=[crow_r[h]], writes=[Q_r[h][par]])

            def m5():
                pt_, ptr_ = gbank()
                st["n%d" % h] = (pt_, ptr_)
                for blk in range(4):
                    k.op("pe", lambda e, blk=blk: e.matmul(
                        pt_[:, blk:blk + 1], lhsT=crow[h][64:65, blk * 128:(blk + 1) * 128], rhs=ones_f[64:65, 0:1],
                        start=True, stop=True),
                        reads=[crow_r[h], ones_f_r], writes=[ptr_])

            def m6():
                pt_, ptr_ = st["n%d" % h]
                k.op("dve", lambda e: e.tensor_scalar(
                    out=negc[:, 4 * gt:4 * gt + 4, h], in0=pt_[:, 0:4], scalar1=-1.0, scalar2=None, op0=ALU.mult),
                    reads=[ptr_], writes=[negc_r[h][gt]])
            return [m1, m2, m3, m4, m5, m6]

        def mk_k(h):
            def m1():
                st["k%d" % h] = proj(770 + 64 * h, 64, 64)

            def m2():
                ps, pr = st["k%d" % h]
                k.op("act", lambda e: e.activation(out=Kaug[h][0:64, gt * TT:(gt + 1) * TT],
                                                   in_=ps[0:64, :], func=AF.Copy),
                     reads=[pr], writes=[K_r[h][gt]])
            return [m1, m2]

        def v1():
            ps, pr = gbank()
            st["v"] = (ps, pr)
            for blk in range(4):
                for kc in range(KC):
                    k.op("pe", lambda e, blk=blk, kc=kc: e.matmul(
                        ps[:, blk * 128:(blk + 1) * 128], lhsT=utile[:, kc, blk * 128:(blk + 1) * 128],
                        rhs=wB[:, kc, 898:1026], start=(kc == 0), stop=(kc == KC - 1)),
                        reads=[wB_r, utile_r], writes=[pr])

        def v2():
            ps, pr = st["v"]
            k.op("dve", lambda e: e.tensor_copy(
                out=Vaug[:, 4 * gt:4 * gt + 4, :, 0:64], in_=ps.rearrange("p (b h d) -> p b h d", b=4, h=2)),
                reads=[pr], writes=[V_r[gt]])
        C["xb0"], C["xb1"] = mk_xb(0), mk_xb(1)
        C["gb0"], C["gb1"] = mk_gb(0), mk_gb(1)
        C["q0"], C["q1"] = mk_q(0), mk_q(1)
        C["k0"], C["k1"] = mk_k(0), mk_k(1)
        C["v"] = [v1, v2]
        return C

    def S2chains(gt):
        r, tt = gt // 4, gt % 4
        par = gt % 2
        tsl = slice(tt * TT, (tt + 1) * TT)
        C = {}
        st = {}

        def pool1():
            ps, pr = gbank()
            st["pool"] = (ps, pr)
            k.op("pe", lambda e: e.matmul(ps, lhsT=wsB[:, 0:128], rhs=d_bf[par], start=True, stop=True),
                 reads=[wsB_r, d_r[par]], writes=[pr])

        def pool2():
            ps, pr = st["pool"]
            k.op("act", lambda e: e.activation(out=outA[par], in_=ps, func=AF.Identity,
                                               scale=vec[:, VB_PSCALE:VB_PSCALE + 1]),
                 reads=[pr, vec_r], writes=[outA_r[par]])
            for dap, ps_ in mdst("A", r, tsl):
                sr = Res(); store_res[0].append(sr)
                k.dma("sp", ds_o, out=dap, in_=outA[par][ps_, :], reads=[outA_r[par]], writes=[sr])
        C["pool"] = [pool1, pool2]

        def mk_lru(hh):
            def m1():
                ps, pr = gbank()
                k.op("pe", lambda e: e.matmul(ps, lhsT=wsB[:, 128 + 128 * hh:256 + 128 * hh],
                                              rhs=xc_bf[hh][par], start=True, stop=True),
                     reads=[wsB_r, xcb_r[hh][par]], writes=[pr])
                ps2, pr2 = gbank()
                k.op("pe", lambda e: e.matmul(ps2, lhsT=wsB[:, 384 + 128 * hh:512 + 128 * hh],
                                              rhs=xc_bf[hh][par], start=True, stop=True),
                     reads=[wsB_r, xcb_r[hh][par]], writes=[pr2])
                st["l%d" % hh] = (ps, pr, ps2, pr2)

            def m2():
                ps, pr, ps2, pr2 = st["l%d" % hh]
                k.op("act", lambda e: e.activation(out=ra, in_=ps, func=AF.Sigmoid,
                                                   bias=vec[:, VB_BA + hh:VB_BA + hh + 1], scale=1.0),
                     reads=[pr, vec_r], writes=[ra_r])
                k.op("act", lambda e: e.activation(out=ig, in_=ps2, func=AF.Sigmoid,
                                                   bias=vec[:, VB_BX + hh:VB_BX + hh + 1], scale=1.0),
                     reads=[pr2, vec_r], writes=[ig_r])
                k.op("act", lambda e: e.activation(out=ra, in_=ra, func=AF.Exp, scale=nsp8[:, hh:hh + 1]),
                     reads=[ra_r, nsp8_r], writes=[ra_r])

            def m3():
                k.op("dve", lambda e: e.tensor_tensor(out=tmp, in0=ra, in1=ra, op=ALU.mult),
                     reads=[ra_r], writes=[tmp_r])
                k.op("dve", lambda e: e.tensor_tensor(out=ig, in0=ig, in1=xc[hh][par], op=ALU.mult),
                     reads=[ig_r, xc_r[hh][par]], writes=[ig_r])

            def m4():
                k.op("act", lambda e: e.activation(out=tmp, in_=tmp, func=AF.Sqrt, bias=ones_f[:, 0:1], scale=-1.0),
                     reads=[tmp_r, ones_f_r], writes=[tmp_r])

            def m5():
                k.op("dve", lambda e: e.tensor_tensor(out=ig, in0=ig, in1=tmp, op=ALU.mult),
                     reads=[ig_r, tmp_r], writes=[ig_r])
                init = 0.0 if gt == 0 else hlast[:, hh:hh + 1]
                k.op("dve", lambda e: e.tensor_tensor_scan(out=hbuf, data0=ra, data1=ig, initial=init,
                                                           op0=ALU.mult, op1=ALU.add),
                     reads=[ra_r, ig_r, hlast_r[hh]], writes=[hbuf_r])
                k.op("dve", lambda e: e.tensor_copy(out=hlast[:, hh:hh + 1], in_=hbuf[:, TT - 1:TT]),
                     reads=[hbuf_r], writes=[hlast_r[hh]])
                k.op("dve", lambda e: e.tensor_tensor(out=outB[par][:, hh, :], in0=hbuf, in1=gg[hh][par], op=ALU.mult),
                     reads=[hbuf_r, gg_r[hh][par]], writes=[outB_r[par]])
                for dap, ps_, hx in mdst("B", r, tsl):
                    if hx == hh:
                        sr = Res(); store_res[hx].append(sr)
                        k.dma("sp", ds_o, out=dap, in_=outB[par][ps_, hx, :], reads=[outB_r[par]], writes=[sr])
                if hh == 0 and mid_hook is not None:
                    mid_hook(gt, store_res)
            return [m1, m2, m3, m4, m5]
        C["lru0"], C["lru1"] = mk_lru(0), mk_lru(1)
        return C

    ORDER = [("pool", 0), ("q0", 0), ("pool", 1), ("q0", 1), ("lru0", 0), ("q0", 2), ("lru0", 1),
             ("q0", 3), ("q1", 0), ("lru0", 2), ("q1", 1), ("lru0", 3), ("q1", 2), ("lru0", 4), ("q1", 3),
             ("lru1", 0), ("q0", 4), ("lru1", 1), ("q0", 5), ("lru1", 2), ("q1", 4), ("lru1", 3), ("q1", 5),
             ("lru1", 4), ("xa", 0), ("xb0", 0), ("xa", 1), ("xb0", 1), ("xb1", 0), ("gb0", 0), ("xb1", 1),
             ("gb0", 1), ("k0", 0), ("gb0", 2), ("k0", 1), ("gb0", 3), ("k1", 0), ("gb0", 4), ("k1", 1),
             ("gb1", 0), ("v", 0), ("loadnext", 0), ("gb1", 1), ("v", 1), ("gb1", 2), ("gb1", 3), ("gb1", 4)]

    def flat(c2, c1):
        out = []
        for name, i in ORDER:
            src_ = c2 if name in ("pool", "lru0", "lru1") else c1
            if src_ is not None:
                out.append(src_[name][i])
        return out

    def S2(gt, steps):
        r, tt = gt // 4, gt % 4
        par = gt % 2
        tsl = slice(tt * TT, (tt + 1) * TT)
        nkb = 4 * gt + 4

        def attn(h):
            oi = OB[(2 * gt + h) % 2]
            ob, obr = P.psum[:, oi, :], P.pres[oi]
            sbs = {}

            def emit_S(kb):
                dd = kb - 4 * gt
                q0 = 128 * dd if dd > 0 else 0
                sb, sbr = sbank()
                sbs[kb] = (sb, sbr, q0, dd)
                k.op("pe", lambda e, sb=sb, q0=q0, kb=kb: e.matmul(
                    sb[:, q0:TT], lhsT=Kaug[h][0:65, kb * 128:(kb + 1) * 128], rhs=Qaug[h][par][0:65, q0:TT],
                    start=True, stop=True),
                    reads=[K_r[h][kb // 4], Q_r[h][par]], writes=[sbr])

            emit_S(0)
            if nkb > 1:
                emit_S(1)
            for kb in range(nkb):
                if kb + 2 < nkb:
                    emit_S(kb + 2)
                sb, sbr, q0, dd = sbs.pop(kb)
                pi = kb % 3
                k.op("act", lambda e, sb=sb, q0=q0, kb=kb, pi=pi: e.activation(
                    out=Pt[pi][:, q0:TT], in_=sb[:, q0:TT], func=AF.Exp, bias=negc[:, kb, h:h + 1], scale=0.125),
                    reads=[sbr, negc_r[h][kb // 4]], writes=[Pt_r[pi]])
                if dd >= 0:
                    k.op("dve", lambda e, q0=q0, pi=pi: e.tensor_tensor(
                        out=Pt[pi][:, q0:q0 + 128], in0=Pt[pi][:, q0:q0 + 128], in1=mask, op=ALU.min),
                        reads=[Pt_r[pi], mask_r], writes=[Pt_r[pi]])
                k.op("pe", lambda e, q0=q0, kb=kb, pi=pi: e.matmul(
                    ob[0:65, q0:TT], lhsT=Vaug[:, kb, h, :], rhs=Pt[pi][:, q0:TT],
                    start=(kb == 0), stop=(kb == nkb - 1)),
                    reads=[V_r[kb // 4], Pt_r[pi]], writes=[obr])
                if steps:
                    steps.pop(0)()
            k.op("dve", lambda e: e.reciprocal(out=rec[64:65, :], in_=ob[64:65, :]), reads=[obr], writes=[rec_r])
            bc, bcr = sbank()
            k.op("pe", lambda e, bc=bc: e.matmul(bc[0:64, :], lhsT=ones_f[64:65, 0:64], rhs=rec[64:65, :],
                                                 start=True, stop=True),
                 reads=[rec_r, ones_f_r], writes=[bcr])
            k.op("act", lambda e: e.activation(out=osb[0:64, :], in_=ob[0:64, :], func=AF.Copy),
                 reads=[obr], writes=[osb_r])
            k.op("dve", lambda e, bc=bc: e.tensor_tensor(out=outC[h][par][0:64, :], in0=osb[0:64, :], in1=bc[0:64, :],
                                                         op=ALU.mult),
                 reads=[osb_r, bcr], writes=[outC_r[h][par]])
            for dap, ps_ in mdst("C", r, tsl, h):
                sr = Res(); store_res[1].append(sr)
                k.dma("sp", ds_o, out=dap, in_=outC[h][par][ps_, :], reads=[outC_r[h][par]], writes=[sr])
        for h in range(2):
            attn(h)
        while steps:
            steps.pop(0)()
        if tile_done is not None:
            tile_done(gt, store_res)

    k.dma("sp", ds_u, out=utile, in_=usrc(0, 0), reads=(ures(0) if ures else []), writes=[utile_r])
    for s_ in flat(None, S1(0)):
        s_()
    for gt in range(NGT):
        S2(gt, flat(S2chains(gt), S1(gt + 1) if gt + 1 < NGT else None))
    ar.release()


def pack_wB(w_in_l, j):
    cols = list(range(128 * j, 128 * j + 128))
    cols += list(range(512 + 256 * j, 512 + 256 * j + 256))
    cols += list(range(1536 + 256 * j, 1536 + 256 * j + 256))
    for h in range(2):
        hd = 2 * j + h
        cols += list(range(2560 + 64 * hd, 2560 + 64 * hd + 64)) + [4096 + hd]
    cols += list(range(3072 + 128 * j, 3072 + 128 * j + 128))
    cols += list(range(3584 + 128 * j, 3584 + 128 * j + 128))
    w = w_in_l[:, cols]
    return np.ascontiguousarray(w.reshape(KC, 128, NCB).transpose(1, 0, 2))


def pack_wsB(inp, l, j):
    parts = [inp["w_pool"][l, j]]
    parts += [inp["w_rg_a"][l, 2 * j + hh] for hh in range(2)]
    parts += [inp["w_rg_x"][l, 2 * j + hh] for hh in range(2)]
    return np.ascontiguousarray(np.concatenate(parts, axis=1))


def pack_vecB(inp, l, j):
    v = np.zeros((128, NVB), np.float32)
    v[:, VB_PSCALE] = inp["pool_scale"][l, 128 * j:128 * j + 128]
    for hh in range(2):
        sl = slice(256 * j + 128 * hh, 256 * j + 128 * hh + 128)
        for kk in range(4):
            v[:, VB_CONVW + 4 * hh + kk] = inp["conv_w"][l, kk, sl]
        v[:, VB_CONVB + hh] = inp["conv_b"][l, sl]
        v[:, VB_BA + hh] = inp["b_rg_a"][l, sl]
        v[:, VB_BX + hh] = inp["b_rg_x"][l, sl]
        v[:, VB_LAM + hh] = inp["lru_lambda"][l, sl]
        v[:, VB_BF + hh] = inp["b_f"][l, 2 * j + hh]
    w = 2 ** (j + 1)
    v[:, VB_SEL + j] = np.float32(1.0 / w)
    for t in range(16):
        v[:, VB_CORR + t] = np.float32(w / min(t + 1, w))
    return v


def const_mask():
    jj = np.arange(128)[:, None]
    tt = np.arange(128)[None, :]
    return np.where(jj <= tt, np.float32(3.0e38), np.float32(0.0)).astype(np.float32)


def emit_phaseC(P, cst, x, xres, umine, mix_all_d, wC_d, bgate_d, wo_d, mix_reads=None, pre_mix=None):
    k, ar = P.k, P.ar
    ar.mark()
    u = ar.alloc([KC, T], BF16); u_r = [Res() for _ in range(NT)]
    mixs = ar.alloc([16, 2 * TT], BF16); mixs_r = [Res() for _ in range(16)]
    merged = ar.alloc([KC, 2 * TT], BF16); mg_r = [[Res(), Res()] for _ in range(KC)]
    wcb = [ar.alloc([5120], BF16) for _ in range(3)]; wcb_r = [Res(), Res(), Res()]
    wob = [ar.alloc([1024], BF16) for _ in range(2)]; wob_r = [Res(), Res()]
    gate = [ar.alloc([TT], F32) for _ in range(3)]; gate_r = [Res(), Res(), Res()]
    acc = [ar.alloc([TT], F32) for _ in range(2)]; acc_r = [Res(), Res()]
    bg = ar.alloc([24], F32); bg_r = Res()
    ds_u = k.dsem("cu")
    for tt in range(NT):
        k.dma("sp", ds_u, out=u[:, :, tt * TT:(tt + 1) * TT], in_=umine(tt), writes=[u_r[tt]])
    k.dma("sp", k.dsem("cbg"), out=bg, in_=bgate_d, writes=[bg_r])
    if pre_mix is not None:
        pre_mix()
    dsm = k.dsem("cmix")
    dsw = [None] * 3
    dso = [None] * 2
    cw = 0
    co = 0
    for st in range(2):
        cs = slice(st * 2 * TT, (st + 1) * 2 * TT)
        for r in range(4):
            if callable(mix_all_d):
                srcs = [(r, mix_all_d(r, 0, cs)), (4 + 2 * r, mix_all_d(r, 1, cs)), (5 + 2 * r, mix_all_d(r, 2, cs)),
                        (12 + r, mix_all_d(r, 3, cs))]
            else:
                srcs = [(r, mix_all_d[r, 0:128, cs]), (4 + 2 * r, mix_all_d[r, 128:256, cs]),
                        (5 + 2 * r, mix_all_d[r, 256:384, cs]), (12 + r, mix_all_d[r, 384:512, cs])]
            for pi_, (m_, s_) in enumerate(srcs):
                k.dma("sp", dsm, out=mixs[:, m_, :], in_=s_, reads=(mix_reads(r, pi_) if mix_reads else []),
                      writes=[mixs_r[m_]])
        for dc in range(KC):
            s = cw % 3
            cw += 1
            k.dma("pool", dsw[s], out=wcb[s], in_=wC_d[dc], writes=[wcb_r[s]], max_dma_last_dim=1024 * 4)
            for t2 in range(2):
                tt = st * 2 + t2
                sl = slice(tt * TT, (tt + 1) * TT)
                sl2 = slice(t2 * TT, (t2 + 1) * TT)
                for br in range(3):
                    ps, pr = P.bank()
                    for kc in range(KC):
                        c0 = 2048 + kc * 384 + br * 128
                        k.op("pe", lambda e, ps=ps, s=s, kc=kc, c0=c0, sl=sl: e.matmul(
                            ps, lhsT=wcb[s][:, c0:c0 + 128], rhs=u[:, kc, sl], start=(kc == 0), stop=(kc == KC - 1)),
                            reads=[wcb_r[s], u_r[tt]], writes=[pr])
                    k.op("act", lambda e, ps=ps, br=br, dc=dc: e.activation(
                        out=gate[br], in_=ps, func=AF.Sigmoid, bias=bg[:, br * 8 + dc:br * 8 + dc + 1], scale=1.0),
                        reads=[pr, bg_r], writes=[gate_r[br]])
                ys = []
                for (m0, m1) in ((0, 4), (4, 12), (12, 16)):
                    ps, pr = P.bank()
                    for m in range(m0, m1):
                        k.op("pe", lambda e, ps=ps, s=s, m=m, sl2=sl2, m0=m0, m1=m1: e.matmul(
                            ps, lhsT=wcb[s][:, m * 128:(m + 1) * 128], rhs=mixs[:, m, sl2],
                            start=(m == m0), stop=(m == m1 - 1)),
                            reads=[wcb_r[s], mixs_r[m]], writes=[pr])
                    ys.append((ps, pr))
                k.op("dve", lambda e, ys=ys: e.tensor_tensor(out=acc[0], in0=ys[0][0], in1=gate[0], op=ALU.mult),
                     reads=[ys[0][1], gate_r[0]], writes=[acc_r[0]])
                k.op("dve", lambda e, ys=ys: e.tensor_tensor(out=acc[1], in0=ys[1][0], in1=gate[1], op=ALU.mult),
                     reads=[ys[1][1], gate_r[1]], writes=[acc_r[1]])
                k.op("dve", lambda e: e.tensor_tensor(out=acc[0], in0=acc[0], in1=acc[1], op=ALU.add),
                     reads=[acc_r[0], acc_r[1]], writes=[acc_r[0]])
                k.op("dve", lambda e, ys=ys: e.tensor_tensor(out=acc[1], in0=ys[2][0], in1=gate[2], op=ALU.mult),
                     reads=[ys[2][1], gate_r[2]], writes=[acc_r[1]])
                k.op("dve", lambda e, dc=dc, sl2=sl2: e.tensor_tensor(out=merged[:, dc, sl2], in0=acc[0], in1=acc[1],
                                                                      op=ALU.add),
                     reads=[acc_r[0], acc_r[1]], writes=[mg_r[dc][t2]])
        for dco in range(KC):
            s = co % 2
            co += 1
            k.dma("pool", dso[s], out=wob[s], in_=wo_d[dco], writes=[wob_r[s]])
            for t2 in range(2):
                tt = st * 2 + t2
                sl = slice(tt * TT, (tt + 1) * TT)
                sl2 = slice(t2 * TT, (t2 + 1) * TT)
                ps, pr = P.bank()
                for dc in range(KC):
                    k.op("pe", lambda e, ps=ps, s=s, dc=dc, sl2=sl2: e.matmul(
                        ps, lhsT=wob[s][:, dc * 128:(dc + 1) * 128], rhs=merged[:, dc, sl2],
                        start=(dc == 0), stop=(dc == KC - 1)),
                        reads=[wob_r[s], mg_r[dc][t2]], writes=[pr])
                k.op("dve", lambda e, ps=ps, dco=dco, sl=sl: e.tensor_tensor(out=x[:, dco, sl], in0=ps, in1=x[:, dco, sl],
                                                                             op=ALU.add),
                     reads=[pr, xres[tt]], writes=[xres[tt]])
    ar.release()


def emit_cross(P, cst, x, xres, memT_d, gmem, gmem_r, gcross, gcross_r, wxq_d, wxkv_d, wxo_d):
    k, ar = P.k, P.ar
    ar.mark()
    NM = 256
    xn = ar.alloc([KC, T], BF16); xn_r = [Res() for _ in range(NT)]
    QT = ar.alloc([KC, T], BF16); QT_r = [[Res() for _ in range(NT)] for _ in range(KC)]
    memf = ar.alloc([KC, NM], F32); memf_r = [Res()]
    memn = ar.alloc([KC, NM], BF16); memn_r = Res()
    KxT = ar.alloc([KC, NM], BF16); Kx_r = [Res() for _ in range(KC)]
    Vx = ar.alloc([2, D], BF16); Vx_r = [Res() for _ in range(KC)]
    wch = [ar.alloc([1024], BF16) for _ in range(2)]; wch_r = [Res(), Res()]
    Pt = [[ar.alloc([TT], BF16) for _ in range(2)] for _ in range(2)]; Pt_r = [[Res(), Res()] for _ in range(2)]
    rec = ar.alloc([TT], F32); rec_r = Res()
    nt = alloc_normtmp(P)
    dsw = [k.dsem("xw0"), k.dsem("xw1")]
    wc = [0]

    def wload(src):
        s = wc[0] % 2
        wc[0] += 1
        k.dma("pool", dsw[s], out=wch[s], in_=src, writes=[wch_r[s]])
        return s

    k.dma("sp", k.dsem("xmem"), out=memf, in_=memT_d, writes=[memf_r[0]])
    emit_rmsnorm(P, cst, nt, memf, memf_r, gmem, gmem_r, lambda tt: (memn, memn_r), ntiles=1, tw=NM)
    for ck in range(16):
        s = wload(wxkv_d[ck])
        ps, pr = P.bank()
        if ck < 8:
            for kc in range(KC):
                k.op("pe", lambda e, ps=ps, s=s, kc=kc: e.matmul(
                    ps[:, 0:NM], lhsT=wch[s][:, kc * 128:(kc + 1) * 128], rhs=memn[:, kc, :],
                    start=(kc == 0), stop=(kc == KC - 1)),
                    reads=[wch_r[s], memn_r], writes=[pr])
            k.op("act", lambda e, ps=ps, ck=ck: e.activation(out=KxT[:, ck, :], in_=ps[:, 0:NM], func=AF.Copy),
                 reads=[pr], writes=[Kx_r[ck]])
        else:
            c = ck - 8
            for mb in range(2):
                for kc in range(KC):
                    k.op("pe", lambda e, ps=ps, s=s, kc=kc, mb=mb: e.matmul(
                        ps[:, mb * 128:(mb + 1) * 128], lhsT=memn[:, kc, mb * 128:(mb + 1) * 128],
                        rhs=wch[s][:, kc * 128:(kc + 1) * 128], start=(kc == 0), stop=(kc == KC - 1)),
                        reads=[wch_r[s], memn_r], writes=[pr])
            k.op("act", lambda e, ps=ps, c=c: e.activation(
                out=Vx[:, :, c * 128:(c + 1) * 128], in_=ps[:, 0:256].rearrange("p (m j) -> p m j", m=2), func=AF.Copy),
                reads=[pr], writes=[Vx_r[c]])
    emit_rmsnorm(P, cst, nt, x, xres, gcross, gcross_r, lambda tt: (xn[:, :, tt * TT:(tt + 1) * TT], xn_r[tt]))
    for ck in range(KC):
        s = wload(wxq_d[ck])
        for tt in range(NT):
            sl = slice(tt * TT, (tt + 1) * TT)
            ps, pr = P.bank()
            for kc in range(KC):
                k.op("pe", lambda e, ps=ps, s=s, kc=kc, sl=sl: e.matmul(
                    ps, lhsT=wch[s][:, kc * 128:(kc + 1) * 128], rhs=xn[:, kc, sl],
                    start=(kc == 0), stop=(kc == KC - 1)),
                    reads=[wch_r[s], xn_r[tt]], writes=[pr])
            if tt % 2 == 0:
                k.op("act", lambda e, ps=ps, ck=ck, sl=sl: e.activation(out=QT[:, ck, sl], in_=ps, func=AF.Copy),
                     reads=[pr], writes=[QT_r[ck][tt]])
            else:
                k.op("dve", lambda e, ps=ps, ck=ck, sl=sl: e.tensor_copy(out=QT[:, ck, sl], in_=ps),
                     reads=[pr], writes=[QT_r[ck][tt]])
    cnt = 0
    for tt in range(NT):
        sl = slice(tt * TT, (tt + 1) * TT)
        for h in range(4):
            pp = cnt % 2
            cnt += 1
            for mb in range(2):
                ps, pr = P.bank()
                for half in range(2):
                    ck = 2 * h + half
                    k.op("pe", lambda e, ps=ps, ck=ck, mb=mb, sl=sl, half=half: e.matmul(
                        ps, lhsT=KxT[:, ck, mb * 128:(mb + 1) * 128], rhs=QT[:, ck, sl],
                        start=(half == 0), stop=(half == 1)),
                        reads=[Kx_r[ck], QT_r[ck][tt]], writes=[pr])
                k.op("act", lambda e, ps=ps, pp=pp, mb=mb: e.activation(out=Pt[pp][mb], in_=ps, func=AF.Exp,
                                                                       scale=0.0625),
                     reads=[pr], writes=[Pt_r[pp][mb]])
            ps, pr = P.bank()
            for mb in range(2):
                k.op("pe", lambda e, ps=ps, pp=pp, mb=mb: e.matmul(ps, lhsT=cst["ones_bf"], rhs=Pt[pp][mb],
                                                                   start=(mb == 0), stop=(mb == 1)),
                     reads=[cst["ones_bf_r"], Pt_r[pp][mb]], writes=[pr])
            k.op("dve", lambda e, ps=ps: e.reciprocal(out=rec, in_=ps), reads=[pr], writes=[rec_r])
            for half in range(2):
                ck = 2 * h + half
                ps, pr = P.bank()
                for mb in range(2):
                    k.op("pe", lambda e, ps=ps, pp=pp, mb=mb, ck=ck: e.matmul(
                        ps, lhsT=Vx[:, mb, ck * 128:(ck + 1) * 128], rhs=Pt[pp][mb], start=(mb == 0), stop=(mb == 1)),
                        reads=[Vx_r[ck], Pt_r[pp][mb]], writes=[pr])
                k.op("dve", lambda e, ps=ps, ck=ck, sl=sl: e.tensor_tensor(out=QT[:, ck, sl], in0=ps, in1=rec, op=ALU.mult),
                     reads=[pr, rec_r], writes=[QT_r[ck][tt]])
    for dco in range(KC):
        s = wload(wxo_d[dco])
        for tt in range(NT):
            sl = slice(tt * TT, (tt + 1) * TT)
            ps, pr = P.bank()
            for ck in range(KC):
                k.op("pe", lambda e, ps=ps, s=s, ck=ck, sl=sl: e.matmul(
                    ps, lhsT=wch[s][:, ck * 128:(ck + 1) * 128], rhs=QT[:, ck, sl],
                    start=(ck == 0), stop=(ck == KC - 1)),
                    reads=[wch_r[s], QT_r[ck][tt]], writes=[pr])
            k.op("dve", lambda e, ps=ps, dco=dco, sl=sl: e.tensor_tensor(out=x[:, dco, sl], in0=ps, in1=x[:, dco, sl],
                                                                         op=ALU.add),
                 reads=[pr, xres[tt]], writes=[xres[tt]])
    ar.release()


def pack_wC(inp, l):
    ua = inp["w_up_a"][l].reshape(4, 128, KC, 128)
    ub = inp["w_up_b"][l].reshape(8, 128, KC, 128)
    uc = inp["w_up_c"][l].reshape(4, 128, KC, 128)
    up = np.concatenate([ua, ub, uc], axis=0)
    up = up.transpose(2, 1, 0, 3).reshape(KC, 128, 2048)
    wg = inp["w_in"][l][:, 4104:].reshape(KC, 128, 3, KC, 128)
    wg = wg.transpose(3, 1, 0, 2, 4).reshape(KC, 128, 3072)
    return np.ascontiguousarray(np.concatenate([up, wg], axis=2))


def pack_bgate(inp, l):
    return np.ascontiguousarray(inp["b_gate"][l].reshape(3, KC, 128).transpose(2, 0, 1).reshape(128, 24))


def pack_sq(w):
    C = w.shape[1] // 128
    a = w.reshape(KC, 128, C, 128)
    return np.ascontiguousarray(a.transpose(2, 1, 0, 3)).reshape(C, 128, KC * 128)


def pack_memT(m):
    return np.ascontiguousarray(m.reshape(256, KC, 128).transpose(2, 1, 0))


ARENA_BYTES = 205 * 1024
RGROUPS = [[0, 1, 2, 3], [4, 5, 6, 7]]


def build(stages, fused=False):
    nc = bass.Bass("TRN2", target_bir_lowering=False)
    dram = {}

    def dten(name, shape, dt, kind):
        if name not in dram:
            dram[name] = nc.dram_tensor(name, list(shape), dt, kind=kind).ap()
        return dram[name]

    def din(name, shape, dt=F32):
        return dten(name, shape, dt, "ExternalInput")

    def dout(name, shape, dt):
        return dten(name, shape, dt, "ExternalOutput")

    def dint(name, shape, dt):
        return dten(name, shape, dt, "Internal")

    with ExitStack() as stack:
        arena_t = stack.enter_context(nc.sbuf_tensor("arena", [128, ARENA_BYTES // 4], F32))
        psum_t = stack.enter_context(nc.psum_tensor("psum", [128, 8, 512], F32))
        k = K(nc, stack)
        ar = Arena(arena_t[:, :], ARENA_BYTES)
        P = Prog(nc, k, ar, psum_t[:, :, :])
        cst = make_consts(P, din("cmask", [128, 128]))
        x = ar.alloc([KC, T], F32)
        xres = [Res() for _ in range(NT)]
        gvs = ar.alloc([16, KC], F32)
        gv_n = [0]
        pid_cache = {}
        gath_res = {}
        uall_res = {}
        mb_res = {}
        ds_g = k.dsem("gv")

        def load_g(name):
            i = gv_n[0] % 16
            gv_n[0] += 1
            r = Res()
            k.dma("sp", ds_g, out=gvs[:, i, :], in_=din(name, [128, KC]), writes=[r])
            return gvs[:, i, :], r

        def sfx(l):
            return "" if l is None else "_%d" % l

        def slab_copies(d, q="sp"):
            for half in range(2):
                gq = dint("mix_gath_%d_%d" % (d, half), [4 * 256, T], BF16)
                for r in range(4):
                    mb = dint("mix_big_%d_%d" % (r, half), [4 * 256, T], BF16)

                    def dstf(e, d=d, mb=mb, q=q):
                        if q not in pid_cache:
                            pid_cache[q] = (nc.sync if q == "sp" else nc.scalar).partition_id() % 4
                        return mb[bass.ds(((d + 4 - pid_cache[q]) % 4) * 256, 256), :]
                    cr = Res()
                    mb_res.setdefault((r, half), []).append(cr)
                    k.dma(q, None, out=dstf, in_=gq[r * 256:(r + 1) * 256, :], reads=[gath_res[(d, half)]], writes=[cr])

        for stg in stages:
            if not (fused and stg[0] == "C"):
                k.barrier(skip_cc=(fused and stg[0] in ("ag_u", "B", "ag_mix")))
            if stg == "load_x":
                xd = din("xT", [128, KC, T])
                for tt in range(NT):
                    sl = slice(tt * TT, (tt + 1) * TT)
                    k.dma("sp", k.dsem("x"), out=x[:, :, sl], in_=xd[:, :, sl], writes=[xres[tt]])
            elif stg == "store_x":
                xo = dout("xT_out", [128, KC, T], F32)
                for tt in range(NT):
                    sl = slice(tt * TT, (tt + 1) * TT)
                    k.dma("sp", k.dsem("xo"), out=xo[:, :, sl], in_=x[:, :, sl], reads=[xres[tt]])
            elif stg[0] == "ffn":
                _, l, i = stg
                ar.mark()
                gv, gr = load_g("gffn%s_%d" % (sfx(l), i))
                xn = ar.alloc([KC, T], BF16); xn_r = [Res() for _ in range(NT)]
                nt = alloc_normtmp(P)
                w1buf = [ar.alloc([KC * 256], BF16) for _ in range(3)]; w1res = [Res(), Res(), Res()]
                w2buf = [ar.alloc([NF * 128], BF16) for _ in range(3)]; w2res = [Res(), Res(), Res()]
                g = ar.alloc([NF, 2 * TT], BF16); gres2 = [[Res(), Res()] for _ in range(NF)]
                stmp = [ar.alloc([TT], F32) for _ in range(2)]; stres = [Res(), Res()]
                emit_rmsnorm(P, cst, nt, x, xres, gv, gr, lambda tt: (xn[:, :, tt * TT:(tt + 1) * TT], xn_r[tt]))
                emit_ffn(P, x, xres, xn, xn_r, din("w1%s_%d" % (sfx(l), i), [NF, 128, KC * 256]),
                         din("w2%s_%d" % (sfx(l), i), [KC, 128, NF * 128]),
                         (w1buf, w1res, w2buf, w2res, g, gres2, stmp, stres), "f")
                ar.release()
            elif stg[0] == "unorm":
                _, l = stg
                ar.mark()
                gv, gr = load_g("gmix%s" % sfx(l))
                xn = ar.alloc([KC, T], BF16); xn_r = [Res() for _ in range(NT)]
                nt = alloc_normtmp(P)
                dsu = k.dsem("ust")
                if fused:
                    def after(tt, o, ores):
                        um = dint("u_mine_%d" % tt, [128, KC * TT], BF16)
                        ua = dint("u_all_%d" % tt, [4 * 128, KC * TT], BF16)
                        ur = Res()
                        k.dma("sp", dsu, out=um.rearrange("p (k t) -> p k t", k=KC), in_=o, reads=[ores], writes=[ur])
                        uall_res[tt] = Res()
                        k.coll("AllGather", RGROUPS, ins=[um], outs=[ua], reads=[ur], writes=[uall_res[tt]])
                else:
                    ud = dout("u_mine%s" % sfx(l), [128, KC, T], BF16)

                    def after(tt, o, ores, ud=ud, dsu=dsu):
                        sl = slice(tt * TT, (tt + 1) * TT)
                        k.dma("sp", dsu, out=ud[:, :, sl], in_=o, reads=[ores], writes=[])

                emit_rmsnorm(P, cst, nt, x, xres, gv, gr, lambda tt: (xn[:, :, tt * TT:(tt + 1) * TT], xn_r[tt]),
                             after=after)
                ar.release()
            elif stg[0] == "B":
                _, l = stg
                if fused:
                    def usrc(r, tt):
                        ua = dint("u_all_%d" % tt, [4 * 128, KC * TT], BF16)
                        return ua[r * 128:(r + 1) * 128, :].rearrange("p (k t) -> p k t", k=KC)

                    def mm(d, half):
                        return dint("mix_mine_%d_%d" % (d, half), [256, T], BF16)

                    def mdst(kind, r, tsl, h=0):
                        if kind == "A":
                            return [(mm(r, 0)[0:128, tsl], slice(0, 128))]
                        if kind == "B":
                            return [(mm(r, 0)[128:256, tsl], slice(0, 128), 0), (mm(r, 1)[0:128, tsl], slice(0, 128), 1)]
                        return [(mm(r, 1)[128 + 64 * h:192 + 64 * h, tsl], slice(0, 64))]

                    def gather_half(d, half, store_res):
                        gq = dint("mix_gath_%d_%d" % (d, half), [4 * 256, T], BF16)
                        gres = Res()
                        gath_res[(d, half)] = gres
                        k.coll("AllGather", RGROUPS, ins=[mm(d, half)], outs=[gq], reads=list(store_res[half]),
                               writes=[gres])
                        del store_res[half][:]

                    def mid_hook(gt, store_res):
                        if gt == NGT - 1:
                            gather_half(3, 0, store_res)

                    def tile_done(gt, store_res):
                        if gt % 4 == 3:
                            d = gt // 4
                            for half in range(2):
                                if (d, half) != (3, 0):
                                    gather_half(d, half, store_res)
                        if gt % 4 == 1 and gt >= 5:
                            slab_copies(gt // 4 - 1, "act")
                else:
                    u_all = din("u_all%s" % sfx(l), [4, 128, KC, T], BF16)
                    mix = dout("mix_mine%s" % sfx(l), [4, 512, T], BF16)

                    def usrc(r, tt, u_all=u_all):
                        return u_all[r, :, :, tt * TT:(tt + 1) * TT]

                    def mdst(kind, r, tsl, h=0, mix=mix):
                        if kind == "A":
                            return [(mix[r, 0:128, tsl], slice(0, 128))]
                        if kind == "B":
                            return [(mix[r, 128 + 128 * hh:256 + 128 * hh, tsl], slice(0, 128), hh) for hh in range(2)]
                        return [(mix[r, 384 + 64 * h:448 + 64 * h, tsl], slice(0, 64))]
                emit_phaseB(P, cst, usrc, din("wB%s" % sfx(l), [128, KC, NCB]), din("wsB%s" % sfx(l), [128, 640]),
                            din("vecB%s" % sfx(l), [128, NVB]), mdst, tile_done if fused else None,
                            (lambda tt: [uall_res[tt]]) if fused else None, mid_hook if fused else None)
            elif stg[0] == "C":
                _, l = stg
                if fused:
                    def umine(tt):
                        return dint("u_mine_%d" % tt, [128, KC * TT], BF16).rearrange("p (k t) -> p k t", k=KC)

                    def mix_all(r, piece, cs):
                        mb = dint("mix_big_%d_%d" % (r, piece // 2), [4 * 256, T], BF16)
                        return mb[128 * (piece % 2):128 * (piece % 2) + 128, cs]
                else:
                    ud = din("u_mine%s" % sfx(l), [128, KC, T], BF16)
                    mix_all = din("mix_all%s" % sfx(l), [4, 512, T], BF16)

                    def umine(tt, ud=ud):
                        return ud[:, :, tt * TT:(tt + 1) * TT]
                emit_phaseC(P, cst, x, xres, umine, mix_all, din("wC%s" % sfx(l), [KC, 128, 5120]),
                            din("bgate%s" % sfx(l), [128, 24]), din("wo%s" % sfx(l), [KC, 128, 1024]),
                            (lambda r, piece: list(mb_res[(r, piece // 2)])) if fused else None,
                            (lambda: slab_copies(3, "sp")) if fused else None)
                if fused:
                    mb_res.clear()
            elif stg[0] == "cross":
                _, l = stg
                gm, gmr = load_g("gmem%s" % sfx(l))
                gc, gcr = load_g("gcross%s" % sfx(l))
                emit_cross(P, cst, x, xres, din("memT", [128, KC, 256]), gm, gmr, gc, gcr,
                           din("wxq%s" % sfx(l), [KC, 128, 1024]), din("wxkv%s" % sfx(l), [16, 128, 1024]),
                           din("wxo%s" % sfx(l), [KC, 128, 1024]))
            elif stg[0] == "ag_u":
                pass
            elif stg[0] == "ag_mix":
                pass
            elif stg == "final":
                ar.mark()
                gv, gr = load_g("gfin")
                nt = alloc_normtmp(P)
                stg_t = [ar.alloc([KC, TT], F32) for _ in range(2)]; stg_r = [Res(), Res()]
                od = dout("outT", [128, KC, T], F32)
                dso = k.dsem("out")

                def after(tt, o, ores, od=od, dso=dso):
                    sl = slice(tt * TT, (tt + 1) * TT)
                    k.dma("sp", dso, out=od[:, :, sl], in_=o, reads=[ores], writes=[])

                emit_rmsnorm(P, cst, nt, x, xres, gv, gr, lambda tt: (stg_t[tt % 2], stg_r[tt % 2]), after=after)
                ar.release()
            else:
                raise ValueError(stg)
        k.barrier()
        k.final_wait("sp")
        k.emit()
    return nc


_PROGS = {}


def get_prog(key, stages, fused=False):
    if key not in _PROGS:
        _PROGS[key] = build(stages, fused)
    return _PROGS[key]


def _run(nc, maps):
    res = run_bass_kernel_spmd(nc, maps, core_ids=list(range(NCORES)))
    return res.results


def _weights_ffn(inp, l, i, names=None):
    sfx = "_%d_%d" % (l, i)
    if i == 1:
        w_in, w_out, g = inp["w_ffn1_in"], inp["w_ffn1_out"], inp["g_ffn1"]
    else:
        w_in, w_out, g = inp["w_ffn2_in"], inp["w_ffn2_out"], inp["g_ffn2"]
    return {"w1" + sfx: pack_w1(w_in[l]), "w2" + sfx: pack_w2(w_out[l]), "gffn" + sfx: pack_vec(g[l])}


def _weights_C(inp, l):
    s = "_%d" % l
    return {"wC" + s: pack_wC(inp, l), "bgate" + s: pack_bgate(inp, l), "wo" + s: pack_sq(inp["w_o"][l]),
            "gmem" + s: pack_vec(inp["g_mem"][l]), "gcross" + s: pack_vec(inp["g_cross"][l]),
            "wxq" + s: pack_sq(inp["w_xq"][l]), "wxkv" + s: pack_sq(inp["w_xkv"][l]), "wxo" + s: pack_sq(inp["w_xo"][l])}


FUSED = True


def fused_stages():
    st = ["load_x"]
    for l in range(L):
        st += [("ffn", l, 1), ("unorm", l), ("ag_u", l), ("B", l), ("ag_mix", l), ("C", l), ("cross", l), ("ffn", l, 2)]
    st.append("final")
    return st


def kernel_fused(inp):
    cmask = const_mask()
    xs = inp["x"].reshape(NCORES, T, D)
    w = {"cmask": cmask, "gfin": pack_vec(inp["g_final"])}
    wBs = {}
    for l in range(L):
        w.update(_weights_ffn(inp, l, 1))
        w.update(_weights_ffn(inp, l, 2))
        w.update(_weights_C(inp, l))
        w["gmix_%d" % l] = pack_vec(inp["g_mix"][l])
        for j in range(4):
            wBs[(l, j)] = {"wB_%d" % l: pack_wB(inp["w_in"][l], j), "wsB_%d" % l: pack_wsB(inp, l, j),
                           "vecB_%d" % l: pack_vecB(inp, l, j)}
    memT = [pack_memT(inp["mem"][b]) for b in range(2)]
    maps = []
    for c in range(NCORES):
        b, j = c // 4, c % 4
        m = dict(w)
        for l in range(L):
            m.update(wBs[(l, j)])
        m["xT"] = pack_xT(xs[c])
        m["memT"] = memT[b]
        maps.append(m)
    prog = get_prog("fused", fused_stages(), fused=True)
    r = _run(prog, maps)
    out = np.stack([unpack_xT(r[c]["outT"]) for c in range(NCORES)], axis=0)
    return out.reshape(2, 8192, D).astype(np.float32)


def kernel(**inp):
    inp = {k_: np.asarray(v) for k_, v in inp.items()}
    if FUSED:
        return kernel_fused(inp)
    cmask = const_mask()
    xs = inp["x"].reshape(NCORES, T, D)
    p1 = get_prog("p1", ["load_x", ("ffn", 0, 1), ("unorm", 0), "store_x"])
    w = {"cmask": cmask, "gmix_0": pack_vec(inp["g_mix"][0])}
    w.update(_weights_ffn(inp, 0, 1))
    r = _run(p1, [dict(w, xT=pack_xT(xs[c])) for c in range(NCORES)])
    xT = [r[c]["xT_out"] for c in range(NCORES)]
    um = [r[c]["u_mine_0"] for c in range(NCORES)]
    out = None
    for l in range(L):
        pB = get_prog("pB", [("B", None)])
        maps = []
        for c in range(NCORES):
            b, j = c // 4, c % 4
            maps.append({"cmask": cmask, "u_all": np.stack([um[4 * b + rr] for rr in range(4)], axis=0),
                         "wB": pack_wB(inp["w_in"][l], j), "wsB": pack_wsB(inp, l, j), "vecB": pack_vecB(inp, l, j)})
        r = _run(pB, maps)
        mix = [r[c]["mix_mine"] for c in range(NCORES)]
        w = {"cmask": cmask}
        w.update(_weights_C(inp, l))
        w.update(_weights_ffn(inp, l, 2))
        if l + 1 < L:
            stages = ["load_x", ("C", l), ("cross", l), ("ffn", l, 2), ("ffn", l + 1, 1), ("unorm", l + 1), "store_x"]
            w.update(_weights_ffn(inp, l + 1, 1))
            w["gmix_%d" % (l + 1)] = pack_vec(inp["g_mix"][l + 1])
        else:
            stages = ["load_x", ("C", l), ("cross", l), ("ffn", l, 2), "final"]
            w["gfin"] = pack_vec(inp["g_final"])
        pC = get_prog("pC%d" % l, stages)
        maps = []
        for c in range(NCORES):
            b, cc = c // 4, c % 4
            m = dict(w)
            m["xT"] = xT[c]
            m["u_mine_%d" % l] = um[c]
            m["mix_all_%d" % l] = np.stack([mix[4 * b + j][cc] for j in range(4)], axis=0)
            m["memT"] = pack_memT(inp["mem"][b])
            maps.append(m)
        r = _run(pC, maps)
        if l + 1 < L:
            xT = [r[c]["xT_out"] for c in range(NCORES)]
            um = [r[c]["u_mine_%d" % (l + 1)] for c in range(NCORES)]
        else:
            out = np.stack([unpack_xT(r[c]["outT"]) for c in range(NCORES)], axis=0)
    return out.reshape(2, 8192, D).astype(np.float32)
```

```python
import numpy as np
from contextlib import ExitStack
import concourse.bass as bass
import concourse.mybir as mybir
from concourse.bass_utils import run_bass_kernel_spmd

F32 = mybir.dt.float32
BF16 = mybir.dt.bfloat16
AF = mybir.ActivationFunctionType
ALU = mybir.AluOpType
AX = mybir.AxisListType

D = 1024
KC = 8
T = 2048
TT = 512
NT = T // TT
DFF = 2816
NF = DFF // 128
EPS = 1e-6
L = 2
NCORES = 8


class Res:
    __slots__ = ("name", "w", "r")

    def __init__(self, name=""):
        self.name = name
        self.w = None
        self.r = {}


class DSem:
    __slots__ = ("key", "sem", "count")

    def __init__(self, key, sem):
        self.key = key
        self.sem = sem
        self.count = 0


class Eng:
    def __init__(self, name, sem):
        self.name = name
        self.sem = sem
        self.count = 0
        self.seen = {}
        self.q = []


class K:
    def __init__(self, nc, stack):
        self.nc = nc
        self.stack = stack
        self.engs = {}
        for n in ("pe", "act", "dve", "pool", "sp"):
            self.engs[n] = Eng(n, stack.enter_context(nc.semaphore("sem_" + n)))
        self.dsems = {}
        self.ninst = 0
        self.sched = {}

    NPOOL = 64

    def semname(self, sem):
        for e in self.engs.values():
            if e.sem is sem:
                return e.name
        for d in self.dsems.values():
            if d.sem is sem:
                return d.key
        return "?"

    def simulate(self):
        cnt = {}
        pos = {e: 0 for e in self.sched}
        progress = True
        while progress:
            progress = False
            for e, lst in self.sched.items():
                while pos[e] < len(lst):
                    waits, key, inc = lst[pos[e]]
                    if all(cnt.get(kk, 0) >= v for kk, v in waits):
                        cnt[key] = cnt.get(key, 0) + inc
                        pos[e] += 1
                        progress = True
                    else:
                        break
        stuck = {e: (pos[e], len(lst), lst[pos[e]][0] if pos[e] < len(lst) else None) for e, lst in self.sched.items()}
        return stuck, cnt

    def dsem(self, key):
        return None

    def _pool_sem(self):
        if not hasattr(self, "dpool"):
            self.dpool = []
            self.dnext = 0
        if len(self.dpool) < self.NPOOL:
            i = len(self.dpool)
            d = DSem("dp%d" % i, self.stack.enter_context(self.nc.semaphore("dsem_p%d" % i)))
            self.dpool.append(d)
            self.dsems[d.key] = d
            return d
        d = self.dpool[self.dnext % self.NPOOL]
        self.dnext += 1
        return d

    def _waits(self, eng, reads, writes):
        deps = {}

        def add(tok):
            if tok is None:
                return
            key, sem, val = tok
            if key not in deps or deps[key][1] < val:
                deps[key] = (sem, val)

        for r in reads:
            add(r.w)
        for w in writes:
            add(w.w)
            for tok in w.r.values():
                add(tok)
        out = []
        for key, (sem, val) in deps.items():
            if eng.seen.get(key, 0) >= val:
                continue
            if key == "pe" and eng.name == "pe":
                continue
            eng.seen[key] = val
            out.append((sem, val))
        return out

    def op(self, ename, fn, reads=(), writes=()):
        eng = self.engs[ename]
        waits = self._waits(eng, reads, writes)
        sem = eng.sem
        eng.count += 1
        tok = (ename, sem, eng.count)

        def run(e, waits=waits, fn=fn, sem=sem):
            for s, v in waits:
                e.wait_ge(s, v)
            fn(e).then_inc(sem, 1)

        eng.q.append(run)
        self.sched.setdefault(ename, []).append(([(self.semname(s), v) for s, v in waits], ename, 1))
        for r in reads:
            r.r[ename] = tok
        for w in writes:
            w.w = tok
            w.r = {}
        self.ninst += 1

    def dma(self, qname, ds, out, in_, reads=(), writes=(), **kw):
        eng = self.engs[qname]
        ds = self._pool_sem()
        waits = self._waits(eng, reads, writes)
        if ds.count > 0 and eng.seen.get(ds.key, 0) < ds.count:
            eng.seen[ds.key] = ds.count
            waits.append((ds.sem, ds.count))
        ds.count += 16
        tok = (ds.key, ds.sem, ds.count)

        def run(e, waits=waits, out=out, in_=in_, kw=kw, sem=ds.sem, cnt=ds.count):
            for s, v in waits:
                e.wait_ge(s, v)
            e.dma_start(out=out(e) if callable(out) else out, in_=in_(e) if callable(in_) else in_,
                        **kw).then_inc(sem, 16)

        eng.q.append(run)
        self.sched.setdefault(qname, []).append(([(self.semname(s), v) for s, v in waits], ds.key, 16))
        for r in reads:
            r.r[ds.key] = tok
        for w in writes:
            w.w = tok
            w.r = {}
        self.ninst += 1

    def coll(self, kind, groups, ins, outs, reads=(), writes=()):
        eng = self.engs["pool"]
        i = len([k_ for k_ in self.dsems if k_.startswith("cc")])
        ds = DSem("cc%d" % i, self.stack.enter_context(self.nc.semaphore("ccsem_%d" % i)))
        self.dsems[ds.key] = ds
        waits = self._waits(eng, reads, writes)
        ds.count += 1
        tok = (ds.key, ds.sem, ds.count)

        def run(e, waits=waits, sem=ds.sem):
            for s, v in waits:
                e.wait_ge(s, v)
            e.collective_compute(kind, ALU.bypass, replica_groups=groups, ins=[a.opt() for a in ins],
                                 outs=[a.opt() for a in outs], dma_qos="P2").then_inc(sem, 1)

        eng.q.append(run)
        self.sched.setdefault("pool", []).append(([(self.semname(s), v) for s, v in waits], ds.key, 1))
        for r in reads:
            r.r[ds.key] = tok
        for w in writes:
            w.w = tok
            w.r = {}

    def barrier(self, skip_cc=False):
        toks = [(e.name, e.sem, e.count) for e in self.engs.values() if e.count > 0]
        toks += [(d.key, d.sem, d.count) for d in self.dsems.values()
                 if d.count > 0 and not (skip_cc and d.key.startswith("cc"))]
        for eng in self.engs.values():
            waits = []
            for key, sem, val in toks:
                if eng.seen.get(key, 0) >= val:
                    continue
                eng.seen[key] = val
                waits.append((sem, val))

            def run(e, waits=waits):
                for s, v in waits:
                    e.wait_ge(s, v)

            eng.q.append(run)
            self.sched.setdefault(eng.name, []).append(([(self.semname(s), v) for s, v in waits], "_nop", 0))

    def final_wait(self, qname="sp"):
        eng = self.engs[qname]
        toks = [(d.key, d.sem, d.count) for d in self.dsems.values() if d.count > 0]
        toks += [(e.name, e.sem, e.count) for e in self.engs.values() if e.count > 0]

        def run(e, toks=toks):
            for key, s, v in toks:
                e.wait_ge(s, v)

        eng.q.append(run)

    def emit(self):
        nc = self.nc
        with nc.Block() as block:
            @block.tensor
            def _(e):
                for f in self.engs["pe"].q:
                    f(e)

            @block.scalar
            def _(e):
                for f in self.engs["act"].q:
                    f(e)

            @block.vector
            def _(e):
                for f in self.engs["dve"].q:
                    f(e)

            @block.gpsimd
            def _(e):
                for f in self.engs["pool"].q:
                    f(e)

            @block.sync
            def _(e):
                for f in self.engs["sp"].q:
                    f(e)


class Arena:
    def __init__(self, ap_f32, nbytes):
        self.ap = ap_f32
        self.nbytes = nbytes
        self.off = 0
        self.marks = []

    def alloc(self, shape_free, dtype, parts=128):
        n = int(np.prod(shape_free))
        esz = 2 if dtype == BF16 else 4
        nb = (n * esz + 31) // 32 * 32
        assert self.off + nb <= self.nbytes, ("arena overflow", self.off, nb, self.nbytes)
        a = self.ap[:, self.off // 4:(self.off + nb) // 4]
        self.off += nb
        if dtype == BF16:
            a = a.bitcast(BF16)
        a = a[:, 0:n]
        if len(shape_free) == 2:
            a = a.rearrange("p (a b) -> p a b", a=shape_free[0])
        elif len(shape_free) == 3:
            a = a.rearrange("p (a b c) -> p a b c", a=shape_free[0], b=shape_free[1])
        return a

    def mark(self):
        self.marks.append(self.off)

    def release(self):
        self.off = self.marks.pop()


class Prog:
    def __init__(self, nc, k, arena, psum):
        self.nc = nc
        self.k = k
        self.ar = arena
        self.psum = psum
        self.pres = [Res("ps%d" % i) for i in range(8)]
        self.pidx = 0

    def bank(self):
        i = self.pidx
        self.pidx = (self.pidx + 1) % 8
        return self.psum[:, i, :], self.pres[i]


def alloc_normtmp(P, tw=TT):
    ar = P.ar
    return dict(sq=ar.alloc([KC, tw], F32), sq_r=[Res() for _ in range(KC)],
                rstd=ar.alloc([tw], F32), rstd_r=Res())


def emit_rmsnorm(P, cst, nt, x, xres, gvec, gres, dst, ntiles=NT, tw=TT, after=None):
    k = P.k
    ones, ones_res = cst["ones_f"], cst["ones_f_r"]
    sq, sq_res, rstd, rstd_res = nt["sq"], nt["sq_r"], nt["rstd"], nt["rstd_r"]
    for tt in range(ntiles):
        sl = slice(tt * tw, (tt + 1) * tw)
        o, ores = dst(tt)
        ps, pr = P.bank()
        for kc in range(KC):
            k.op("act", lambda e, sl=sl, kc=kc: e.activation(out=sq[:, kc, 0:tw], in_=x[:, kc, sl], func=AF.Square),
                 reads=[xres[tt]], writes=[sq_res[kc]])
            k.op("pe", lambda e, ps=ps, kc=kc: e.matmul(ps[:, 0:tw], lhsT=ones, rhs=sq[:, kc, 0:tw],
                                                        start=(kc == 0), stop=(kc == KC - 1)),
                 reads=[sq_res[kc], ones_res], writes=[pr])
        k.op("act", lambda e, ps=ps: e.activation(out=rstd[:, 0:tw], in_=ps[:, 0:tw], func=AF.Ln,
                                                  bias=cst["eps"], scale=1.0 / D),
             reads=[pr, cst["eps_r"]], writes=[rstd_res])
        k.op("act", lambda e: e.activation(out=rstd[:, 0:tw], in_=rstd[:, 0:tw], func=AF.Exp, scale=-0.5),
             reads=[rstd_res], writes=[rstd_res])
        for kc in range(KC):
            k.op("dve", lambda e, kc=kc, sl=sl, o=o: e.scalar_tensor_tensor(
                out=o[:, kc, :], in0=x[:, kc, sl], scalar=gvec[:, kc:kc + 1], in1=rstd[:, 0:tw],
                op0=ALU.mult, op1=ALU.mult),
                reads=[xres[tt], gres, rstd_res], writes=[ores])
        if after is not None:
            after(tt, o, ores)


def consts_eps(P):
    return P.eps_ap


def emit_ffn(P, x, xres, xn, xnres, w1d, w2d, wb, tag):
    k = P.k
    (w1buf, w1res, w2buf, w2res, g, gres, stmp, stres) = wb
    ds1 = [None] * 3
    ds2 = [None] * 3
    NB1 = len(w1buf)
    NB2 = len(w2buf)
    cnt1 = 0
    cnt2 = 0
    for st in range(2):
        for f in range(NF):
            s = cnt1 % NB1
            cnt1 += 1
            k.dma("pool", ds1[s], out=w1buf[s], in_=w1d[f], reads=[], writes=[w1res[s]])
            for t2 in range(2):
                tt = st * 2 + t2
                sl = slice(tt * TT, (tt + 1) * TT)
                pa, par = P.bank()
                pb, pbr = P.bank()
                for kc in range(KC):
                    k.op("pe", lambda e, pa=pa, s=s, kc=kc, sl=sl: e.matmul(
                        pa, lhsT=w1buf[s][:, kc * 256:kc * 256 + 128], rhs=xn[:, kc, sl],
                        start=(kc == 0), stop=(kc == KC - 1)),
                        reads=[w1res[s], xnres[tt]], writes=[par])
                for kc in range(KC):
                    k.op("pe", lambda e, pb=pb, s=s, kc=kc, sl=sl: e.matmul(
                        pb, lhsT=w1buf[s][:, kc * 256 + 128:kc * 256 + 256], rhs=xn[:, kc, sl],
                        start=(kc == 0), stop=(kc == KC - 1)),
                        reads=[w1res[s], xnres[tt]], writes=[pbr])
                ts_ = (f * 2 + t2) % 2
                k.op("act", lambda e, pa=pa, ts_=ts_: e.activation(out=stmp[ts_], in_=pa, func=AF.Silu),
                     reads=[par], writes=[stres[ts_]])
                k.op("dve", lambda e, pb=pb, ts_=ts_, f=f, t2=t2: e.tensor_tensor(
                    out=g[:, f, t2 * TT:(t2 + 1) * TT], in0=stmp[ts_], in1=pb, op=ALU.mult),
                    reads=[stres[ts_], pbr], writes=[gres[f][t2]])
        for dc in range(KC):
            s = cnt2 % NB2
            cnt2 += 1
            k.dma("pool", ds2[s], out=w2buf[s], in_=w2d[dc], reads=[], writes=[w2res[s]],
                  max_dma_last_dim=1408 * 4)
            for t2 in range(2):
                tt = st * 2 + t2
                sl = slice(tt * TT, (tt + 1) * TT)
                po, por = P.bank()
                for f in range(NF):
                    k.op("pe", lambda e, po=po, s=s, f=f, t2=t2: e.matmul(
                        po, lhsT=w2buf[s][:, f * 128:(f + 1) * 128], rhs=g[:, f, t2 * TT:(t2 + 1) * TT],
                        start=(f == 0), stop=(f == NF - 1)),
                        reads=[w2res[s], gres[f][t2]], writes=[por])
                k.op("dve", lambda e, po=po, dc=dc, sl=sl: e.scalar_tensor_tensor(
                    out=x[:, dc, sl], in0=po, scalar=0.5, in1=x[:, dc, sl], op0=ALU.mult, op1=ALU.add),
                    reads=[por, xres[tt]], writes=[xres[tt]])


def pack_w1(w):
    a = w[:, :DFF].reshape(KC, 128, NF, 128)
    b = w[:, DFF:].reshape(KC, 128, NF, 128)
    ab = np.stack([a, b], axis=3)
    return np.ascontiguousarray(ab.transpose(2, 1, 0, 3, 4)).reshape(NF, 128, KC * 256)


def pack_w2(w):
    a = w.reshape(NF, 128, KC, 128)
    return np.ascontiguousarray(a.transpose(2, 1, 0, 3)).reshape(KC, 128, NF * 128)


def pack_vec(v):
    return np.ascontiguousarray(v.reshape(KC, 128).T)


def pack_xT(xc):
    return np.ascontiguousarray(xc.reshape(T, KC, 128).transpose(2, 1, 0))


def unpack_xT(a):
    return np.ascontiguousarray(a.transpose(2, 1, 0)).reshape(T, D)


def make_consts(P, mask_d):
    k, ar = P.k, P.ar
    c = {}
    c["ones_f"] = ar.alloc([128], F32); c["ones_f_r"] = Res()
    c["ones_bf"] = ar.alloc([128], BF16); c["ones_bf_r"] = Res()
    c["mask"] = ar.alloc([128], BF16); c["mask_r"] = Res()
    c["eps"] = ar.alloc([1], F32); c["eps_r"] = Res()
    P.eps_ap = c["eps"]; P.eps_res = c["eps_r"]
    k.op("pool", lambda e: e.memset(c["ones_f"], 1.0), writes=[c["ones_f_r"]])
    k.op("pool", lambda e: e.memset(c["ones_bf"], 1.0), writes=[c["ones_bf_r"]])
    k.op("pool", lambda e: e.memset(c["eps"], EPS), writes=[c["eps_r"]])
    k.dma("pool", k.dsem("cmask"), out=c["mask"], in_=mask_d, writes=[c["mask_r"]])
    return c


NGT = 16
NCB = 1026
VB_PSCALE = 0
VB_CONVW = 1
VB_CONVB = 9
VB_BA = 11
VB_BX = 13
VB_LAM = 15
VB_BF = 17
VB_SEL = 19
VB_CORR = 23
NVB = 39


def emit_phaseB(P, cst, usrc, wB_d, wsB_d, vecB_d, mdst, tile_done=None, ures=None, mid_hook=None):
    k = P.k
    ar = P.ar
    ar.mark()
    ones_f, ones_f_r, mask, mask_r = cst["ones_f"], cst["ones_f_r"], cst["mask"], cst["mask_r"]
    GB = [5, 6, 7]
    SB = [2, 3, 4]
    OB = [0, 1]
    gcnt = [0]
    scnt = [0]

    def gbank():
        i = GB[gcnt[0] % 3]
        gcnt[0] += 1
        return P.psum[:, i, :], P.pres[i]

    def sbank():
        i = SB[scnt[0] % 3]
        scnt[0] += 1
        return P.psum[:, i, :], P.pres[i]

    utile = ar.alloc([KC, TT], BF16); utile_r = Res()
    wB = ar.alloc([KC, NCB], BF16); wB_r = Res()
    wsB = ar.alloc([640], BF16); wsB_r = Res()
    vec = ar.alloc([NVB], F32); vec_r = Res()
    nbf = ar.alloc([2], F32); nbf_r = Res()
    nsp8 = ar.alloc([2], F32); nsp8_r = Res()
    Kaug = [ar.alloc([NGT * TT], BF16) for _ in range(2)]
    K_r = [[Res() for _ in range(NGT)] for _ in range(2)]
    Vaug = ar.alloc([NGT * 4, 2, 65], BF16)
    V_r = [Res() for _ in range(NGT)]
    negc = ar.alloc([NGT * 4, 2], F32)
    negc_r = [[Res() for _ in range(NGT)] for _ in range(2)]
    xa_buf = ar.alloc([528], F32); xa_r = Res()
    sA = ar.alloc([528], F32); sA_r = Res()
    sBf = ar.alloc([528], F32); sB_r = Res()
    acc = ar.alloc([TT], F32); acc_r = Res()
    d_bf = [ar.alloc([TT], BF16) for _ in range(2)]; d_r = [Res(), Res()]
    xb_buf = [ar.alloc([515], F32) for _ in range(2)]; xb_r = [Res(), Res()]
    xc = [[ar.alloc([TT], F32) for _ in range(2)] for _ in range(2)]
    xc_r = [[Res(), Res()] for _ in range(2)]
    xc_bf = [[ar.alloc([TT], BF16) for _ in range(2)] for _ in range(2)]
    xcb_r = [[Res(), Res()] for _ in range(2)]
    gg = [[ar.alloc([TT], BF16) for _ in range(2)] for _ in range(2)]
    gg_r = [[Res(), Res()] for _ in range(2)]
    t1 = ar.alloc([TT], F32); t1_r = Res()
    Qaug = [[ar.alloc([TT], BF16) for _ in range(2)] for _ in range(2)]
    Q_r = [[Res(), Res()] for _ in range(2)]
    rowe = ar.alloc([TT], F32); rowe_r = Res()
    crow = [ar.alloc([TT], F32) for _ in range(2)]; crow_r = [Res(), Res()]
    clast = ar.alloc([2], F32); clast_r = [Res(), Res()]
    ra = ar.alloc([TT], F32); ra_r = Res()
    tmp = ar.alloc([TT], F32); tmp_r = Res()
    ig = ar.alloc([TT], F32); ig_r = Res()
    hbuf = ar.alloc([TT], F32); hbuf_r = Res()
    hlast = ar.alloc([2], F32); hlast_r = [Res(), Res()]
    Pt = [ar.alloc([TT], BF16) for _ in range(3)]; Pt_r = [Res(), Res(), Res()]
    rec = ar.alloc([TT], F32); rec_r = Res()
    osb = ar.alloc([TT], F32); osb_r = Res()
    outA = [ar.alloc([TT], BF16)] * 2; outA_r = [Res()] * 2
    outB = [ar.alloc([2, TT], BF16) for _ in range(2)]; outB_r = [Res(), Res()]
    outC = [[ar.alloc([TT], BF16)] * 2 for _ in range(2)]
    outC_r = [[Res()] * 2 for _ in range(2)]
    ds_u = k.dsem("bu")
    ds_o = k.dsem("bo")
    store_res = {0: [], 1: []}

    WGRP = [(640, 770), (0, 128), (128, 384), (384, 640), (770, 898), (898, 1026)]
    wB_rs = {g_: Res() for g_ in WGRP}
    for (c0_, c1_) in WGRP:
        k.dma("pool", k.dsem("bw"), out=wB[:, :, c0_:c1_], in_=wB_d[:, :, c0_:c1_], writes=[wB_rs[(c0_, c1_)]])

    def wres(c0):
        for (a_, b_), r_ in wB_rs.items():
            if a_ <= c0 < b_:
                return r_
    k.dma("pool", k.dsem("bws"), out=wsB, in_=wsB_d, writes=[wsB_r])
    k.dma("sp", k.dsem("bv"), out=vec, in_=vecB_d, writes=[vec_r])
    for h in range(2):
        k.op("pool", lambda e, h=h: e.memset(Kaug[h][64:65, :], 1.0), writes=K_r[h])
    k.op("pool", lambda e: e.memset(Vaug[:, :, :, 64:65], 1.0), writes=V_r)
    k.op("dve", lambda e: e.tensor_scalar(out=nbf, in0=vec[:, VB_BF:VB_BF + 2], scalar1=-1.0, scalar2=None,
                                          op0=ALU.mult), reads=[vec_r], writes=[nbf_r])
    k.op("act", lambda e: e.activation(out=nsp8, in_=vec[:, VB_LAM:VB_LAM + 2], func=AF.Exp, scale=-1.0),
         reads=[vec_r], writes=[nsp8_r])
    k.op("act", lambda e: e.activation(out=nsp8, in_=nsp8, func=AF.Ln, bias=ones_f[:, 0:1], scale=1.0),
         reads=[nsp8_r, ones_f_r], writes=[nsp8_r])
    k.op("dve", lambda e: e.tensor_scalar(out=nsp8, in0=nsp8, scalar1=-8.0, scalar2=None, op0=ALU.mult),
         reads=[nsp8_r], writes=[nsp8_r])

    def bank_any():
        return gbank()

    def S1(gt):
        r, tt = gt // 4, gt % 4
        par = gt % 2
        C = {}

        def proj(c0, ncol, nparts):
            ps, pr = gbank()
            for kc in range(KC):
                k.op("pe", lambda e, ps=ps, kc=kc: e.matmul(ps[0:nparts, :], lhsT=wB[:, kc, c0:c0 + ncol],
                                                          rhs=utile[:, kc, :], start=(kc == 0), stop=(kc == KC - 1)),
                     reads=[wres(c0), utile_r], writes=[pr])
            return ps, pr

        def m_loadnext():
            if gt + 1 < NGT:
                r2, tt2 = (gt + 1) // 4, (gt + 1) % 4
                k.dma("sp", ds_u, out=utile, in_=usrc(r2, tt2), reads=(ures(tt2) if ures else []), writes=[utile_r])
        C["loadnext"] = [m_loadnext]

        st = {}

        def xa1():
            if gt == 0:
                k.op("dve", lambda e: e.memset(xa_buf[:, 0:16], 0.0), writes=[xa_r])
            else:
                k.op("dve", lambda e: e.tensor_copy(out=xa_buf[:, 0:16], in_=xa_buf[:, 512:528]),
                     reads=[xa_r], writes=[xa_r])
            st["xa"] = proj(0, 128, 128)

        def xa2():
            ps, pr = st["xa"]
            k.op("dve", lambda e: e.tensor_copy(out=xa_buf[:, 16:528], in_=ps), reads=[pr], writes=[xa_r])

            def padd(dst, dr, srcb, sr, lo, sh):
                k.op("dve", lambda e: e.tensor_tensor(out=dst[:, lo:528], in0=srcb[:, lo:528],
                                                      in1=srcb[:, lo - sh:528 - sh], op=ALU.add),
                     reads=[sr], writes=[dr])

            def pacc(srcb, sr, i):
                if i == 0:
                    k.op("dve", lambda e: e.tensor_scalar(out=acc, in0=srcb[:, 16:528],
                                                          scalar1=vec[:, VB_SEL:VB_SEL + 1], scalar2=None, op0=ALU.mult),
                         reads=[sr, vec_r], writes=[acc_r])
                else:
                    k.op("dve", lambda e: e.scalar_tensor_tensor(out=acc, in0=srcb[:, 16:528],
                                                                 scalar=vec[:, VB_SEL + i:VB_SEL + i + 1], in1=acc,
                                                                 op0=ALU.mult, op1=ALU.add),
                         reads=[sr, vec_r, acc_r], writes=[acc_r])

            padd(sA, sA_r, xa_buf, xa_r, 1, 1)
            pacc(sA, sA_r, 0)
            padd(sBf, sB_r, sA, sA_r, 3, 2)
            pacc(sBf, sB_r, 1)
            padd(sA, sA_r, sBf, sB_r, 7, 4)
            pacc(sA, sA_r, 2)
            padd(sBf, sB_r, sA, sA_r, 15, 8)
            pacc(sBf, sB_r, 3)
            if gt == 0:
                k.op("dve", lambda e: e.tensor_tensor(out=acc[:, 0:16], in0=acc[:, 0:16],
                                                      in1=vec[:, VB_CORR:VB_CORR + 16], op=ALU.mult),
                     reads=[acc_r, vec_r], writes=[acc_r])
            k.op("dve", lambda e: e.tensor_tensor(out=d_bf[par], in0=acc, in1=xa_buf[:, 16:528], op=ALU.subtract),
                 reads=[xa_r, acc_r], writes=[d_r[par]])
        C["xa"] = [xa1, xa2]

        def mk_xb(hh):
            def m1():
                if gt == 0:
                    k.op("dve", lambda e: e.memset(xb_buf[hh][:, 0:3], 0.0), writes=[xb_r[hh]])
                else:
                    k.op("dve", lambda e: e.tensor_copy(out=xb_buf[hh][:, 0:3], in_=xb_buf[hh][:, 512:515]),
                         reads=[xb_r[hh]], writes=[xb_r[hh]])
                st["xb%d" % hh] = proj(128 + 128 * hh, 128, 128)

            def m2():
                ps, pr = st["xb%d" % hh]
                k.op("dve", lambda e: e.tensor_copy(out=xb_buf[hh][:, 3:515], in_=ps), reads=[pr], writes=[xb_r[hh]])
                cw = VB_CONVW + 4 * hh
                k.op("dve", lambda e: e.tensor_scalar(
                    out=xc[hh][par], in0=xb_buf[hh][:, 0:512], scalar1=vec[:, cw:cw + 1],
                    scalar2=vec[:, VB_CONVB + hh:VB_CONVB + hh + 1], op0=ALU.mult, op1=ALU.add),
                    reads=[xb_r[hh], vec_r], writes=[xc_r[hh][par]])
                for kk in range(1, 4):
                    k.op("dve", lambda e, kk=kk: e.scalar_tensor_tensor(
                        out=xc[hh][par], in0=xb_buf[hh][:, kk:kk + 512], scalar=vec[:, cw + kk:cw + kk + 1],
                        in1=xc[hh][par], op0=ALU.mult, op1=ALU.add),
                        reads=[xb_r[hh], vec_r, xc_r[hh][par]], writes=[xc_r[hh][par]])
                k.op("dve", lambda e: e.tensor_copy(out=xc_bf[hh][par], in_=xc[hh][par]),
                     reads=[xc_r[hh][par]], writes=[xcb_r[hh][par]])
            return [m1, m2]

        def mk_gb(hh):
            def m1():
                st["gb%d" % hh] = proj(384 + 128 * hh, 128, 128)

            def m2():
                ps, pr = st["gb%d" % hh]
                k.op("act", lambda e: e.activation(out=t1, in_=ps, func=AF.Square), reads=[pr], writes=[t1_r])
                k.op("dve", lambda e: e.tensor_copy(out=gg[hh][par], in_=ps), reads=[pr, t1_r], writes=[gg_r[hh][par]])

            def m3():
                ps, pr = st["gb%d" % hh]
                k.op("dve", lambda e: e.tensor_scalar(out=t1, in0=t1, scalar1=0.044715, scalar2=1.0, op0=ALU.mult,
                                                      op1=ALU.add), reads=[t1_r], writes=[t1_r])
                k.op("dve", lambda e: e.tensor_tensor(out=t1, in0=t1, in1=ps, op=ALU.mult),
                     reads=[t1_r, pr], writes=[t1_r])

            def m4():
                k.op("act", lambda e: e.activation(out=t1, in_=t1, func=AF.Sigmoid, scale=1.5957691216057308),
                     reads=[t1_r], writes=[t1_r])

            def m5():
                k.op("dve", lambda e: e.tensor_tensor(out=gg[hh][par], in0=gg[hh][par], in1=t1, op=ALU.mult),
                     reads=[t1_r, gg_r[hh][par]], writes=[gg_r[hh][par]])
            return [m1, m2, m3, m4, m5]

        def mk_q(h):
            def m1():
                st["q%d" % h] = proj(640 + 65 * h, 65, 65)

            def m2():
                ps, pr = st["q%d" % h]
                k.op("dve", lambda e: e.tensor_copy(out=Qaug[h][par][0:64, :], in_=ps[0:64, :]),
                     reads=[pr], writes=[Q_r[h][par]])
                k.op("act", lambda e: e.activation(out=rowe[64:65, :], in_=ps[64:65, :], func=AF.Exp,
                                                   bias=nbf[64:65, h:h + 1], scale=-1.0),
                     reads=[pr, nbf_r], writes=[rowe_r])
                k.op("act", lambda e: e.activation(out=rowe[64:65, :], in_=rowe[64:65, :], func=AF.Ln,
                                                   bias=ones_f[64:65, 0:1], scale=1.0),
                     reads=[rowe_r, ones_f_r], writes=[rowe_r])

            def m3():
                init = 0.0 if gt == 0 else clast[64:65, h:h + 1]
                k.op("dve", lambda e: e.tensor_tensor_scan(
                    out=crow[h][64:65, :], data0=ones_f[64:65, 0:1].to_broadcast([1, TT]),
                    data1=rowe[64:65, :], initial=init, op0=ALU.mult, op1=ALU.subtract),
                    reads=[rowe_r, clast_r[h], ones_f_r], writes=[crow_r[h]])
                k.op("dve", lambda e: e.tensor_copy(out=clast[64:65, h:h + 1], in_=crow[h][64:65, TT - 1:TT]),
                     reads=[crow_r[h]], writes=[clast_r[h]])

            def m4():
                k.op("act", lambda e: e.activation(out=Qaug[h][par][64:65, :], in_=crow[h][64:65, :],
                                                   func=AF.Copy, scale=8.0),
                     reads=[crow_r[h]], writes=[Q_r[h][par]])

            def m5():
                pt_, ptr_ = gbank()
                st["n%d" % h] = (pt_, ptr_)
                for blk in range(4):
                    k.op("pe", lambda e, blk=blk: e.matmul(
                        pt_[:, blk:blk + 1], lhsT=crow[h][64:65, blk * 128:(blk + 1) * 128], rhs=ones_f[64:65, 0:1],
                        start=True, stop=True),
                        reads=[crow_r[h], ones_f_r], writes=[ptr_])

            def m6():
                pt_, ptr_ = st["n%d" % h]
                k.op("dve", lambda e: e.tensor_scalar(
                    out=negc[:, 4 * gt:4 * gt + 4, h], in0=pt_[:, 0:4], scalar1=-1.0, scalar2=None, op0=ALU.mult),
                    reads=[ptr_], writes=[negc_r[h][gt]])
            return [m1, m2, m3, m4, m5, m6]

        def mk_k(h):
            def m1():
                st["k%d" % h] = proj(770 + 64 * h, 64, 64)

            def m2():
                ps, pr = st["k%d" % h]
                k.op("act", lambda e: e.activation(out=Kaug[h][0:64, gt * TT:(gt + 1) * TT],
                                                   in_=ps[0:64, :], func=AF.Copy),
                     reads=[pr], writes=[K_r[h][gt]])
            return [m1, m2]

        def v1():
            ps, pr = gbank()
            st["v"] = (ps, pr)
            for blk in range(4):
                for kc in range(KC):
                    k.op("pe", lambda e, blk=blk, kc=kc: e.matmul(
                        ps[:, blk * 128:(blk + 1) * 128], lhsT=utile[:, kc, blk * 128:(blk + 1) * 128],
                        rhs=wB[:, kc, 898:1026], start=(kc == 0), stop=(kc == KC - 1)),
                        reads=[wres(898), utile_r], writes=[pr])

        def v2():
            ps, pr = st["v"]
            k.op("dve", lambda e: e.tensor_copy(
                out=Vaug[:, 4 * gt:4 * gt + 4, :, 0:64], in_=ps.rearrange("p (b h d) -> p b h d", b=4, h=2)),
                reads=[pr], writes=[V_r[gt]])
        C["xb0"], C["xb1"] = mk_xb(0), mk_xb(1)
        C["gb0"], C["gb1"] = mk_gb(0), mk_gb(1)
        C["q0"], C["q1"] = mk_q(0), mk_q(1)
        C["k0"], C["k1"] = mk_k(0), mk_k(1)
        C["v"] = [v1, v2]
        return C

    def S2chains(gt):
        r, tt = gt // 4, gt % 4
        par = gt % 2
        tsl = slice(tt * TT, (tt + 1) * TT)
        C = {}
        st = {}

        def pool1():
            ps, pr = gbank()
            st["pool"] = (ps, pr)
            k.op("pe", lambda e: e.matmul(ps, lhsT=wsB[:, 0:128], rhs=d_bf[par], start=True, stop=True),
                 reads=[wsB_r, d_r[par]], writes=[pr])

        def pool2():
            ps, pr = st["pool"]
            k.op("act", lambda e: e.activation(out=outA[par], in_=ps, func=AF.Identity,
                                               scale=vec[:, VB_PSCALE:VB_PSCALE + 1]),
                 reads=[pr, vec_r], writes=[outA_r[par]])
            for dap, ps_ in mdst("A", r, tsl):
                sr = Res(); store_res[0].append(sr)
                k.dma("sp", ds_o, out=dap, in_=outA[par][ps_, :], reads=[outA_r[par]], writes=[sr])
        C["pool"] = [pool1, pool2]

        def mk_lru(hh):
            def m1():
                ps, pr = gbank()
                k.op("pe", lambda e: e.matmul(ps, lhsT=wsB[:, 128 + 128 * hh:256 + 128 * hh],
                                              rhs=xc_bf[hh][par], start=True, stop=True),
                     reads=[wsB_r, xcb_r[hh][par]], writes=[pr])
                ps2, pr2 = gbank()
                k.op("pe", lambda e: e.matmul(ps2, lhsT=wsB[:, 384 + 128 * hh:512 + 128 * hh],
                                              rhs=xc_bf[hh][par], start=True, stop=True),
                     reads=[wsB_r, xcb_r[hh][par]], writes=[pr2])
                st["l%d" % hh] = (ps, pr, ps2, pr2)

            def m2():
                ps, pr, ps2, pr2 = st["l%d" % hh]
                k.op("act", lambda e: e.activation(out=ra, in_=ps, func=AF.Sigmoid,
                                                   bias=vec[:, VB_BA + hh:VB_BA + hh + 1], scale=1.0),
                     reads=[pr, vec_r], writes=[ra_r])
                k.op("act", lambda e: e.activation(out=ig, in_=ps2, func=AF.Sigmoid,
                                                   bias=vec[:, VB_BX + hh:VB_BX + hh + 1], scale=1.0),
                     reads=[pr2, vec_r], writes=[ig_r])
                k.op("act", lambda e: e.activation(out=ra, in_=ra, func=AF.Exp, scale=nsp8[:, hh:hh + 1]),
                     reads=[ra_r, nsp8_r], writes=[ra_r])

            def m3():
                k.op("dve", lambda e: e.tensor_tensor(out=tmp, in0=ra, in1=ra, op=ALU.mult),
                     reads=[ra_r], writes=[tmp_r])
                k.op("dve", lambda e: e.tensor_tensor(out=ig, in0=ig, in1=xc[hh][par], op=ALU.mult),
                     reads=[ig_r, xc_r[hh][par]], writes=[ig_r])

            def m4():
                k.op("act", lambda e: e.activation(out=tmp, in_=tmp, func=AF.Sqrt, bias=ones_f[:, 0:1], scale=-1.0),
                     reads=[tmp_r, ones_f_r], writes=[tmp_r])

            def m5():
                k.op("dve", lambda e: e.tensor_tensor(out=ig, in0=ig, in1=tmp, op=ALU.mult),
                     reads=[ig_r, tmp_r], writes=[ig_r])
                init = 0.0 if gt == 0 else hlast[:, hh:hh + 1]
                k.op("dve", lambda e: e.tensor_tensor_scan(out=hbuf, data0=ra, data1=ig, initial=init,
                                                           op0=ALU.mult, op1=ALU.add),
                     reads=[ra_r, ig_r, hlast_r[hh]], writes=[hbuf_r])
                k.op("dve", lambda e: e.tensor_copy(out=hlast[:, hh:hh + 1], in_=hbuf[:, TT - 1:TT]),
                     reads=[hbuf_r], writes=[hlast_r[hh]])
                k.op("dve", lambda e: e.tensor_tensor(out=outB[par][:, hh, :], in0=hbuf, in1=gg[hh][par], op=ALU.mult),
                     reads=[hbuf_r, gg_r[hh][par]], writes=[outB_r[par]])
                for dap, ps_, hx in mdst("B", r, tsl):
                    if hx == hh:
                        sr = Res(); store_res[hx].append(sr)
                        k.dma("sp", ds_o, out=dap, in_=outB[par][ps_, hx, :], reads=[outB_r[par]], writes=[sr])
                if hh == 0 and mid_hook is not None:
                    mid_hook(gt, store_res)
            return [m1, m2, m3, m4, m5]
        C["lru0"], C["lru1"] = mk_lru(0), mk_lru(1)
        return C

    ORDER = [("pool", 0), ("q0", 0), ("pool", 1), ("q0", 1), ("lru0", 0), ("q0", 2), ("lru0", 1),
             ("q0", 3), ("q1", 0), ("lru0", 2), ("q1", 1), ("lru0", 3), ("q1", 2), ("lru0", 4), ("q1", 3),
             ("lru1", 0), ("q0", 4), ("lru1", 1), ("q0", 5), ("lru1", 2), ("q1", 4), ("lru1", 3), ("q1", 5),
             ("lru1", 4), ("xa", 0), ("xb0", 0), ("xa", 1), ("xb0", 1), ("xb1", 0), ("gb0", 0), ("xb1", 1),
             ("gb0", 1), ("k0", 0), ("gb0", 2), ("k0", 1), ("gb0", 3), ("k1", 0), ("gb0", 4), ("k1", 1),
             ("gb1", 0), ("v", 0), ("loadnext", 0), ("gb1", 1), ("v", 1), ("gb1", 2), ("gb1", 3), ("gb1", 4)]

    def flat(c2, c1):
        out = []
        for name, i in ORDER:
            src_ = c2 if name in ("pool", "lru0", "lru1") else c1
            if src_ is not None:
                out.append(src_[name][i])
        return out

    def S2(gt, steps):
        r, tt = gt // 4, gt % 4
        par = gt % 2
        tsl = slice(tt * TT, (tt + 1) * TT)
        nkb = 4 * gt + 4

        def attn(h):
            oi = OB[(2 * gt + h) % 2]
            ob, obr = P.psum[:, oi, :], P.pres[oi]
            sbs = {}

            def emit_S(kb):
                dd = kb - 4 * gt
                q0 = 128 * dd if dd > 0 else 0
                sb, sbr = sbank()
                sbs[kb] = (sb, sbr, q0, dd)
                k.op("pe", lambda e, sb=sb, q0=q0, kb=kb: e.matmul(
                    sb[:, q0:TT], lhsT=Kaug[h][0:65, kb * 128:(kb + 1) * 128], rhs=Qaug[h][par][0:65, q0:TT],
                    start=True, stop=True),
                    reads=[K_r[h][kb // 4], Q_r[h][par]], writes=[sbr])

            emit_S(0)
            if nkb > 1:
                emit_S(1)
            for kb in range(nkb):
                if kb + 2 < nkb:
                    emit_S(kb + 2)
                sb, sbr, q0, dd = sbs.pop(kb)
                pi = kb % 3
                k.op("act", lambda e, sb=sb, q0=q0, kb=kb, pi=pi: e.activation(
                    out=Pt[pi][:, q0:TT], in_=sb[:, q0:TT], func=AF.Exp, bias=negc[:, kb, h:h + 1], scale=0.125),
                    reads=[sbr, negc_r[h][kb // 4]], writes=[Pt_r[pi]])
                if dd >= 0:
                    k.op("dve", lambda e, q0=q0, pi=pi: e.tensor_tensor(
                        out=Pt[pi][:, q0:q0 + 128], in0=Pt[pi][:, q0:q0 + 128], in1=mask, op=ALU.min),
                        reads=[Pt_r[pi], mask_r], writes=[Pt_r[pi]])
                k.op("pe", lambda e, q0=q0, kb=kb, pi=pi: e.matmul(
                    ob[0:65, q0:TT], lhsT=Vaug[:, kb, h, :], rhs=Pt[pi][:, q0:TT],
                    start=(kb == 0), stop=(kb == nkb - 1)),
                    reads=[V_r[kb // 4], Pt_r[pi]], writes=[obr])
                if steps:
                    steps.pop(0)()
            k.op("dve", lambda e: e.reciprocal(out=rec[64:65, :], in_=ob[64:65, :]), reads=[obr], writes=[rec_r])
            bc, bcr = sbank()
            k.op("pe", lambda e, bc=bc: e.matmul(bc[0:64, :], lhsT=ones_f[64:65, 0:64], rhs=rec[64:65, :],
                                                 start=True, stop=True),
                 reads=[rec_r, ones_f_r], writes=[bcr])
            k.op("act", lambda e: e.activation(out=osb[0:64, :], in_=ob[0:64, :], func=AF.Copy),
                 reads=[obr], writes=[osb_r])
            k.op("dve", lambda e, bc=bc: e.tensor_tensor(out=outC[h][par][0:64, :], in0=osb[0:64, :], in1=bc[0:64, :],
                                                         op=ALU.mult),
                 reads=[osb_r, bcr], writes=[outC_r[h][par]])
            for dap, ps_ in mdst("C", r, tsl, h):
                sr = Res(); store_res[1].append(sr)
                k.dma("sp", ds_o, out=dap, in_=outC[h][par][ps_, :], reads=[outC_r[h][par]], writes=[sr])
        for h in range(2):
            attn(h)
        while steps:
            steps.pop(0)()
        if tile_done is not None:
            tile_done(gt, store_res)

    k.dma("sp", ds_u, out=utile, in_=usrc(0, 0), reads=(ures(0) if ures else []), writes=[utile_r])
    for s_ in flat(None, S1(0)):
        s_()
    for gt in range(NGT):
        S2(gt, flat(S2chains(gt), S1(gt + 1) if gt + 1 < NGT else None))
    ar.release()


def pack_wB(w_in_l, j):
    cols = list(range(128 * j, 128 * j + 128))
    cols += list(range(512 + 256 * j, 512 + 256 * j + 256))
    cols += list(range(1536 + 256 * j, 1536 + 256 * j + 256))
    for h in range(2):
        hd = 2 * j + h
        cols += list(range(2560 + 64 * hd, 2560 + 64 * hd + 64)) + [4096 + hd]
    cols += list(range(3072 + 128 * j, 3072 + 128 * j + 128))
    cols += list(range(3584 + 128 * j, 3584 + 128 * j + 128))
    w = w_in_l[:, cols]
    return np.ascontiguousarray(w.reshape(KC, 128, NCB).transpose(1, 0, 2))


def pack_wsB(inp, l, j):
    parts = [inp["w_pool"][l, j]]
    parts += [inp["w_rg_a"][l, 2 * j + hh] for hh in range(2)]
    parts += [inp["w_rg_x"][l, 2 * j + hh] for hh in range(2)]
    return np.ascontiguousarray(np.concatenate(parts, axis=1))


def pack_vecB(inp, l, j):
    v = np.zeros((128, NVB), np.float32)
    v[:, VB_PSCALE] = inp["pool_scale"][l, 128 * j:128 * j + 128]
    for hh in range(2):
        sl = slice(256 * j + 128 * hh, 256 * j + 128 * hh + 128)
        for kk in range(4):
            v[:, VB_CONVW + 4 * hh + kk] = inp["conv_w"][l, kk, sl]
        v[:, VB_CONVB + hh] = inp["conv_b"][l, sl]
        v[:, VB_BA + hh] = inp["b_rg_a"][l, sl]
        v[:, VB_BX + hh] = inp["b_rg_x"][l, sl]
        v[:, VB_LAM + hh] = inp["lru_lambda"][l, sl]
        v[:, VB_BF + hh] = inp["b_f"][l, 2 * j + hh]
    w = 2 ** (j + 1)
    v[:, VB_SEL + j] = np.float32(1.0 / w)
    for t in range(16):
        v[:, VB_CORR + t] = np.float32(w / min(t + 1, w))
    return v


def const_mask():
    jj = np.arange(128)[:, None]
    tt = np.arange(128)[None, :]
    return np.where(jj <= tt, np.float32(3.0e38), np.float32(0.0)).astype(np.float32)


def emit_phaseC(P, cst, x, xres, umine, mix_all_d, wC_d, bgate_d, wo_d, mix_reads=None, pre_mix=None):
    k, ar = P.k, P.ar
    ar.mark()
    u = ar.alloc([KC, T], BF16); u_r = [Res() for _ in range(NT)]
    mixs = ar.alloc([16, 2 * TT], BF16); mixs_r = [Res() for _ in range(16)]
    merged = ar.alloc([KC, 2 * TT], BF16); mg_r = [[Res(), Res()] for _ in range(KC)]
    wcb = [ar.alloc([5120], BF16) for _ in range(3)]; wcb_r = [Res(), Res(), Res()]
    wob = [ar.alloc([1024], BF16) for _ in range(2)]; wob_r = [Res(), Res()]
    gate = [ar.alloc([TT], F32) for _ in range(3)]; gate_r = [Res(), Res(), Res()]
    acc = [ar.alloc([TT], F32) for _ in range(2)]; acc_r = [Res(), Res()]
    bg = ar.alloc([24], F32); bg_r = Res()
    ds_u = k.dsem("cu")
    for tt in range(NT):
        k.dma("sp", ds_u, out=u[:, :, tt * TT:(tt + 1) * TT], in_=umine(tt), writes=[u_r[tt]])
    k.dma("sp", k.dsem("cbg"), out=bg, in_=bgate_d, writes=[bg_r])
    if pre_mix is not None:
        pre_mix()
    dsm = k.dsem("cmix")
    dsw = [None] * 3
    dso = [None] * 2
    cw = 0
    co = 0
    for st in range(2):
        cs = slice(st * 2 * TT, (st + 1) * 2 * TT)
        for r in range(4):
            if callable(mix_all_d):
                srcs = [(r, mix_all_d(r, 0, cs)), (4 + 2 * r, mix_all_d(r, 1, cs)), (5 + 2 * r, mix_all_d(r, 2, cs)),
                        (12 + r, mix_all_d(r, 3, cs))]
            else:
                srcs = [(r, mix_all_d[r, 0:128, cs]), (4 + 2 * r, mix_all_d[r, 128:256, cs]),
                        (5 + 2 * r, mix_all_d[r, 256:384, cs]), (12 + r, mix_all_d[r, 384:512, cs])]
            for pi_, (m_, s_) in enumerate(srcs):
                k.dma("sp", dsm, out=mixs[:, m_, :], in_=s_, reads=(mix_reads(r, pi_) if mix_reads else []),
                      writes=[mixs_r[m_]])
        for dc in range(KC):
            s = cw % 3
            cw += 1
            k.dma("pool", dsw[s], out=wcb[s], in_=wC_d[dc], writes=[wcb_r[s]], max_dma_last_dim=1024 * 4)
            for t2 in range(2):
                tt = st * 2 + t2
                sl = slice(tt * TT, (tt + 1) * TT)
                sl2 = slice(t2 * TT, (t2 + 1) * TT)
                for br in range(3):
                    ps, pr = P.bank()
                    for kc in range(KC):
                        c0 = 2048 + kc * 384 + br * 128
                        k.op("pe", lambda e, ps=ps, s=s, kc=kc, c0=c0, sl=sl: e.matmul(
                            ps, lhsT=wcb[s][:, c0:c0 + 128], rhs=u[:, kc, sl], start=(kc == 0), stop=(kc == KC - 1)),
                            reads=[wcb_r[s], u_r[tt]], writes=[pr])
                    k.op("act", lambda e, ps=ps, br=br, dc=dc: e.activation(
                        out=gate[br], in_=ps, func=AF.Sigmoid, bias=bg[:, br * 8 + dc:br * 8 + dc + 1], scale=1.0),
                        reads=[pr, bg_r], writes=[gate_r[br]])
                ys = []
                for (m0, m1) in ((0, 4), (4, 12), (12, 16)):
                    ps, pr = P.bank()
                    for m in range(m0, m1):
                        k.op("pe", lambda e, ps=ps, s=s, m=m, sl2=sl2, m0=m0, m1=m1: e.matmul(
                            ps, lhsT=wcb[s][:, m * 128:(m + 1) * 128], rhs=mixs[:, m, sl2],
                            start=(m == m0), stop=(m == m1 - 1)),
                            reads=[wcb_r[s], mixs_r[m]], writes=[pr])
                    ys.append((ps, pr))
                k.op("dve", lambda e, ys=ys: e.tensor_tensor(out=acc[0], in0=ys[0][0], in1=gate[0], op=ALU.mult),
                     reads=[ys[0][1], gate_r[0]], writes=[acc_r[0]])
                k.op("dve", lambda e, ys=ys: e.tensor_tensor(out=acc[1], in0=ys[1][0], in1=gate[1], op=ALU.mult),
                     reads=[ys[1][1], gate_r[1]], writes=[acc_r[1]])
                k.op("dve", lambda e: e.tensor_tensor(out=acc[0], in0=acc[0], in1=acc[1], op=ALU.add),
                     reads=[acc_r[0], acc_r[1]], writes=[acc_r[0]])
                k.op("dve", lambda e, ys=ys: e.tensor_tensor(out=acc[1], in0=ys[2][0], in1=gate[2], op=ALU.mult),
                     reads=[ys[2][1], gate_r[2]], writes=[acc_r[1]])
                k.op("dve", lambda e, dc=dc, sl2=sl2: e.tensor_tensor(out=merged[:, dc, sl2], in0=acc[0], in1=acc[1],
                                                                      op=ALU.add),
                     reads=[acc_r[0], acc_r[1]], writes=[mg_r[dc][t2]])
        for dco in range(KC):
            s = co % 2
            co += 1
            k.dma("pool", dso[s], out=wob[s], in_=wo_d[dco], writes=[wob_r[s]])
            for t2 in range(2):
                tt = st * 2 + t2
                sl = slice(tt * TT, (tt + 1) * TT)
                sl2 = slice(t2 * TT, (t2 + 1) * TT)
                ps, pr = P.bank()
                for dc in range(KC):
                    k.op("pe", lambda e, ps=ps, s=s, dc=dc, sl2=sl2: e.matmul(
                        ps, lhsT=wob[s][:, dc * 128:(dc + 1) * 128], rhs=merged[:, dc, sl2],
                        start=(dc == 0), stop=(dc == KC - 1)),
                        reads=[wob_r[s], mg_r[dc][t2]], writes=[pr])
                k.op("dve", lambda e, ps=ps, dco=dco, sl=sl: e.tensor_tensor(out=x[:, dco, sl], in0=ps, in1=x[:, dco, sl],
                                                                             op=ALU.add),
                     reads=[pr, xres[tt]], writes=[xres[tt]])
    ar.release()


def emit_cross(P, cst, x, xres, memT_d, gmem, gmem_r, gcross, gcross_r, wxq_d, wxkv_d, wxo_d):
    k, ar = P.k, P.ar
    ar.mark()
    NM = 256
    xn = ar.alloc([KC, T], BF16); xn_r = [Res() for _ in range(NT)]
    QT = ar.alloc([KC, T], BF16); QT_r = [[Res() for _ in range(NT)] for _ in range(KC)]
    memf = ar.alloc([KC, NM], F32); memf_r = [Res()]
    memn = ar.alloc([KC, NM], BF16); memn_r = Res()
    KxT = ar.alloc([KC, NM], BF16); Kx_r = [Res() for _ in range(KC)]
    Vx = ar.alloc([2, D], BF16); Vx_r = [Res() for _ in range(KC)]
    wch = [ar.alloc([1024], BF16) for _ in range(2)]; wch_r = [Res(), Res()]
    Pt = [[ar.alloc([TT], BF16) for _ in range(2)] for _ in range(2)]; Pt_r = [[Res(), Res()] for _ in range(2)]
    rec = ar.alloc([TT], F32); rec_r = Res()
    nt = alloc_normtmp(P)
    dsw = [k.dsem("xw0"), k.dsem("xw1")]
    wc = [0]

    def wload(src):
        s = wc[0] % 2
        wc[0] += 1
        k.dma("pool", dsw[s], out=wch[s], in_=src, writes=[wch_r[s]])
        return s

    k.dma("sp", k.dsem("xmem"), out=memf, in_=memT_d, writes=[memf_r[0]])
    emit_rmsnorm(P, cst, nt, memf, memf_r, gmem, gmem_r, lambda tt: (memn, memn_r), ntiles=1, tw=NM)
    for ck in range(16):
        s = wload(wxkv_d[ck])
        ps, pr = P.bank()
        if ck < 8:
            for kc in range(KC):
                k.op("pe", lambda e, ps=ps, s=s, kc=kc: e.matmul(
                    ps[:, 0:NM], lhsT=wch[s][:, kc * 128:(kc + 1) * 128], rhs=memn[:, kc, :],
                    start=(kc == 0), stop=(kc == KC - 1)),
                    reads=[wch_r[s], memn_r], writes=[pr])
            k.op("act", lambda e, ps=ps, ck=ck: e.activation(out=KxT[:, ck, :], in_=ps[:, 0:NM], func=AF.Copy),
                 reads=[pr], writes=[Kx_r[ck]])
        else:
            c = ck - 8
            for mb in range(2):
                for kc in range(KC):
                    k.op("pe", lambda e, ps=ps, s=s, kc=kc, mb=mb: e.matmul(
                        ps[:, mb * 128:(mb + 1) * 128], lhsT=memn[:, kc, mb * 128:(mb + 1) * 128],
                        rhs=wch[s][:, kc * 128:(kc + 1) * 128], start=(kc == 0), stop=(kc == KC - 1)),
                        reads=[wch_r[s], memn_r], writes=[pr])
            k.op("act", lambda e, ps=ps, c=c: e.activation(
                out=Vx[:, :, c * 128:(c + 1) * 128], in_=ps[:, 0:256].rearrange("p (m j) -> p m j", m=2), func=AF.Copy),
                reads=[pr], writes=[Vx_r[c]])
    emit_rmsnorm(P, cst, nt, x, xres, gcross, gcross_r, lambda tt: (xn[:, :, tt * TT:(tt + 1) * TT], xn_r[tt]))
    for ck in range(KC):
        s = wload(wxq_d[ck])
        for tt in range(NT):
            sl = slice(tt * TT, (tt + 1) * TT)
            ps, pr = P.bank()
            for kc in range(KC):
                k.op("pe", lambda e, ps=ps, s=s, kc=kc, sl=sl: e.matmul(
                    ps, lhsT=wch[s][:, kc * 128:(kc + 1) * 128], rhs=xn[:, kc, sl],
                    start=(kc == 0), stop=(kc == KC - 1)),
                    reads=[wch_r[s], xn_r[tt]], writes=[pr])
            if tt % 2 == 0:
                k.op("act", lambda e, ps=ps, ck=ck, sl=sl: e.activation(out=QT[:, ck, sl], in_=ps, func=AF.Copy),
                     reads=[pr], writes=[QT_r[ck][tt]])
            else:
                k.op("dve", lambda e, ps=ps, ck=ck, sl=sl: e.tensor_copy(out=QT[:, ck, sl], in_=ps),
                     reads=[pr], writes=[QT_r[ck][tt]])
    cnt = 0
    for tt in range(NT):
        sl = slice(tt * TT, (tt + 1) * TT)
        for h in range(4):
            pp = cnt % 2
            cnt += 1
            for mb in range(2):
                ps, pr = P.bank()
                for half in range(2):
                    ck = 2 * h + half
                    k.op("pe", lambda e, ps=ps, ck=ck, mb=mb, sl=sl, half=half: e.matmul(
                        ps, lhsT=KxT[:, ck, mb * 128:(mb + 1) * 128], rhs=QT[:, ck, sl],
                        start=(half == 0), stop=(half == 1)),
                        reads=[Kx_r[ck], QT_r[ck][tt]], writes=[pr])
                k.op("act", lambda e, ps=ps, pp=pp, mb=mb: e.activation(out=Pt[pp][mb], in_=ps, func=AF.Exp,
                                                                       scale=0.0625),
                     reads=[pr], writes=[Pt_r[pp][mb]])
            ps, pr = P.bank()
            for mb in range(2):
                k.op("pe", lambda e, ps=ps, pp=pp, mb=mb: e.matmul(ps, lhsT=cst["ones_bf"], rhs=Pt[pp][mb],
                                                                   start=(mb == 0), stop=(mb == 1)),
                     reads=[cst["ones_bf_r"], Pt_r[pp][mb]], writes=[pr])
            k.op("dve", lambda e, ps=ps: e.reciprocal(out=rec, in_=ps), reads=[pr], writes=[rec_r])
            for half in range(2):
                ck = 2 * h + half
                ps, pr = P.bank()
                for mb in range(2):
                    k.op("pe", lambda e, ps=ps, pp=pp, mb=mb, ck=ck: e.matmul(
                        ps, lhsT=Vx[:, mb, ck * 128:(ck + 1) * 128], rhs=Pt[pp][mb], start=(mb == 0), stop=(mb == 1)),
                        reads=[Vx_r[ck], Pt_r[pp][mb]], writes=[pr])
                k.op("dve", lambda e, ps=ps, ck=ck, sl=sl: e.tensor_tensor(out=QT[:, ck, sl], in0=ps, in1=rec, op=ALU.mult),
                     reads=[pr, rec_r], writes=[QT_r[ck][tt]])
    for dco in range(KC):
        s = wload(wxo_d[dco])
        for tt in range(NT):
            sl = slice(tt * TT, (tt + 1) * TT)
            ps, pr = P.bank()
            for ck in range(KC):
                k.op("pe", lambda e, ps=ps, s=s, ck=ck, sl=sl: e.matmul(
                    ps, lhsT=wch[s][:, ck * 128:(ck + 1) * 128], rhs=QT[:, ck, sl],
                    start=(ck == 0), stop=(ck == KC - 1)),
                    reads=[wch_r[s], QT_r[ck][tt]], writes=[pr])
            k.op("dve", lambda e, ps=ps, dco=dco, sl=sl: e.tensor_tensor(out=x[:, dco, sl], in0=ps, in1=x[:, dco, sl],
                                                                         op=ALU.add),
                 reads=[pr, xres[tt]], writes=[xres[tt]])
    ar.release()


def pack_wC(inp, l):
    ua = inp["w_up_a"][l].reshape(4, 128, KC, 128)
    ub = inp["w_up_b"][l].reshape(8, 128, KC, 128)
    uc = inp["w_up_c"][l].reshape(4, 128, KC, 128)
    up = np.concatenate([ua, ub, uc], axis=0)
    up = up.transpose(2, 1, 0, 3).reshape(KC, 128, 2048)
    wg = inp["w_in"][l][:, 4104:].reshape(KC, 128, 3, KC, 128)
    wg = wg.transpose(3, 1, 0, 2, 4).reshape(KC, 128, 3072)
    return np.ascontiguousarray(np.concatenate([up, wg], axis=2))


def pack_bgate(inp, l):
    return np.ascontiguousarray(inp["b_gate"][l].reshape(3, KC, 128).transpose(2, 0, 1).reshape(128, 24))


def pack_sq(w):
    C = w.shape[1] // 128
    a = w.reshape(KC, 128, C, 128)
    return np.ascontiguousarray(a.transpose(2, 1, 0, 3)).reshape(C, 128, KC * 128)


def pack_memT(m):
    return np.ascontiguousarray(m.reshape(256, KC, 128).transpose(2, 1, 0))


ARENA_BYTES = 205 * 1024
RGROUPS = [[0, 1, 2, 3], [4, 5, 6, 7]]


def build(stages, fused=False):
    nc = bass.Bass("TRN2", target_bir_lowering=False)
    dram = {}

    def dten(name, shape, dt, kind):
        if name not in dram:
            dram[name] = nc.dram_tensor(name, list(shape), dt, kind=kind).ap()
        return dram[name]

    def din(name, shape, dt=F32):
        return dten(name, shape, dt, "ExternalInput")

    def dout(name, shape, dt):
        return dten(name, shape, dt, "ExternalOutput")

    def dint(name, shape, dt):
        return dten(name, shape, dt, "Internal")

    with ExitStack() as stack:
        arena_t = stack.enter_context(nc.sbuf_tensor("arena", [128, ARENA_BYTES // 4], F32))
        psum_t = stack.enter_context(nc.psum_tensor("psum", [128, 8, 512], F32))
        k = K(nc, stack)
        ar = Arena(arena_t[:, :], ARENA_BYTES)
        P = Prog(nc, k, ar, psum_t[:, :, :])
        cst = make_consts(P, din("cmask", [128, 128]))
        x = ar.alloc([KC, T], F32)
        xres = [Res() for _ in range(NT)]
        gvs = ar.alloc([16, KC], F32)
        gv_n = [0]
        pid_cache = {}
        gath_res = {}
        uall_res = {}
        mb_res = {}
        ds_g = k.dsem("gv")

        def load_g(name):
            i = gv_n[0] % 16
            gv_n[0] += 1
            r = Res()
            k.dma("sp", ds_g, out=gvs[:, i, :], in_=din(name, [128, KC]), writes=[r])
            return gvs[:, i, :], r

        def sfx(l):
            return "" if l is None else "_%d" % l

        def slab_copies(d, q="sp"):
            for half in range(2):
                gq = dint("mix_gath_%d_%d" % (d, half), [4 * 256, T], BF16)
                for r in range(4):
                    mb = dint("mix_big_%d_%d" % (r, half), [4 * 256, T], BF16)

                    def dstf(e, d=d, mb=mb, q=q):
                        if q not in pid_cache:
                            pid_cache[q] = (nc.sync if q == "sp" else nc.scalar).partition_id() % 4
                        return mb[bass.ds(((d + 4 - pid_cache[q]) % 4) * 256, 256), :]
                    cr = Res()
                    mb_res.setdefault((r, half), []).append(cr)
                    k.dma(q, None, out=dstf, in_=gq[r * 256:(r + 1) * 256, :], reads=[gath_res[(d, half)]], writes=[cr])

        for stg in stages:
            if not (fused and stg[0] == "C"):
                k.barrier(skip_cc=(fused and stg[0] in ("ag_u", "B", "ag_mix")))
            if stg == "load_x":
                xd = din("xT", [128, KC, T])
                for tt in range(NT):
                    sl = slice(tt * TT, (tt + 1) * TT)
                    k.dma("sp", k.dsem("x"), out=x[:, :, sl], in_=xd[:, :, sl], writes=[xres[tt]])
            elif stg == "store_x":
                xo = dout("xT_out", [128, KC, T], F32)
                for tt in range(NT):
                    sl = slice(tt * TT, (tt + 1) * TT)
                    k.dma("sp", k.dsem("xo"), out=xo[:, :, sl], in_=x[:, :, sl], reads=[xres[tt]])
            elif stg[0] == "ffn":
                _, l, i = stg
                ar.mark()
                gv, gr = load_g("gffn%s_%d" % (sfx(l), i))
                xn = ar.alloc([KC, T], BF16); xn_r = [Res() for _ in range(NT)]
                nt = alloc_normtmp(P)
                w1buf = [ar.alloc([KC * 256], BF16) for _ in range(3)]; w1res = [Res(), Res(), Res()]
                w2buf = [ar.alloc([NF * 128], BF16) for _ in range(3)]; w2res = [Res(), Res(), Res()]
                g = ar.alloc([NF, 2 * TT], BF16); gres2 = [[Res(), Res()] for _ in range(NF)]
                stmp = [ar.alloc([TT], F32) for _ in range(2)]; stres = [Res(), Res()]
                emit_rmsnorm(P, cst, nt, x, xres, gv, gr, lambda tt: (xn[:, :, tt * TT:(tt + 1) * TT], xn_r[tt]))
                emit_ffn(P, x, xres, xn, xn_r, din("w1%s_%d" % (sfx(l), i), [NF, 128, KC * 256]),
                         din("w2%s_%d" % (sfx(l), i), [KC, 128, NF * 128]),
                         (w1buf, w1res, w2buf, w2res, g, gres2, stmp, stres), "f")
                ar.release()
            elif stg[0] == "unorm":
                _, l = stg
                ar.mark()
                gv, gr = load_g("gmix%s" % sfx(l))
                xn = ar.alloc([KC, T], BF16); xn_r = [Res() for _ in range(NT)]
                nt = alloc_normtmp(P)
                dsu = k.dsem("ust")
                if fused:
                    def after(tt, o, ores):
                        um = dint("u_mine_%d" % tt, [128, KC * TT], BF16)
                        ua = dint("u_all_%d" % tt, [4 * 128, KC * TT], BF16)
                        ur = Res()
                        k.dma("sp", dsu, out=um.rearrange("p (k t) -> p k t", k=KC), in_=o, reads=[ores], writes=[ur])
                        uall_res[tt] = Res()
                        k.coll("AllGather", RGROUPS, ins=[um], outs=[ua], reads=[ur], writes=[uall_res[tt]])
                else:
                    ud = dout("u_mine%s" % sfx(l), [128, KC, T], BF16)

                    def after(tt, o, ores, ud=ud, dsu=dsu):
                        sl = slice(tt * TT, (tt + 1) * TT)
                        k.dma("sp", dsu, out=ud[:, :, sl], in_=o, reads=[ores], writes=[])

                emit_rmsnorm(P, cst, nt, x, xres, gv, gr, lambda tt: (xn[:, :, tt * TT:(tt + 1) * TT], xn_r[tt]),
                             after=after)
                ar.release()
            elif stg[0] == "B":
                _, l = stg
                if fused:
                    def usrc(r, tt):
                        ua = dint("u_all_%d" % tt, [4 * 128, KC * TT], BF16)
                        return ua[r * 128:(r + 1) * 128, :].rearrange("p (k t) -> p k t", k=KC)

                    def mm(d, half):
                        return dint("mix_mine_%d_%d" % (d, half), [256, T], BF16)

                    def mdst(kind, r, tsl, h=0):
                        if kind == "A":
                            return [(mm(r, 0)[0:128, tsl], slice(0, 128))]
                        if kind == "B":
                            return [(mm(r, 0)[128:256, tsl], slice(0, 128), 0), (mm(r, 1)[0:128, tsl], slice(0, 128), 1)]
                        return [(mm(r, 1)[128 + 64 * h:192 + 64 * h, tsl], slice(0, 64))]

                    def gather_half(d, half, store_res):
                        gq = dint("mix_gath_%d_%d" % (d, half), [4 * 256, T], BF16)
                        gres = Res()
                        gath_res[(d, half)] = gres
                        k.coll("AllGather", RGROUPS, ins=[mm(d, half)], outs=[gq], reads=list(store_res[half]),
                               writes=[gres])
                        del store_res[half][:]

                    def mid_hook(gt, store_res):
                        if gt == NGT - 1:
                            gather_half(3, 0, store_res)

                    def tile_done(gt, store_res):
                        if gt % 4 == 3:
                            d = gt // 4
                            for half in range(2):
                                if (d, half) != (3, 0):
                                    gather_half(d, half, store_res)
                        if gt % 4 == 1 and gt >= 5:
                            slab_copies(gt // 4 - 1, "act")
                else:
                    u_all = din("u_all%s" % sfx(l), [4, 128, KC, T], BF16)
                    mix = dout("mix_mine%s" % sfx(l), [4, 512, T], BF16)

                    def usrc(r, tt, u_all=u_all):
                        return u_all[r, :, :, tt * TT:(tt + 1) * TT]

                    def mdst(kind, r, tsl, h=0, mix=mix):
                        if kind == "A":
                            return [(mix[r, 0:128, tsl], slice(0, 128))]
                        if kind == "B":
                            return [(mix[r, 128 + 128 * hh:256 + 128 * hh, tsl], slice(0, 128), hh) for hh in range(2)]
                        return [(mix[r, 384 + 64 * h:448 + 64 * h, tsl], slice(0, 64))]
                emit_phaseB(P, cst, usrc, din("wB%s" % sfx(l), [128, KC, NCB]), din("wsB%s" % sfx(l), [128, 640]),
                            din("vecB%s" % sfx(l), [128, NVB]), mdst, tile_done if fused else None,
                            (lambda tt: [uall_res[tt]]) if fused else None, mid_hook if fused else None)
            elif stg[0] == "C":
                _, l = stg
                if fused:
                    def umine(tt):
                        return dint("u_mine_%d" % tt, [128, KC * TT], BF16).rearrange("p (k t) -> p k t", k=KC)

                    def mix_all(r, piece, cs):
                        mb = dint("mix_big_%d_%d" % (r, piece // 2), [4 * 256, T], BF16)
                        return mb[128 * (piece % 2):128 * (piece % 2) + 128, cs]
                else:
                    ud = din("u_mine%s" % sfx(l), [128, KC, T], BF16)
                    mix_all = din("mix_all%s" % sfx(l), [4, 512, T], BF16)

                    def umine(tt, ud=ud):
                        return ud[:, :, tt * TT:(tt + 1) * TT]
                emit_phaseC(P, cst, x, xres, umine, mix_all, din("wC%s" % sfx(l), [KC, 128, 5120]),
                            din("bgate%s" % sfx(l), [128, 24]), din("wo%s" % sfx(l), [KC, 128, 1024]),
                            (lambda r, piece: list(mb_res[(r, piece // 2)])) if fused else None,
                            (lambda: slab_copies(3, "sp")) if fused else None)
                if fused:
                    mb_res.clear()
            elif stg[0] == "cross":
                _, l = stg
                gm, gmr = load_g("gmem%s" % sfx(l))
                gc, gcr = load_g("gcross%s" % sfx(l))
                emit_cross(P, cst, x, xres, din("memT", [128, KC, 256]), gm, gmr, gc, gcr,
                           din("wxq%s" % sfx(l), [KC, 128, 1024]), din("wxkv%s" % sfx(l), [16, 128, 1024]),
                           din("wxo%s" % sfx(l), [KC, 128, 1024]))
            elif stg[0] == "ag_u":
                pass
            elif stg[0] == "ag_mix":
                pass
            elif stg == "final":
                ar.mark()
                gv, gr = load_g("gfin")
                nt = alloc_normtmp(P)
                stg_t = [ar.alloc([KC, TT], F32) for _ in range(2)]; stg_r = [Res(), Res()]
                od = dout("outT", [128, KC, T], F32)
                dso = k.dsem("out")

                def after(tt, o, ores, od=od, dso=dso):
                    sl = slice(tt * TT, (tt + 1) * TT)
                    k.dma("sp", dso, out=od[:, :, sl], in_=o, reads=[ores], writes=[])

                emit_rmsnorm(P, cst, nt, x, xres, gv, gr, lambda tt: (stg_t[tt % 2], stg_r[tt % 2]), after=after)
                ar.release()
            else:
                raise ValueError(stg)
        k.barrier()
        k.final_wait("sp")
        k.emit()
    return nc


_PROGS = {}


def get_prog(key, stages, fused=False):
    if key not in _PROGS:
        _PROGS[key] = build(stages, fused)
    return _PROGS[key]


def _run(nc, maps):
    res = run_bass_kernel_spmd(nc, maps, core_ids=list(range(NCORES)))
    return res.results


def _weights_ffn(inp, l, i, names=None):
    sfx = "_%d_%d" % (l, i)
    if i == 1:
        w_in, w_out, g = inp["w_ffn1_in"], inp["w_ffn1_out"], inp["g_ffn1"]
    else:
        w_in, w_out, g = inp["w_ffn2_in"], inp["w_ffn2_out"], inp["g_ffn2"]
    return {"w1" + sfx: pack_w1(w_in[l]), "w2" + sfx: pack_w2(w_out[l]), "gffn" + sfx: pack_vec(g[l])}


def _weights_C(inp, l):
    s = "_%d" % l
    return {"wC" + s: pack_wC(inp, l), "bgate" + s: pack_bgate(inp, l), "wo" + s: pack_sq(inp["w_o"][l]),
            "gmem" + s: pack_vec(inp["g_mem"][l]), "gcross" + s: pack_vec(inp["g_cross"][l]),
            "wxq" + s: pack_sq(inp["w_xq"][l]), "wxkv" + s: pack_sq(inp["w_xkv"][l]), "wxo" + s: pack_sq(inp["w_xo"][l])}


FUSED = True


def fused_stages():
    st = ["load_x"]
    for l in range(L):
        st += [("ffn", l, 1), ("unorm", l), ("ag_u", l), ("B", l), ("ag_mix", l), ("C", l), ("cross", l), ("ffn", l, 2)]
    st.append("final")
    return st


def kernel_fused(inp):
    cmask = const_mask()
    xs = inp["x"].reshape(NCORES, T, D)
    w = {"cmask": cmask, "gfin": pack_vec(inp["g_final"])}
    wBs = {}
    for l in range(L):
        w.update(_weights_ffn(inp, l, 1))
        w.update(_weights_ffn(inp, l, 2))
        w.update(_weights_C(inp, l))
        w["gmix_%d" % l] = pack_vec(inp["g_mix"][l])
        for j in range(4):
            wBs[(l, j)] = {"wB_%d" % l: pack_wB(inp["w_in"][l], j), "wsB_%d" % l: pack_wsB(inp, l, j),
                           "vecB_%d" % l: pack_vecB(inp, l, j)}
    memT = [pack_memT(inp["mem"][b]) for b in range(2)]
    maps = []
    for c in range(NCORES):
        b, j = c // 4, c % 4
        m = dict(w)
        for l in range(L):
            m.update(wBs[(l, j)])
        m["xT"] = pack_xT(xs[c])
        m["memT"] = memT[b]
        maps.append(m)
    prog = get_prog("fused", fused_stages(), fused=True)
    r = _run(prog, maps)
    out = np.stack([unpack_xT(r[c]["outT"]) for c in range(NCORES)], axis=0)
    return out.reshape(2, 8192, D).astype(np.float32)


def kernel(**inp):
    inp = {k_: np.asarray(v) for k_, v in inp.items()}
    if FUSED:
        return kernel_fused(inp)
    cmask = const_mask()
    xs = inp["x"].reshape(NCORES, T, D)
    p1 = get_prog("p1", ["load_x", ("ffn", 0, 1), ("unorm", 0), "store_x"])
    w = {"cmask": cmask, "gmix_0": pack_vec(inp["g_mix"][0])}
    w.update(_weights_ffn(inp, 0, 1))
    r = _run(p1, [dict(w, xT=pack_xT(xs[c])) for c in range(NCORES)])
    xT = [r[c]["xT_out"] for c in range(NCORES)]
    um = [r[c]["u_mine_0"] for c in range(NCORES)]
    out = None
    for l in range(L):
        pB = get_prog("pB", [("B", None)])
        maps = []
        for c in range(NCORES):
            b, j = c // 4, c % 4
            maps.append({"cmask": cmask, "u_all": np.stack([um[4 * b + rr] for rr in range(4)], axis=0),
                         "wB": pack_wB(inp["w_in"][l], j), "wsB": pack_wsB(inp, l, j), "vecB": pack_vecB(inp, l, j)})
        r = _run(pB, maps)
        mix = [r[c]["mix_mine"] for c in range(NCORES)]
        w = {"cmask": cmask}
        w.update(_weights_C(inp, l))
        w.update(_weights_ffn(inp, l, 2))
        if l + 1 < L:
            stages = ["load_x", ("C", l), ("cross", l), ("ffn", l, 2), ("ffn", l + 1, 1), ("unorm", l + 1), "store_x"]
            w.update(_weights_ffn(inp, l + 1, 1))
            w["gmix_%d" % (l + 1)] = pack_vec(inp["g_mix"][l + 1])
        else:
            stages = ["load_x", ("C", l), ("cross", l), ("ffn", l, 2), "final"]
            w["gfin"] = pack_vec(inp["g_final"])
        pC = get_prog("pC%d" % l, stages)
        maps = []
        for c in range(NCORES):
            b, cc = c // 4, c % 4
            m = dict(w)
            m["xT"] = xT[c]
            m["u_mine_%d" % l] = um[c]
            m["mix_all_%d" % l] = np.stack([mix[4 * b + j][cc] for j in range(4)], axis=0)
            m["memT"] = pack_memT(inp["mem"][b])
            maps.append(m)
        r = _run(pC, maps)
        if l + 1 < L:
            xT = [r[c]["xT_out"] for c in range(NCORES)]
            um = [r[c]["u_mine_%d" % (l + 1)] for c in range(NCORES)]
        else:
            out = np.stack([unpack_xT(r[c]["outT"]) for c in range(NCORES)], axis=0)
    return out.reshape(2, 8192, D).astype(np.float32)
```
